# Optimizing a Trainium2 kernel written in Bass

```python
import math
import jax, jax.numpy as jnp
from jax import lax
import numpy as np

D_MODEL = 1024
BATCH = 8
SEQ = 4096
DEPTH = 4

N_MIXERS = 3
N_LAYERS_RG = (DEPTH + 2) // N_MIXERS
N_LAYERS_SSD = (DEPTH + 1) // N_MIXERS
N_LAYERS_S5 = DEPTH // N_MIXERS
RMS_EPS = 1e-6
CONV_WIDTH = 4
D_FF = 4 * D_MODEL

RG_WIDTH = D_MODEL
RG_HEADS = 16
RG_HEAD_DIM = RG_WIDTH // RG_HEADS
RG_C = 8.0

SSD_INNER = 2 * D_MODEL
SSD_HEAD_DIM = 64
SSD_HEADS = SSD_INNER // SSD_HEAD_DIM
SSD_GROUPS = 8
SSD_HPG = SSD_HEADS // SSD_GROUPS
SSD_STATE = 128
SSD_CHUNK = 128
SSD_CONV_DIM = SSD_INNER + 2 * SSD_GROUPS * SSD_STATE
SSD_IN_DIM = SSD_INNER + SSD_CONV_DIM + SSD_HEADS
SSD_NORM_GROUP = SSD_INNER // SSD_GROUPS

S5_WIDTH = D_MODEL
S5_GROUP = 16
S5_GROUPS = S5_WIDTH // S5_GROUP
S5_STATE = 64
S5_EIG_CLIP = -1e-4

kernel_name = "hybrid_rglru_ssd_s5_trunk"

F32 = jnp.float32


def rms_norm(x, g):
    xf = x.astype(F32)
    y = xf * lax.rsqrt(jnp.mean(xf * xf, axis=-1, keepdims=True) + RMS_EPS)
    return (y * g.astype(F32)).astype(x.dtype)


def causal_depthwise_conv(x, w, b):
    ch = x.shape[-1]
    y = lax.conv_general_dilated(
        x, w[:, None, :].astype(x.dtype), window_strides=(1,),
        padding=[(CONV_WIDTH - 1, 0)], dimension_numbers=("NWC", "WIO", "NWC"),
        feature_group_count=ch)
    return y + b.astype(x.dtype)


def _real_combine(e1, e2):
    a1, b1 = e1
    a2, b2 = e2
    return a1 * a2, a2 * b1 + b2


def _complex_combine(e1, e2):
    a1r, a1i, b1r, b1i = e1
    a2r, a2i, b2r, b2i = e2
    return (a1r * a2r - a1i * a2i,
            a1r * a2i + a1i * a2r,
            a2r * b1r - a2i * b1i + b2r,
            a2r * b1i + a2i * b1r + b2i)


def segsum_from_cumsum(cs):
    t = cs.shape[-1]
    diff = cs[..., :, None] - cs[..., None, :]
    mask = jnp.tril(jnp.ones((t, t), dtype=bool))
    return jnp.where(mask, diff, -jnp.inf)


def rglru_mixer(x, w_in, conv_w, conv_b, w_a, b_a, w_x, b_x, lam, w_out):
    bsz, seq, _ = x.shape
    proj = x @ w_in
    gate_branch, u = jnp.split(proj, 2, axis=-1)
    u = causal_depthwise_conv(u, conv_w, conv_b)
    ub = u.reshape(bsz, seq, RG_HEADS, RG_HEAD_DIM)
    r = jax.nn.sigmoid(jnp.einsum("blhi,hij->blhj", ub, w_a).reshape(bsz, seq, RG_WIDTH) + b_a)
    i = jax.nn.sigmoid(jnp.einsum("blhi,hij->blhj", ub, w_x).reshape(bsz, seq, RG_WIDTH) + b_x)
    log_a = (-RG_C * r.astype(F32)) * jax.nn.softplus(-lam.astype(F32))
    a = jnp.exp(log_a)
    mult = jnp.sqrt(-jnp.expm1(2.0 * log_a))
    _, h = lax.associative_scan(_real_combine, (a, mult * (i * u).astype(F32)), axis=1)
    y = h.astype(x.dtype) * jax.nn.gelu(gate_branch)
    return y @ w_out


def gated_group_rms_norm(y, z, g):
    v = (y * jax.nn.silu(z)).astype(F32)
    shp = v.shape
    vg = v.reshape(shp[:-1] + (SSD_GROUPS, SSD_NORM_GROUP))
    vg = vg * lax.rsqrt(jnp.mean(vg * vg, axis=-1, keepdims=True) + RMS_EPS)
    return vg.reshape(shp) * g.astype(F32)


def ssd_mixer(x, w_in, conv_w, conv_b, dt_bias, a_log, d_skip, norm_g, w_out):
    bsz, seq, _ = x.shape
    nc = seq // SSD_CHUNK
    proj = x @ w_in
    z, xbc, dt_raw = jnp.split(proj, [SSD_INNER, SSD_INNER + SSD_CONV_DIM], axis=-1)
    xbc = jax.nn.silu(causal_depthwise_conv(xbc, conv_w, conv_b))
    xs, bm, cm = jnp.split(xbc, [SSD_INNER, SSD_INNER + SSD_GROUPS * SSD_STATE], axis=-1)
    dt = jax.nn.softplus((dt_raw + dt_bias).astype(F32))
    a = -jnp.exp(a_log.astype(F32))
    chunk_shape = (bsz, nc, SSD_CHUNK, SSD_GROUPS, SSD_HPG)
    xh = xs.reshape(chunk_shape + (SSD_HEAD_DIM,))
    dtc = dt.reshape(chunk_shape)
    xdt = xh * dtc[..., None]
    bc = bm.reshape(bsz, nc, SSD_CHUNK, SSD_GROUPS, SSD_STATE)
    cc = cm.reshape(bsz, nc, SSD_CHUNK, SSD_GROUPS, SSD_STATE)
    a_cs = jnp.cumsum(dtc * a.reshape(SSD_GROUPS, SSD_HPG), axis=2)
    lmat = jnp.exp(segsum_from_cumsum(a_cs.transpose(0, 3, 4, 1, 2)))
    cb = jnp.einsum("bclgn,bcsgn->bgcls", cc, bc)
    y_diag = jnp.einsum("bgrcls,bcsgrp->bclgrp", cb[:, :, None] * lmat, xdt)
    decay_states = jnp.exp(a_cs[:, :, -1:] - a_cs)
    states = jnp.einsum("bclgn,bclgrp->bcgrpn", bc, xdt * decay_states[..., None])
    a_tot = jnp.pad(a_cs[:, :, -1].transpose(0, 2, 3, 1), ((0, 0), (0, 0), (0, 0), (1, 0)))
    decay_chunk = jnp.exp(segsum_from_cumsum(jnp.cumsum(a_tot, axis=-1)))
    states_pad = jnp.pad(states, ((0, 0), (1, 0), (0, 0), (0, 0), (0, 0), (0, 0)))
    entering = jnp.einsum("bgrzc,bcgrpn->bzgrpn", decay_chunk, states_pad)[:, :nc]
    y_off = jnp.einsum("bclgn,bcgrpn->bclgrp", cc, entering) * jnp.exp(a_cs)[..., None]
    y = y_diag + y_off + xh * d_skip.reshape(SSD_GROUPS, SSD_HPG, 1)
    y = y.reshape(bsz, seq, SSD_INNER)
    y = gated_group_rms_norm(y, z, norm_g).astype(x.dtype)
    return y @ w_out


def s5_mixer(x, w_in, lam_re, lam_im, log_step, b_re, b_im, c_re, c_im, d_skip, w_out):
    bsz, seq, _ = x.shape
    u = x @ w_in
    lr = jnp.minimum(lam_re.astype(F32), S5_EIG_CLIP)
    li = lam_im.astype(F32)
    step = jnp.exp(log_step.astype(F32))[:, None]
    mag = jnp.exp(lr * step)
    ab_re = mag * jnp.cos(li * step)
    ab_im = mag * jnp.sin(li * step)
    den = lr * lr + li * li
    nr = ab_re - 1.0
    q_re = (nr * lr + ab_im * li) / den
    q_im = (ab_im * lr - nr * li) / den
    bb_re = q_re[..., None] * b_re - q_im[..., None] * b_im
    bb_im = q_re[..., None] * b_im + q_im[..., None] * b_re
    ug = u.reshape(bsz, seq, S5_GROUPS, S5_GROUP).astype(F32)
    bu_re = jnp.einsum("blgc,gpc->blgp", ug, bb_re)
    bu_im = jnp.einsum("blgc,gpc->blgp", ug, bb_im)
    a_re = jnp.broadcast_to(ab_re, (1, seq, S5_GROUPS, S5_STATE))
    a_im = jnp.broadcast_to(ab_im, (1, seq, S5_GROUPS, S5_STATE))
    _, _, h_re, h_im = lax.associative_scan(_complex_combine, (a_re, a_im, bu_re, bu_im), axis=1)
    y = jnp.einsum("blgp,gcp->blgc", h_re, c_re) - jnp.einsum("blgp,gcp->blgc", h_im, c_im)
    y = y.reshape(bsz, seq, S5_WIDTH) + d_skip * ug.reshape(bsz, seq, S5_WIDTH)
    y = jax.nn.gelu(y).astype(x.dtype)
    val, gate = jnp.split(y @ w_out, 2, axis=-1)
    return val * jax.nn.sigmoid(gate)


def sqrelu_mlp(x, w1, w2):
    return jnp.square(jax.nn.relu(x @ w1)) @ w2


def _normal(key, shape, scale):
    return jax.random.normal(key, shape, F32) * scale


def _log_uniform(key, shape, lo, hi):
    return jnp.exp(jax.random.uniform(key, shape, F32, math.log(lo), math.log(hi)))


def setup_inputs(seed: int = 0) -> dict:
    key = jax.random.key(seed)
    ks = list(jax.random.split(key, 40))
    nrg, nssd, ns5 = N_LAYERS_RG, N_LAYERS_SSD, N_LAYERS_S5

    x = _normal(ks[0], (BATCH, SEQ, D_MODEL), 1.0)
    norm_g = 1.0 + _normal(ks[1], (DEPTH, 4, D_MODEL), 0.02)
    mlp_w1 = _normal(ks[2], (DEPTH, D_MODEL, D_FF), D_MODEL ** -0.5)
    mlp_w2 = _normal(ks[3], (DEPTH, D_FF, D_MODEL), D_FF ** -0.5)

    rg_w_in = _normal(ks[4], (nrg, D_MODEL, 2 * RG_WIDTH), D_MODEL ** -0.5)
    rg_conv_w = _normal(ks[5], (nrg, CONV_WIDTH, RG_WIDTH), CONV_WIDTH ** -0.5)
    rg_conv_b = _normal(ks[6], (nrg, RG_WIDTH), 0.01)
    rg_w_a = _normal(ks[7], (nrg, RG_HEADS, RG_HEAD_DIM, RG_HEAD_DIM), RG_HEAD_DIM ** -0.5)
    rg_b_a = _normal(ks[8], (nrg, RG_WIDTH), 0.01)
    rg_w_x = _normal(ks[9], (nrg, RG_HEADS, RG_HEAD_DIM, RG_HEAD_DIM), RG_HEAD_DIM ** -0.5)
    rg_b_x = _normal(ks[10], (nrg, RG_WIDTH), 0.01)
    a_c = jax.random.uniform(ks[11], (nrg, RG_WIDTH), F32, 0.9, 0.999)
    s = a_c ** (1.0 / RG_C)
    rg_lam = jnp.log(s) - jnp.log1p(-s)
    rg_w_out = _normal(ks[12], (nrg, RG_WIDTH, D_MODEL), RG_WIDTH ** -0.5)

    ssd_w_in = _normal(ks[13], (nssd, D_MODEL, SSD_IN_DIM), D_MODEL ** -0.5)
    ssd_conv_w = _normal(ks[14], (nssd, CONV_WIDTH, SSD_CONV_DIM), CONV_WIDTH ** -0.5)
    ssd_conv_b = _normal(ks[15], (nssd, SSD_CONV_DIM), 0.01)
    dt0 = jnp.maximum(_log_uniform(ks[16], (nssd, SSD_HEADS), 1e-3, 1e-1), 1e-4)
    ssd_dt_bias = dt0 + jnp.log(-jnp.expm1(-dt0))
    ssd_a_log = jnp.log(jax.random.uniform(ks[17], (nssd, SSD_HEADS), F32, 1.0, 16.0))
    ssd_d = 1.0 + _normal(ks[18], (nssd, SSD_HEADS), 0.01)
    ssd_norm_g = 1.0 + _normal(ks[19], (nssd, SSD_INNER), 0.02)
    ssd_w_out = _normal(ks[20], (nssd, SSD_INNER, D_MODEL), SSD_INNER ** -0.5)

    s5_w_in = _normal(ks[21], (ns5, D_MODEL, S5_WIDTH), D_MODEL ** -0.5)
    s5_lam_re = -0.5 + _normal(ks[22], (ns5, S5_GROUPS, S5_STATE), 0.01)
    n_idx = jnp.pi * jnp.arange(S5_STATE, dtype=F32)
    s5_lam_im = n_idx + _normal(ks[23], (ns5, S5_GROUPS, S5_STATE), 0.01)
    s5_log_step = jnp.log(_log_uniform(ks[24], (ns5, S5_GROUPS), 1e-3, 1e-1))
    bsc = (2.0 * S5_GROUP) ** -0.5
    s5_b_re = _normal(ks[25], (ns5, S5_GROUPS, S5_STATE, S5_GROUP), bsc)
    s5_b_im = _normal(ks[26], (ns5, S5_GROUPS, S5_STATE, S5_GROUP), bsc)
    csc = (2.0 * S5_STATE) ** -0.5
    s5_c_re = _normal(ks[27], (ns5, S5_GROUPS, S5_GROUP, S5_STATE), csc)
    s5_c_im = _normal(ks[28], (ns5, S5_GROUPS, S5_GROUP, S5_STATE), csc)
    s5_d = _normal(ks[29], (ns5, S5_WIDTH), 1.0)
    s5_w_out = _normal(ks[30], (ns5, S5_WIDTH, 2 * D_MODEL), S5_WIDTH ** -0.5)

    return {
        "x": x, "norm_g": norm_g, "mlp_w1": mlp_w1, "mlp_w2": mlp_w2,
        "rg_w_in": rg_w_in, "rg_conv_w": rg_conv_w, "rg_conv_b": rg_conv_b,
        "rg_w_a": rg_w_a, "rg_b_a": rg_b_a, "rg_w_x": rg_w_x, "rg_b_x": rg_b_x,
        "rg_lam": rg_lam, "rg_w_out": rg_w_out,
        "ssd_w_in": ssd_w_in, "ssd_conv_w": ssd_conv_w, "ssd_conv_b": ssd_conv_b,
        "ssd_dt_bias": ssd_dt_bias, "ssd_a_log": ssd_a_log, "ssd_d": ssd_d,
        "ssd_norm_g": ssd_norm_g, "ssd_w_out": ssd_w_out,
        "s5_w_in": s5_w_in, "s5_lam_re": s5_lam_re, "s5_lam_im": s5_lam_im,
        "s5_log_step": s5_log_step, "s5_b_re": s5_b_re, "s5_b_im": s5_b_im,
        "s5_c_re": s5_c_re, "s5_c_im": s5_c_im, "s5_d": s5_d, "s5_w_out": s5_w_out,
    }


def reference(x, norm_g, mlp_w1, mlp_w2,
              rg_w_in, rg_conv_w, rg_conv_b, rg_w_a, rg_b_a, rg_w_x, rg_b_x, rg_lam, rg_w_out,
              ssd_w_in, ssd_conv_w, ssd_conv_b, ssd_dt_bias, ssd_a_log, ssd_d, ssd_norm_g, ssd_w_out,
              s5_w_in, s5_lam_re, s5_lam_im, s5_log_step, s5_b_re, s5_b_im, s5_c_re, s5_c_im,
              s5_d, s5_w_out):
    for layer in range(DEPTH):
        kind = layer % N_MIXERS
        j = layer // N_MIXERS
        g = norm_g[layer]
        h = rms_norm(x, g[0])
        if kind == 0:
            h = rglru_mixer(h, rg_w_in[j], rg_conv_w[j], rg_conv_b[j], rg_w_a[j], rg_b_a[j],
                            rg_w_x[j], rg_b_x[j], rg_lam[j], rg_w_out[j])
        elif kind == 1:
            h = ssd_mixer(h, ssd_w_in[j], ssd_conv_w[j], ssd_conv_b[j], ssd_dt_bias[j],
                          ssd_a_log[j], ssd_d[j], ssd_norm_g[j], ssd_w_out[j])
        else:
            h = s5_mixer(h, s5_w_in[j], s5_lam_re[j], s5_lam_im[j], s5_log_step[j],
                         s5_b_re[j], s5_b_im[j], s5_c_re[j], s5_c_im[j], s5_d[j], s5_w_out[j])
        x = x + rms_norm(h, g[1])
        h = sqrelu_mlp(rms_norm(x, g[2]), mlp_w1[layer], mlp_w2[layer])
        x = x + rms_norm(h, g[3])
    return x
```

```python
import numpy as np
import concourse.bass as bass
import concourse.mybir as mybir
from concourse.bass_utils import run_bass_kernel_spmd

F32 = mybir.dt.float32
BF16 = mybir.dt.bfloat16
AF = mybir.ActivationFunctionType
ALU = mybir.AluOpType

D = 1024
SEQ = 4096
NCORES = 8
EPS = 1e-6


class Res:
    __slots__ = ("w", "r")

    def __init__(self):
        self.w = None
        self.r = {}


class Prog:
    ENGS = ("pe", "act", "dve", "pool", "sp")

    def __init__(self, nc, n_dma_sems=40, same_engine_sync=("act", "dve", "pool")):
        self.nc = nc
        self.q = {e: [] for e in self.ENGS}
        self.cnt = {e: 0 for e in self.ENGS}
        self.sems = {e: nc.alloc_semaphore("sem_" + e) for e in self.ENGS}
        self.n_dma = n_dma_sems
        for i in range(n_dma_sems):
            self.sems[("d", i)] = nc.alloc_semaphore("sem_d%d" % i)
        self.dma_tot = [0] * n_dma_sems
        self.dma_i = 0
        self.waited = {e: {} for e in self.ENGS}
        self.ses = set(same_engine_sync)
        self.ninst = 0

    def _deps(self, eng, reads, writes, extra=()):
        deps = {}

        def need(tok):
            if tok is None:
                return
            k, v = tok
            if k == eng and eng not in self.ses:
                return
            if deps.get(k, 0) < v:
                deps[k] = v

        for r in reads:
            need(r.w)
        for w in writes:
            need(w.w)
            for k, v in w.r.items():
                need((k, v))
        for tok in extra:
            need(tok)
        wd = self.waited[eng]
        waits = []
        for k, v in deps.items():
            if wd.get(k, 0) < v:
                wd[k] = v
                waits.append((self.sems[k], v))
        return waits

    def _mark(self, tok, reads, writes):
        k, v = tok
        for r in reads:
            if r.r.get(k, 0) < v:
                r.r[k] = v
        for w in writes:
            w.w = tok
            w.r = {}

    def op(self, eng, fn, reads=(), writes=(), inc=True):
        waits = self._deps(eng, reads, writes)
        tok = (eng, self.cnt[eng] + 1)
        sem = self.sems[eng]
        if inc:
            self.cnt[eng] += 1

            def closure(e):
                for s, v in waits:
                    e.wait_ge(s, v)
                fn(e).then_inc(sem, 1)
        else:
            def closure(e):
                for s, v in waits:
                    e.wait_ge(s, v)
                fn(e)
        self.q[eng].append(closure)
        self._mark(tok, reads, writes)
        self.ninst += 1
        return tok

    def dma(self, eng, out, in_, reads=(), writes=(), **kw):
        idx = self.dma_i % self.n_dma
        self.dma_i += 1
        key = ("d", idx)
        extra = ()
        if self.dma_tot[idx] > 0:
            extra = ((key, 16 * self.dma_tot[idx]),)
        waits = self._deps(eng, reads, writes, extra)
        self.dma_tot[idx] += 1
        tok = (key, 16 * self.dma_tot[idx])
        sem = self.sems[key]

        def closure(e):
            for s, v in waits:
                e.wait_ge(s, v)
            e.dma_start(out=out, in_=in_, **kw).then_inc(sem, 16)
        self.q[eng].append(closure)
        self._mark(tok, reads, writes)
        self.ninst += 1
        return tok

    def all_tokens(self):
        toks = [(e, self.cnt[e]) for e in self.ENGS if self.cnt[e] > 0]
        toks += [(("d", i), 16 * n) for i, n in enumerate(self.dma_tot) if n > 0]
        return toks

    def barrier(self, engs=None):
        toks = self.all_tokens()
        for eng in (engs or self.ENGS):
            wd = self.waited[eng]
            ws = []
            for k, v in toks:
                if k == eng:
                    continue
                if wd.get(k, 0) < v:
                    wd[k] = v
                    ws.append((self.sems[k], v))
            if ws:
                def closure(e, ws=ws):
                    for s, v in ws:
                        e.wait_ge(s, v)
                self.q[eng].append(closure)

    def emit(self):
        nc = self.nc
        q = self.q
        with nc.Block() as block:
            @block.tensor
            def _(e):
                for f in q["pe"]:
                    f(e)

            @block.scalar
            def _(e):
                for f in q["act"]:
                    f(e)

            @block.vector
            def _(e):
                for f in q["dve"]:
                    f(e)

            @block.gpsimd
            def _(e):
                for f in q["pool"]:
                    f(e)

            @block.sync
            def _(e):
                for f in q["sp"]:
                    f(e)


class Arena:
    def __init__(self, nc, nbytes=212800):
        self.t = nc.alloc_sbuf_tensor("arena", [128, nbytes // 2], BF16)
        self.nbytes = nbytes
        self.off = 0

    def alloc(self, free, dtype):
        if isinstance(free, int):
            free = (free,)
        n = 1
        for s in free:
            n *= s
        sz = n * (4 if dtype == F32 else 2)
        off = (self.off + 63) // 64 * 64
        assert off + sz <= self.nbytes, ("SBUF arena overflow", off, sz, self.nbytes)
        self.off = off + sz
        ap = self.t[:, off // 2:(off + sz) // 2]
        if dtype == F32:
            ap = ap.bitcast(F32)
        if len(free) == 2:
            ap = ap.rearrange("p (a b) -> p a b", a=free[0])
        elif len(free) == 3:
            ap = ap.rearrange("p (a b c) -> p a b c", a=free[0], b=free[1])
        return ap


class DramX:
    def __init__(self, ap, L):
        self.ap = ap
        self.res = [Res() for _ in range(L // 256)]

    def blocks(self, t0, T):
        return self.res[t0 // 256:(t0 + T) // 256]


class Builder:
    def __init__(self, L):
        self.L = L
        nc = self.nc = bass.Bass("TRN2", target_bir_lowering=False)
        self.P = Prog(nc)
        self.A = Arena(nc)
        self.bank = [nc.alloc_psum_tensor("bank%d" % i, [128, 512], F32) for i in range(8)]
        self.bres = [Res() for _ in range(8)]
        self.ext = {}
        A = self.A
        self.ones = A.alloc(128, BF16)
        self.g_sb = A.alloc(128, F32)
        self.epsc = A.alloc(8, F32)
        self.persist_end = A.off
        P = self.P
        r = Res()
        P.op("pool", lambda e: e.memset(self.ones, 1.0), writes=[r])
        P.op("pool", lambda e: e.memset(self.epsc[:, 0:1], EPS), writes=[r])
        P.op("pool", lambda e: e.memset(self.epsc[:, 1:2], 1.0), writes=[r])
        g = self.inp("norm_g", [128, 128])
        P.dma("sp", self.g_sb, g, writes=[r])
        P.barrier()

    def inp(self, name, shape, dtype=F32):
        ap = self.nc.dram_tensor(name, list(shape), dtype, kind="ExternalInput").ap()
        self.ext[name] = ap
        return ap

    def outp(self, name, shape, dtype=F32):
        return self.nc.dram_tensor(name, list(shape), dtype, kind="ExternalOutput").ap()

    def scratch(self, name, shape, dtype=F32):
        return self.nc.dram_tensor(name, list(shape), dtype, kind="Internal").ap()

    def full(self, b):
        return self.bank[b][:, :], [self.bres[b]]

    def gcol(self, layer, j, c):
        i = (layer * 4 + j) * 8 + c
        return self.g_sb[:, i:i + 1]

    def rstd_from_sq(self, sq, r_sq, nchunks, T, slot, ms, r_ms, rstd, r_rstd, denom):
        P = self.P
        ps, rps = slot
        for c in range(nchunks):
            P.op("pe", lambda e, c=c: e.matmul(ps[:, 0:T], self.ones, sq[:, c, :], start=(c == 0), stop=(c == nchunks - 1)),
                 reads=[r_sq], writes=rps, inc=(c == nchunks - 1))
        P.op("act", lambda e: e.activation(out=ms, in_=ps[:, 0:T], func=AF.Ln, scale=1.0 / denom, bias=self.epsc[:, 0:1]),
             reads=rps, writes=[r_ms])
        P.op("act", lambda e: e.activation(out=rstd, in_=ms, func=AF.Exp, scale=-0.5),
             reads=[r_ms], writes=[r_rstd])

    def mlp_phase(self, layer, src, dst, w1, w2):
        P, A = self.P, self.A
        T = 256
        NT = self.L // T
        A.off = self.persist_end
        W1 = A.alloc((8, 4096), BF16)
        W2 = A.alloc((32, 1024), BF16)
        rW1 = [Res() for _ in range(8)]
        rW2 = [Res() for _ in range(8)]
        for kc in range(8):
            P.dma("pool", W1[:, :, kc * 512:(kc + 1) * 512], w1[:, :, kc * 512:(kc + 1) * 512], writes=[rW1[kc]])
        for g in range(8):
            P.dma("pool", W2[:, 4 * g:4 * g + 4, :], w2[:, 4 * g:4 * g + 4, :], writes=[rW2[g]])
        xt = [A.alloc((8, T), F32) for _ in range(2)]
        r_xt = [Res() for _ in range(2)]
        xn = [A.alloc((8, T), BF16) for _ in range(2)]
        r_xn = [Res() for _ in range(2)]
        sq = [A.alloc((8, T), BF16) for _ in range(2)]
        r_sq = [Res() for _ in range(2)]
        ms = [A.alloc(T, F32) for _ in range(2)]
        r_ms = [Res() for _ in range(2)]
        rstd = [A.alloc(T, F32) for _ in range(2)]
        r_rstd = [Res() for _ in range(2)]
        h = A.alloc((32, T), BF16)
        r_h = [Res() for _ in range(32)]
        hout = A.alloc((8, T), F32)
        r_hout = [Res() for _ in range(8)]
        sq2 = A.alloc((8, T), BF16)
        r_sq2 = Res()
        NTMP = 2
        tmp = [A.alloc(2 * T, F32) for _ in range(NTMP)]
        r_tmp = [Res() for _ in range(NTMP)]
        ms2 = A.alloc(T, F32)
        r_ms2 = Res()
        rstd2 = A.alloc(T, F32)
        r_rstd2 = Res()
        hslots = [self.full(b) for b in range(4)]
        oslots = [self.full(4), self.full(5)]
        nslots = [self.full(6), self.full(6)]
        n2slot = self.full(7)
        cnt = {"h": 0, "o": 0}

        def load(i):
            b = i % 2
            P.dma("sp", xt[b], src.ap[:, :, i * T:(i + 1) * T], reads=src.blocks(i * T, T), writes=[r_xt[b]])

        def pre(i):
            b = i % 2
            xf = xt[b].rearrange("p c t -> p (c t)")
            sf = sq[b].rearrange("p c t -> p (c t)")
            P.op("act", lambda e: e.activation(out=sf, in_=xf, func=AF.Square), reads=[r_xt[b]], writes=[r_sq[b]])
            self.rstd_from_sq(sq[b], r_sq[b], 8, T, nslots[b], ms[b], r_ms[b], rstd[b], r_rstd[b], float(D))
            for c in range(8):
                P.op("dve", lambda e, c=c: e.scalar_tensor_tensor(
                    out=xn[b][:, c, :], in0=xt[b][:, c, :], scalar=self.gcol(layer, 2, c), in1=rstd[b],
                    op0=ALU.mult, op1=ALU.mult), reads=[r_xt[b], r_rstd[b]], writes=[r_xn[b]])

        def mm1(i):
            b = i % 2
            for m2 in range(16):
                ps, rps = hslots[cnt["h"] % len(hslots)]
                tp = cnt["h"] % NTMP
                cnt["h"] += 1
                for hh in range(2):
                    m = 2 * m2 + hh
                    for kc in range(8):
                        P.op("pe", lambda e, kc=kc, m=m, hh=hh, ps=ps: e.matmul(
                            ps[:, hh * T:(hh + 1) * T], W1[:, kc, m * 128:(m + 1) * 128], xn[b][:, kc, :],
                            start=(kc == 0), stop=(kc == 7)),
                            reads=[rW1[m // 4], r_xn[b]], writes=rps, inc=(kc == 7 and hh == 1))
                P.op("act", lambda e, ps=ps, tp=tp: e.activation(out=tmp[tp], in_=ps, func=AF.Relu),
                     reads=rps, writes=[r_tmp[tp]])
                hv = h[:, 2 * m2:2 * m2 + 2, :].rearrange("p c t -> p (c t)")
                P.op("pool", lambda e, hv=hv, tp=tp: e.tensor_tensor(out=hv, in0=tmp[tp], in1=tmp[tp], op=ALU.mult),
                     reads=[r_tmp[tp]], writes=[r_h[2 * m2], r_h[2 * m2 + 1]])

        def mm2(i):
            for m2 in range(4):
                ps, rps = oslots[cnt["o"] % len(oslots)]
                cnt["o"] += 1
                for hh in range(2):
                    m = 2 * m2 + hh
                    for kc in range(32):
                        P.op("pe", lambda e, kc=kc, m=m, hh=hh, ps=ps: e.matmul(
                            ps[:, hh * T:(hh + 1) * T], W2[:, kc, m * 128:(m + 1) * 128], h[:, kc, :],
                            start=(kc == 0), stop=(kc == 31)),
                            reads=[rW2[kc // 4], r_h[kc]], writes=rps, inc=(kc == 31 and hh == 1))
                ho = hout[:, 2 * m2:2 * m2 + 2, :].rearrange("p c t -> p (c t)")
                so = sq2[:, 2 * m2:2 * m2 + 2, :].rearrange("p c t -> p (c t)")
                P.op("act", lambda e, ho=ho, ps=ps: e.activation(out=ho, in_=ps, func=AF.Copy),
                     reads=rps, writes=[r_hout[2 * m2], r_hout[2 * m2 + 1]])
                P.op("act", lambda e, so=so, ps=ps: e.activation(out=so, in_=ps, func=AF.Square),
                     reads=rps, writes=[r_sq2])

        def post(i):
            b = i % 2
            self.rstd_from_sq(sq2, r_sq2, 8, T, n2slot, ms2, r_ms2, rstd2, r_rstd2, float(D))
            for c in range(8):
                P.op("dve", lambda e, c=c: e.scalar_tensor_tensor(
                    out=hout[:, c, :], in0=hout[:, c, :], scalar=self.gcol(layer, 3, c), in1=rstd2,
                    op0=ALU.mult, op1=ALU.mult), reads=[r_hout[c], r_rstd2], writes=[r_hout[c]])
                P.op("pool", lambda e, c=c: e.tensor_tensor(out=xt[b][:, c, :], in0=xt[b][:, c, :], in1=hout[:, c, :], op=ALU.add),
                     reads=[r_hout[c], r_xt[b]], writes=[r_xt[b]])
            P.dma("sp", dst.ap[:, :, i * T:(i + 1) * T], xt[b], reads=[r_xt[b]], writes=dst.blocks(i * T, T))

        load(0)
        pre(0)
        if NT > 1:
            load(1)
        for i in range(NT):
            mm1(i)
            if i + 1 < NT:
                pre(i + 1)
            mm2(i)
            post(i)
            if i + 2 < NT:
                load(i + 2)
        P.barrier()


    def rglru_phase(self, layer, src, dst, w_in, w_a, w_x, vec, w_out):
        P, A = self.P, self.A
        T = 512
        NT = self.L // T
        A.off = self.persist_end
        Win = A.alloc((8, 2048), BF16)
        Wout = A.alloc((8, 1024), BF16)
        Wa = A.alloc((8, 128), BF16)
        Wx = A.alloc((8, 128), BF16)
        V = A.alloc((8, 8), F32)
        rW = Res()
        rWin = [Res() for _ in range(8)]
        for kc in range(8):
            cg = (kc + 4) % 8
            P.dma("pool", Win[:, :, cg * 256:(cg + 1) * 256], w_in[:, :, cg * 256:(cg + 1) * 256], writes=[rWin[cg]])
        P.dma("pool", Wout, w_out, writes=[rW])
        P.op("dve", lambda e: e.memset(Wa, 0.0), writes=[rW])
        P.op("dve", lambda e: e.memset(Wx, 0.0), writes=[rW])
        for hh in range(2):
            for (Wd, wsrc) in ((Wa, w_a), (Wx, w_x)):
                sv = wsrc.rearrange("(c hh) i j -> hh i c j", hh=2)[hh]
                P.dma("pool", Wd[hh * 64:(hh + 1) * 64, :, hh * 64:(hh + 1) * 64], sv, writes=[rW])
        P.dma("sp", V, vec, writes=[rW])
        DV = A.alloc((4, 8), F32)
        halfc = A.alloc(512, F32)
        P.op("pool", lambda e: e.memset(halfc, 0.5), writes=[rW])
        P.op("dve", lambda e: e.tensor_scalar(out=DV[:, 0, :], in0=V[:, 5, :], scalar1=0.5, scalar2=None, op0=ALU.mult), reads=[rW], writes=[rW])
        P.op("dve", lambda e: e.tensor_scalar(out=DV[:, 1, :], in0=V[:, 6, :], scalar1=0.5, scalar2=None, op0=ALU.mult), reads=[rW], writes=[rW])
        P.op("act", lambda e: e.activation(out=DV[:, 3, :], in_=V[:, 7, :], func=AF.Exp, scale=-1.0), reads=[rW], writes=[rW])
        P.op("dve", lambda e: e.tensor_scalar(out=DV[:, 3, :], in0=DV[:, 3, :], scalar1=1.0, scalar2=None, op0=ALU.add), reads=[rW], writes=[rW])
        P.op("act", lambda e: e.activation(out=DV[:, 3, :], in_=DV[:, 3, :], func=AF.Ln), reads=[rW], writes=[rW])
        P.op("dve", lambda e: e.tensor_scalar(out=DV[:, 2, :], in0=DV[:, 3, :], scalar1=-4.0, scalar2=None, op0=ALU.mult), reads=[rW], writes=[rW])
        P.op("dve", lambda e: e.tensor_scalar(out=DV[:, 3, :], in0=DV[:, 3, :], scalar1=-8.0, scalar2=None, op0=ALU.mult), reads=[rW], writes=[rW])

        xts = [A.alloc((8, T), F32) for _ in range(2)]; r_xts = [Res() for _ in range(2)]
        sq = A.alloc((8, T), BF16); r_sq = Res()
        xn = A.alloc((8, T), BF16); r_xn = Res()
        gate = A.alloc((8, T), F32); r_gate = [Res() for _ in range(8)]
        hout, r_hout = gate, r_gate
        UB = T + 4
        ubuf = A.alloc((8, UB), F32); r_ub = [Res() for _ in range(8)]
        uc = A.alloc((8, T), F32); r_uc = [Res() for _ in range(8)]
        ucb = A.alloc((8, T), BF16); r_ucb = [Res() for _ in range(8)]
        hb = A.alloc((8, T), F32); r_h = [Res() for _ in range(8)]
        hst = A.alloc(8, F32); r_hst = Res()
        ms = A.alloc(T, F32); r_ms = Res()
        rstd = A.alloc(T, F32); r_rstd = Res()
        NS = 2
        tA = [[A.alloc(T, F32) for _ in range(5)] for _ in range(NS)]
        r_tA = [[Res() for _ in range(5)] for _ in range(NS)]
        yb = xn
        r_yb = r_xn
        sq2 = sq
        r_sq2 = r_sq
        pslots = [self.full(b) for b in range(4)]
        gslots = [self.full(4), self.full(5)]
        nslot = self.full(6)
        n2slot = self.full(7)
        cnt = {"p": 0, "g": 0, "t": 0}
        P.op("pool", lambda e: e.memset(ubuf, 0.0), writes=r_ub)
        P.op("pool", lambda e: e.memset(hst, 0.0), writes=[r_hst])

        def vcol(v, c):
            return V[:, v, c:c + 1]

        def dcol(v, c):
            return DV[:, v, c:c + 1]

        def load(i):
            P.dma("sp", xts[i % 2], src.ap[:, :, i * T:(i + 1) * T], reads=src.blocks(i * T, T), writes=[r_xts[i % 2]])

        def tile(i, xt, r_xt):
            xf = xt.rearrange("p c t -> p (c t)")
            sf = sq.rearrange("p c t -> p (c t)")
            P.op("act", lambda e: e.activation(out=sf, in_=xf, func=AF.Square), reads=[r_xt], writes=[r_sq])
            self.rstd_from_sq(sq, r_sq, 8, T, nslot, ms, r_ms, rstd, r_rstd, float(D))
            for c in range(8):
                P.op("dve", lambda e, c=c: e.scalar_tensor_tensor(
                    out=xn[:, c, :], in0=xt[:, c, :], scalar=self.gcol(layer, 0, c), in1=rstd,
                    op0=ALU.mult, op1=ALU.mult), reads=[r_xt, r_rstd], writes=[r_xn])
            for m in list(range(8, 16)) + list(range(8)):
                ps, rps = pslots[cnt["p"] % 4]
                cnt["p"] += 1
                for kc in range(8):
                    P.op("pe", lambda e, kc=kc, m=m, ps=ps: e.matmul(
                        ps, Win[:, kc, m * 128:(m + 1) * 128], xn[:, kc, :], start=(kc == 0), stop=(kc == 7)),
                        reads=[rWin[m // 2], r_xn], writes=rps, inc=(kc == 7))
                if m < 8:
                    P.op("act", lambda e, m=m, ps=ps: e.activation(out=gate[:, m, :], in_=ps, func=AF.Gelu_apprx_tanh),
                         reads=rps, writes=[r_gate[m]])
                else:
                    c = m - 8
                    P.op("act", lambda e, c=c, ps=ps: e.activation(out=ubuf[:, c, 4:4 + T], in_=ps, func=AF.Copy),
                         reads=rps, writes=[r_ub[c]])
                    P.op("act", lambda e, c=c, ps=ps: e.activation(out=uc[:, c, :], in_=ps, func=AF.Identity,
                                                                   scale=vcol(3, c), bias=vcol(4, c)),
                         reads=rps + [rW], writes=[r_uc[c]])
                    for k in range(3):
                        P.op("dve", lambda e, c=c, k=k: e.scalar_tensor_tensor(
                            out=uc[:, c, :], in0=ubuf[:, c, 1 + k:1 + k + T], scalar=vcol(k, c), in1=uc[:, c, :],
                            op0=ALU.mult, op1=ALU.add), reads=[r_ub[c], r_uc[c], rW], writes=[r_uc[c]])
                    P.op("pool", lambda e, c=c: e.tensor_copy(out=ubuf[:, c, 1:4], in_=ubuf[:, c, T + 1:T + 4]),
                         reads=[r_ub[c]], writes=[r_ub[c]])
                    P.op("pool", lambda e, c=c: e.tensor_copy(out=ucb[:, c, :], in_=uc[:, c, :]),
                         reads=[r_uc[c]], writes=[r_ucb[c]])
            for c in range(8):
                s = cnt["t"] % NS
                cnt["t"] += 1
                th, aa, a2, thx, bb = tA[s]
                r_th, r_aa, r_a2, r_thx, r_bb = r_tA[s]
                psr, rpsr = gslots[0]
                psx, rpsx = gslots[1]
                P.op("pe", lambda e, c=c, psr=psr: e.matmul(psr, Wa[:, c, :], ucb[:, c, :], start=True, stop=True),
                     reads=[rW, r_ucb[c]], writes=rpsr)
                P.op("pe", lambda e, c=c, psx=psx: e.matmul(psx, Wx[:, c, :], ucb[:, c, :], start=True, stop=True),
                     reads=[rW, r_ucb[c]], writes=rpsx)
                P.op("act", lambda e, c=c, th=th, psr=psr: e.activation(out=th, in_=psr, func=AF.Tanh, scale=0.5, bias=dcol(0, c)),
                     reads=rpsr + [rW], writes=[r_th])
                P.op("act", lambda e, c=c, thx=thx, psx=psx: e.activation(out=thx, in_=psx, func=AF.Tanh, scale=0.5, bias=dcol(1, c)),
                     reads=rpsx + [rW], writes=[r_thx])
                P.op("act", lambda e, c=c, th=th, aa=aa: e.activation(out=aa, in_=th, func=AF.Exp, scale=dcol(2, c), bias=dcol(2, c)),
                     reads=[r_th, rW], writes=[r_aa])
                P.op("act", lambda e, c=c, th=th, a2=a2: e.activation(out=a2, in_=th, func=AF.Exp, scale=dcol(3, c), bias=dcol(3, c)),
                     reads=[r_th, rW], writes=[r_a2])
                P.op("act", lambda e, a2=a2: e.activation(out=a2, in_=a2, func=AF.Ln, scale=-1.0, bias=self.epsc[:, 1:2]),
                     reads=[r_a2], writes=[r_a2])
                P.op("act", lambda e, a2=a2: e.activation(out=a2, in_=a2, func=AF.Exp, scale=0.5),
                     reads=[r_a2], writes=[r_a2])
                P.op("dve", lambda e, c=c, thx=thx, bb=bb: e.scalar_tensor_tensor(
                    out=bb, in0=thx, scalar=1.0, in1=uc[:, c, :], op0=ALU.add, op1=ALU.mult),
                    reads=[r_thx, r_uc[c]], writes=[r_bb])
                P.op("dve", lambda e, a2=a2, bb=bb: e.scalar_tensor_tensor(
                    out=bb, in0=bb, scalar=0.5, in1=a2, op0=ALU.mult, op1=ALU.mult),
                    reads=[r_bb, r_a2], writes=[r_bb])
                P.op("dve", lambda e, c=c, aa=aa, bb=bb: e.tensor_tensor_scan(
                    out=hb[:, c, :], data0=aa, data1=bb, initial=hst[:, c:c + 1], op0=ALU.mult, op1=ALU.add),
                    reads=[r_aa, r_bb, r_hst], writes=[r_h[c]])
            P.op("dve", lambda e: e.tensor_copy(out=hst, in_=hb[:, :, T - 1]), reads=r_h, writes=[r_hst])
            P.op("dve", lambda e: e.tensor_tensor(out=yb.rearrange("p c t -> p (c t)"), in0=hb.rearrange("p c t -> p (c t)"),
                                                  in1=gate.rearrange("p c t -> p (c t)"), op=ALU.mult),
                 reads=r_h + r_gate, writes=[r_yb])
            for m in range(8):
                ps, rps = pslots[cnt["p"] % 4]
                cnt["p"] += 1
                for kc in range(8):
                    P.op("pe", lambda e, kc=kc, m=m, ps=ps: e.matmul(
                        ps, Wout[:, kc, m * 128:(m + 1) * 128], yb[:, kc, :], start=(kc == 0), stop=(kc == 7)),
                        reads=[rW, r_yb], writes=rps, inc=(kc == 7))
                P.op("act", lambda e, m=m, ps=ps: e.activation(out=hout[:, m, :], in_=ps, func=AF.Copy),
                     reads=rps, writes=[r_hout[m]])
                P.op("act", lambda e, m=m, ps=ps: e.activation(out=sq2[:, m, :], in_=ps, func=AF.Square),
                     reads=rps, writes=[r_sq2])
            self.post_norm_residual(layer, 1, hout, r_hout, sq2, r_sq2, xt, r_xt, T, n2slot, ms, r_ms, rstd, r_rstd)
            P.dma("sp", dst.ap[:, :, i * T:(i + 1) * T], xt, reads=[r_xt], writes=dst.blocks(i * T, T))
        load(0)
        for i in range(NT):
            if i + 1 < NT:
                load(i + 1)
            tile(i, xts[i % 2], r_xts[i % 2])
        P.barrier()

    def post_norm_residual(self, layer, j, hout, r_hout, sq2, r_sq2, xt, r_xt, T, slot, ms, r_ms, rstd, r_rstd):
        P = self.P
        self.rstd_from_sq(sq2, r_sq2, 8, T, slot, ms, r_ms, rstd, r_rstd, float(D))
        for c in range(8):
            P.op("dve", lambda e, c=c: e.scalar_tensor_tensor(
                out=hout[:, c, :], in0=hout[:, c, :], scalar=self.gcol(layer, j, c), in1=rstd,
                op0=ALU.mult, op1=ALU.mult), reads=[r_hout[c], r_rstd], writes=[r_hout[c]])
            P.op("pool", lambda e, c=c: e.tensor_tensor(out=xt[:, c, :], in0=xt[:, c, :], in1=hout[:, c, :], op=ALU.add),
                 reads=[r_hout[c], r_xt], writes=[r_xt])


    def sincos(self, th, out_sin, out_cos, tmp, r, eng="dve"):
        P = self.P
        MAGIC = 12582912.0
        TWO_PI = 6.283185307179586
        for (shift, dst) in ((0.0, out_sin), (1.5707963267948966, out_cos)):
            P.op(eng, lambda e, shift=shift: e.tensor_scalar(out=tmp, in0=th, scalar1=shift, scalar2=1.0 / TWO_PI, op0=ALU.add, op1=ALU.mult),
                 reads=[r], writes=[r])
            P.op(eng, lambda e: e.tensor_scalar(out=tmp, in0=tmp, scalar1=MAGIC, scalar2=None, op0=ALU.add), reads=[r], writes=[r])
            P.op(eng, lambda e: e.tensor_scalar(out=tmp, in0=tmp, scalar1=-MAGIC, scalar2=-TWO_PI, op0=ALU.add, op1=ALU.mult), reads=[r], writes=[r])
            P.op(eng, lambda e, shift=shift, dst=dst: e.scalar_tensor_tensor(out=dst, in0=th, scalar=shift, in1=tmp, op0=ALU.add, op1=ALU.add),
                 reads=[r], writes=[r])
            P.op("act", lambda e, dst=dst: e.activation(out=dst, in_=dst, func=AF.Sin), reads=[r], writes=[r])

    def s5_phase(self, layer, src, dst, w_in, w_out, par, par_bc, bpad_re, bpad_im, cpad_re, cpad_im, dvec, ident):
        P, A = self.P, self.A
        T = 256
        NT = self.L // T
        A.off = self.persist_end
        rW = Res()
        Win = A.alloc((8, 1024), BF16)
        Wout = A.alloc((8, 2048), BF16)
        Bre = A.alloc((32, 128), BF16)
        Bim = A.alloc((32, 128), BF16)
        Cre = A.alloc((32, 128), BF16)
        Cimn = A.alloc((32, 128), BF16)
        cosT = A.alloc((32, T), F32)
        sinT = A.alloc((32, T), F32)
        Dv = A.alloc(8, F32)
        SP = A.alloc((12, 32), F32)
        RT = A.alloc((2, 32), F32)
        GL = A.alloc((2, 32), F32)
        INIT = A.alloc((2, 32), F32)
        IDP = A.alloc(128, F32)
        IDN = A.alloc(128, F32)
        P.dma("sp", IDP, ident, writes=[rW])
        P.op("dve", lambda e: e.tensor_scalar(out=IDN, in0=IDP, scalar1=-1.0, scalar2=None, op0=ALU.mult), reads=[rW], writes=[rW])
        base = A.off
        P.dma("pool", Win, w_in, writes=[rW])
        for hh in range(2):
            P.dma("pool", Wout[:, :, hh * 1024:(hh + 1) * 1024], w_out[:, :, hh * 1024:(hh + 1) * 1024], writes=[rW])
        P.dma("pool", Cre, cpad_re, writes=[rW])
        P.dma("pool", Cimn, cpad_im, writes=[rW])
        P.op("pool", lambda e: e.tensor_scalar(out=Cimn.rearrange("p a b -> p (a b)"), in0=Cimn.rearrange("p a b -> p (a b)"),
                                               scalar1=-1.0, scalar2=None, op0=ALU.mult), reads=[rW], writes=[rW])
        P.dma("sp", Dv, dvec, writes=[rW])
        P.dma("sp", SP[:, 0:3, :], par, writes=[rW])
        Q = 1024
        tl = [A.alloc(Q, F32) for _ in range(14)]
        rq = Res()
        for qi in range(4):
            self.s5_bbar_quarter(qi, Q, tl, rq, rW, par_bc, bpad_re, bpad_im, Bre, Bim)
        sp = [SP[:, i, :] for i in range(12)]
        self.s5_disc(sp[0], sp[1], sp[2], sp[3], sp[4], sp[5], sp[6], sp[7], sp[8], sp[9], sp[10], sp[11], rW, unit=True)
        MAG = sp[3]
        rm_re, rm_im = sp[6], sp[5]
        P.barrier()
        A.off = base
        ta = A.alloc((32, 128), F32)
        tb = A.alloc((32, 128), F32)
        P.op("dve", lambda e: e.memset(cosT[:, :, 0:1], 1.0), writes=[rW])
        P.op("dve", lambda e: e.memset(sinT[:, :, 0:1], 0.0), writes=[rW])
        m = 1
        while m <= T // 2:
            bre_ = rm_re.unsqueeze(2).to_broadcast([128, 32, m])
            bim_ = rm_im.unsqueeze(2).to_broadcast([128, 32, m])
            P.op("dve", lambda e, m=m, b=bre_: e.tensor_tensor(out=ta[:, :, 0:m], in0=cosT[:, :, 0:m], in1=b, op=ALU.mult), reads=[rW], writes=[rW])
            P.op("dve", lambda e, m=m, b=bim_: e.tensor_tensor(out=tb[:, :, 0:m], in0=sinT[:, :, 0:m], in1=b, op=ALU.mult), reads=[rW], writes=[rW])
            P.op("dve", lambda e, m=m: e.tensor_tensor(out=cosT[:, :, m:2 * m], in0=ta[:, :, 0:m], in1=tb[:, :, 0:m], op=ALU.subtract), reads=[rW], writes=[rW])
            P.op("dve", lambda e, m=m, b=bim_: e.tensor_tensor(out=ta[:, :, 0:m], in0=cosT[:, :, 0:m], in1=b, op=ALU.mult), reads=[rW], writes=[rW])
            P.op("dve", lambda e, m=m, b=bre_: e.tensor_tensor(out=tb[:, :, 0:m], in0=sinT[:, :, 0:m], in1=b, op=ALU.mult), reads=[rW], writes=[rW])
            P.op("dve", lambda e, m=m: e.tensor_tensor(out=sinT[:, :, m:2 * m], in0=ta[:, :, 0:m], in1=tb[:, :, 0:m], op=ALU.add), reads=[rW], writes=[rW])
            P.op("dve", lambda e: e.tensor_tensor(out=sp[7], in0=rm_re, in1=rm_re, op=ALU.mult), reads=[rW], writes=[rW])
            P.op("dve", lambda e: e.tensor_tensor(out=sp[8], in0=rm_im, in1=rm_im, op=ALU.mult), reads=[rW], writes=[rW])
            P.op("dve", lambda e: e.scalar_tensor_tensor(out=rm_im, in0=rm_re, scalar=2.0, in1=rm_im, op0=ALU.mult, op1=ALU.mult), reads=[rW], writes=[rW])
            P.op("dve", lambda e: e.tensor_tensor(out=rm_re, in0=sp[7], in1=sp[8], op=ALU.subtract), reads=[rW], writes=[rW])
            m *= 2
        P.op("dve", lambda e: e.tensor_copy(out=RT[:, 0, :], in_=rm_re), reads=[rW], writes=[rW])
        P.op("dve", lambda e: e.tensor_copy(out=RT[:, 1, :], in_=rm_im), reads=[rW], writes=[rW])
        P.op("dve", lambda e: e.memset(INIT, 0.0), writes=[rW])
        P.barrier()
        A.off = base
        xt = A.alloc((8, T), F32); r_xt = Res()
        sq = A.alloc((8, T), BF16); r_sq = Res()
        xn = A.alloc((8, T), BF16); r_xn = Res()
        u = A.alloc((8, T), F32); r_u = [Res() for _ in range(8)]
        ub, r_ub = sq, [r_sq] * 8
        hout, r_hout = u, r_u
        yb, r_yb = xn, r_xn
        sq2, r_sq2 = sq, r_sq
        NB = 2
        bu = [A.alloc(3 * T, F32) for _ in range(NB)]; r_bu = [Res() for _ in range(NB)]
        t1 = [A.alloc(2 * T, F32) for _ in range(NB)]; r_t1 = [Res() for _ in range(NB)]
        t2 = [A.alloc(2 * T, F32) for _ in range(NB)]; r_t2 = [Res() for _ in range(NB)]
        G = [A.alloc(3 * T, F32) for _ in range(NB)]; r_G = [Res() for _ in range(NB)]
        t3 = [A.alloc(2 * T, F32)] * NB; r_t3 = [Res()] * NB
        t4 = [A.alloc(2 * T, F32)] * NB; r_t4 = [Res()] * NB
        hb = [A.alloc((4, 2, T), BF16) for _ in range(2)]; r_hb = [Res() for _ in range(2)]
        yv = [A.alloc(T, F32)] * 2; r_yv = [Res()] * 2
        tg = [A.alloc(T, F32)] * 2; r_tg = [Res()] * 2
        ms, r_ms = yv[0], r_yv[0]
        rstd, r_rstd = tg[0], r_tg[0]
        r_GL = Res()
        r_INIT = Res()
        pslots = [self.full(5), self.full(6)]
        bslots = [self.full(0), self.full(1)]
        zslots = [self.full(2), self.full(3)]
        yslots = [self.full(4)]
        nslot = self.full(7)
        n2slot = self.full(7)
        cnt = {"p": 0, "b": 0, "s": 0}

        def b2(ap2):
            return ap2.unsqueeze(1).to_broadcast([128, 2, T])

        def v2(ap, off):
            return ap[:, off:off + 2 * T].rearrange("p (a b) -> p a b", a=2)

        for i in range(NT):
            P.dma("sp", xt, src.ap[:, :, i * T:(i + 1) * T], reads=src.blocks(i * T, T), writes=[r_xt])
            P.op("act", lambda e: e.activation(out=sq.rearrange("p c t -> p (c t)"), in_=xt.rearrange("p c t -> p (c t)"), func=AF.Square),
                 reads=[r_xt], writes=[r_sq])
            self.rstd_from_sq(sq, r_sq, 8, T, nslot, ms, r_ms, rstd, r_rstd, float(D))
            for c in range(8):
                P.op("dve", lambda e, c=c: e.scalar_tensor_tensor(
                    out=xn[:, c, :], in0=xt[:, c, :], scalar=self.gcol(layer, 0, c), in1=rstd,
                    op0=ALU.mult, op1=ALU.mult), reads=[r_xt, r_rstd], writes=[r_xn])
            for m2 in range(4):
                ps, rps = pslots[cnt["p"] % len(pslots)]
                cnt["p"] += 1
                for hh in range(2):
                    mm = 2 * m2 + hh
                    for kc in range(8):
                        P.op("pe", lambda e, kc=kc, mm=mm, hh=hh, ps=ps: e.matmul(
                            ps[:, hh * T:(hh + 1) * T], Win[:, kc, mm * 128:(mm + 1) * 128], xn[:, kc, :],
                            start=(kc == 0), stop=(kc == 7)), reads=[rW, r_xn], writes=rps, inc=(kc == 7 and hh == 1))
                uo = u[:, 2 * m2:2 * m2 + 2, :].rearrange("p c t -> p (c t)")
                ubo = ub[:, 2 * m2:2 * m2 + 2, :].rearrange("p c t -> p (c t)")
                P.op("act", lambda e, uo=uo, ps=ps: e.activation(out=uo, in_=ps, func=AF.Copy), reads=rps, writes=r_u[2 * m2:2 * m2 + 2])
                P.op("act", lambda e, ubo=ubo, ps=ps: e.activation(out=ubo, in_=ps, func=AF.Copy), reads=rps, writes=r_ub[2 * m2:2 * m2 + 2])
            def s1(k):
                kc = k // 4
                s = k % NB
                ps, rps = bslots[k % 2]
                P.op("pe", lambda e: e.matmul(ps[:, 0:T], Bre[:, k, :], ub[:, kc, :], start=True, stop=True),
                     reads=[rW, r_ub[kc]], writes=rps, inc=False)
                P.op("pe", lambda e: e.matmul(ps[:, T:2 * T], Bim[:, k, :], ub[:, kc, :], start=True, stop=True),
                     reads=[rW, r_ub[kc]], writes=rps)
                P.op("act", lambda e: e.activation(out=bu[s][:, 0:2 * T], in_=ps, func=AF.Copy), reads=rps, writes=[r_bu[s]])
                P.op("act", lambda e: e.activation(out=bu[s][:, 2 * T:3 * T], in_=ps[:, 0:T], func=AF.Copy, scale=-1.0), reads=rps, writes=[r_bu[s]])
                P.op("pool", lambda e: e.tensor_tensor(out=v2(t1[s], 0), in0=v2(bu[s], 0), in1=b2(cosT[:, k, :]), op=ALU.mult),
                     reads=[r_bu[s], rW], writes=[r_t1[s]])
                P.op("dve", lambda e: e.tensor_tensor(out=v2(t2[s], 0), in0=v2(bu[s], T), in1=b2(sinT[:, k, :]), op=ALU.mult),
                     reads=[r_bu[s], rW], writes=[r_t2[s]])
                P.op("dve", lambda e: e.tensor_tensor(out=t1[s], in0=t1[s], in1=t2[s], op=ALU.add),
                     reads=[r_t1[s], r_t2[s]], writes=[r_t1[s]])

            def s2a(k):
                kk = k % 4
                s = k % NB
                hs = (k // 4) % 2
                magb = MAG[:, k:k + 1].to_broadcast([128, T])
                P.op("dve", lambda e: e.tensor_tensor_scan(
                    out=G[s][:, T:2 * T], data0=magb, data1=t1[s][:, 0:T], initial=INIT[:, 0, k:k + 1], op0=ALU.mult, op1=ALU.add),
                    reads=[r_t1[s], rW, r_INIT], writes=[r_G[s]])
                P.op("dve", lambda e: e.tensor_tensor_scan(
                    out=G[s][:, 2 * T:3 * T], data0=magb, data1=t1[s][:, T:2 * T], initial=INIT[:, 1, k:k + 1], op0=ALU.mult, op1=ALU.add),
                    reads=[r_t1[s], rW, r_INIT], writes=[r_G[s]])
                P.op("act", lambda e: e.activation(out=G[s][:, 0:T], in_=G[s][:, 2 * T:3 * T], func=AF.Copy, scale=-1.0), reads=[r_G[s]], writes=[r_G[s]])
                P.op("act", lambda e: e.activation(out=GL[:, :, k], in_=v2(G[s], T)[:, :, T - 1], func=AF.Copy),
                     reads=[r_G[s]], writes=[r_GL])

            def s2b(k):
                s = k % NB
                P.op("dve", lambda e: e.tensor_tensor(out=v2(t3[s], 0), in0=v2(G[s], T), in1=b2(cosT[:, k, :]), op=ALU.mult),
                     reads=[r_G[s], rW], writes=[r_t3[s]])
                P.op("dve", lambda e: e.tensor_tensor(out=v2(t4[s], 0), in0=v2(G[s], 0), in1=b2(sinT[:, k, :]), op=ALU.mult),
                     reads=[r_G[s], rW], writes=[r_t4[s]])

            def s3(k):
                kk = k % 4
                s = k % NB
                hs = (k // 4) % 2
                P.op("dve", lambda e: e.tensor_tensor(out=hb[hs][:, kk, :, :], in0=v2(t3[s], 0), in1=v2(t4[s], 0), op=ALU.add),
                     reads=[r_t3[s], r_t4[s]], writes=[r_hb[hs]])
                if kk == 3:
                    mo = k // 4
                    psy, rpsy = yslots[0]
                    for k4 in range(4):
                        kq = mo * 4 + k4
                        P.op("pe", lambda e, kq=kq, k4=k4, hs=hs, psy=psy: e.matmul(
                            psy[:, 0:T], Cre[:, kq, :], hb[hs][:, k4, 0, :], start=(k4 == 0), stop=False),
                            reads=[rW, r_hb[hs]], writes=rpsy, inc=False)
                        P.op("pe", lambda e, kq=kq, k4=k4, hs=hs, psy=psy: e.matmul(
                            psy[:, 0:T], Cimn[:, kq, :], hb[hs][:, k4, 1, :], start=False, stop=(k4 == 3)),
                            reads=[rW, r_hb[hs]], writes=rpsy, inc=(k4 == 3))
                    yy = mo % 2
                    P.op("dve", lambda e, mo=mo, yy=yy, psy=psy: e.scalar_tensor_tensor(
                        out=yv[yy], in0=u[:, mo, :], scalar=Dv[:, mo:mo + 1], in1=psy[:, 0:T], op0=ALU.mult, op1=ALU.add),
                        reads=rpsy + [r_u[mo], rW], writes=[r_yv[yy]])
                    P.op("act", lambda e, mo=mo, yy=yy: e.activation(out=yb[:, mo, :], in_=yv[yy], func=AF.Gelu_apprx_tanh),
                         reads=[r_yv[yy]], writes=[r_yb])

            s1(0)
            s1(1)
            s2a(0)
            s2b(0)
            for k in range(32):
                if k + 2 < 32:
                    s1(k + 2)
                if k + 1 < 32:
                    s2a(k + 1)
                s3(k)
                if k + 1 < 32:
                    s2b(k + 1)
            P.op("dve", lambda e: e.tensor_tensor(out=INIT[:, 0, :], in0=RT[:, 0, :], in1=GL[:, 0, :], op=ALU.mult), reads=[r_GL, rW], writes=[r_INIT])
            P.op("dve", lambda e: e.tensor_tensor(out=INIT[:, 1, :], in0=RT[:, 1, :], in1=GL[:, 1, :], op=ALU.mult), reads=[r_GL, rW], writes=[r_INIT])
            P.op("dve", lambda e: e.tensor_tensor(out=INIT[:, 0, :], in0=INIT[:, 0, :], in1=INIT[:, 1, :], op=ALU.subtract), reads=[r_INIT], writes=[r_INIT])
            P.op("dve", lambda e: e.tensor_tensor(out=INIT[:, 1, :], in0=RT[:, 0, :], in1=GL[:, 1, :], op=ALU.mult), reads=[r_GL, rW], writes=[r_INIT])
            P.op("dve", lambda e: e.tensor_tensor(out=GL[:, 0, :], in0=RT[:, 1, :], in1=GL[:, 0, :], op=ALU.mult), reads=[r_GL, rW], writes=[r_GL])
            P.op("dve", lambda e: e.tensor_tensor(out=INIT[:, 1, :], in0=INIT[:, 1, :], in1=GL[:, 0, :], op=ALU.add), reads=[r_GL, r_INIT], writes=[r_INIT])
            for mo in range(8):
                ps, rps = pslots[cnt["p"] % len(pslots)]
                cnt["p"] += 1
                for hh in range(2):
                    col = (hh * 8 + mo) * 128
                    for kc in range(8):
                        P.op("pe", lambda e, kc=kc, col=col, hh=hh, ps=ps: e.matmul(
                            ps[:, hh * T:(hh + 1) * T], Wout[:, kc, col:col + 128], yb[:, kc, :],
                            start=(kc == 0), stop=(kc == 7)), reads=[rW, r_yb], writes=rps, inc=(kc == 7 and hh == 1))
                yy = mo % 2
                P.op("act", lambda e, yy=yy, ps=ps: e.activation(out=tg[yy], in_=ps[:, T:2 * T], func=AF.Tanh, scale=0.5),
                     reads=rps, writes=[r_tg[yy]])
                P.op("dve", lambda e, yy=yy, ps=ps: e.scalar_tensor_tensor(
                    out=tg[yy], in0=tg[yy], scalar=1.0, in1=ps[:, 0:T], op0=ALU.add, op1=ALU.mult),
                    reads=rps + [r_tg[yy]], writes=[r_tg[yy]])
                P.op("act", lambda e, mo=mo, yy=yy: e.activation(out=hout[:, mo, :], in_=tg[yy], func=AF.Copy, scale=0.5),
                     reads=[r_tg[yy]], writes=[r_hout[mo]])
                P.op("act", lambda e, mo=mo, yy=yy: e.activation(out=sq2[:, mo, :], in_=tg[yy], func=AF.Square, scale=0.5),
                     reads=[r_tg[yy]], writes=[r_sq2])
            self.post_norm_residual(layer, 1, hout, r_hout, sq2, r_sq2, xt, r_xt, T, n2slot, ms, r_ms, rstd, r_rstd)
            P.dma("sp", dst.ap[:, :, i * T:(i + 1) * T], xt, reads=[r_xt], writes=dst.blocks(i * T, T))
        P.barrier()

    def s5_bbar_quarter(self, qi, Q, tl, rq, rW, par_bc, bpad_re, bpad_im, Bre, Bim):
        P = self.P
        cs = slice(qi * Q, (qi + 1) * Q)
        lre, lim, lst, bre, bim, t0, t1, t2, t3, t4, t5, t6, t7, t8 = tl
        P.dma("sp", lre, par_bc[0:1, cs].partition_broadcast(128), writes=[rq])
        P.dma("sp", lim, par_bc[1:2, cs].partition_broadcast(128), writes=[rq])
        P.dma("sp", lst, par_bc[2:3, cs].partition_broadcast(128), writes=[rq])
        P.dma("sp", bre, bpad_re.rearrange("p a b -> p (a b)")[:, cs], writes=[rq])
        P.dma("sp", bim, bpad_im.rearrange("p a b -> p (a b)")[:, cs], writes=[rq])
        self.s5_disc(lre, lim, lst, t0, t1, t2, t3, t4, t5, t6, t7, t8, rq)
        bo_re = Bre.rearrange("p a b -> p (a b)")[:, cs]
        bo_im = Bim.rearrange("p a b -> p (a b)")[:, cs]
        P.op("dve", lambda e: e.tensor_tensor(out=t0, in0=t5, in1=bre, op=ALU.mult), reads=[rq], writes=[rq])
        P.op("dve", lambda e: e.tensor_tensor(out=t1, in0=t6, in1=bim, op=ALU.mult), reads=[rq], writes=[rq])
        P.op("dve", lambda e: e.tensor_tensor(out=bo_re, in0=t0, in1=t1, op=ALU.subtract), reads=[rq], writes=[rq, rW])
        P.op("dve", lambda e: e.tensor_tensor(out=t0, in0=t5, in1=bim, op=ALU.mult), reads=[rq], writes=[rq])
        P.op("dve", lambda e: e.tensor_tensor(out=t1, in0=t6, in1=bre, op=ALU.mult), reads=[rq], writes=[rq])
        P.op("dve", lambda e: e.tensor_tensor(out=bo_im, in0=t0, in1=t1, op=ALU.add), reads=[rq], writes=[rq, rW])

    def s5_disc(self, lre, lim, lst, t0, t1, t2, t3, t4, t5, t6, t7, t8, r, unit=False):
        P = self.P
        P.op("dve", lambda e: e.tensor_scalar(out=lre, in0=lre, scalar1=-1e-4, scalar2=None, op0=ALU.min), reads=[r], writes=[r])
        P.op("act", lambda e: e.activation(out=lst, in_=lst, func=AF.Exp), reads=[r], writes=[r])
        P.op("dve", lambda e: e.tensor_tensor(out=t0, in0=lre, in1=lst, op=ALU.mult), reads=[r], writes=[r])
        P.op("act", lambda e: e.activation(out=t0, in_=t0, func=AF.Exp), reads=[r], writes=[r])
        P.op("dve", lambda e: e.tensor_tensor(out=t1, in0=lim, in1=lst, op=ALU.mult), reads=[r], writes=[r])
        self.sincos(t1, t2, t3, t4, r)
        if unit:
            return
        P.op("dve", lambda e: e.tensor_tensor(out=t2, in0=t2, in1=t0, op=ALU.mult), reads=[r], writes=[r])
        P.op("dve", lambda e: e.tensor_tensor(out=t3, in0=t3, in1=t0, op=ALU.mult), reads=[r], writes=[r])
        P.op("dve", lambda e: e.tensor_scalar(out=t3, in0=t3, scalar1=-1.0, scalar2=None, op0=ALU.add), reads=[r], writes=[r])
        P.op("dve", lambda e: e.tensor_tensor(out=t4, in0=lre, in1=lre, op=ALU.mult), reads=[r], writes=[r])
        P.op("dve", lambda e: e.tensor_tensor(out=t7, in0=lim, in1=lim, op=ALU.mult), reads=[r], writes=[r])
        P.op("dve", lambda e: e.tensor_tensor(out=t4, in0=t4, in1=t7, op=ALU.add), reads=[r], writes=[r])
        P.op("dve", lambda e: e.reciprocal(out=t4, in_=t4), reads=[r], writes=[r])
        P.op("dve", lambda e: e.tensor_tensor(out=t5, in0=t3, in1=lre, op=ALU.mult), reads=[r], writes=[r])
        P.op("dve", lambda e: e.tensor_tensor(out=t7, in0=t2, in1=lim, op=ALU.mult), reads=[r], writes=[r])
        P.op("dve", lambda e: e.tensor_tensor(out=t5, in0=t5, in1=t7, op=ALU.add), reads=[r], writes=[r])
        P.op("dve", lambda e: e.tensor_tensor(out=t5, in0=t5, in1=t4, op=ALU.mult), reads=[r], writes=[r])
        P.op("dve", lambda e: e.tensor_tensor(out=t6, in0=t2, in1=lre, op=ALU.mult), reads=[r], writes=[r])
        P.op("dve", lambda e: e.tensor_tensor(out=t7, in0=t3, in1=lim, op=ALU.mult), reads=[r], writes=[r])
        P.op("dve", lambda e: e.tensor_tensor(out=t6, in0=t6, in1=t7, op=ALU.subtract), reads=[r], writes=[r])
        P.op("dve", lambda e: e.tensor_tensor(out=t6, in0=t6, in1=t4, op=ALU.mult), reads=[r], writes=[r])


    def ssd_a_phase(self, layer, src, w_in, cw, zs, xs, bt, ct, dtt):
        P, A = self.P, self.A
        T = 512
        NT = self.L // T
        A.off = self.persist_end
        rW = Res()
        Win = A.alloc((8, 6176), BF16)
        rWin = [Res() for _ in range(8)]
        for kc in range(8):
            P.dma("pool", Win[:, :, kc * 772:(kc + 1) * 772], w_in[:, :, kc * 772:(kc + 1) * 772], writes=[rWin[kc]])
        CW = A.alloc((5, 32), F32)
        P.dma("sp", CW, cw, writes=[rW])
        H = A.alloc((32, 4), F32)
        r_H = [Res() for _ in range(32)]
        P.op("pool", lambda e: e.memset(H, 0.0), writes=r_H)
        xts = [A.alloc((8, T), F32) for _ in range(2)]; r_xts = [Res() for _ in range(2)]
        sq = A.alloc((8, T), BF16); r_sq = Res()
        xn = A.alloc((8, T), BF16); r_xn = Res()
        ms = A.alloc(T, F32); r_ms = Res()
        rstd = A.alloc(T, F32); r_rstd = Res()
        dtr = A.alloc((4, 32), F32); r_dtr = Res()
        NU = 3
        ubuf = [A.alloc(T + 4, F32) for _ in range(NU)]; r_ub = [Res() for _ in range(NU)]
        uc = [A.alloc(T, F32) for _ in range(NU)]; r_uc = [Res() for _ in range(NU)]
        NSG = 3
        stg = [A.alloc((4, T), BF16) for _ in range(NSG)]; r_stg = [Res() for _ in range(NSG)]
        pslots = [self.full(b) for b in range(5)]
        dslot = self.full(5)
        nslot = self.full(6)
        cnt = {"p": 0, "u": 0, "g": 0}

        def load(i):
            P.dma("sp", xts[i % 2], src.ap[:, :, i * T:(i + 1) * T], reads=src.blocks(i * T, T), writes=[r_xts[i % 2]])

        def tile(i, xt, r_xt):
            tsl = slice(i * T, (i + 1) * T)
            P.op("act", lambda e: e.activation(out=sq.rearrange("p c t -> p (c t)"), in_=xt.rearrange("p c t -> p (c t)"), func=AF.Square),
                 reads=[r_xt], writes=[r_sq])
            self.rstd_from_sq(sq, r_sq, 8, T, nslot, ms, r_ms, rstd, r_rstd, float(D))
            for c in range(8):
                P.op("dve", lambda e, c=c: e.scalar_tensor_tensor(
                    out=xn[:, c, :], in0=xt[:, c, :], scalar=self.gcol(layer, 0, c), in1=rstd,
                    op0=ALU.mult, op1=ALU.mult), reads=[r_xt, r_rstd], writes=[r_xn])
            def flush(m, sg):
                m0 = m - 3
                if m < 16:
                    dd, c0 = zs, m0
                elif m < 32:
                    dd, c0 = xs, m0 - 16
                elif m < 40:
                    dd, c0 = bt, m0 - 32
                else:
                    dd, c0 = ct, m0 - 40
                P.dma("sp", dd.ap[:, c0:c0 + 4, tsl], stg[sg], reads=[r_stg[sg]])

            def fin(m, sg, q4, ub_i):
                P.op("act", lambda e: e.activation(out=stg[sg][:, q4, :], in_=uc[ub_i], func=AF.Silu),
                     reads=[r_uc[ub_i]], writes=[r_stg[sg]])
                if q4 == 3:
                    flush(m, sg)

            def chunk_m(m, sg, q4):
                ps, rps = pslots[cnt["p"] % len(pslots)]
                cnt["p"] += 1
                for kc in range(8):
                    P.op("pe", lambda e, kc=kc: e.matmul(
                        ps, Win[:, kc, m * 128:(m + 1) * 128], xn[:, kc, :], start=(kc == 0), stop=(kc == 7)),
                        reads=[rWin[(m * 128) // 772], rWin[(m * 128 + 127) // 772], r_xn], writes=rps, inc=(kc == 7))
                if m < 16:
                    P.op("act", lambda e: e.activation(out=stg[sg][:, q4, :], in_=ps, func=AF.Silu),
                         reads=rps, writes=[r_stg[sg]])
                    if q4 == 3:
                        flush(m, sg)
                    return None
                cc = m - 16
                ub_i = cnt["u"] % NU
                cnt["u"] += 1
                ub_, uc_ = ubuf[ub_i], uc[ub_i]
                P.op("pool", lambda e: e.tensor_copy(out=ub_[:, 1:4], in_=H[:, cc, 0:3]),
                     reads=[r_H[cc]], writes=[r_ub[ub_i]])
                P.op("act", lambda e: e.activation(out=ub_[:, 4:4 + T], in_=ps, func=AF.Copy),
                     reads=rps, writes=[r_ub[ub_i]])
                P.op("act", lambda e: e.activation(out=uc_, in_=ps, func=AF.Identity,
                                                   scale=CW[:, 3, cc:cc + 1], bias=CW[:, 4, cc:cc + 1]),
                     reads=rps + [rW], writes=[r_uc[ub_i]])
                for k in range(3):
                    P.op("dve", lambda e, k=k: e.scalar_tensor_tensor(
                        out=uc_, in0=ub_[:, 1 + k:1 + k + T], scalar=CW[:, k, cc:cc + 1], in1=uc_,
                        op0=ALU.mult, op1=ALU.add), reads=[r_ub[ub_i], r_uc[ub_i], rW], writes=[r_uc[ub_i]])
                P.op("pool", lambda e: e.tensor_copy(out=H[:, cc, 0:3], in_=ub_[:, T + 1:T + 4]),
                     reads=[r_ub[ub_i]], writes=[r_H[cc]])
                return (m, sg, q4, ub_i)

            pend = None
            for m in range(48):
                sg = cnt["g"] % NSG
                q4 = m % 4
                if q4 == 3:
                    cnt["g"] += 1
                nxt = chunk_m(m, sg, q4)
                if pend is not None:
                    fin(*pend)
                pend = nxt
            if pend is not None:
                fin(*pend)
            psd, rpsd = dslot
            for j in range(4):
                for kc in range(8):
                    P.op("pe", lambda e, j=j, kc=kc: e.matmul(
                        psd[:, j * 32:(j + 1) * 32], xn[:, kc, j * 128:(j + 1) * 128], Win[:, kc, 6144:6176],
                        start=(kc == 0), stop=(kc == 7)), reads=[rWin[7], r_xn], writes=rpsd, inc=(kc == 7 and j == 3))
            P.op("act", lambda e: e.activation(out=dtr.rearrange("p a b -> p (a b)"), in_=psd[:, 0:128], func=AF.Copy),
                 reads=rpsd, writes=[r_dtr])
            P.dma("sp", dtt.ap[i * 4:(i + 1) * 4].rearrange("j p h -> p j h"), dtr, reads=[r_dtr])
        load(0)
        for i in range(NT):
            if i + 1 < NT:
                load(i + 1)
            tile(i, xts[i % 2], r_xts[i % 2])
        P.barrier()


    def ssd_b_phase(self, layer, src, dst, zs, xs, bt, ct, dtt, w_out, consts, hp_bc, dch, ngv):
        P, A = self.P, self.A
        T = 256
        NT = self.L // T
        A.off = self.persist_end
        rW = Res()
        Wout = A.alloc((16, 1024), BF16)
        for hh in range(2):
            P.dma("pool", Wout[:, hh * 8:(hh + 1) * 8, :], w_out[:, hh * 8:(hh + 1) * 8, :], writes=[rW])
        CF = A.alloc((4, 128), F32)
        P.dma("sp", CF, consts, writes=[rW])
        IDF, U, MS, ONES = CF[:, 0, :], CF[:, 1, :], CF[:, 2, :], CF[:, 3, :]
        identb = A.alloc(128, BF16)
        P.op("dve", lambda e: e.tensor_copy(out=identb, in_=IDF), reads=[rW], writes=[rW])
        Ub = A.alloc(128, BF16)
        MSb = A.alloc(128, BF16)
        ONESb = A.alloc(128, BF16)
        P.op("dve", lambda e: e.tensor_copy(out=Ub, in_=U), reads=[rW], writes=[rW])
        P.op("dve", lambda e: e.tensor_copy(out=MSb, in_=MS), reads=[rW], writes=[rW])
        P.op("dve", lambda e: e.tensor_copy(out=ONESb, in_=ONES), reads=[rW], writes=[rW])

        HB = A.alloc((2, 32), F32)
        P.dma("sp", HB[:, 0, :], hp_bc[0:1, :].partition_broadcast(128), writes=[rW])
        P.dma("sp", HB[:, 1, :], hp_bc[1:2, :].partition_broadcast(128), writes=[rW])
        P.op("act", lambda e: e.activation(out=HB[:, 1, :], in_=HB[:, 1, :], func=AF.Exp), reads=[rW], writes=[rW])
        P.op("dve", lambda e: e.tensor_scalar(out=HB[:, 1, :], in0=HB[:, 1, :], scalar1=-1.0, scalar2=None, op0=ALU.mult), reads=[rW], writes=[rW])
        DBIAS, ANEG = HB[:, 0, :], HB[:, 1, :]
        DCH = A.alloc(16, F32)
        NG = A.alloc(16, F32)
        P.dma("sp", DCH, dch, writes=[rW])
        P.dma("sp", NG, ngv, writes=[rW])
        diagD = A.alloc((16, 128), BF16)
        for m in range(16):
            P.op("dve", lambda e, m=m: e.tensor_scalar(out=diagD[:, m, :], in0=IDF, scalar1=DCH[:, m:m + 1], scalar2=None, op0=ALU.mult),
                 reads=[rW], writes=[rW])
        ent = A.alloc(2048, F32); r_ent = [Res() for _ in range(8)]
        entb = A.alloc(2048, BF16); r_entb = [Res() for _ in range(8)]
        P.op("pool", lambda e: e.memset(ent, 0.0), writes=r_ent)
        P.op("pool", lambda e: e.memset(entb, 0.0), writes=r_entb)
        zs_t = [A.alloc((16, T), BF16) for _ in range(2)]
        xs_t = [A.alloc((16, T), BF16) for _ in range(2)]
        bt_t = [A.alloc((8, T), BF16) for _ in range(2)]
        ct_t = [A.alloc((8, T), BF16) for _ in range(2)]
        dtr_t = [A.alloc((2, 32), F32) for _ in range(2)]
        xt = [A.alloc((8, T), F32) for _ in range(2)]
        r_ld = [Res() for _ in range(2)]
        r_xt = [Res() for _ in range(2)]
        smset = []
        for _ in range(2):
            t = [A.alloc(32, F32) for _ in range(8)]
            t += [A.alloc(32, BF16), A.alloc(32, BF16), A.alloc(32, F32)]
            smset.append(t)
        r_smset = [Res(), Res()]
        xdt2 = [A.alloc(2048, BF16) for _ in range(2)]; r_xdt2 = [Res(), Res()]
        xdtd2 = [A.alloc(2048, BF16) for _ in range(2)]; r_xdtd2 = [Res(), Res()]
        btok2 = [A.alloc(1024, BF16) for _ in range(2)]; r_btok2 = [Res(), Res()]
        NR = 2
        Rg = [A.alloc((4, 128), BF16) for _ in range(NR)]; r_Rg = [Res() for _ in range(NR)]
        Rl = [A.alloc((4, 128), BF16) for _ in range(NR)]
        Lx = [A.alloc((4, 128), F32) for _ in range(NR)]; r_Lx = [Res() for _ in range(NR)]
        Ex = [A.alloc((4, 128), F32) for _ in range(NR)]; r_Ex = [Res() for _ in range(NR)]
        cbm = [A.alloc(128, F32) for _ in range(NR)]; r_cbm = [Res() for _ in range(NR)]
        Mg = [A.alloc((4, 128), BF16) for _ in range(NR)]; r_Mg = [Res() for _ in range(NR)]
        Cd = [A.alloc((4, 128), BF16) for _ in range(NR)]; r_Cd = [Res() for _ in range(NR)]
        v = A.alloc((16, T), F32); r_v = [Res() for _ in range(8)]
        sqv = A.alloc((16, T), BF16); r_sqv = Res()
        vn = A.alloc((16, T), BF16); r_vn = [Res() for _ in range(8)]
        msg = A.alloc(T, F32); r_msg = Res()
        rsg = [A.alloc(T, F32) for _ in range(2)]; r_rsg = [Res() for _ in range(2)]
        hout = A.alloc((8, T), F32); r_hout = [Res() for _ in range(8)]
        sq2 = A.alloc((8, T), BF16); r_sq2 = Res()
        ms = A.alloc(T, F32); r_ms = Res()
        rstd = A.alloc(T, F32); r_rstd = Res()
        tslots = [self.full(0), self.full(1)]
        dtslot = self.full(2)
        lslot = self.full(3)
        aslot = self.full(4)
        yslot = self.full(5)
        sslot = self.full(6)
        cslot = self.full(7)
        cnt = {"t": 0, "r": 0}

        def load(i):
            b = i % 2
            tsl = slice(i * T, (i + 1) * T)
            P.dma("sp", zs_t[b], zs.ap[:, :, tsl], writes=[r_ld[b]])
            P.dma("sp", xs_t[b], xs.ap[:, :, tsl], writes=[r_ld[b]])
            P.dma("sp", bt_t[b], bt.ap[:, :, tsl], writes=[r_ld[b]])
            P.dma("sp", ct_t[b], ct.ap[:, :, tsl], writes=[r_ld[b]])
            P.dma("sp", dtr_t[b], dtt.ap[i * 2:(i + 1) * 2].rearrange("j p h -> p j h"), writes=[r_ld[b]])
            P.dma("sp", xt[b], src.ap[:, :, tsl], reads=src.blocks(i * T, T), writes=[r_xt[b]])

        def bc4(ap2):
            return ap2.unsqueeze(1).to_broadcast([128, 4, 128])

        def pre_a(i, c, pbi):
            b = i % 2
            cols = slice(c * 128, (c + 1) * 128)
            rl = [r_ld[b]]
            dv, av, dtv, dA, dsv, eav, dtds, lv, dAh, dAl, dAt = smset[pbi]
            r_sm = r_smset[pbi]
            xdt, xdtd, btok = xdt2[pbi], xdtd2[pbi], btok2[pbi]
            r_xdt, r_xdtd, r_btok = r_xdt2[pbi], r_xdtd2[pbi], r_btok2[pbi]
            P.op("dve", lambda e: e.tensor_tensor(out=dv, in0=dtr_t[b][:, c, :], in1=DBIAS, op=ALU.add), reads=rl + [rW], writes=[r_sm])
            P.op("dve", lambda e: e.scalar_tensor_tensor(out=av, in0=dv, scalar=-1.0, in1=dv, op0=ALU.mult, op1=ALU.max), reads=[r_sm], writes=[r_sm])
            P.op("act", lambda e: e.activation(out=av, in_=av, func=AF.Exp, scale=-1.0), reads=[r_sm], writes=[r_sm])
            P.op("act", lambda e: e.activation(out=lv, in_=av, func=AF.Ln, bias=1.0), reads=[r_sm], writes=[r_sm])
            P.op("dve", lambda e: e.scalar_tensor_tensor(out=dtv, in0=dv, scalar=0.0, in1=lv, op0=ALU.max, op1=ALU.add), reads=[r_sm], writes=[r_sm])
            P.op("dve", lambda e: e.tensor_tensor(out=dA, in0=dtv, in1=ANEG, op=ALU.mult), reads=[r_sm, rW], writes=[r_sm])
            P.op("dve", lambda e: e.tensor_copy(out=dAh, in_=dA), reads=[r_sm], writes=[r_sm])
            P.op("dve", lambda e: e.tensor_tensor(out=dAt, in0=dA, in1=dAh, op=ALU.subtract), reads=[r_sm], writes=[r_sm])
            P.op("dve", lambda e: e.tensor_copy(out=dAl, in_=dAt), reads=[r_sm], writes=[r_sm])
            psd, rpsd = dtslot
            P.op("pe", lambda e: e.matmul(psd[:, 0:32], MS, dA, start=True, stop=True), reads=[rW, r_sm], writes=rpsd, inc=False)
            P.op("pe", lambda e: e.matmul(psd[:, 32:64], ONES, dA, start=True, stop=True), reads=[rW, r_sm], writes=rpsd)
            P.op("act", lambda e: e.activation(out=dsv, in_=psd[:, 0:32], func=AF.Exp), reads=rpsd, writes=[r_sm])
            P.op("act", lambda e: e.activation(out=eav, in_=psd[:, 32:64], func=AF.Exp), reads=rpsd, writes=[r_sm])
            P.op("dve", lambda e: e.tensor_tensor(out=dtds, in0=dtv, in1=dsv, op=ALU.mult), reads=[r_sm], writes=[r_sm])

        def pre_b(i, c, pbi):
            b = i % 2
            cols = slice(c * 128, (c + 1) * 128)
            rl = [r_ld[b]]
            dv, av, dtv, dA, dsv, eav, dtds, lv, dAh, dAl, dAt = smset[pbi]
            r_sm = r_smset[pbi]
            xdt, xdtd, btok = xdt2[pbi], xdtd2[pbi], btok2[pbi]
            r_xdt, r_xdtd, r_btok = r_xdt2[pbi], r_xdtd2[pbi], r_btok2[pbi]
            for half in range(2):
                pst, rpst = tslots[cnt["t"] % 2]
                cnt["t"] += 1
                pb = pst.bitcast(BF16)
                for q in range(8):
                    m = half * 8 + q
                    P.op("pe", lambda e, m=m, q=q, pb=pb: e.transpose(pb[:, q * 128:(q + 1) * 128], xs_t[b][:, m, cols], identb),
                         reads=rl + [rW], writes=rpst, inc=(q == 7))
                pv = pb.rearrange("p (h d) -> p h d", h=16)
                hs = slice(half * 16, (half + 1) * 16)
                o1 = xdt[:, half * 1024:(half + 1) * 1024].rearrange("p (h d) -> p h d", h=16)
                o2 = xdtd[:, half * 1024:(half + 1) * 1024].rearrange("p (h d) -> p h d", h=16)
                P.op("dve", lambda e, pv=pv, o1=o1, hs=hs: e.tensor_tensor(out=o1, in0=pv, in1=dtv[:, hs].unsqueeze(2).to_broadcast([128, 16, 64]), op=ALU.mult),
                     reads=rpst + [r_sm], writes=[r_xdt])
                P.op("dve", lambda e, pv=pv, o2=o2, hs=hs: e.tensor_tensor(out=o2, in0=pv, in1=dtds[:, hs].unsqueeze(2).to_broadcast([128, 16, 64]), op=ALU.mult),
                     reads=rpst + [r_sm], writes=[r_xdtd])

        def pre_c(i, c, pbi):
            b = i % 2
            cols = slice(c * 128, (c + 1) * 128)
            rl = [r_ld[b]]
            dv, av, dtv, dA, dsv, eav, dtds, lv, dAh, dAl, dAt = smset[pbi]
            r_sm = r_smset[pbi]
            xdt, xdtd, btok = xdt2[pbi], xdtd2[pbi], btok2[pbi]
            r_xdt, r_xdtd, r_btok = r_xdt2[pbi], r_xdtd2[pbi], r_btok2[pbi]
            pst, rpst = tslots[cnt["t"] % 2]
            cnt["t"] += 1
            pb = pst.bitcast(BF16)
            for g in range(8):
                P.op("pe", lambda e, g=g, pb=pb: e.transpose(pb[:, g * 128:(g + 1) * 128], bt_t[b][:, g, cols], identb),
                     reads=rl + [rW], writes=rpst, inc=(g == 7))
            P.op("act", lambda e, pb=pb: e.activation(out=btok, in_=pb, func=AF.Copy), reads=rpst, writes=[r_btok])

        def groups(i, c, pbi, hooks):
            b = i % 2
            cols = slice(c * 128, (c + 1) * 128)
            rl = [r_ld[b]]
            dv, av, dtv, dA, dsv, eav, dtds, lv, dAh, dAl, dAt = smset[pbi]
            r_sm = r_smset[pbi]
            xdt, xdtd, btok = xdt2[pbi], xdtd2[pbi], btok2[pbi]
            r_xdt, r_xdtd, r_btok = r_xdt2[pbi], r_xdtd2[pbi], r_btok2[pbi]
            def g1(g):
                rr = g % NR
                Rv = Rg[rr].rearrange("p a b -> p (a b)")
                Rlv = Rl[rr].rearrange("p a b -> p (a b)")
                P.op("dve", lambda e, g=g, rr=rr: e.tensor_tensor(out=Rg[rr], in0=bc4(Ub), in1=dAh[:, 4 * g:4 * g + 4].unsqueeze(2).to_broadcast([128, 4, 128]), op=ALU.mult),
                     reads=[rW, r_sm], writes=[r_Rg[rr]])
                P.op("dve", lambda e, g=g, rr=rr: e.tensor_tensor(out=Rl[rr], in0=bc4(Ub), in1=dAl[:, 4 * g:4 * g + 4].unsqueeze(2).to_broadcast([128, 4, 128]), op=ALU.mult),
                     reads=[rW, r_sm], writes=[r_Rg[rr]])
                psl, rpsl = lslot
                psa, rpsa = aslot
                P.op("pe", lambda e, Rv=Rv: e.matmul(psl, MSb, Rv, start=True, stop=False), reads=[rW, r_Rg[rr]], writes=rpsl, inc=False)
                P.op("pe", lambda e, Rlv=Rlv: e.matmul(psl, MSb, Rlv, start=False, stop=True), reads=[rW, r_Rg[rr]], writes=rpsl)
                P.op("pe", lambda e, Rv=Rv: e.matmul(psa, ONESb, Rv, start=True, stop=False), reads=[rW, r_Rg[rr]], writes=rpsa, inc=False)
                P.op("pe", lambda e, Rlv=Rlv: e.matmul(psa, ONESb, Rlv, start=False, stop=True), reads=[rW, r_Rg[rr]], writes=rpsa)
                P.op("act", lambda e, rr=rr: e.activation(out=Lx[rr].rearrange("p a b -> p (a b)"), in_=psl, func=AF.Exp), reads=rpsl, writes=[r_Lx[rr]])
                P.op("act", lambda e, rr=rr: e.activation(out=Ex[rr].rearrange("p a b -> p (a b)"), in_=psa, func=AF.Exp), reads=rpsa, writes=[r_Ex[rr]])
                psc, rpsc = cslot
                P.op("pe", lambda e, g=g: e.matmul(psc[:, 0:128], bt_t[b][:, g, cols], ct_t[b][:, g, cols], start=True, stop=True),
                     reads=rl, writes=rpsc)
                P.op("dve", lambda e, rr=rr: e.tensor_tensor(out=cbm[rr], in0=psc[:, 0:128], in1=U, op=ALU.mult), reads=rpsc + [rW], writes=[r_cbm[rr]])
                P.op("dve", lambda e, rr=rr: e.tensor_tensor(out=Mg[rr], in0=Lx[rr], in1=bc4(cbm[rr]), op=ALU.mult),
                     reads=[r_Lx[rr], r_cbm[rr]], writes=[r_Mg[rr]])
                P.op("dve", lambda e, rr=rr, g=g: e.tensor_tensor(out=Cd[rr], in0=Ex[rr], in1=bc4(ct_t[b][:, g, cols]), op=ALU.mult),
                     reads=[r_Ex[rr]] + rl, writes=[r_Cd[rr]])

            def g2(g):
                rr = g % NR
                psy, rpsy = yslot
                for mm in range(2):
                    m = 2 * g + mm
                    mc = slice(mm * 128, (mm + 1) * 128)
                    P.op("pe", lambda e, m=m, mc=mc: e.matmul(psy[:, mc], diagD[:, m, :], xs_t[b][:, m, cols], start=True, stop=False),
                         reads=rl + [rW], writes=rpsy, inc=False)
                    for hh in range(2):
                        h = 2 * m + hh
                        hc = slice(h * 64, (h + 1) * 64)
                        pr = slice(hh * 64, (hh + 1) * 64)
                        P.op("pe", lambda e, mc=mc, hc=hc, pr=pr, rr=rr, mm=mm, hh=hh: e.matmul(
                            psy[pr, mc], xdt[:, hc], Mg[rr][:, 2 * mm + hh, :], start=False, stop=False),
                            reads=[r_xdt, r_Mg[rr]], writes=rpsy, inc=False)
                        P.op("pe", lambda e, mc=mc, hc=hc, pr=pr, rr=rr, mm=mm, hh=hh: e.matmul(
                            psy[pr, mc], entb[:, hc], Cd[rr][:, 2 * mm + hh, :], start=False, stop=(hh == 1)),
                            reads=[r_entb[g], r_Cd[rr]], writes=rpsy, inc=(hh == 1 and mm == 1))
                P.op("dve", lambda e, g=g: e.tensor_tensor(out=v[:, 2 * g:2 * g + 2, cols], in0=psy[:, 0:256].rearrange("p (a b) -> p a b", a=2),
                                                           in1=zs_t[b][:, 2 * g:2 * g + 2, cols], op=ALU.mult),
                     reads=rpsy + rl, writes=[r_v[g]])
                pss, rpss = sslot
                gs = slice(g * 256, (g + 1) * 256)
                P.op("pe", lambda e, g=g, gs=gs: e.matmul(pss[:, 0:256], btok[:, g * 128:(g + 1) * 128], xdtd[:, gs], start=True, stop=True),
                     reads=[r_btok, r_xdtd], writes=rpss)
                ev = ent[:, gs].rearrange("p (h d) -> p h d", h=4)
                P.op("dve", lambda e, g=g, ev=ev: e.tensor_tensor(out=ev, in0=ev, in1=eav[:, 4 * g:4 * g + 4].unsqueeze(2).to_broadcast([128, 4, 64]), op=ALU.mult),
                     reads=[r_ent[g], r_sm], writes=[r_ent[g]])
                P.op("dve", lambda e, gs=gs: e.tensor_tensor(out=ent[:, gs], in0=ent[:, gs], in1=pss[:, 0:256], op=ALU.add),
                     reads=[r_ent[g]] + rpss, writes=[r_ent[g]])
                P.op("pool", lambda e, gs=gs: e.tensor_copy(out=entb[:, gs], in_=ent[:, gs]), reads=[r_ent[g]], writes=[r_entb[g]])

            g1(0)
            for g in range(8):
                if g + 1 < 8:
                    g1(g + 1)
                g2(g)
                if g in hooks:
                    hooks[g]()

        def tail(i):
            b = i % 2
            P.op("act", lambda e: e.activation(out=sqv.rearrange("p c t -> p (c t)"), in_=v.rearrange("p c t -> p (c t)"), func=AF.Square),
                 reads=r_v, writes=[r_sqv])
            for g in range(8):
                k2 = g % 2
                self.rstd_from_sq(sqv[:, 2 * g:2 * g + 2, :], r_sqv, 2, T, dtslot, msg, r_msg, rsg[k2], r_rsg[k2], 256.0)
                for mm in range(2):
                    m = 2 * g + mm
                    P.op("dve", lambda e, m=m, k2=k2: e.scalar_tensor_tensor(
                        out=vn[:, m, :], in0=v[:, m, :], scalar=NG[:, m:m + 1], in1=rsg[k2], op0=ALU.mult, op1=ALU.mult),
                        reads=[r_v[g], r_rsg[k2], rW], writes=[r_vn[g]])
            for m2 in range(4):
                ps, rps = tslots[cnt["t"] % 2]
                cnt["t"] += 1
                for hh in range(2):
                    mo = 2 * m2 + hh
                    for kc in range(16):
                        P.op("pe", lambda e, kc=kc, mo=mo, hh=hh, ps=ps: e.matmul(
                            ps[:, hh * T:(hh + 1) * T], Wout[:, kc, mo * 128:(mo + 1) * 128], vn[:, kc, :],
                            start=(kc == 0), stop=(kc == 15)), reads=[rW, r_vn[kc // 2]], writes=rps, inc=(kc == 15 and hh == 1))
                ho = hout[:, 2 * m2:2 * m2 + 2, :].rearrange("p c t -> p (c t)")
                so = sq2[:, 2 * m2:2 * m2 + 2, :].rearrange("p c t -> p (c t)")
                P.op("act", lambda e, ho=ho, ps=ps: e.activation(out=ho, in_=ps, func=AF.Copy), reads=rps, writes=r_hout[2 * m2:2 * m2 + 2])
                P.op("act", lambda e, so=so, ps=ps: e.activation(out=so, in_=ps, func=AF.Square), reads=rps, writes=[r_sq2])
            self.post_norm_residual(layer, 1, hout, r_hout, sq2, r_sq2, xt[b], r_xt[b], T, cslot, ms, r_ms, rstd, r_rstd)
            P.dma("sp", dst.ap[:, :, i * T:(i + 1) * T], xt[b], reads=[r_xt[b]], writes=dst.blocks(i * T, T))

        chunks = [(i, c) for i in range(NT) for c in range(2)]
        load(0)
        pre_a(0, 0, 0)
        pre_b(0, 0, 0)
        pre_c(0, 0, 0)
        for n, (i, c) in enumerate(chunks):
            if c == 0 and i + 1 < NT:
                load(i + 1)
            hooks = {}
            if n + 1 < len(chunks):
                ni, ncc = chunks[n + 1]
                nb = (n + 1) % 2
                hooks[1] = (lambda ni=ni, ncc=ncc, nb=nb: pre_a(ni, ncc, nb))
                hooks[3] = (lambda ni=ni, ncc=ncc, nb=nb: pre_b(ni, ncc, nb))
                hooks[5] = (lambda ni=ni, ncc=ncc, nb=nb: pre_c(ni, ncc, nb))
            groups(i, c, n % 2, hooks)
            if c == 1:
                tail(i)
        P.barrier()


def x_to_dev(xb):
    L = xb.shape[0]
    return np.ascontiguousarray(xb.T.reshape(8, 128, L).transpose(1, 0, 2))


def x_from_dev(xd):
    L = xd.shape[2]
    return np.ascontiguousarray(xd.transpose(1, 0, 2).reshape(1024, L).T)


def lay_norm_g(norm_g):
    return np.ascontiguousarray(norm_g.reshape(4, 4, 8, 128).transpose(3, 0, 1, 2).reshape(128, 128))


def lay_rows(w):
    K, N = w.shape
    return np.ascontiguousarray(w.reshape(K // 128, 128, N).transpose(1, 0, 2))


def lay_vec(v):
    return np.ascontiguousarray(v.reshape(8, 128).T)


def lay_s5(lam_re, lam_im, log_step, b_re, b_im, c_re, c_im, d):
    f = np.float32
    par = np.zeros((128, 3, 32), f)
    l4 = lam_re.reshape(32, 2, 64)
    par[:, 0, :] = l4.transpose(1, 2, 0).reshape(128, 32)
    par[:, 1, :] = lam_im.reshape(32, 2, 64).transpose(1, 2, 0).reshape(128, 32)
    par[:, 2, :] = np.repeat(log_step.reshape(32, 2, 1), 64, axis=2).transpose(1, 2, 0).reshape(128, 32)
    par_bc = np.stack([lam_re.reshape(4096), lam_im.reshape(4096), np.repeat(log_step, 64)]).astype(f)

    def bpad(b):
        out = np.zeros((8, 16, 32, 2, 64), f)
        bb = b.reshape(8, 4, 2, 64, 16)
        for kc in range(8):
            for j in range(4):
                g8 = j * 2
                for gg in range(2):
                    out[g8 + gg, :, kc * 4 + j, gg, :] = bb[kc, j, gg].T
        return np.ascontiguousarray(out.reshape(128, 32, 128))

    def cpad(c):
        out = np.zeros((2, 64, 32, 8, 16), f)
        for g in range(64):
            out[g % 2, :, g // 2, g % 8, :] = c[g].T
        return np.ascontiguousarray(out.reshape(128, 32, 128))

    return {"par": par, "par_bc": np.ascontiguousarray(par_bc), "bpad_re": bpad(b_re), "bpad_im": bpad(b_im),
            "cpad_re": cpad(c_re), "cpad_im": cpad(c_im), "dvec": lay_vec(d).astype(f), "ident": np.eye(128, dtype=f)}


def ssd_consts():
    j = np.arange(128)
    ident = np.eye(128, dtype=np.float32)
    U = (j[:, None] <= j[None, :]).astype(np.float32)
    MS = (j[:, None] > j[None, :]).astype(np.float32)
    ones = np.ones((128, 128), np.float32)
    return np.ascontiguousarray(np.stack([ident, U, MS, ones], axis=1))


def lay_ssd(conv_w, conv_b, dt_bias, a_log, d_skip, norm_g):
    f = np.float32
    cw = np.zeros((128, 5, 32), f)
    for k in range(4):
        cw[:, k, :] = conv_w[k].reshape(32, 128).T
    cw[:, 4, :] = conv_b.reshape(32, 128).T
    hp = np.stack([dt_bias, a_log]).astype(f)
    dch = np.repeat(d_skip, 64).reshape(16, 128).T
    ng = norm_g.reshape(16, 128).T
    return {"cw": np.ascontiguousarray(cw), "hp_bc": np.ascontiguousarray(hp),
            "dch": np.ascontiguousarray(dch).astype(f), "ngv": np.ascontiguousarray(ng).astype(f)}


def lay_rg_vec(conv_w, conv_b, b_a, b_x, lam):
    vs = [conv_w[0], conv_w[1], conv_w[2], conv_w[3], conv_b, b_a, b_x, lam]
    return np.ascontiguousarray(np.stack([lay_vec(v) for v in vs], axis=1)).astype(np.float32)


def add_layer(B, layer, src, mid, dst):
    L = B.L
    kind = layer % 3
    pre = "l%d_" % layer
    if kind == 0:
        w_in = B.inp(pre + "w_in", [128, 8, 2048])
        w_a = B.inp(pre + "w_a", [16, 64, 64])
        w_x = B.inp(pre + "w_x", [16, 64, 64])
        vec = B.inp(pre + "vec", [128, 8, 8])
        w_out = B.inp(pre + "w_out", [128, 8, 1024])
        B.rglru_phase(layer, src, mid, w_in, w_a, w_x, vec, w_out)
    elif kind == 1:
        w_in = B.inp(pre + "w_in", [128, 8, 6176])
        w_out = B.inp(pre + "w_out", [128, 16, 1024])
        cw = B.inp(pre + "cw", [128, 5, 32])
        hp_bc = B.inp(pre + "hp_bc", [2, 32])
        dch = B.inp(pre + "dch", [128, 16])
        ngv = B.inp(pre + "ngv", [128, 16])
        consts = B.inp(pre + "consts", [128, 4, 128])
        zs = DramX(B.scratch(pre + "zs", [128, 16, L], BF16), L)
        xs = DramX(B.scratch(pre + "xs", [128, 16, L], BF16), L)
        bt = DramX(B.scratch(pre + "bt", [128, 8, L], BF16), L)
        ct = DramX(B.scratch(pre + "ct", [128, 8, L], BF16), L)
        dtt = DramX(B.scratch(pre + "dtt", [L // 128, 128, 32], F32), L)
        B.ssd_a_phase(layer, src, w_in, cw, zs, xs, bt, ct, dtt)
        B.ssd_b_phase(layer, src, mid, zs, xs, bt, ct, dtt, w_out, consts, hp_bc, dch, ngv)
    else:
        w_in = B.inp(pre + "w_in", [128, 8, 1024])
        w_out = B.inp(pre + "w_out", [128, 8, 2048])
        par = B.inp(pre + "par", [128, 3, 32])
        par_bc = B.inp(pre + "par_bc", [3, 4096])
        bre = B.inp(pre + "bpad_re", [128, 32, 128])
        bim = B.inp(pre + "bpad_im", [128, 32, 128])
        cre = B.inp(pre + "cpad_re", [128, 32, 128])
        cim = B.inp(pre + "cpad_im", [128, 32, 128])
        dvec = B.inp(pre + "dvec", [128, 8])
        ident = B.inp(pre + "ident", [128, 128])
        B.s5_phase(layer, src, mid, w_in, w_out, par, par_bc, bre, bim, cre, cim, dvec, ident)
    if ONLY_MIXER:
        return
    w1 = B.inp(pre + "w1", [128, 8, 4096])
    w2 = B.inp(pre + "w2", [128, 32, 1024])
    B.mlp_phase(layer, mid, dst, w1, w2)


def layer_params(layer, inp):
    kind = layer % 3
    j = layer // 3
    pre = "l%d_" % layer
    d = {}
    f = np.float32
    if kind == 0:
        d["w_in"] = lay_rows(inp["rg_w_in"][j])
        d["w_a"] = np.ascontiguousarray(inp["rg_w_a"][j])
        d["w_x"] = np.ascontiguousarray(inp["rg_w_x"][j])
        d["vec"] = lay_rg_vec(inp["rg_conv_w"][j], inp["rg_conv_b"][j], inp["rg_b_a"][j], inp["rg_b_x"][j], inp["rg_lam"][j])
        d["w_out"] = lay_rows(inp["rg_w_out"][j])
    elif kind == 1:
        d["w_in"] = lay_rows(inp["ssd_w_in"][j])
        d["w_out"] = lay_rows(inp["ssd_w_out"][j])
        d.update(lay_ssd(inp["ssd_conv_w"][j], inp["ssd_conv_b"][j], inp["ssd_dt_bias"][j], inp["ssd_a_log"][j],
                         inp["ssd_d"][j], inp["ssd_norm_g"][j]))
        d["consts"] = ssd_consts()
    else:
        d["w_in"] = lay_rows(inp["s5_w_in"][j])
        d["w_out"] = lay_rows(inp["s5_w_out"][j])
        d.update(lay_s5(inp["s5_lam_re"][j], inp["s5_lam_im"][j], inp["s5_log_step"][j], inp["s5_b_re"][j], inp["s5_b_im"][j],
                        inp["s5_c_re"][j], inp["s5_c_im"][j], inp["s5_d"][j]))
    d["w1"] = lay_rows(inp["mlp_w1"][layer])
    d["w2"] = lay_rows(inp["mlp_w2"][layer])
    return {pre + k: np.ascontiguousarray(v, dtype=f) for k, v in d.items()}


def build_program(L, layers):
    B = Builder(L)
    xin = DramX(B.inp("x", [128, 8, L]), L)
    xout = DramX(B.outp("y", [128, 8, L]), L)
    xres = DramX(B.scratch("xres", [128, 8, L]), L)
    n = len(layers)
    for idx, layer in enumerate(layers):
        src = xin if idx == 0 else xres
        dst = xout if idx == n - 1 else xres
        add_layer(B, layer, src, xres, dst)
    B.P.barrier(["sp"])
    B.P.emit()
    return B


ONLY_MIXER = False
LAUNCH_GROUPS = [[0, 1, 2, 3]]


def kernel(**inp):
    inp = {k: np.asarray(v) for k, v in inp.items()}
    x = inp["x"]
    nb, L, _ = x.shape
    xs = [x_to_dev(np.asarray(x[b], dtype=np.float32)) for b in range(nb)]
    g = lay_norm_g(inp["norm_g"].astype(np.float32))
    for layers in LAUNCH_GROUPS:
        B = build_program(L, layers)
        common = {"norm_g": g}
        for layer in layers:
            common.update(layer_params(layer, inp))
        in_maps = [dict(common, x=xs[b]) for b in range(nb)]
        res = run_bass_kernel_spmd(B.nc, in_maps, core_ids=list(range(nb)))
        xs = [np.asarray(res.results[b]["y"]) for b in range(nb)]
    out = np.stack([x_from_dev(xd) for xd in xs]).astype(np.float32)
    return out
```

```python
import numpy as np
import concourse.bass as bass
import concourse.mybir as mybir
from concourse.bass_utils import run_bass_kernel_spmd

F32 = mybir.dt.float32
BF16 = mybir.dt.bfloat16
AF = mybir.ActivationFunctionType
ALU = mybir.AluOpType

D = 1024
SEQ = 4096
NCORES = 8
EPS = 1e-6


class Res:
    __slots__ = ("w", "r")

    def __init__(self):
        self.w = None
        self.r = {}


class Prog:
    ENGS = ("pe", "act", "dve", "pool", "sp")

    def __init__(self, nc, n_dma_sems=40, same_engine_sync=("act", "dve", "pool")):
        self.nc = nc
        self.q = {e: [] for e in self.ENGS}
        self.cnt = {e: 0 for e in self.ENGS}
        self.sems = {e: nc.alloc_semaphore("sem_" + e) for e in self.ENGS}
        self.n_dma = n_dma_sems
        for i in range(n_dma_sems):
            self.sems[("d", i)] = nc.alloc_semaphore("sem_d%d" % i)
        self.dma_tot = [0] * n_dma_sems
        self.dma_i = 0
        self.waited = {e: {} for e in self.ENGS}
        self.ses = set(same_engine_sync)
        self.ninst = 0

    def _deps(self, eng, reads, writes, extra=()):
        deps = {}

        def need(tok):
            if tok is None:
                return
            k, v = tok
            if k == eng and eng not in self.ses:
                return
            if deps.get(k, 0) < v:
                deps[k] = v

        for r in reads:
            need(r.w)
        for w in writes:
            need(w.w)
            for k, v in w.r.items():
                need((k, v))
        for tok in extra:
            need(tok)
        wd = self.waited[eng]
        waits = []
        for k, v in deps.items():
            if wd.get(k, 0) < v:
                wd[k] = v
                waits.append((self.sems[k], v))
        return waits

    def _mark(self, tok, reads, writes):
        k, v = tok
        for r in reads:
            if r.r.get(k, 0) < v:
                r.r[k] = v
        for w in writes:
            w.w = tok
            w.r = {}

    def op(self, eng, fn, reads=(), writes=(), inc=True):
        waits = self._deps(eng, reads, writes)
        tok = (eng, self.cnt[eng] + 1)
        sem = self.sems[eng]
        if inc:
            self.cnt[eng] += 1

            def closure(e):
                for s, v in waits:
                    e.wait_ge(s, v)
                fn(e).then_inc(sem, 1)
        else:
            def closure(e):
                for s, v in waits:
                    e.wait_ge(s, v)
                fn(e)
        self.q[eng].append(closure)
        self._mark(tok, reads, writes)
        self.ninst += 1
        return tok

    def dma(self, eng, out, in_, reads=(), writes=(), **kw):
        idx = self.dma_i % self.n_dma
        self.dma_i += 1
        key = ("d", idx)
        extra = ()
        if self.dma_tot[idx] > 0:
            extra = ((key, 16 * self.dma_tot[idx]),)
        waits = self._deps(eng, reads, writes, extra)
        self.dma_tot[idx] += 1
        tok = (key, 16 * self.dma_tot[idx])
        sem = self.sems[key]

        def closure(e):
            for s, v in waits:
                e.wait_ge(s, v)
            e.dma_start(out=out, in_=in_, **kw).then_inc(sem, 16)
        self.q[eng].append(closure)
        self._mark(tok, reads, writes)
        self.ninst += 1
        return tok

    def all_tokens(self):
        toks = [(e, self.cnt[e]) for e in self.ENGS if self.cnt[e] > 0]
        toks += [(("d", i), 16 * n) for i, n in enumerate(self.dma_tot) if n > 0]
        return toks

    def barrier(self, engs=None):
        toks = self.all_tokens()
        for eng in (engs or self.ENGS):
            wd = self.waited[eng]
            ws = []
            for k, v in toks:
                if k == eng:
                    continue
                if wd.get(k, 0) < v:
                    wd[k] = v
                    ws.append((self.sems[k], v))
            if ws:
                def closure(e, ws=ws):
                    for s, v in ws:
                        e.wait_ge(s, v)
                self.q[eng].append(closure)

    def emit(self):
        nc = self.nc
        q = self.q
        with nc.Block() as block:
            @block.tensor
            def _(e):
                for f in q["pe"]:
                    f(e)

            @block.scalar
            def _(e):
                for f in q["act"]:
                    f(e)

            @block.vector
            def _(e):
                for f in q["dve"]:
                    f(e)

            @block.gpsimd
            def _(e):
                for f in q["pool"]:
                    f(e)

            @block.sync
            def _(e):
                for f in q["sp"]:
                    f(e)


class Arena:
    def __init__(self, nc, nbytes=212800):
        self.t = nc.alloc_sbuf_tensor("arena", [128, nbytes // 2], BF16)
        self.nbytes = nbytes
        self.off = 0

    def alloc(self, free, dtype):
        if isinstance(free, int):
            free = (free,)
        n = 1
        for s in free:
            n *= s
        sz = n * (4 if dtype == F32 else 2)
        off = (self.off + 63) // 64 * 64
        assert off + sz <= self.nbytes, ("SBUF arena overflow", off, sz, self.nbytes)
        self.off = off + sz
        ap = self.t[:, off // 2:(off + sz) // 2]
        if dtype == F32:
            ap = ap.bitcast(F32)
        if len(free) == 2:
            ap = ap.rearrange("p (a b) -> p a b", a=free[0])
        elif len(free) == 3:
            ap = ap.rearrange("p (a b c) -> p a b c", a=free[0], b=free[1])
        return ap


class DramX:
    def __init__(self, ap, L):
        self.ap = ap
        self.res = [Res() for _ in range(L // 256)]

    def blocks(self, t0, T):
        return self.res[t0 // 256:(t0 + T) // 256]


class Builder:
    def __init__(self, L):
        self.L = L
        nc = self.nc = bass.Bass("TRN2", target_bir_lowering=False)
        self.P = Prog(nc)
        self.A = Arena(nc)
        self.bank = [nc.alloc_psum_tensor("bank%d" % i, [128, 512], F32) for i in range(8)]
        self.bres = [Res() for _ in range(8)]
        self.ext = {}
        A = self.A
        self.ones = A.alloc(128, BF16)
        self.g_sb = A.alloc(128, F32)
        self.epsc = A.alloc(8, F32)
        self.persist_end = A.off
        P = self.P
        r = Res()
        P.op("pool", lambda e: e.memset(self.ones, 1.0), writes=[r])
        P.op("pool", lambda e: e.memset(self.epsc[:, 0:1], EPS), writes=[r])
        P.op("pool", lambda e: e.memset(self.epsc[:, 1:2], 1.0), writes=[r])
        g = self.inp("norm_g", [128, 128])
        P.dma("sp", self.g_sb, g, writes=[r])
        P.barrier()

    def inp(self, name, shape, dtype=F32):
        ap = self.nc.dram_tensor(name, list(shape), dtype, kind="ExternalInput").ap()
        self.ext[name] = ap
        return ap

    def outp(self, name, shape, dtype=F32):
        return self.nc.dram_tensor(name, list(shape), dtype, kind="ExternalOutput").ap()

    def scratch(self, name, shape, dtype=F32):
        return self.nc.dram_tensor(name, list(shape), dtype, kind="Internal").ap()

    def full(self, b):
        return self.bank[b][:, :], [self.bres[b]]

    def gcol(self, layer, j, c):
        i = (layer * 4 + j) * 8 + c
        return self.g_sb[:, i:i + 1]

    def rstd_from_sq(self, sq, r_sq, nchunks, T, slot, ms, r_ms, rstd, r_rstd, denom):
        P = self.P
        ps, rps = slot
        for c in range(nchunks):
            P.op("pe", lambda e, c=c: e.matmul(ps[:, 0:T], self.ones, sq[:, c, :], start=(c == 0), stop=(c == nchunks - 1)),
                 reads=[r_sq], writes=rps, inc=(c == nchunks - 1))
        P.op("act", lambda e: e.activation(out=ms, in_=ps[:, 0:T], func=AF.Ln, scale=1.0 / denom, bias=self.epsc[:, 0:1]),
             reads=rps, writes=[r_ms])
        P.op("act", lambda e: e.activation(out=rstd, in_=ms, func=AF.Exp, scale=-0.5),
             reads=[r_ms], writes=[r_rstd])

    def mlp_phase(self, layer, src, dst, w1, w2):
        P, A = self.P, self.A
        T = 256
        NT = self.L // T
        A.off = self.persist_end
        W1 = A.alloc((8, 4096), BF16)
        W2 = A.alloc((32, 1024), BF16)
        rW1 = [Res() for _ in range(8)]
        rW2 = [Res() for _ in range(8)]
        for kc in range(8):
            P.dma("pool", W1[:, :, kc * 512:(kc + 1) * 512], w1[:, :, kc * 512:(kc + 1) * 512], writes=[rW1[kc]])
        for g in range(8):
            P.dma("pool", W2[:, 4 * g:4 * g + 4, :], w2[:, 4 * g:4 * g + 4, :], writes=[rW2[g]])
        xt = [A.alloc((8, T), F32) for _ in range(2)]
        r_xt = [Res() for _ in range(2)]
        xn = [A.alloc((8, T), BF16) for _ in range(2)]
        r_xn = [Res() for _ in range(2)]
        sq = [A.alloc((8, T), BF16) for _ in range(2)]
        r_sq = [Res() for _ in range(2)]
        ms = [A.alloc(T, F32) for _ in range(2)]
        r_ms = [Res() for _ in range(2)]
        rstd = [A.alloc(T, F32) for _ in range(2)]
        r_rstd = [Res() for _ in range(2)]
        h = A.alloc((32, T), BF16)
        r_h = [Res() for _ in range(32)]
        hout = A.alloc((8, T), F32)
        r_hout = [Res() for _ in range(8)]
        sq2 = A.alloc((8, T), BF16)
        r_sq2 = Res()
        NTMP = 2
        tmp = [A.alloc(2 * T, F32) for _ in range(NTMP)]
        r_tmp = [Res() for _ in range(NTMP)]
        ms2 = A.alloc(T, F32)
        r_ms2 = Res()
        rstd2 = A.alloc(T, F32)
        r_rstd2 = Res()
        hslots = [self.full(b) for b in range(4)]
        oslots = [self.full(4), self.full(5)]
        nslots = [self.full(6), self.full(6)]
        n2slot = self.full(7)
        cnt = {"h": 0, "o": 0}

        def load(i):
            b = i % 2
            P.dma("sp", xt[b], src.ap[:, :, i * T:(i + 1) * T], reads=src.blocks(i * T, T), writes=[r_xt[b]])

        def pre(i):
            b = i % 2
            xf = xt[b].rearrange("p c t -> p (c t)")
            sf = sq[b].rearrange("p c t -> p (c t)")
            P.op("act", lambda e: e.activation(out=sf, in_=xf, func=AF.Square), reads=[r_xt[b]], writes=[r_sq[b]])
            self.rstd_from_sq(sq[b], r_sq[b], 8, T, nslots[b], ms[b], r_ms[b], rstd[b], r_rstd[b], float(D))
            for c in range(8):
                P.op("dve", lambda e, c=c: e.scalar_tensor_tensor(
                    out=xn[b][:, c, :], in0=xt[b][:, c, :], scalar=self.gcol(layer, 2, c), in1=rstd[b],
                    op0=ALU.mult, op1=ALU.mult), reads=[r_xt[b], r_rstd[b]], writes=[r_xn[b]])

        def mm1(i):
            b = i % 2
            for m2 in range(16):
                ps, rps = hslots[cnt["h"] % len(hslots)]
                tp = cnt["h"] % NTMP
                cnt["h"] += 1
                for hh in range(2):
                    m = 2 * m2 + hh
                    for kc in range(8):
                        P.op("pe", lambda e, kc=kc, m=m, hh=hh, ps=ps: e.matmul(
                            ps[:, hh * T:(hh + 1) * T], W1[:, kc, m * 128:(m + 1) * 128], xn[b][:, kc, :],
                            start=(kc == 0), stop=(kc == 7)),
                            reads=[rW1[m // 4], r_xn[b]], writes=rps, inc=(kc == 7 and hh == 1))
                P.op("act", lambda e, ps=ps, tp=tp: e.activation(out=tmp[tp], in_=ps, func=AF.Relu),
                     reads=rps, writes=[r_tmp[tp]])
                hv = h[:, 2 * m2:2 * m2 + 2, :].rearrange("p c t -> p (c t)")
                P.op("pool", lambda e, hv=hv, tp=tp: e.tensor_tensor(out=hv, in0=tmp[tp], in1=tmp[tp], op=ALU.mult),
                     reads=[r_tmp[tp]], writes=[r_h[2 * m2], r_h[2 * m2 + 1]])

        def mm2(i):
            for m2 in range(4):
                ps, rps = oslots[cnt["o"] % len(oslots)]
                cnt["o"] += 1
                for hh in range(2):
                    m = 2 * m2 + hh
                    for kc in range(32):
                        P.op("pe", lambda e, kc=kc, m=m, hh=hh, ps=ps: e.matmul(
                            ps[:, hh * T:(hh + 1) * T], W2[:, kc, m * 128:(m + 1) * 128], h[:, kc, :],
                            start=(kc == 0), stop=(kc == 31)),
                            reads=[rW2[kc // 4], r_h[kc]], writes=rps, inc=(kc == 31 and hh == 1))
                ho = hout[:, 2 * m2:2 * m2 + 2, :].rearrange("p c t -> p (c t)")
                so = sq2[:, 2 * m2:2 * m2 + 2, :].rearrange("p c t -> p (c t)")
                P.op("act", lambda e, ho=ho, ps=ps: e.activation(out=ho, in_=ps, func=AF.Copy),
                     reads=rps, writes=[r_hout[2 * m2], r_hout[2 * m2 + 1]])
                P.op("act", lambda e, so=so, ps=ps: e.activation(out=so, in_=ps, func=AF.Square),
                     reads=rps, writes=[r_sq2])

        def post(i):
            b = i % 2
            self.rstd_from_sq(sq2, r_sq2, 8, T, n2slot, ms2, r_ms2, rstd2, r_rstd2, float(D))
            for c in range(8):
                P.op("dve", lambda e, c=c: e.scalar_tensor_tensor(
                    out=hout[:, c, :], in0=hout[:, c, :], scalar=self.gcol(layer, 3, c), in1=rstd2,
                    op0=ALU.mult, op1=ALU.mult), reads=[r_hout[c], r_rstd2], writes=[r_hout[c]])
                P.op("pool", lambda e, c=c: e.tensor_tensor(out=xt[b][:, c, :], in0=xt[b][:, c, :], in1=hout[:, c, :], op=ALU.add),
                     reads=[r_hout[c], r_xt[b]], writes=[r_xt[b]])
            P.dma("sp", dst.ap[:, :, i * T:(i + 1) * T], xt[b], reads=[r_xt[b]], writes=dst.blocks(i * T, T))

        load(0)
        pre(0)
        if NT > 1:
            load(1)
        for i in range(NT):
            mm1(i)
            if i + 1 < NT:
                pre(i + 1)
            mm2(i)
            post(i)
            if i + 2 < NT:
                load(i + 2)
        P.barrier()


    def rglru_phase(self, layer, src, dst, w_in, w_a, w_x, vec, w_out):
        P, A = self.P, self.A
        T = 512
        NT = self.L // T
        A.off = self.persist_end
        Win = A.alloc((8, 2048), BF16)
        Wout = A.alloc((8, 1024), BF16)
        Wa = A.alloc((8, 128), BF16)
        Wx = A.alloc((8, 128), BF16)
        V = A.alloc((8, 8), F32)
        rW = Res()
        rWin = [Res() for _ in range(8)]
        for kc in range(8):
            cg = (kc + 4) % 8
            P.dma("pool", Win[:, :, cg * 256:(cg + 1) * 256], w_in[:, :, cg * 256:(cg + 1) * 256], writes=[rWin[cg]])
        P.dma("pool", Wout, w_out, writes=[rW])
        P.op("dve", lambda e: e.memset(Wa, 0.0), writes=[rW])
        P.op("dve", lambda e: e.memset(Wx, 0.0), writes=[rW])
        for hh in range(2):
            for (Wd, wsrc) in ((Wa, w_a), (Wx, w_x)):
                sv = wsrc.rearrange("(c hh) i j -> hh i c j", hh=2)[hh]
                P.dma("pool", Wd[hh * 64:(hh + 1) * 64, :, hh * 64:(hh + 1) * 64], sv, writes=[rW])
        P.dma("sp", V, vec, writes=[rW])
        DV = A.alloc((4, 8), F32)
        halfc = A.alloc(512, F32)
        P.op("pool", lambda e: e.memset(halfc, 0.5), writes=[rW])
        P.op("dve", lambda e: e.tensor_scalar(out=DV[:, 0, :], in0=V[:, 5, :], scalar1=0.5, scalar2=None, op0=ALU.mult), reads=[rW], writes=[rW])
        P.op("dve", lambda e: e.tensor_scalar(out=DV[:, 1, :], in0=V[:, 6, :], scalar1=0.5, scalar2=None, op0=ALU.mult), reads=[rW], writes=[rW])
        P.op("act", lambda e: e.activation(out=DV[:, 3, :], in_=V[:, 7, :], func=AF.Exp, scale=-1.0), reads=[rW], writes=[rW])
        P.op("dve", lambda e: e.tensor_scalar(out=DV[:, 3, :], in0=DV[:, 3, :], scalar1=1.0, scalar2=None, op0=ALU.add), reads=[rW], writes=[rW])
        P.op("act", lambda e: e.activation(out=DV[:, 3, :], in_=DV[:, 3, :], func=AF.Ln), reads=[rW], writes=[rW])
        P.op("dve", lambda e: e.tensor_scalar(out=DV[:, 2, :], in0=DV[:, 3, :], scalar1=-4.0, scalar2=None, op0=ALU.mult), reads=[rW], writes=[rW])
        P.op("dve", lambda e: e.tensor_scalar(out=DV[:, 3, :], in0=DV[:, 3, :], scalar1=-8.0, scalar2=None, op0=ALU.mult), reads=[rW], writes=[rW])

        xts = [A.alloc((8, T), F32) for _ in range(2)]; r_xts = [Res() for _ in range(2)]
        sq = A.alloc((8, T), BF16); r_sq = Res()
        xn = A.alloc((8, T), BF16); r_xn = Res()
        gate = A.alloc((8, T), F32); r_gate = [Res() for _ in range(8)]
        hout, r_hout = gate, r_gate
        UB = T + 4
        ubuf = A.alloc((8, UB), F32); r_ub = [Res() for _ in range(8)]
        uc = A.alloc((8, T), F32); r_uc = [Res() for _ in range(8)]
        ucb = A.alloc((8, T), BF16); r_ucb = [Res() for _ in range(8)]
        hb = A.alloc((8, T), F32); r_h = [Res() for _ in range(8)]
        hst = A.alloc(8, F32); r_hst = Res()
        ms = A.alloc(T, F32); r_ms = Res()
        rstd = A.alloc(T, F32); r_rstd = Res()
        NS = 2
        tA = [[A.alloc(T, F32) for _ in range(5)] for _ in range(NS)]
        r_tA = [[Res() for _ in range(5)] for _ in range(NS)]
        yb = xn
        r_yb = r_xn
        sq2 = sq
        r_sq2 = r_sq
        pslots = [self.full(b) for b in range(4)]
        gslots = [self.full(4), self.full(5)]
        nslot = self.full(6)
        n2slot = self.full(7)
        cnt = {"p": 0, "g": 0, "t": 0}
        P.op("pool", lambda e: e.memset(ubuf, 0.0), writes=r_ub)
        P.op("pool", lambda e: e.memset(hst, 0.0), writes=[r_hst])

        def vcol(v, c):
            return V[:, v, c:c + 1]

        def dcol(v, c):
            return DV[:, v, c:c + 1]

        def load(i):
            P.dma("sp", xts[i % 2], src.ap[:, :, i * T:(i + 1) * T], reads=src.blocks(i * T, T), writes=[r_xts[i % 2]])

        def tile(i, xt, r_xt):
            xf = xt.rearrange("p c t -> p (c t)")
            sf = sq.rearrange("p c t -> p (c t)")
            P.op("act", lambda e: e.activation(out=sf, in_=xf, func=AF.Square), reads=[r_xt], writes=[r_sq])
            self.rstd_from_sq(sq, r_sq, 8, T, nslot, ms, r_ms, rstd, r_rstd, float(D))
            for c in range(8):
                P.op("dve", lambda e, c=c: e.scalar_tensor_tensor(
                    out=xn[:, c, :], in0=xt[:, c, :], scalar=self.gcol(layer, 0, c), in1=rstd,
                    op0=ALU.mult, op1=ALU.mult), reads=[r_xt, r_rstd], writes=[r_xn])
            for m in list(range(8, 16)) + list(range(8)):
                ps, rps = pslots[cnt["p"] % 4]
                cnt["p"] += 1
                for kc in range(8):
                    P.op("pe", lambda e, kc=kc, m=m, ps=ps: e.matmul(
                        ps, Win[:, kc, m * 128:(m + 1) * 128], xn[:, kc, :], start=(kc == 0), stop=(kc == 7)),
                        reads=[rWin[m // 2], r_xn], writes=rps, inc=(kc == 7))
                if m < 8:
                    P.op("act", lambda e, m=m, ps=ps: e.activation(out=gate[:, m, :], in_=ps, func=AF.Gelu_apprx_tanh),
                         reads=rps, writes=[r_gate[m]])
                else:
                    c = m - 8
                    P.op("act", lambda e, c=c, ps=ps: e.activation(out=ubuf[:, c, 4:4 + T], in_=ps, func=AF.Copy),
                         reads=rps, writes=[r_ub[c]])
                    P.op("act", lambda e, c=c, ps=ps: e.activation(out=uc[:, c, :], in_=ps, func=AF.Identity,
                                                                   scale=vcol(3, c), bias=vcol(4, c)),
                         reads=rps + [rW], writes=[r_uc[c]])
                    for k in range(3):
                        P.op("dve", lambda e, c=c, k=k: e.scalar_tensor_tensor(
                            out=uc[:, c, :], in0=ubuf[:, c, 1 + k:1 + k + T], scalar=vcol(k, c), in1=uc[:, c, :],
                            op0=ALU.mult, op1=ALU.add), reads=[r_ub[c], r_uc[c], rW], writes=[r_uc[c]])
                    P.op("pool", lambda e, c=c: e.tensor_copy(out=ubuf[:, c, 1:4], in_=ubuf[:, c, T + 1:T + 4]),
                         reads=[r_ub[c]], writes=[r_ub[c]])
                    P.op("pool", lambda e, c=c: e.tensor_copy(out=ucb[:, c, :], in_=uc[:, c, :]),
                         reads=[r_uc[c]], writes=[r_ucb[c]])
            for c in range(8):
                s = cnt["t"] % NS
                cnt["t"] += 1
                th, aa, a2, thx, bb = tA[s]
                r_th, r_aa, r_a2, r_thx, r_bb = r_tA[s]
                psr, rpsr = gslots[0]
                psx, rpsx = gslots[1]
                P.op("pe", lambda e, c=c, psr=psr: e.matmul(psr, Wa[:, c, :], ucb[:, c, :], start=True, stop=True),
                     reads=[rW, r_ucb[c]], writes=rpsr)
                P.op("pe", lambda e, c=c, psx=psx: e.matmul(psx, Wx[:, c, :], ucb[:, c, :], start=True, stop=True),
                     reads=[rW, r_ucb[c]], writes=rpsx)
                P.op("act", lambda e, c=c, th=th, psr=psr: e.activation(out=th, in_=psr, func=AF.Tanh, scale=0.5, bias=dcol(0, c)),
                     reads=rpsr + [rW], writes=[r_th])
                P.op("act", lambda e, c=c, thx=thx, psx=psx: e.activation(out=thx, in_=psx, func=AF.Tanh, scale=0.5, bias=dcol(1, c)),
                     reads=rpsx + [rW], writes=[r_thx])
                P.op("act", lambda e, c=c, th=th, aa=aa: e.activation(out=aa, in_=th, func=AF.Exp, scale=dcol(2, c), bias=dcol(2, c)),
                     reads=[r_th, rW], writes=[r_aa])
                P.op("act", lambda e, c=c, th=th, a2=a2: e.activation(out=a2, in_=th, func=AF.Exp, scale=dcol(3, c), bias=dcol(3, c)),
                     reads=[r_th, rW], writes=[r_a2])
                P.op("act", lambda e, a2=a2: e.activation(out=a2, in_=a2, func=AF.Ln, scale=-1.0, bias=self.epsc[:, 1:2]),
                     reads=[r_a2], writes=[r_a2])
                P.op("act", lambda e, a2=a2: e.activation(out=a2, in_=a2, func=AF.Exp, scale=0.5),
                     reads=[r_a2], writes=[r_a2])
                P.op("dve", lambda e, c=c, thx=thx, bb=bb: e.scalar_tensor_tensor(
                    out=bb, in0=thx, scalar=1.0, in1=uc[:, c, :], op0=ALU.add, op1=ALU.mult),
                    reads=[r_thx, r_uc[c]], writes=[r_bb])
                P.op("dve", lambda e, a2=a2, bb=bb: e.scalar_tensor_tensor(
                    out=bb, in0=bb, scalar=0.5, in1=a2, op0=ALU.mult, op1=ALU.mult),
                    reads=[r_bb, r_a2], writes=[r_bb])
                P.op("dve", lambda e, c=c, aa=aa, bb=bb: e.tensor_tensor_scan(
                    out=hb[:, c, :], data0=aa, data1=bb, initial=hst[:, c:c + 1], op0=ALU.mult, op1=ALU.add),
                    reads=[r_aa, r_bb, r_hst], writes=[r_h[c]])
            P.op("dve", lambda e: e.tensor_copy(out=hst, in_=hb[:, :, T - 1]), reads=r_h, writes=[r_hst])
            P.op("dve", lambda e: e.tensor_tensor(out=yb.rearrange("p c t -> p (c t)"), in0=hb.rearrange("p c t -> p (c t)"),
                                                  in1=gate.rearrange("p c t -> p (c t)"), op=ALU.mult),
                 reads=r_h + r_gate, writes=[r_yb])
            for m in range(8):
                ps, rps = pslots[cnt["p"] % 4]
                cnt["p"] += 1
                for kc in range(8):
                    P.op("pe", lambda e, kc=kc, m=m, ps=ps: e.matmul(
                        ps, Wout[:, kc, m * 128:(m + 1) * 128], yb[:, kc, :], start=(kc == 0), stop=(kc == 7)),
                        reads=[rW, r_yb], writes=rps, inc=(kc == 7))
                P.op("act", lambda e, m=m, ps=ps: e.activation(out=hout[:, m, :], in_=ps, func=AF.Copy),
                     reads=rps, writes=[r_hout[m]])
                P.op("act", lambda e, m=m, ps=ps: e.activation(out=sq2[:, m, :], in_=ps, func=AF.Square),
                     reads=rps, writes=[r_sq2])
            self.post_norm_residual(layer, 1, hout, r_hout, sq2, r_sq2, xt, r_xt, T, n2slot, ms, r_ms, rstd, r_rstd)
            P.dma("sp", dst.ap[:, :, i * T:(i + 1) * T], xt, reads=[r_xt], writes=dst.blocks(i * T, T))
        load(0)
        for i in range(NT):
            if i + 1 < NT:
                load(i + 1)
            tile(i, xts[i % 2], r_xts[i % 2])
        P.barrier()

    def post_norm_residual(self, layer, j, hout, r_hout, sq2, r_sq2, xt, r_xt, T, slot, ms, r_ms, rstd, r_rstd):
        P = self.P
        self.rstd_from_sq(sq2, r_sq2, 8, T, slot, ms, r_ms, rstd, r_rstd, float(D))
        for c in range(8):
            P.op("dve", lambda e, c=c: e.scalar_tensor_tensor(
                out=hout[:, c, :], in0=hout[:, c, :], scalar=self.gcol(layer, j, c), in1=rstd,
                op0=ALU.mult, op1=ALU.mult), reads=[r_hout[c], r_rstd], writes=[r_hout[c]])
            P.op("pool", lambda e, c=c: e.tensor_tensor(out=xt[:, c, :], in0=xt[:, c, :], in1=hout[:, c, :], op=ALU.add),
                 reads=[r_hout[c], r_xt], writes=[r_xt])


    def sincos(self, th, out_sin, out_cos, tmp, r, eng="dve"):
        P = self.P
        MAGIC = 12582912.0
        TWO_PI = 6.283185307179586
        for (shift, dst) in ((0.0, out_sin), (1.5707963267948966, out_cos)):
            P.op(eng, lambda e, shift=shift: e.tensor_scalar(out=tmp, in0=th, scalar1=shift, scalar2=1.0 / TWO_PI, op0=ALU.add, op1=ALU.mult),
                 reads=[r], writes=[r])
            P.op(eng, lambda e: e.tensor_scalar(out=tmp, in0=tmp, scalar1=MAGIC, scalar2=None, op0=ALU.add), reads=[r], writes=[r])
            P.op(eng, lambda e: e.tensor_scalar(out=tmp, in0=tmp, scalar1=-MAGIC, scalar2=-TWO_PI, op0=ALU.add, op1=ALU.mult), reads=[r], writes=[r])
            P.op(eng, lambda e, shift=shift, dst=dst: e.scalar_tensor_tensor(out=dst, in0=th, scalar=shift, in1=tmp, op0=ALU.add, op1=ALU.add),
                 reads=[r], writes=[r])
            P.op("act", lambda e, dst=dst: e.activation(out=dst, in_=dst, func=AF.Sin), reads=[r], writes=[r])

    def s5a_phase(self, layer, src, w_in, su, sb):
        P, A = self.P, self.A
        T = 256
        NT = self.L // T
        A.off = self.persist_end
        rW = Res()
        Win = A.alloc((8, 1024), BF16)
        P.dma("pool", Win, w_in, writes=[rW])
        xt = [A.alloc((8, T), F32) for _ in range(2)]; r_xt = [Res() for _ in range(2)]
        sq = [A.alloc((8, T), BF16) for _ in range(2)]; r_sq = [Res() for _ in range(2)]
        xn = [A.alloc((8, T), BF16) for _ in range(2)]; r_xn = [Res() for _ in range(2)]
        ms = [A.alloc(T, F32) for _ in range(2)]; r_ms = [Res() for _ in range(2)]
        rstd = [A.alloc(T, F32) for _ in range(2)]; r_rstd = [Res() for _ in range(2)]
        uo = [A.alloc((8, T), F32) for _ in range(2)]; r_uo = [Res() for _ in range(2)]
        ubo = [A.alloc((8, T), BF16) for _ in range(2)]; r_ubo = [Res() for _ in range(2)]
        pslots = [self.full(b) for b in range(4)]
        nslots = [self.full(6), self.full(7)]
        cnt = {"p": 0}

        def load(i):
            b = i % 2
            P.dma("sp", xt[b], src.ap[:, :, i * T:(i + 1) * T], reads=src.blocks(i * T, T), writes=[r_xt[b]])

        def pre(i):
            b = i % 2
            P.op("act", lambda e: e.activation(out=sq[b].rearrange("p c t -> p (c t)"), in_=xt[b].rearrange("p c t -> p (c t)"), func=AF.Square),
                 reads=[r_xt[b]], writes=[r_sq[b]])
            self.rstd_from_sq(sq[b], r_sq[b], 8, T, nslots[b], ms[b], r_ms[b], rstd[b], r_rstd[b], float(D))
            for c in range(8):
                P.op("dve", lambda e, c=c: e.scalar_tensor_tensor(
                    out=xn[b][:, c, :], in0=xt[b][:, c, :], scalar=self.gcol(layer, 0, c), in1=rstd[b],
                    op0=ALU.mult, op1=ALU.mult), reads=[r_xt[b], r_rstd[b]], writes=[r_xn[b]])

        def mm(i):
            b = i % 2
            for m2 in range(4):
                ps, rps = pslots[cnt["p"] % len(pslots)]
                cnt["p"] += 1
                for hh in range(2):
                    mm_ = 2 * m2 + hh
                    for kc in range(8):
                        P.op("pe", lambda e, kc=kc, mm_=mm_, hh=hh, ps=ps: e.matmul(
                            ps[:, hh * T:(hh + 1) * T], Win[:, kc, mm_ * 128:(mm_ + 1) * 128], xn[b][:, kc, :],
                            start=(kc == 0), stop=(kc == 7)), reads=[rW, r_xn[b]], writes=rps, inc=(kc == 7 and hh == 1))
                o1 = uo[b][:, 2 * m2:2 * m2 + 2, :].rearrange("p c t -> p (c t)")
                o2 = ubo[b][:, 2 * m2:2 * m2 + 2, :].rearrange("p c t -> p (c t)")
                P.op("act", lambda e, o1=o1, ps=ps: e.activation(out=o1, in_=ps, func=AF.Copy), reads=rps, writes=[r_uo[b]])
                P.op("act", lambda e, o2=o2, ps=ps: e.activation(out=o2, in_=ps, func=AF.Copy), reads=rps, writes=[r_ubo[b]])
            P.dma("sp", su.ap[:, :, i * T:(i + 1) * T], uo[b], reads=[r_uo[b]])
            P.dma("sp", sb.ap[:, :, i * T:(i + 1) * T], ubo[b], reads=[r_ubo[b]])

        load(0)
        pre(0)
        if NT > 1:
            load(1)
        for i in range(NT):
            if i + 1 < NT:
                pre(i + 1)
            mm(i)
            if i + 2 < NT:
                load(i + 2)
        P.barrier()

    def s5c_phase(self, layer, src, dst, w_out, sy):
        P, A = self.P, self.A
        T = 256
        NT = self.L // T
        A.off = self.persist_end
        rW = Res()
        Wout = A.alloc((8, 2048), BF16)
        for hh in range(2):
            P.dma("pool", Wout[:, :, hh * 1024:(hh + 1) * 1024], w_out[:, :, hh * 1024:(hh + 1) * 1024], writes=[rW])
        xt = [A.alloc((8, T), F32) for _ in range(2)]; r_xt = [Res() for _ in range(2)]
        yb = [A.alloc((8, T), BF16) for _ in range(2)]; r_yb = [Res() for _ in range(2)]
        hout = [A.alloc((8, T), F32) for _ in range(2)]; r_hout = [[Res() for _ in range(8)] for _ in range(2)]
        sq2 = [A.alloc((8, T), BF16) for _ in range(2)]; r_sq2 = [Res() for _ in range(2)]
        tg = [A.alloc(T, F32) for _ in range(3)]; r_tg = [Res() for _ in range(3)]
        ms = [A.alloc(T, F32) for _ in range(2)]; r_ms = [Res() for _ in range(2)]
        rstd = [A.alloc(T, F32) for _ in range(2)]; r_rstd = [Res() for _ in range(2)]
        pslots = [self.full(b) for b in range(5)]
        nslots = [self.full(6), self.full(7)]
        cnt = {"p": 0, "t": 0}

        def load(i):
            b = i % 2
            P.dma("sp", yb[b], sy.ap[:, :, i * T:(i + 1) * T], writes=[r_yb[b]])
            P.dma("sp", xt[b], src.ap[:, :, i * T:(i + 1) * T], reads=src.blocks(i * T, T), writes=[r_xt[b]])

        def body(i):
            b = i % 2
            for mo in range(8):
                ps, rps = pslots[cnt["p"] % len(pslots)]
                cnt["p"] += 1
                for hh in range(2):
                    col = (hh * 8 + mo) * 128
                    for kc in range(8):
                        P.op("pe", lambda e, kc=kc, col=col, hh=hh, ps=ps: e.matmul(
                            ps[:, hh * T:(hh + 1) * T], Wout[:, kc, col:col + 128], yb[b][:, kc, :],
                            start=(kc == 0), stop=(kc == 7)), reads=[rW, r_yb[b]], writes=rps, inc=(kc == 7 and hh == 1))
                yy = cnt["t"] % 3
                cnt["t"] += 1
                P.op("act", lambda e, yy=yy, ps=ps: e.activation(out=tg[yy], in_=ps[:, T:2 * T], func=AF.Tanh, scale=0.5),
                     reads=rps, writes=[r_tg[yy]])
                P.op("dve", lambda e, yy=yy, ps=ps: e.scalar_tensor_tensor(
                    out=tg[yy], in0=tg[yy], scalar=1.0, in1=ps[:, 0:T], op0=ALU.add, op1=ALU.mult),
                    reads=rps + [r_tg[yy]], writes=[r_tg[yy]])
                P.op("act", lambda e, mo=mo, yy=yy: e.activation(out=hout[b][:, mo, :], in_=tg[yy], func=AF.Copy, scale=0.5),
                     reads=[r_tg[yy]], writes=[r_hout[b][mo]])
                P.op("act", lambda e, mo=mo, yy=yy: e.activation(out=sq2[b][:, mo, :], in_=tg[yy], func=AF.Square, scale=0.5),
                     reads=[r_tg[yy]], writes=[r_sq2[b]])
            self.post_norm_residual(layer, 1, hout[b], r_hout[b], sq2[b], r_sq2[b], xt[b], r_xt[b], T, nslots[b], ms[b], r_ms[b], rstd[b], r_rstd[b])
            P.dma("sp", dst.ap[:, :, i * T:(i + 1) * T], xt[b], reads=[r_xt[b]], writes=dst.blocks(i * T, T))

        load(0)
        for i in range(NT):
            if i + 1 < NT:
                load(i + 1)
            body(i)
        P.barrier()

    def s5b_phase(self, layer, su, sb, sy, par, par_bc, bpad_re, bpad_im, cpad_re, cpad_im, dvec):
        P, A = self.P, self.A
        T = 256
        NT = self.L // T
        A.off = self.persist_end
        rW = Res()
        Bre = A.alloc((32, 128), BF16)
        Bim = A.alloc((32, 128), BF16)
        Cre = A.alloc((32, 128), BF16)
        Cimn = A.alloc((32, 128), BF16)
        cosT = A.alloc((32, T), F32)
        sinT = A.alloc((32, T), F32)
        Dv = A.alloc(8, F32)
        SP = A.alloc((12, 32), F32)
        RT = A.alloc((2, 32), F32)
        GL = A.alloc((2, 32), F32)
        INIT = A.alloc((2, 32), F32)
        base = A.off
        P.dma("pool", Cre, cpad_re, writes=[rW])
        P.dma("pool", Cimn, cpad_im, writes=[rW])
        P.op("pool", lambda e: e.tensor_scalar(out=Cimn.rearrange("p a b -> p (a b)"), in0=Cimn.rearrange("p a b -> p (a b)"),
                                               scalar1=-1.0, scalar2=None, op0=ALU.mult), reads=[rW], writes=[rW])
        P.dma("sp", Dv, dvec, writes=[rW])
        P.dma("sp", SP[:, 0:3, :], par, writes=[rW])
        Q = 1024
        tl = [A.alloc(Q, F32) for _ in range(14)]
        rq = Res()
        for qi in range(4):
            self.s5_bbar_quarter(qi, Q, tl, rq, rW, par_bc, bpad_re, bpad_im, Bre, Bim)
        sp = [SP[:, i, :] for i in range(12)]
        self.s5_disc(sp[0], sp[1], sp[2], sp[3], sp[4], sp[5], sp[6], sp[7], sp[8], sp[9], sp[10], sp[11], rW, unit=True)
        MAG = sp[3]
        rm_re, rm_im = sp[6], sp[5]
        P.barrier()
        A.off = base
        ta = A.alloc((32, 128), F32)
        tb = A.alloc((32, 128), F32)
        P.op("dve", lambda e: e.memset(cosT[:, :, 0:1], 1.0), writes=[rW])
        P.op("dve", lambda e: e.memset(sinT[:, :, 0:1], 0.0), writes=[rW])
        m = 1
        while m <= T // 2:
            bre_ = rm_re.unsqueeze(2).to_broadcast([128, 32, m])
            bim_ = rm_im.unsqueeze(2).to_broadcast([128, 32, m])
            P.op("dve", lambda e, m=m, b=bre_: e.tensor_tensor(out=ta[:, :, 0:m], in0=cosT[:, :, 0:m], in1=b, op=ALU.mult), reads=[rW], writes=[rW])
            P.op("dve", lambda e, m=m, b=bim_: e.tensor_tensor(out=tb[:, :, 0:m], in0=sinT[:, :, 0:m], in1=b, op=ALU.mult), reads=[rW], writes=[rW])
            P.op("dve", lambda e, m=m: e.tensor_tensor(out=cosT[:, :, m:2 * m], in0=ta[:, :, 0:m], in1=tb[:, :, 0:m], op=ALU.subtract), reads=[rW], writes=[rW])
            P.op("dve", lambda e, m=m, b=bim_: e.tensor_tensor(out=ta[:, :, 0:m], in0=cosT[:, :, 0:m], in1=b, op=ALU.mult), reads=[rW], writes=[rW])
            P.op("dve", lambda e, m=m, b=bre_: e.tensor_tensor(out=tb[:, :, 0:m], in0=sinT[:, :, 0:m], in1=b, op=ALU.mult), reads=[rW], writes=[rW])
            P.op("dve", lambda e, m=m: e.tensor_tensor(out=sinT[:, :, m:2 * m], in0=ta[:, :, 0:m], in1=tb[:, :, 0:m], op=ALU.add), reads=[rW], writes=[rW])
            P.op("dve", lambda e: e.tensor_tensor(out=sp[7], in0=rm_re, in1=rm_re, op=ALU.mult), reads=[rW], writes=[rW])
            P.op("dve", lambda e: e.tensor_tensor(out=sp[8], in0=rm_im, in1=rm_im, op=ALU.mult), reads=[rW], writes=[rW])
            P.op("dve", lambda e: e.scalar_tensor_tensor(out=rm_im, in0=rm_re, scalar=2.0, in1=rm_im, op0=ALU.mult, op1=ALU.mult), reads=[rW], writes=[rW])
            P.op("dve", lambda e: e.tensor_tensor(out=rm_re, in0=sp[7], in1=sp[8], op=ALU.subtract), reads=[rW], writes=[rW])
            m *= 2
        P.op("dve", lambda e: e.tensor_copy(out=RT[:, 0, :], in_=rm_re), reads=[rW], writes=[rW])
        P.op("dve", lambda e: e.tensor_copy(out=RT[:, 1, :], in_=rm_im), reads=[rW], writes=[rW])
        P.op("dve", lambda e: e.memset(INIT, 0.0), writes=[rW])
        P.barrier()
        A.off = base
        u2 = [A.alloc((8, T), F32) for _ in range(2)]
        ub2 = [A.alloc((8, T), BF16) for _ in range(2)]
        r_ld = [Res() for _ in range(2)]
        yb2 = [A.alloc((8, T), BF16) for _ in range(2)]; r_yb2 = [Res() for _ in range(2)]
        NB = 3
        bu = [A.alloc(3 * T, F32) for _ in range(NB)]; r_bu = [Res() for _ in range(NB)]
        t1 = [A.alloc(2 * T, F32) for _ in range(NB)]; r_t1 = [Res() for _ in range(NB)]
        t2 = [A.alloc(2 * T, F32) for _ in range(NB)]; r_t2 = [Res() for _ in range(NB)]
        G = [A.alloc(3 * T, F32) for _ in range(NB)]; r_G = [Res() for _ in range(NB)]
        t3 = [A.alloc(2 * T, F32) for _ in range(NB)]; r_t3 = [Res() for _ in range(NB)]
        t4 = [A.alloc(2 * T, F32) for _ in range(NB)]; r_t4 = [Res() for _ in range(NB)]
        hb = [A.alloc((4, 2, T), BF16) for _ in range(2)]; r_hb = [Res() for _ in range(2)]
        yv = [A.alloc(T, F32) for _ in range(2)]; r_yv = [Res() for _ in range(2)]
        r_GL = Res()
        r_INIT = Res()
        bslots = [self.full(0), self.full(1)]
        yslots = [self.full(4)]

        def b2(ap2):
            return ap2.unsqueeze(1).to_broadcast([128, 2, T])

        def v2(ap, off):
            return ap[:, off:off + 2 * T].rearrange("p (a b) -> p a b", a=2)

        def tile_chunks(i, u, ub, yb, r_u, r_ub, r_yb):
            def s1(k):
                kc = k // 4
                s = k % NB
                ps, rps = bslots[k % 2]
                P.op("pe", lambda e: e.matmul(ps[:, 0:T], Bre[:, k, :], ub[:, kc, :], start=True, stop=True),
                     reads=[rW, r_ub[kc]], writes=rps, inc=False)
                P.op("pe", lambda e: e.matmul(ps[:, T:2 * T], Bim[:, k, :], ub[:, kc, :], start=True, stop=True),
                     reads=[rW, r_ub[kc]], writes=rps)
                P.op("act", lambda e: e.activation(out=bu[s][:, 0:2 * T], in_=ps, func=AF.Copy), reads=rps, writes=[r_bu[s]])
                P.op("act", lambda e: e.activation(out=bu[s][:, 2 * T:3 * T], in_=ps[:, 0:T], func=AF.Copy, scale=-1.0), reads=rps, writes=[r_bu[s]])
                P.op("pool", lambda e: e.tensor_tensor(out=v2(t1[s], 0), in0=v2(bu[s], 0), in1=b2(cosT[:, k, :]), op=ALU.mult),
                     reads=[r_bu[s], rW], writes=[r_t1[s]])
                P.op("dve", lambda e: e.tensor_tensor(out=v2(t2[s], 0), in0=v2(bu[s], T), in1=b2(sinT[:, k, :]), op=ALU.mult),
                     reads=[r_bu[s], rW], writes=[r_t2[s]])
                P.op("dve", lambda e: e.tensor_tensor(out=t1[s], in0=t1[s], in1=t2[s], op=ALU.add),
                     reads=[r_t1[s], r_t2[s]], writes=[r_t1[s]])

            def s2a(k):
                kk = k % 4
                s = k % NB
                hs = (k // 4) % 2
                magb = MAG[:, k:k + 1].to_broadcast([128, T])
                P.op("dve", lambda e: e.tensor_tensor_scan(
                    out=G[s][:, T:2 * T], data0=magb, data1=t1[s][:, 0:T], initial=INIT[:, 0, k:k + 1], op0=ALU.mult, op1=ALU.add),
                    reads=[r_t1[s], rW, r_INIT], writes=[r_G[s]])
                P.op("dve", lambda e: e.tensor_tensor_scan(
                    out=G[s][:, 2 * T:3 * T], data0=magb, data1=t1[s][:, T:2 * T], initial=INIT[:, 1, k:k + 1], op0=ALU.mult, op1=ALU.add),
                    reads=[r_t1[s], rW, r_INIT], writes=[r_G[s]])
                P.op("act", lambda e: e.activation(out=G[s][:, 0:T], in_=G[s][:, 2 * T:3 * T], func=AF.Copy, scale=-1.0), reads=[r_G[s]], writes=[r_G[s]])
                P.op("act", lambda e: e.activation(out=GL[:, :, k], in_=v2(G[s], T)[:, :, T - 1], func=AF.Copy),
                     reads=[r_G[s]], writes=[r_GL])

            def s2b(k):
                s = k % NB
                P.op("dve", lambda e: e.tensor_tensor(out=v2(t3[s], 0), in0=v2(G[s], T), in1=b2(cosT[:, k, :]), op=ALU.mult),
                     reads=[r_G[s], rW], writes=[r_t3[s]])
                P.op("dve", lambda e: e.tensor_tensor(out=v2(t4[s], 0), in0=v2(G[s], 0), in1=b2(sinT[:, k, :]), op=ALU.mult),
                     reads=[r_G[s], rW], writes=[r_t4[s]])

            def s3(k):
                kk = k % 4
                s = k % NB
                hs = (k // 4) % 2
                P.op("dve", lambda e: e.tensor_tensor(out=hb[hs][:, kk, :, :], in0=v2(t3[s], 0), in1=v2(t4[s], 0), op=ALU.add),
                     reads=[r_t3[s], r_t4[s]], writes=[r_hb[hs]])
                if kk == 3:
                    mo = k // 4
                    psy, rpsy = yslots[0]
                    for k4 in range(4):
                        kq = mo * 4 + k4
                        P.op("pe", lambda e, kq=kq, k4=k4, hs=hs, psy=psy: e.matmul(
                            psy[:, 0:T], Cre[:, kq, :], hb[hs][:, k4, 0, :], start=(k4 == 0), stop=False),
                            reads=[rW, r_hb[hs]], writes=rpsy, inc=False)
                        P.op("pe", lambda e, kq=kq, k4=k4, hs=hs, psy=psy: e.matmul(
                            psy[:, 0:T], Cimn[:, kq, :], hb[hs][:, k4, 1, :], start=False, stop=(k4 == 3)),
                            reads=[rW, r_hb[hs]], writes=rpsy, inc=(k4 == 3))
                    yy = mo % 2
                    P.op("dve", lambda e, mo=mo, yy=yy, psy=psy: e.scalar_tensor_tensor(
                        out=yv[yy], in0=u[:, mo, :], scalar=Dv[:, mo:mo + 1], in1=psy[:, 0:T], op0=ALU.mult, op1=ALU.add),
                        reads=rpsy + [r_u[mo], rW], writes=[r_yv[yy]])
                    P.op("act", lambda e, mo=mo, yy=yy: e.activation(out=yb[:, mo, :], in_=yv[yy], func=AF.Gelu_apprx_tanh),
                         reads=[r_yv[yy]], writes=[r_yb])

            s1(0)
            s1(1)
            s2a(0)
            s2b(0)
            for k in range(32):
                if k + 2 < 32:
                    s1(k + 2)
                if k + 1 < 32:
                    s2a(k + 1)
                s3(k)
                if k + 1 < 32:
                    s2b(k + 1)
            P.op("dve", lambda e: e.tensor_tensor(out=INIT[:, 0, :], in0=RT[:, 0, :], in1=GL[:, 0, :], op=ALU.mult), reads=[r_GL, rW], writes=[r_INIT])
            P.op("dve", lambda e: e.tensor_tensor(out=INIT[:, 1, :], in0=RT[:, 1, :], in1=GL[:, 1, :], op=ALU.mult), reads=[r_GL, rW], writes=[r_INIT])
            P.op("dve", lambda e: e.tensor_tensor(out=INIT[:, 0, :], in0=INIT[:, 0, :], in1=INIT[:, 1, :], op=ALU.subtract), reads=[r_INIT], writes=[r_INIT])
            P.op("dve", lambda e: e.tensor_tensor(out=INIT[:, 1, :], in0=RT[:, 0, :], in1=GL[:, 1, :], op=ALU.mult), reads=[r_GL, rW], writes=[r_INIT])
            P.op("dve", lambda e: e.tensor_tensor(out=GL[:, 0, :], in0=RT[:, 1, :], in1=GL[:, 0, :], op=ALU.mult), reads=[r_GL, rW], writes=[r_GL])
            P.op("dve", lambda e: e.tensor_tensor(out=INIT[:, 1, :], in0=INIT[:, 1, :], in1=GL[:, 0, :], op=ALU.add), reads=[r_GL, r_INIT], writes=[r_INIT])
            P.dma("sp", sy.ap[:, :, i * T:(i + 1) * T], yb, reads=[r_yb])

        def load(i):
            b_ = i % 2
            P.dma("sp", u2[b_], su.ap[:, :, i * T:(i + 1) * T], writes=[r_ld[b_]])
            P.dma("sp", ub2[b_], sb.ap[:, :, i * T:(i + 1) * T], writes=[r_ld[b_]])

        load(0)
        for i in range(NT):
            if i + 1 < NT:
                load(i + 1)
            rl = r_ld[i % 2]
            tile_chunks(i, u2[i % 2], ub2[i % 2], yb2[i % 2], [rl] * 8, [rl] * 8, r_yb2[i % 2])
        P.barrier()

    def s5_bbar_quarter(self, qi, Q, tl, rq, rW, par_bc, bpad_re, bpad_im, Bre, Bim):
        P = self.P
        cs = slice(qi * Q, (qi + 1) * Q)
        lre, lim, lst, bre, bim, t0, t1, t2, t3, t4, t5, t6, t7, t8 = tl
        P.dma("sp", lre, par_bc[0:1, cs].partition_broadcast(128), writes=[rq])
        P.dma("sp", lim, par_bc[1:2, cs].partition_broadcast(128), writes=[rq])
        P.dma("sp", lst, par_bc[2:3, cs].partition_broadcast(128), writes=[rq])
        P.dma("sp", bre, bpad_re.rearrange("p a b -> p (a b)")[:, cs], writes=[rq])
        P.dma("sp", bim, bpad_im.rearrange("p a b -> p (a b)")[:, cs], writes=[rq])
        self.s5_disc(lre, lim, lst, t0, t1, t2, t3, t4, t5, t6, t7, t8, rq)
        bo_re = Bre.rearrange("p a b -> p (a b)")[:, cs]
        bo_im = Bim.rearrange("p a b -> p (a b)")[:, cs]
        P.op("dve", lambda e: e.tensor_tensor(out=t0, in0=t5, in1=bre, op=ALU.mult), reads=[rq], writes=[rq])
        P.op("dve", lambda e: e.tensor_tensor(out=t1, in0=t6, in1=bim, op=ALU.mult), reads=[rq], writes=[rq])
        P.op("dve", lambda e: e.tensor_tensor(out=bo_re, in0=t0, in1=t1, op=ALU.subtract), reads=[rq], writes=[rq, rW])
        P.op("dve", lambda e: e.tensor_tensor(out=t0, in0=t5, in1=bim, op=ALU.mult), reads=[rq], writes=[rq])
        P.op("dve", lambda e: e.tensor_tensor(out=t1, in0=t6, in1=bre, op=ALU.mult), reads=[rq], writes=[rq])
        P.op("dve", lambda e: e.tensor_tensor(out=bo_im, in0=t0, in1=t1, op=ALU.add), reads=[rq], writes=[rq, rW])

    def s5_disc(self, lre, lim, lst, t0, t1, t2, t3, t4, t5, t6, t7, t8, r, unit=False):
        P = self.P
        P.op("dve", lambda e: e.tensor_scalar(out=lre, in0=lre, scalar1=-1e-4, scalar2=None, op0=ALU.min), reads=[r], writes=[r])
        P.op("act", lambda e: e.activation(out=lst, in_=lst, func=AF.Exp), reads=[r], writes=[r])
        P.op("dve", lambda e: e.tensor_tensor(out=t0, in0=lre, in1=lst, op=ALU.mult), reads=[r], writes=[r])
        P.op("act", lambda e: e.activation(out=t0, in_=t0, func=AF.Exp), reads=[r], writes=[r])
        P.op("dve", lambda e: e.tensor_tensor(out=t1, in0=lim, in1=lst, op=ALU.mult), reads=[r], writes=[r])
        self.sincos(t1, t2, t3, t4, r)
        if unit:
            return
        P.op("dve", lambda e: e.tensor_tensor(out=t2, in0=t2, in1=t0, op=ALU.mult), reads=[r], writes=[r])
        P.op("dve", lambda e: e.tensor_tensor(out=t3, in0=t3, in1=t0, op=ALU.mult), reads=[r], writes=[r])
        P.op("dve", lambda e: e.tensor_scalar(out=t3, in0=t3, scalar1=-1.0, scalar2=None, op0=ALU.add), reads=[r], writes=[r])
        P.op("dve", lambda e: e.tensor_tensor(out=t4, in0=lre, in1=lre, op=ALU.mult), reads=[r], writes=[r])
        P.op("dve", lambda e: e.tensor_tensor(out=t7, in0=lim, in1=lim, op=ALU.mult), reads=[r], writes=[r])
        P.op("dve", lambda e: e.tensor_tensor(out=t4, in0=t4, in1=t7, op=ALU.add), reads=[r], writes=[r])
        P.op("dve", lambda e: e.reciprocal(out=t4, in_=t4), reads=[r], writes=[r])
        P.op("dve", lambda e: e.tensor_tensor(out=t5, in0=t3, in1=lre, op=ALU.mult), reads=[r], writes=[r])
        P.op("dve", lambda e: e.tensor_tensor(out=t7, in0=t2, in1=lim, op=ALU.mult), reads=[r], writes=[r])
        P.op("dve", lambda e: e.tensor_tensor(out=t5, in0=t5, in1=t7, op=ALU.add), reads=[r], writes=[r])
        P.op("dve", lambda e: e.tensor_tensor(out=t5, in0=t5, in1=t4, op=ALU.mult), reads=[r], writes=[r])
        P.op("dve", lambda e: e.tensor_tensor(out=t6, in0=t2, in1=lre, op=ALU.mult), reads=[r], writes=[r])
        P.op("dve", lambda e: e.tensor_tensor(out=t7, in0=t3, in1=lim, op=ALU.mult), reads=[r], writes=[r])
        P.op("dve", lambda e: e.tensor_tensor(out=t6, in0=t6, in1=t7, op=ALU.subtract), reads=[r], writes=[r])
        P.op("dve", lambda e: e.tensor_tensor(out=t6, in0=t6, in1=t4, op=ALU.mult), reads=[r], writes=[r])


    def ssd_a_phase(self, layer, src, w_in, cw, zs, xs, bt, ct, dtt):
        P, A = self.P, self.A
        T = 512
        NT = self.L // T
        A.off = self.persist_end
        rW = Res()
        Win = A.alloc((8, 6176), BF16)
        rWin = [Res() for _ in range(8)]
        for kc in range(8):
            P.dma("pool", Win[:, :, kc * 772:(kc + 1) * 772], w_in[:, :, kc * 772:(kc + 1) * 772], writes=[rWin[kc]])
        CW = A.alloc((5, 32), F32)
        P.dma("sp", CW, cw, writes=[rW])
        H = A.alloc((32, 4), F32)
        r_H = [Res() for _ in range(32)]
        P.op("pool", lambda e: e.memset(H, 0.0), writes=r_H)
        xts = [A.alloc((8, T), F32) for _ in range(2)]; r_xts = [Res() for _ in range(2)]
        sq = A.alloc((8, T), BF16); r_sq = Res()
        xn = A.alloc((8, T), BF16); r_xn = Res()
        ms = A.alloc(T, F32); r_ms = Res()
        rstd = A.alloc(T, F32); r_rstd = Res()
        dtr = A.alloc((4, 32), F32); r_dtr = Res()
        NU = 3
        ubuf = [A.alloc(T + 4, F32) for _ in range(NU)]; r_ub = [Res() for _ in range(NU)]
        uc = [A.alloc(T, F32) for _ in range(NU)]; r_uc = [Res() for _ in range(NU)]
        NSG = 3
        stg = [A.alloc((4, T), BF16) for _ in range(NSG)]; r_stg = [Res() for _ in range(NSG)]
        pslots = [self.full(b) for b in range(5)]
        dslot = self.full(5)
        nslot = self.full(6)
        cnt = {"p": 0, "u": 0, "g": 0}

        def load(i):
            P.dma("sp", xts[i % 2], src.ap[:, :, i * T:(i + 1) * T], reads=src.blocks(i * T, T), writes=[r_xts[i % 2]])

        def tile(i, xt, r_xt):
            tsl = slice(i * T, (i + 1) * T)
            P.op("act", lambda e: e.activation(out=sq.rearrange("p c t -> p (c t)"), in_=xt.rearrange("p c t -> p (c t)"), func=AF.Square),
                 reads=[r_xt], writes=[r_sq])
            self.rstd_from_sq(sq, r_sq, 8, T, nslot, ms, r_ms, rstd, r_rstd, float(D))
            for c in range(8):
                P.op("dve", lambda e, c=c: e.scalar_tensor_tensor(
                    out=xn[:, c, :], in0=xt[:, c, :], scalar=self.gcol(layer, 0, c), in1=rstd,
                    op0=ALU.mult, op1=ALU.mult), reads=[r_xt, r_rstd], writes=[r_xn])
            def flush(m, sg):
                m0 = m - 3
                if m < 16:
                    dd, c0 = zs, m0
                elif m < 32:
                    dd, c0 = xs, m0 - 16
                elif m < 40:
                    dd, c0 = bt, m0 - 32
                else:
                    dd, c0 = ct, m0 - 40
                P.dma("sp", dd.ap[:, c0:c0 + 4, tsl], stg[sg], reads=[r_stg[sg]])

            def fin(m, sg, q4, ub_i):
                P.op("act", lambda e: e.activation(out=stg[sg][:, q4, :], in_=uc[ub_i], func=AF.Silu),
                     reads=[r_uc[ub_i]], writes=[r_stg[sg]])
                if q4 == 3:
                    flush(m, sg)

            def chunk_m(m, sg, q4):
                ps, rps = pslots[cnt["p"] % len(pslots)]
                cnt["p"] += 1
                for kc in range(8):
                    P.op("pe", lambda e, kc=kc: e.matmul(
                        ps, Win[:, kc, m * 128:(m + 1) * 128], xn[:, kc, :], start=(kc == 0), stop=(kc == 7)),
                        reads=[rWin[(m * 128) // 772], rWin[(m * 128 + 127) // 772], r_xn], writes=rps, inc=(kc == 7))
                if m < 16:
                    P.op("act", lambda e: e.activation(out=stg[sg][:, q4, :], in_=ps, func=AF.Silu),
                         reads=rps, writes=[r_stg[sg]])
                    if q4 == 3:
                        flush(m, sg)
                    return None
                cc = m - 16
                ub_i = cnt["u"] % NU
                cnt["u"] += 1
                ub_, uc_ = ubuf[ub_i], uc[ub_i]
                P.op("pool", lambda e: e.tensor_copy(out=ub_[:, 1:4], in_=H[:, cc, 0:3]),
                     reads=[r_H[cc]], writes=[r_ub[ub_i]])
                P.op("act", lambda e: e.activation(out=ub_[:, 4:4 + T], in_=ps, func=AF.Copy),
                     reads=rps, writes=[r_ub[ub_i]])
                P.op("act", lambda e: e.activation(out=uc_, in_=ps, func=AF.Identity,
                                                   scale=CW[:, 3, cc:cc + 1], bias=CW[:, 4, cc:cc + 1]),
                     reads=rps + [rW], writes=[r_uc[ub_i]])
                for k in range(3):
                    P.op("dve", lambda e, k=k: e.scalar_tensor_tensor(
                        out=uc_, in0=ub_[:, 1 + k:1 + k + T], scalar=CW[:, k, cc:cc + 1], in1=uc_,
                        op0=ALU.mult, op1=ALU.add), reads=[r_ub[ub_i], r_uc[ub_i], rW], writes=[r_uc[ub_i]])
                P.op("pool", lambda e: e.tensor_copy(out=H[:, cc, 0:3], in_=ub_[:, T + 1:T + 4]),
                     reads=[r_ub[ub_i]], writes=[r_H[cc]])
                return (m, sg, q4, ub_i)

            pend = None
            for m in range(48):
                sg = cnt["g"] % NSG
                q4 = m % 4
                if q4 == 3:
                    cnt["g"] += 1
                nxt = chunk_m(m, sg, q4)
                if pend is not None:
                    fin(*pend)
                pend = nxt
            if pend is not None:
                fin(*pend)
            psd, rpsd = dslot
            for j in range(4):
                for kc in range(8):
                    P.op("pe", lambda e, j=j, kc=kc: e.matmul(
                        psd[:, j * 32:(j + 1) * 32], xn[:, kc, j * 128:(j + 1) * 128], Win[:, kc, 6144:6176],
                        start=(kc == 0), stop=(kc == 7)), reads=[rWin[7], r_xn], writes=rpsd, inc=(kc == 7 and j == 3))
            P.op("act", lambda e: e.activation(out=dtr.rearrange("p a b -> p (a b)"), in_=psd[:, 0:128], func=AF.Copy),
                 reads=rpsd, writes=[r_dtr])
            P.dma("sp", dtt.ap[i * 4:(i + 1) * 4].rearrange("j p h -> p j h"), dtr, reads=[r_dtr])
        load(0)
        for i in range(NT):
            if i + 1 < NT:
                load(i + 1)
            tile(i, xts[i % 2], r_xts[i % 2])
        P.barrier()


    def ssd_b_phase(self, layer, src, dst, zs, xs, bt, ct, dtt, w_out, consts, hp_bc, dch, ngv):
        P, A = self.P, self.A
        T = 256
        NT = self.L // T
        A.off = self.persist_end
        rW = Res()
        Wout = A.alloc((16, 1024), BF16)
        for hh in range(2):
            P.dma("pool", Wout[:, hh * 8:(hh + 1) * 8, :], w_out[:, hh * 8:(hh + 1) * 8, :], writes=[rW])
        CF = A.alloc((4, 128), F32)
        P.dma("sp", CF, consts, writes=[rW])
        IDF, U, MS, ONES = CF[:, 0, :], CF[:, 1, :], CF[:, 2, :], CF[:, 3, :]
        identb = A.alloc(128, BF16)
        P.op("dve", lambda e: e.tensor_copy(out=identb, in_=IDF), reads=[rW], writes=[rW])
        Ub = A.alloc(128, BF16)
        MSb = A.alloc(128, BF16)
        ONESb = A.alloc(128, BF16)
        P.op("dve", lambda e: e.tensor_copy(out=Ub, in_=U), reads=[rW], writes=[rW])
        P.op("dve", lambda e: e.tensor_copy(out=MSb, in_=MS), reads=[rW], writes=[rW])
        P.op("dve", lambda e: e.tensor_copy(out=ONESb, in_=ONES), reads=[rW], writes=[rW])

        HB = A.alloc((2, 32), F32)
        P.dma("sp", HB[:, 0, :], hp_bc[0:1, :].partition_broadcast(128), writes=[rW])
        P.dma("sp", HB[:, 1, :], hp_bc[1:2, :].partition_broadcast(128), writes=[rW])
        P.op("act", lambda e: e.activation(out=HB[:, 1, :], in_=HB[:, 1, :], func=AF.Exp), reads=[rW], writes=[rW])
        P.op("dve", lambda e: e.tensor_scalar(out=HB[:, 1, :], in0=HB[:, 1, :], scalar1=-1.0, scalar2=None, op0=ALU.mult), reads=[rW], writes=[rW])
        DBIAS, ANEG = HB[:, 0, :], HB[:, 1, :]
        DCH = A.alloc(16, F32)
        NG = A.alloc(16, F32)
        P.dma("sp", DCH, dch, writes=[rW])
        P.dma("sp", NG, ngv, writes=[rW])
        diagD = A.alloc((16, 128), BF16)
        for m in range(16):
            P.op("dve", lambda e, m=m: e.tensor_scalar(out=diagD[:, m, :], in0=IDF, scalar1=DCH[:, m:m + 1], scalar2=None, op0=ALU.mult),
                 reads=[rW], writes=[rW])
        ent = A.alloc(2048, F32); r_ent = [Res() for _ in range(8)]
        entb = A.alloc(2048, BF16); r_entb = [Res() for _ in range(8)]
        P.op("pool", lambda e: e.memset(ent, 0.0), writes=r_ent)
        P.op("pool", lambda e: e.memset(entb, 0.0), writes=r_entb)
        zs_t = [A.alloc((16, T), BF16) for _ in range(2)]
        xs_t = [A.alloc((16, T), BF16) for _ in range(2)]
        bt_t = [A.alloc((8, T), BF16) for _ in range(2)]
        ct_t = [A.alloc((8, T), BF16) for _ in range(2)]
        dtr_t = [A.alloc((2, 32), F32) for _ in range(2)]
        xt = [A.alloc((8, T), F32) for _ in range(2)]
        r_ld = [Res() for _ in range(2)]
        r_xt = [Res() for _ in range(2)]
        smset = []
        for _ in range(2):
            t = [A.alloc(32, F32) for _ in range(8)]
            t += [A.alloc(32, BF16), A.alloc(32, BF16), A.alloc(32, F32)]
            smset.append(t)
        r_smset = [Res(), Res()]
        xdt2 = [A.alloc(2048, BF16) for _ in range(2)]; r_xdt2 = [Res(), Res()]
        xdtd2 = [A.alloc(2048, BF16) for _ in range(2)]; r_xdtd2 = [Res(), Res()]
        btok2 = [A.alloc(1024, BF16) for _ in range(2)]; r_btok2 = [Res(), Res()]
        NR = 2
        Rg = [A.alloc((4, 128), BF16) for _ in range(NR)]; r_Rg = [Res() for _ in range(NR)]
        Rl = [A.alloc((4, 128), BF16) for _ in range(NR)]
        Lx = [A.alloc((4, 128), F32) for _ in range(NR)]; r_Lx = [Res() for _ in range(NR)]
        Ex = [A.alloc((4, 128), F32) for _ in range(NR)]; r_Ex = [Res() for _ in range(NR)]
        cbm = [A.alloc(128, F32) for _ in range(NR)]; r_cbm = [Res() for _ in range(NR)]
        Mg = [A.alloc((4, 128), BF16) for _ in range(NR)]; r_Mg = [Res() for _ in range(NR)]
        Cd = [A.alloc((4, 128), BF16) for _ in range(NR)]; r_Cd = [Res() for _ in range(NR)]
        v = A.alloc((16, T), F32); r_v = [Res() for _ in range(8)]
        sqv = A.alloc((16, T), BF16); r_sqv = Res()
        vn = A.alloc((16, T), BF16); r_vn = [Res() for _ in range(8)]
        msg = A.alloc(T, F32); r_msg = Res()
        rsg = [A.alloc(T, F32) for _ in range(2)]; r_rsg = [Res() for _ in range(2)]
        hout = A.alloc((8, T), F32); r_hout = [Res() for _ in range(8)]
        sq2 = A.alloc((8, T), BF16); r_sq2 = Res()
        ms = A.alloc(T, F32); r_ms = Res()
        rstd = A.alloc(T, F32); r_rstd = Res()
        tslots = [self.full(0), self.full(1)]
        dtslot = self.full(2)
        lslot = self.full(3)
        aslot = self.full(4)
        yslot = self.full(5)
        sslot = self.full(6)
        cslot = self.full(7)
        cnt = {"t": 0, "r": 0}

        def load(i):
            b = i % 2
            tsl = slice(i * T, (i + 1) * T)
            P.dma("sp", zs_t[b], zs.ap[:, :, tsl], writes=[r_ld[b]])
            P.dma("sp", xs_t[b], xs.ap[:, :, tsl], writes=[r_ld[b]])
            P.dma("sp", bt_t[b], bt.ap[:, :, tsl], writes=[r_ld[b]])
            P.dma("sp", ct_t[b], ct.ap[:, :, tsl], writes=[r_ld[b]])
            P.dma("sp", dtr_t[b], dtt.ap[i * 2:(i + 1) * 2].rearrange("j p h -> p j h"), writes=[r_ld[b]])
            P.dma("sp", xt[b], src.ap[:, :, tsl], reads=src.blocks(i * T, T), writes=[r_xt[b]])

        def bc4(ap2):
            return ap2.unsqueeze(1).to_broadcast([128, 4, 128])

        def pre_a(i, c, pbi):
            b = i % 2
            cols = slice(c * 128, (c + 1) * 128)
            rl = [r_ld[b]]
            dv, av, dtv, dA, dsv, eav, dtds, lv, dAh, dAl, dAt = smset[pbi]
            r_sm = r_smset[pbi]
            xdt, xdtd, btok = xdt2[pbi], xdtd2[pbi], btok2[pbi]
            r_xdt, r_xdtd, r_btok = r_xdt2[pbi], r_xdtd2[pbi], r_btok2[pbi]
            P.op("dve", lambda e: e.tensor_tensor(out=dv, in0=dtr_t[b][:, c, :], in1=DBIAS, op=ALU.add), reads=rl + [rW], writes=[r_sm])
            P.op("dve", lambda e: e.scalar_tensor_tensor(out=av, in0=dv, scalar=-1.0, in1=dv, op0=ALU.mult, op1=ALU.max), reads=[r_sm], writes=[r_sm])
            P.op("act", lambda e: e.activation(out=av, in_=av, func=AF.Exp, scale=-1.0), reads=[r_sm], writes=[r_sm])
            P.op("act", lambda e: e.activation(out=lv, in_=av, func=AF.Ln, bias=1.0), reads=[r_sm], writes=[r_sm])
            P.op("dve", lambda e: e.scalar_tensor_tensor(out=dtv, in0=dv, scalar=0.0, in1=lv, op0=ALU.max, op1=ALU.add), reads=[r_sm], writes=[r_sm])
            P.op("dve", lambda e: e.tensor_tensor(out=dA, in0=dtv, in1=ANEG, op=ALU.mult), reads=[r_sm, rW], writes=[r_sm])
            P.op("dve", lambda e: e.tensor_copy(out=dAh, in_=dA), reads=[r_sm], writes=[r_sm])
            P.op("dve", lambda e: e.tensor_tensor(out=dAt, in0=dA, in1=dAh, op=ALU.subtract), reads=[r_sm], writes=[r_sm])
            P.op("dve", lambda e: e.tensor_copy(out=dAl, in_=dAt), reads=[r_sm], writes=[r_sm])
            psd, rpsd = dtslot
            P.op("pe", lambda e: e.matmul(psd[:, 0:32], MS, dA, start=True, stop=True), reads=[rW, r_sm], writes=rpsd, inc=False)
            P.op("pe", lambda e: e.matmul(psd[:, 32:64], ONES, dA, start=True, stop=True), reads=[rW, r_sm], writes=rpsd)
            P.op("act", lambda e: e.activation(out=dsv, in_=psd[:, 0:32], func=AF.Exp), reads=rpsd, writes=[r_sm])
            P.op("act", lambda e: e.activation(out=eav, in_=psd[:, 32:64], func=AF.Exp), reads=rpsd, writes=[r_sm])
            P.op("dve", lambda e: e.tensor_tensor(out=dtds, in0=dtv, in1=dsv, op=ALU.mult), reads=[r_sm], writes=[r_sm])

        def pre_b(i, c, pbi):
            b = i % 2
            cols = slice(c * 128, (c + 1) * 128)
            rl = [r_ld[b]]
            dv, av, dtv, dA, dsv, eav, dtds, lv, dAh, dAl, dAt = smset[pbi]
            r_sm = r_smset[pbi]
            xdt, xdtd, btok = xdt2[pbi], xdtd2[pbi], btok2[pbi]
            r_xdt, r_xdtd, r_btok = r_xdt2[pbi], r_xdtd2[pbi], r_btok2[pbi]
            for half in range(2):
                pst, rpst = tslots[cnt["t"] % 2]
                cnt["t"] += 1
                pb = pst.bitcast(BF16)
                for q in range(8):
                    m = half * 8 + q
                    P.op("pe", lambda e, m=m, q=q, pb=pb: e.transpose(pb[:, q * 128:(q + 1) * 128], xs_t[b][:, m, cols], identb),
                         reads=rl + [rW], writes=rpst, inc=(q == 7))
                pv = pb.rearrange("p (h d) -> p h d", h=16)
                hs = slice(half * 16, (half + 1) * 16)
                o1 = xdt[:, half * 1024:(half + 1) * 1024].rearrange("p (h d) -> p h d", h=16)
                o2 = xdtd[:, half * 1024:(half + 1) * 1024].rearrange("p (h d) -> p h d", h=16)
                P.op("dve", lambda e, pv=pv, o1=o1, hs=hs: e.tensor_tensor(out=o1, in0=pv, in1=dtv[:, hs].unsqueeze(2).to_broadcast([128, 16, 64]), op=ALU.mult),
                     reads=rpst + [r_sm], writes=[r_xdt])
                P.op("dve", lambda e, pv=pv, o2=o2, hs=hs: e.tensor_tensor(out=o2, in0=pv, in1=dtds[:, hs].unsqueeze(2).to_broadcast([128, 16, 64]), op=ALU.mult),
                     reads=rpst + [r_sm], writes=[r_xdtd])

        def pre_c(i, c, pbi):
            b = i % 2
            cols = slice(c * 128, (c + 1) * 128)
            rl = [r_ld[b]]
            dv, av, dtv, dA, dsv, eav, dtds, lv, dAh, dAl, dAt = smset[pbi]
            r_sm = r_smset[pbi]
            xdt, xdtd, btok = xdt2[pbi], xdtd2[pbi], btok2[pbi]
            r_xdt, r_xdtd, r_btok = r_xdt2[pbi], r_xdtd2[pbi], r_btok2[pbi]
            pst, rpst = tslots[cnt["t"] % 2]
            cnt["t"] += 1
            pb = pst.bitcast(BF16)
            for g in range(8):
                P.op("pe", lambda e, g=g, pb=pb: e.transpose(pb[:, g * 128:(g + 1) * 128], bt_t[b][:, g, cols], identb),
                     reads=rl + [rW], writes=rpst, inc=(g == 7))
            P.op("act", lambda e, pb=pb: e.activation(out=btok, in_=pb, func=AF.Copy), reads=rpst, writes=[r_btok])

        def groups(i, c, pbi, hooks):
            b = i % 2
            cols = slice(c * 128, (c + 1) * 128)
            rl = [r_ld[b]]
            dv, av, dtv, dA, dsv, eav, dtds, lv, dAh, dAl, dAt = smset[pbi]
            r_sm = r_smset[pbi]
            xdt, xdtd, btok = xdt2[pbi], xdtd2[pbi], btok2[pbi]
            r_xdt, r_xdtd, r_btok = r_xdt2[pbi], r_xdtd2[pbi], r_btok2[pbi]
            def g1(g):
                rr = g % NR
                Rv = Rg[rr].rearrange("p a b -> p (a b)")
                Rlv = Rl[rr].rearrange("p a b -> p (a b)")
                P.op("dve", lambda e, g=g, rr=rr: e.tensor_tensor(out=Rg[rr], in0=bc4(Ub), in1=dAh[:, 4 * g:4 * g + 4].unsqueeze(2).to_broadcast([128, 4, 128]), op=ALU.mult),
                     reads=[rW, r_sm], writes=[r_Rg[rr]])
                P.op("dve", lambda e, g=g, rr=rr: e.tensor_tensor(out=Rl[rr], in0=bc4(Ub), in1=dAl[:, 4 * g:4 * g + 4].unsqueeze(2).to_broadcast([128, 4, 128]), op=ALU.mult),
                     reads=[rW, r_sm], writes=[r_Rg[rr]])
                psl, rpsl = lslot
                psa, rpsa = aslot
                P.op("pe", lambda e, Rv=Rv: e.matmul(psl, MSb, Rv, start=True, stop=False), reads=[rW, r_Rg[rr]], writes=rpsl, inc=False)
                P.op("pe", lambda e, Rlv=Rlv: e.matmul(psl, MSb, Rlv, start=False, stop=True), reads=[rW, r_Rg[rr]], writes=rpsl)
                P.op("pe", lambda e, Rv=Rv: e.matmul(psa, ONESb, Rv, start=True, stop=False), reads=[rW, r_Rg[rr]], writes=rpsa, inc=False)
                P.op("pe", lambda e, Rlv=Rlv: e.matmul(psa, ONESb, Rlv, start=False, stop=True), reads=[rW, r_Rg[rr]], writes=rpsa)
                P.op("act", lambda e, rr=rr: e.activation(out=Lx[rr].rearrange("p a b -> p (a b)"), in_=psl, func=AF.Exp), reads=rpsl, writes=[r_Lx[rr]])
                P.op("act", lambda e, rr=rr: e.activation(out=Ex[rr].rearrange("p a b -> p (a b)"), in_=psa, func=AF.Exp), reads=rpsa, writes=[r_Ex[rr]])
                psc, rpsc = cslot
                P.op("pe", lambda e, g=g: e.matmul(psc[:, 0:128], bt_t[b][:, g, cols], ct_t[b][:, g, cols], start=True, stop=True),
                     reads=rl, writes=rpsc)
                P.op("dve", lambda e, rr=rr: e.tensor_tensor(out=cbm[rr], in0=psc[:, 0:128], in1=U, op=ALU.mult), reads=rpsc + [rW], writes=[r_cbm[rr]])
                P.op("dve", lambda e, rr=rr: e.tensor_tensor(out=Mg[rr], in0=Lx[rr], in1=bc4(cbm[rr]), op=ALU.mult),
                     reads=[r_Lx[rr], r_cbm[rr]], writes=[r_Mg[rr]])
                P.op("dve", lambda e, rr=rr, g=g: e.tensor_tensor(out=Cd[rr], in0=Ex[rr], in1=bc4(ct_t[b][:, g, cols]), op=ALU.mult),
                     reads=[r_Ex[rr]] + rl, writes=[r_Cd[rr]])

            def g2(g):
                rr = g % NR
                psy, rpsy = yslot
                for mm in range(2):
                    m = 2 * g + mm
                    mc = slice(mm * 128, (mm + 1) * 128)
                    P.op("pe", lambda e, m=m, mc=mc: e.matmul(psy[:, mc], diagD[:, m, :], xs_t[b][:, m, cols], start=True, stop=False),
                         reads=rl + [rW], writes=rpsy, inc=False)
                    for hh in range(2):
                        h = 2 * m + hh
                        hc = slice(h * 64, (h + 1) * 64)
                        pr = slice(hh * 64, (hh + 1) * 64)
                        P.op("pe", lambda e, mc=mc, hc=hc, pr=pr, rr=rr, mm=mm, hh=hh: e.matmul(
                            psy[pr, mc], xdt[:, hc], Mg[rr][:, 2 * mm + hh, :], start=False, stop=False),
                            reads=[r_xdt, r_Mg[rr]], writes=rpsy, inc=False)
                        P.op("pe", lambda e, mc=mc, hc=hc, pr=pr, rr=rr, mm=mm, hh=hh: e.matmul(
                            psy[pr, mc], entb[:, hc], Cd[rr][:, 2 * mm + hh, :], start=False, stop=(hh == 1)),
                            reads=[r_entb[g], r_Cd[rr]], writes=rpsy, inc=(hh == 1 and mm == 1))
                P.op("dve", lambda e, g=g: e.tensor_tensor(out=v[:, 2 * g:2 * g + 2, cols], in0=psy[:, 0:256].rearrange("p (a b) -> p a b", a=2),
                                                           in1=zs_t[b][:, 2 * g:2 * g + 2, cols], op=ALU.mult),
                     reads=rpsy + rl, writes=[r_v[g]])
                pss, rpss = sslot
                gs = slice(g * 256, (g + 1) * 256)
                P.op("pe", lambda e, g=g, gs=gs: e.matmul(pss[:, 0:256], btok[:, g * 128:(g + 1) * 128], xdtd[:, gs], start=True, stop=True),
                     reads=[r_btok, r_xdtd], writes=rpss)
                ev = ent[:, gs].rearrange("p (h d) -> p h d", h=4)
                P.op("dve", lambda e, g=g, ev=ev: e.tensor_tensor(out=ev, in0=ev, in1=eav[:, 4 * g:4 * g + 4].unsqueeze(2).to_broadcast([128, 4, 64]), op=ALU.mult),
                     reads=[r_ent[g], r_sm], writes=[r_ent[g]])
                P.op("dve", lambda e, gs=gs: e.tensor_tensor(out=ent[:, gs], in0=ent[:, gs], in1=pss[:, 0:256], op=ALU.add),
                     reads=[r_ent[g]] + rpss, writes=[r_ent[g]])
                P.op("pool", lambda e, gs=gs: e.tensor_copy(out=entb[:, gs], in_=ent[:, gs]), reads=[r_ent[g]], writes=[r_entb[g]])

            g1(0)
            for g in range(8):
                if g + 1 < 8:
                    g1(g + 1)
                g2(g)
                if g in hooks:
                    hooks[g]()

        def tail(i):
            b = i % 2
            P.op("act", lambda e: e.activation(out=sqv.rearrange("p c t -> p (c t)"), in_=v.rearrange("p c t -> p (c t)"), func=AF.Square),
                 reads=r_v, writes=[r_sqv])
            for g in range(8):
                k2 = g % 2
                self.rstd_from_sq(sqv[:, 2 * g:2 * g + 2, :], r_sqv, 2, T, dtslot, msg, r_msg, rsg[k2], r_rsg[k2], 256.0)
                for mm in range(2):
                    m = 2 * g + mm
                    P.op("dve", lambda e, m=m, k2=k2: e.scalar_tensor_tensor(
                        out=vn[:, m, :], in0=v[:, m, :], scalar=NG[:, m:m + 1], in1=rsg[k2], op0=ALU.mult, op1=ALU.mult),
                        reads=[r_v[g], r_rsg[k2], rW], writes=[r_vn[g]])
            for m2 in range(4):
                ps, rps = tslots[cnt["t"] % 2]
                cnt["t"] += 1
                for hh in range(2):
                    mo = 2 * m2 + hh
                    for kc in range(16):
                        P.op("pe", lambda e, kc=kc, mo=mo, hh=hh, ps=ps: e.matmul(
                            ps[:, hh * T:(hh + 1) * T], Wout[:, kc, mo * 128:(mo + 1) * 128], vn[:, kc, :],
                            start=(kc == 0), stop=(kc == 15)), reads=[rW, r_vn[kc // 2]], writes=rps, inc=(kc == 15 and hh == 1))
                ho = hout[:, 2 * m2:2 * m2 + 2, :].rearrange("p c t -> p (c t)")
                so = sq2[:, 2 * m2:2 * m2 + 2, :].rearrange("p c t -> p (c t)")
                P.op("act", lambda e, ho=ho, ps=ps: e.activation(out=ho, in_=ps, func=AF.Copy), reads=rps, writes=r_hout[2 * m2:2 * m2 + 2])
                P.op("act", lambda e, so=so, ps=ps: e.activation(out=so, in_=ps, func=AF.Square), reads=rps, writes=[r_sq2])
            self.post_norm_residual(layer, 1, hout, r_hout, sq2, r_sq2, xt[b], r_xt[b], T, cslot, ms, r_ms, rstd, r_rstd)
            P.dma("sp", dst.ap[:, :, i * T:(i + 1) * T], xt[b], reads=[r_xt[b]], writes=dst.blocks(i * T, T))

        chunks = [(i, c) for i in range(NT) for c in range(2)]
        load(0)
        pre_a(0, 0, 0)
        pre_b(0, 0, 0)
        pre_c(0, 0, 0)
        for n, (i, c) in enumerate(chunks):
            if c == 0 and i + 1 < NT:
                load(i + 1)
            hooks = {}
            if n + 1 < len(chunks):
                ni, ncc = chunks[n + 1]
                nb = (n + 1) % 2
                hooks[1] = (lambda ni=ni, ncc=ncc, nb=nb: pre_a(ni, ncc, nb))
                hooks[3] = (lambda ni=ni, ncc=ncc, nb=nb: pre_b(ni, ncc, nb))
                hooks[5] = (lambda ni=ni, ncc=ncc, nb=nb: pre_c(ni, ncc, nb))
            groups(i, c, n % 2, hooks)
            if c == 1:
                tail(i)
        P.barrier()


def x_to_dev(xb):
    L = xb.shape[0]
    return np.ascontiguousarray(xb.T.reshape(8, 128, L).transpose(1, 0, 2))


def x_from_dev(xd):
    L = xd.shape[2]
    return np.ascontiguousarray(xd.transpose(1, 0, 2).reshape(1024, L).T)


def lay_norm_g(norm_g):
    return np.ascontiguousarray(norm_g.reshape(4, 4, 8, 128).transpose(3, 0, 1, 2).reshape(128, 128))


def lay_rows(w):
    K, N = w.shape
    return np.ascontiguousarray(w.reshape(K // 128, 128, N).transpose(1, 0, 2))


def lay_vec(v):
    return np.ascontiguousarray(v.reshape(8, 128).T)


def lay_s5(lam_re, lam_im, log_step, b_re, b_im, c_re, c_im, d):
    f = np.float32
    par = np.zeros((128, 3, 32), f)
    l4 = lam_re.reshape(32, 2, 64)
    par[:, 0, :] = l4.transpose(1, 2, 0).reshape(128, 32)
    par[:, 1, :] = lam_im.reshape(32, 2, 64).transpose(1, 2, 0).reshape(128, 32)
    par[:, 2, :] = np.repeat(log_step.reshape(32, 2, 1), 64, axis=2).transpose(1, 2, 0).reshape(128, 32)
    par_bc = np.stack([lam_re.reshape(4096), lam_im.reshape(4096), np.repeat(log_step, 64)]).astype(f)

    def bpad(b):
        out = np.zeros((8, 16, 32, 2, 64), f)
        bb = b.reshape(8, 4, 2, 64, 16)
        for kc in range(8):
            for j in range(4):
                g8 = j * 2
                for gg in range(2):
                    out[g8 + gg, :, kc * 4 + j, gg, :] = bb[kc, j, gg].T
        return np.ascontiguousarray(out.reshape(128, 32, 128))

    def cpad(c):
        out = np.zeros((2, 64, 32, 8, 16), f)
        for g in range(64):
            out[g % 2, :, g // 2, g % 8, :] = c[g].T
        return np.ascontiguousarray(out.reshape(128, 32, 128))

    return {"par": par, "par_bc": np.ascontiguousarray(par_bc), "bpad_re": bpad(b_re), "bpad_im": bpad(b_im),
            "cpad_re": cpad(c_re), "cpad_im": cpad(c_im), "dvec": lay_vec(d).astype(f)}


def ssd_consts():
    j = np.arange(128)
    ident = np.eye(128, dtype=np.float32)
    U = (j[:, None] <= j[None, :]).astype(np.float32)
    MS = (j[:, None] > j[None, :]).astype(np.float32)
    ones = np.ones((128, 128), np.float32)
    return np.ascontiguousarray(np.stack([ident, U, MS, ones], axis=1))


def lay_ssd(conv_w, conv_b, dt_bias, a_log, d_skip, norm_g):
    f = np.float32
    cw = np.zeros((128, 5, 32), f)
    for k in range(4):
        cw[:, k, :] = conv_w[k].reshape(32, 128).T
    cw[:, 4, :] = conv_b.reshape(32, 128).T
    hp = np.stack([dt_bias, a_log]).astype(f)
    dch = np.repeat(d_skip, 64).reshape(16, 128).T
    ng = norm_g.reshape(16, 128).T
    return {"cw": np.ascontiguousarray(cw), "hp_bc": np.ascontiguousarray(hp),
            "dch": np.ascontiguousarray(dch).astype(f), "ngv": np.ascontiguousarray(ng).astype(f)}


def lay_rg_vec(conv_w, conv_b, b_a, b_x, lam):
    vs = [conv_w[0], conv_w[1], conv_w[2], conv_w[3], conv_b, b_a, b_x, lam]
    return np.ascontiguousarray(np.stack([lay_vec(v) for v in vs], axis=1)).astype(np.float32)


def add_layer(B, layer, src, mid, dst):
    L = B.L
    kind = layer % 3
    pre = "l%d_" % layer
    if kind == 0:
        w_in = B.inp(pre + "w_in", [128, 8, 2048])
        w_a = B.inp(pre + "w_a", [16, 64, 64])
        w_x = B.inp(pre + "w_x", [16, 64, 64])
        vec = B.inp(pre + "vec", [128, 8, 8])
        w_out = B.inp(pre + "w_out", [128, 8, 1024])
        B.rglru_phase(layer, src, mid, w_in, w_a, w_x, vec, w_out)
    elif kind == 1:
        w_in = B.inp(pre + "w_in", [128, 8, 6176])
        w_out = B.inp(pre + "w_out", [128, 16, 1024])
        cw = B.inp(pre + "cw", [128, 5, 32])
        hp_bc = B.inp(pre + "hp_bc", [2, 32])
        dch = B.inp(pre + "dch", [128, 16])
        ngv = B.inp(pre + "ngv", [128, 16])
        consts = B.inp(pre + "consts", [128, 4, 128])
        zs = DramX(B.scratch(pre + "zs", [128, 16, L], BF16), L)
        xs = DramX(B.scratch(pre + "xs", [128, 16, L], BF16), L)
        bt = DramX(B.scratch(pre + "bt", [128, 8, L], BF16), L)
        ct = DramX(B.scratch(pre + "ct", [128, 8, L], BF16), L)
        dtt = DramX(B.scratch(pre + "dtt", [L // 128, 128, 32], F32), L)
        B.ssd_a_phase(layer, src, w_in, cw, zs, xs, bt, ct, dtt)
        B.ssd_b_phase(layer, src, mid, zs, xs, bt, ct, dtt, w_out, consts, hp_bc, dch, ngv)
    else:
        w_in = B.inp(pre + "w_in", [128, 8, 1024])
        w_out = B.inp(pre + "w_out", [128, 8, 2048])
        par = B.inp(pre + "par", [128, 3, 32])
        par_bc = B.inp(pre + "par_bc", [3, 4096])
        bre = B.inp(pre + "bpad_re", [128, 32, 128])
        bim = B.inp(pre + "bpad_im", [128, 32, 128])
        cre = B.inp(pre + "cpad_re", [128, 32, 128])
        cim = B.inp(pre + "cpad_im", [128, 32, 128])
        dvec = B.inp(pre + "dvec", [128, 8])
        su = DramX(B.scratch(pre + "su", [128, 8, L]), L)
        sb = DramX(B.scratch(pre + "sb", [128, 8, L], BF16), L)
        sy = DramX(B.scratch(pre + "sy", [128, 8, L], BF16), L)
        B.s5a_phase(layer, src, w_in, su, sb)
        B.s5b_phase(layer, su, sb, sy, par, par_bc, bre, bim, cre, cim, dvec)
        B.s5c_phase(layer, src, mid, w_out, sy)
    if ONLY_MIXER:
        return
    w1 = B.inp(pre + "w1", [128, 8, 4096])
    w2 = B.inp(pre + "w2", [128, 32, 1024])
    B.mlp_phase(layer, mid, dst, w1, w2)


def layer_params(layer, inp):
    kind = layer % 3
    j = layer // 3
    pre = "l%d_" % layer
    d = {}
    f = np.float32
    if kind == 0:
        d["w_in"] = lay_rows(inp["rg_w_in"][j])
        d["w_a"] = np.ascontiguousarray(inp["rg_w_a"][j])
        d["w_x"] = np.ascontiguousarray(inp["rg_w_x"][j])
        d["vec"] = lay_rg_vec(inp["rg_conv_w"][j], inp["rg_conv_b"][j], inp["rg_b_a"][j], inp["rg_b_x"][j], inp["rg_lam"][j])
        d["w_out"] = lay_rows(inp["rg_w_out"][j])
    elif kind == 1:
        d["w_in"] = lay_rows(inp["ssd_w_in"][j])
        d["w_out"] = lay_rows(inp["ssd_w_out"][j])
        d.update(lay_ssd(inp["ssd_conv_w"][j], inp["ssd_conv_b"][j], inp["ssd_dt_bias"][j], inp["ssd_a_log"][j],
                         inp["ssd_d"][j], inp["ssd_norm_g"][j]))
        d["consts"] = ssd_consts()
    else:
        d["w_in"] = lay_rows(inp["s5_w_in"][j])
        d["w_out"] = lay_rows(inp["s5_w_out"][j])
        d.update(lay_s5(inp["s5_lam_re"][j], inp["s5_lam_im"][j], inp["s5_log_step"][j], inp["s5_b_re"][j], inp["s5_b_im"][j],
                        inp["s5_c_re"][j], inp["s5_c_im"][j], inp["s5_d"][j]))
    d["w1"] = lay_rows(inp["mlp_w1"][layer])
    d["w2"] = lay_rows(inp["mlp_w2"][layer])
    return {pre + k: np.ascontiguousarray(v, dtype=f) for k, v in d.items()}


def build_program(L, layers):
    B = Builder(L)
    xin = DramX(B.inp("x", [128, 8, L]), L)
    xout = DramX(B.outp("y", [128, 8, L]), L)
    xres = DramX(B.scratch("xres", [128, 8, L]), L)
    n = len(layers)
    for idx, layer in enumerate(layers):
        src = xin if idx == 0 else xres
        dst = xout if idx == n - 1 else xres
        add_layer(B, layer, src, xres, dst)
    B.P.barrier(["sp"])
    B.P.emit()
    return B


ONLY_MIXER = False
LAUNCH_GROUPS = [[0, 1, 2, 3]]


def kernel(**inp):
    inp = {k: np.asarray(v) for k, v in inp.items()}
    x = inp["x"]
    nb, L, _ = x.shape
    xs = [x_to_dev(np.asarray(x[b], dtype=np.float32)) for b in range(nb)]
    g = lay_norm_g(inp["norm_g"].astype(np.float32))
    for layers in LAUNCH_GROUPS:
        B = build_program(L, layers)
        common = {"norm_g": g}
        for layer in layers:
            common.update(layer_params(layer, inp))
        common = {k: v for k, v in common.items() if k in B.ext}
        in_maps = [dict(common, x=xs[b]) for b in range(nb)]
        res = run_bass_kernel_spmd(B.nc, in_maps, core_ids=list(range(nb)))
        xs = [np.asarray(res.results[b]["y"]) for b in range(nb)]
    out = np.stack([x_from_dev(xd) for xd in xs]).astype(np.float32)
    return out
```

```python
import numpy as np
import concourse.bass as bass
import concourse.mybir as mybir
from concourse.bass_utils import run_bass_kernel_spmd

F32 = mybir.dt.float32
BF16 = mybir.dt.bfloat16
AF = mybir.ActivationFunctionType
ALU = mybir.AluOpType

D = 1024
SEQ = 4096
NCORES = 8
EPS = 1e-6


class Res:
    __slots__ = ("w", "r")

    def __init__(self):
        self.w = None
        self.r = {}


class Prog:
    ENGS = ("pe", "act", "dve", "pool", "sp")

    def __init__(self, nc, n_dma_sems=40, same_engine_sync=("act", "dve", "pool")):
        self.nc = nc
        self.q = {e: [] for e in self.ENGS}
        self.cnt = {e: 0 for e in self.ENGS}
        self.sems = {e: nc.alloc_semaphore("sem_" + e) for e in self.ENGS}
        self.n_dma = n_dma_sems
        for i in range(n_dma_sems):
            self.sems[("d", i)] = nc.alloc_semaphore("sem_d%d" % i)
        self.dma_tot = [0] * n_dma_sems
        self.dma_i = 0
        self.waited = {e: {} for e in self.ENGS}
        self.ses = set(same_engine_sync)
        self.ninst = 0

    def _deps(self, eng, reads, writes, extra=()):
        deps = {}

        def need(tok):
            if tok is None:
                return
            k, v = tok
            if k == eng and eng not in self.ses:
                return
            if deps.get(k, 0) < v:
                deps[k] = v

        for r in reads:
            need(r.w)
        for w in writes:
            need(w.w)
            for k, v in w.r.items():
                need((k, v))
        for tok in extra:
            need(tok)
        wd = self.waited[eng]
        waits = []
        for k, v in deps.items():
            if wd.get(k, 0) < v:
                wd[k] = v
                waits.append((self.sems[k], v))
        return waits

    def _mark(self, tok, reads, writes):
        k, v = tok
        for r in reads:
            if r.r.get(k, 0) < v:
                r.r[k] = v
        for w in writes:
            w.w = tok
            w.r = {}

    def op(self, eng, fn, reads=(), writes=(), inc=True):
        waits = self._deps(eng, reads, writes)
        tok = (eng, self.cnt[eng] + 1)
        sem = self.sems[eng]
        if inc:
            self.cnt[eng] += 1

            def closure(e):
                for s, v in waits:
                    e.wait_ge(s, v)
                fn(e).then_inc(sem, 1)
        else:
            def closure(e):
                for s, v in waits:
                    e.wait_ge(s, v)
                fn(e)
        self.q[eng].append(closure)
        self._mark(tok, reads, writes)
        self.ninst += 1
        return tok

    def dma(self, eng, out, in_, reads=(), writes=(), **kw):
        idx = self.dma_i % self.n_dma
        self.dma_i += 1
        key = ("d", idx)
        extra = ()
        if self.dma_tot[idx] > 0:
            extra = ((key, 16 * self.dma_tot[idx]),)
        waits = self._deps(eng, reads, writes, extra)
        self.dma_tot[idx] += 1
        tok = (key, 16 * self.dma_tot[idx])
        sem = self.sems[key]

        def closure(e):
            for s, v in waits:
                e.wait_ge(s, v)
            e.dma_start(out=out, in_=in_, **kw).then_inc(sem, 16)
        self.q[eng].append(closure)
        self._mark(tok, reads, writes)
        self.ninst += 1
        return tok

    def all_tokens(self):
        toks = [(e, self.cnt[e]) for e in self.ENGS if self.cnt[e] > 0]
        toks += [(("d", i), 16 * n) for i, n in enumerate(self.dma_tot) if n > 0]
        return toks

    def barrier(self, engs=None):
        toks = self.all_tokens()
        for eng in (engs or self.ENGS):
            wd = self.waited[eng]
            ws = []
            for k, v in toks:
                if k == eng:
                    continue
                if wd.get(k, 0) < v:
                    wd[k] = v
                    ws.append((self.sems[k], v))
            if ws:
                def closure(e, ws=ws):
                    for s, v in ws:
                        e.wait_ge(s, v)
                self.q[eng].append(closure)

    def emit(self):
        nc = self.nc
        q = self.q
        with nc.Block() as block:
            @block.tensor
            def _(e):
                for f in q["pe"]:
                    f(e)

            @block.scalar
            def _(e):
                for f in q["act"]:
                    f(e)

            @block.vector
            def _(e):
                for f in q["dve"]:
                    f(e)

            @block.gpsimd
            def _(e):
                for f in q["pool"]:
                    f(e)

            @block.sync
            def _(e):
                for f in q["sp"]:
                    f(e)


class Arena:
    def __init__(self, nc, nbytes=212800):
        self.t = nc.alloc_sbuf_tensor("arena", [128, nbytes // 2], BF16)
        self.nbytes = nbytes
        self.off = 0

    def alloc(self, free, dtype):
        if isinstance(free, int):
            free = (free,)
        n = 1
        for s in free:
            n *= s
        sz = n * (4 if dtype == F32 else 2)
        off = (self.off + 63) // 64 * 64
        assert off + sz <= self.nbytes, ("SBUF arena overflow", off, sz, self.nbytes)
        self.off = off + sz
        ap = self.t[:, off // 2:(off + sz) // 2]
        if dtype == F32:
            ap = ap.bitcast(F32)
        if len(free) == 2:
            ap = ap.rearrange("p (a b) -> p a b", a=free[0])
        elif len(free) == 3:
            ap = ap.rearrange("p (a b c) -> p a b c", a=free[0], b=free[1])
        return ap


class DramX:
    def __init__(self, ap, L):
        self.ap = ap
        self.res = [Res() for _ in range(L // 256)]

    def blocks(self, t0, T):
        return self.res[t0 // 256:(t0 + T) // 256]


class Builder:
    def __init__(self, L):
        self.L = L
        nc = self.nc = bass.Bass("TRN2", target_bir_lowering=False)
        self.P = Prog(nc)
        self.A = Arena(nc)
        self.bank = [nc.alloc_psum_tensor("bank%d" % i, [128, 512], F32) for i in range(8)]
        self.bres = [Res() for _ in range(8)]
        self.ext = {}
        A = self.A
        self.ones = A.alloc(128, BF16)
        self.g_sb = A.alloc(128, F32)
        self.epsc = A.alloc(8, F32)
        self.persist_end = A.off
        P = self.P
        r = Res()
        P.op("pool", lambda e: e.memset(self.ones, 1.0), writes=[r])
        P.op("pool", lambda e: e.memset(self.epsc[:, 0:1], EPS), writes=[r])
        P.op("pool", lambda e: e.memset(self.epsc[:, 1:2], 1.0), writes=[r])
        g = self.inp("norm_g", [128, 128])
        P.dma("sp", self.g_sb, g, writes=[r])
        P.barrier()

    def inp(self, name, shape, dtype=F32):
        ap = self.nc.dram_tensor(name, list(shape), dtype, kind="ExternalInput").ap()
        self.ext[name] = ap
        return ap

    def outp(self, name, shape, dtype=F32):
        return self.nc.dram_tensor(name, list(shape), dtype, kind="ExternalOutput").ap()

    def scratch(self, name, shape, dtype=F32):
        return self.nc.dram_tensor(name, list(shape), dtype, kind="Internal").ap()

    def full(self, b):
        return self.bank[b][:, :], [self.bres[b]]

    def gcol(self, layer, j, c):
        i = (layer * 4 + j) * 8 + c
        return self.g_sb[:, i:i + 1]

    def rstd_from_sq(self, sq, r_sq, nchunks, T, slot, ms, r_ms, rstd, r_rstd, denom):
        P = self.P
        ps, rps = slot
        for c in range(nchunks):
            P.op("pe", lambda e, c=c: e.matmul(ps[:, 0:T], self.ones, sq[:, c, :], start=(c == 0), stop=(c == nchunks - 1)),
                 reads=[r_sq], writes=rps, inc=(c == nchunks - 1))
        P.op("act", lambda e: e.activation(out=ms, in_=ps[:, 0:T], func=AF.Ln, scale=1.0 / denom, bias=self.epsc[:, 0:1]),
             reads=rps, writes=[r_ms])
        P.op("act", lambda e: e.activation(out=rstd, in_=ms, func=AF.Exp, scale=-0.5),
             reads=[r_ms], writes=[r_rstd])

    def mlp_phase(self, layer, src, dst, w1, w2):
        P, A = self.P, self.A
        T = 256
        NT = self.L // T
        A.off = self.persist_end
        W1 = A.alloc((8, 4096), BF16)
        W2 = A.alloc((32, 1024), BF16)
        rW1 = [Res() for _ in range(8)]
        rW2 = [Res() for _ in range(8)]
        for kc in range(8):
            P.dma("pool", W1[:, :, kc * 512:(kc + 1) * 512], w1[:, :, kc * 512:(kc + 1) * 512], writes=[rW1[kc]])
        for g in range(8):
            P.dma("pool", W2[:, 4 * g:4 * g + 4, :], w2[:, 4 * g:4 * g + 4, :], writes=[rW2[g]])
        xt = [A.alloc((8, T), F32) for _ in range(2)]
        r_xt = [Res() for _ in range(2)]
        xn = [A.alloc((8, T), BF16) for _ in range(2)]
        r_xn = [Res() for _ in range(2)]
        sq = [A.alloc((8, T), BF16) for _ in range(2)]
        r_sq = [Res() for _ in range(2)]
        ms = [A.alloc(T, F32) for _ in range(2)]
        r_ms = [Res() for _ in range(2)]
        rstd = [A.alloc(T, F32) for _ in range(2)]
        r_rstd = [Res() for _ in range(2)]
        h = A.alloc((32, T), BF16)
        r_h = [Res() for _ in range(32)]
        hout = A.alloc((8, T), F32)
        r_hout = [Res() for _ in range(8)]
        sq2 = A.alloc((8, T), BF16)
        r_sq2 = Res()
        NTMP = 2
        tmp = [A.alloc(2 * T, F32) for _ in range(NTMP)]
        r_tmp = [Res() for _ in range(NTMP)]
        ms2 = A.alloc(T, F32)
        r_ms2 = Res()
        rstd2 = A.alloc(T, F32)
        r_rstd2 = Res()
        hslots = [self.full(b) for b in range(4)]
        oslots = [self.full(4), self.full(5)]
        nslots = [self.full(6), self.full(6)]
        n2slot = self.full(7)
        cnt = {"h": 0, "o": 0}

        def load(i):
            b = i % 2
            P.dma("sp", xt[b], src.ap[:, :, i * T:(i + 1) * T], reads=src.blocks(i * T, T), writes=[r_xt[b]])

        def pre(i):
            b = i % 2
            xf = xt[b].rearrange("p c t -> p (c t)")
            sf = sq[b].rearrange("p c t -> p (c t)")
            P.op("act", lambda e: e.activation(out=sf, in_=xf, func=AF.Square), reads=[r_xt[b]], writes=[r_sq[b]])
            self.rstd_from_sq(sq[b], r_sq[b], 8, T, nslots[b], ms[b], r_ms[b], rstd[b], r_rstd[b], float(D))
            for c in range(8):
                P.op("dve", lambda e, c=c: e.scalar_tensor_tensor(
                    out=xn[b][:, c, :], in0=xt[b][:, c, :], scalar=self.gcol(layer, 2, c), in1=rstd[b],
                    op0=ALU.mult, op1=ALU.mult), reads=[r_xt[b], r_rstd[b]], writes=[r_xn[b]])

        def mm1(i):
            b = i % 2
            for m2 in range(16):
                ps, rps = hslots[cnt["h"] % len(hslots)]
                tp = cnt["h"] % NTMP
                cnt["h"] += 1
                for hh in range(2):
                    m = 2 * m2 + hh
                    for kc in range(8):
                        P.op("pe", lambda e, kc=kc, m=m, hh=hh, ps=ps: e.matmul(
                            ps[:, hh * T:(hh + 1) * T], W1[:, kc, m * 128:(m + 1) * 128], xn[b][:, kc, :],
                            start=(kc == 0), stop=(kc == 7)),
                            reads=[rW1[m // 4], r_xn[b]], writes=rps, inc=(kc == 7 and hh == 1))
                P.op("act", lambda e, ps=ps, tp=tp: e.activation(out=tmp[tp], in_=ps, func=AF.Relu),
                     reads=rps, writes=[r_tmp[tp]])
                hv = h[:, 2 * m2:2 * m2 + 2, :].rearrange("p c t -> p (c t)")
                P.op("pool", lambda e, hv=hv, tp=tp: e.tensor_tensor(out=hv, in0=tmp[tp], in1=tmp[tp], op=ALU.mult),
                     reads=[r_tmp[tp]], writes=[r_h[2 * m2], r_h[2 * m2 + 1]])

        def mm2(i):
            for m2 in range(4):
                ps, rps = oslots[cnt["o"] % len(oslots)]
                cnt["o"] += 1
                for hh in range(2):
                    m = 2 * m2 + hh
                    for kc in range(32):
                        P.op("pe", lambda e, kc=kc, m=m, hh=hh, ps=ps: e.matmul(
                            ps[:, hh * T:(hh + 1) * T], W2[:, kc, m * 128:(m + 1) * 128], h[:, kc, :],
                            start=(kc == 0), stop=(kc == 31)),
                            reads=[rW2[kc // 4], r_h[kc]], writes=rps, inc=(kc == 31 and hh == 1))
                ho = hout[:, 2 * m2:2 * m2 + 2, :].rearrange("p c t -> p (c t)")
                so = sq2[:, 2 * m2:2 * m2 + 2, :].rearrange("p c t -> p (c t)")
                P.op("act", lambda e, ho=ho, ps=ps: e.activation(out=ho, in_=ps, func=AF.Copy),
                     reads=rps, writes=[r_hout[2 * m2], r_hout[2 * m2 + 1]])
                P.op("act", lambda e, so=so, ps=ps: e.activation(out=so, in_=ps, func=AF.Square),
                     reads=rps, writes=[r_sq2])

        def post(i):
            b = i % 2
            self.rstd_from_sq(sq2, r_sq2, 8, T, n2slot, ms2, r_ms2, rstd2, r_rstd2, float(D))
            for c in range(8):
                P.op("dve", lambda e, c=c: e.scalar_tensor_tensor(
                    out=hout[:, c, :], in0=hout[:, c, :], scalar=self.gcol(layer, 3, c), in1=rstd2,
                    op0=ALU.mult, op1=ALU.mult), reads=[r_hout[c], r_rstd2], writes=[r_hout[c]])
                P.op("pool", lambda e, c=c: e.tensor_tensor(out=xt[b][:, c, :], in0=xt[b][:, c, :], in1=hout[:, c, :], op=ALU.add),
                     reads=[r_hout[c], r_xt[b]], writes=[r_xt[b]])
            P.dma("sp", dst.ap[:, :, i * T:(i + 1) * T], xt[b], reads=[r_xt[b]], writes=dst.blocks(i * T, T))

        load(0)
        pre(0)
        if NT > 1:
            load(1)
        for i in range(NT):
            mm1(i)
            if i + 1 < NT:
                pre(i + 1)
            mm2(i)
            post(i)
            if i + 2 < NT:
                load(i + 2)
        P.barrier()


    def rglru_phase(self, layer, src, dst, w_in, w_a, w_x, vec, w_out):
        P, A = self.P, self.A
        T = 512
        NT = self.L // T
        A.off = self.persist_end
        Win = A.alloc((8, 2048), BF16)
        Wout = A.alloc((8, 1024), BF16)
        Wa = A.alloc((8, 128), BF16)
        Wx = A.alloc((8, 128), BF16)
        V = A.alloc((8, 8), F32)
        rW = Res()
        rWin = [Res() for _ in range(8)]
        for kc in range(8):
            cg = (kc + 4) % 8
            P.dma("pool", Win[:, :, cg * 256:(cg + 1) * 256], w_in[:, :, cg * 256:(cg + 1) * 256], writes=[rWin[cg]])
        P.dma("pool", Wout, w_out, writes=[rW])
        P.op("dve", lambda e: e.memset(Wa, 0.0), writes=[rW])
        P.op("dve", lambda e: e.memset(Wx, 0.0), writes=[rW])
        for hh in range(2):
            for (Wd, wsrc) in ((Wa, w_a), (Wx, w_x)):
                sv = wsrc.rearrange("(c hh) i j -> hh i c j", hh=2)[hh]
                P.dma("pool", Wd[hh * 64:(hh + 1) * 64, :, hh * 64:(hh + 1) * 64], sv, writes=[rW])
        P.dma("sp", V, vec, writes=[rW])
        DV = A.alloc((4, 8), F32)
        halfc = A.alloc(512, F32)
        P.op("pool", lambda e: e.memset(halfc, 0.5), writes=[rW])
        P.op("dve", lambda e: e.tensor_scalar(out=DV[:, 0, :], in0=V[:, 5, :], scalar1=0.5, scalar2=None, op0=ALU.mult), reads=[rW], writes=[rW])
        P.op("dve", lambda e: e.tensor_scalar(out=DV[:, 1, :], in0=V[:, 6, :], scalar1=0.5, scalar2=None, op0=ALU.mult), reads=[rW], writes=[rW])
        P.op("act", lambda e: e.activation(out=DV[:, 3, :], in_=V[:, 7, :], func=AF.Exp, scale=-1.0), reads=[rW], writes=[rW])
        P.op("dve", lambda e: e.tensor_scalar(out=DV[:, 3, :], in0=DV[:, 3, :], scalar1=1.0, scalar2=None, op0=ALU.add), reads=[rW], writes=[rW])
        P.op("act", lambda e: e.activation(out=DV[:, 3, :], in_=DV[:, 3, :], func=AF.Ln), reads=[rW], writes=[rW])
        P.op("dve", lambda e: e.tensor_scalar(out=DV[:, 2, :], in0=DV[:, 3, :], scalar1=-4.0, scalar2=None, op0=ALU.mult), reads=[rW], writes=[rW])
        P.op("dve", lambda e: e.tensor_scalar(out=DV[:, 3, :], in0=DV[:, 3, :], scalar1=-8.0, scalar2=None, op0=ALU.mult), reads=[rW], writes=[rW])

        xts = [A.alloc((8, T), F32) for _ in range(2)]; r_xts = [Res() for _ in range(2)]
        sq = A.alloc((8, T), BF16); r_sq = Res()
        xn = A.alloc((8, T), BF16); r_xn = Res()
        gate = A.alloc((8, T), F32); r_gate = [Res() for _ in range(8)]
        hout, r_hout = gate, r_gate
        UB = T + 4
        ubuf = A.alloc((8, UB), F32); r_ub = [Res() for _ in range(8)]
        uc = A.alloc((8, T), F32); r_uc = [Res() for _ in range(8)]
        ucb = A.alloc((8, T), BF16); r_ucb = [Res() for _ in range(8)]
        hb = A.alloc((8, T), F32); r_h = [Res() for _ in range(8)]
        hst = A.alloc(8, F32); r_hst = Res()
        ms = A.alloc(T, F32); r_ms = Res()
        rstd = A.alloc(T, F32); r_rstd = Res()
        NS = 2
        tA = [[A.alloc(T, F32) for _ in range(5)] for _ in range(NS)]
        r_tA = [[Res() for _ in range(5)] for _ in range(NS)]
        yb = xn
        r_yb = r_xn
        sq2 = sq
        r_sq2 = r_sq
        pslots = [self.full(b) for b in range(4)]
        gslots = [self.full(4), self.full(5)]
        nslot = self.full(6)
        n2slot = self.full(7)
        cnt = {"p": 0, "g": 0, "t": 0}
        P.op("pool", lambda e: e.memset(ubuf, 0.0), writes=r_ub)
        P.op("pool", lambda e: e.memset(hst, 0.0), writes=[r_hst])

        def vcol(v, c):
            return V[:, v, c:c + 1]

        def dcol(v, c):
            return DV[:, v, c:c + 1]

        def load(i):
            P.dma("sp", xts[i % 2], src.ap[:, :, i * T:(i + 1) * T], reads=src.blocks(i * T, T), writes=[r_xts[i % 2]])

        def tile(i, xt, r_xt):
            xf = xt.rearrange("p c t -> p (c t)")
            sf = sq.rearrange("p c t -> p (c t)")
            P.op("act", lambda e: e.activation(out=sf, in_=xf, func=AF.Square), reads=[r_xt], writes=[r_sq])
            self.rstd_from_sq(sq, r_sq, 8, T, nslot, ms, r_ms, rstd, r_rstd, float(D))
            for c in range(8):
                P.op("dve", lambda e, c=c: e.scalar_tensor_tensor(
                    out=xn[:, c, :], in0=xt[:, c, :], scalar=self.gcol(layer, 0, c), in1=rstd,
                    op0=ALU.mult, op1=ALU.mult), reads=[r_xt, r_rstd], writes=[r_xn])
            for m in list(range(8, 16)) + list(range(8)):
                ps, rps = pslots[cnt["p"] % 4]
                cnt["p"] += 1
                for kc in range(8):
                    P.op("pe", lambda e, kc=kc, m=m, ps=ps: e.matmul(
                        ps, Win[:, kc, m * 128:(m + 1) * 128], xn[:, kc, :], start=(kc == 0), stop=(kc == 7)),
                        reads=[rWin[m // 2], r_xn], writes=rps, inc=(kc == 7))
                if m < 8:
                    P.op("act", lambda e, m=m, ps=ps: e.activation(out=gate[:, m, :], in_=ps, func=AF.Gelu_apprx_tanh),
                         reads=rps, writes=[r_gate[m]])
                else:
                    c = m - 8
                    P.op("act", lambda e, c=c, ps=ps: e.activation(out=ubuf[:, c, 4:4 + T], in_=ps, func=AF.Copy),
                         reads=rps, writes=[r_ub[c]])
                    P.op("act", lambda e, c=c, ps=ps: e.activation(out=uc[:, c, :], in_=ps, func=AF.Identity,
                                                                   scale=vcol(3, c), bias=vcol(4, c)),
                         reads=rps + [rW], writes=[r_uc[c]])
                    for k in range(3):
                        P.op("dve", lambda e, c=c, k=k: e.scalar_tensor_tensor(
                            out=uc[:, c, :], in0=ubuf[:, c, 1 + k:1 + k + T], scalar=vcol(k, c), in1=uc[:, c, :],
                            op0=ALU.mult, op1=ALU.add), reads=[r_ub[c], r_uc[c], rW], writes=[r_uc[c]])
                    P.op("pool", lambda e, c=c: e.tensor_copy(out=ubuf[:, c, 1:4], in_=ubuf[:, c, T + 1:T + 4]),
                         reads=[r_ub[c]], writes=[r_ub[c]])
                    P.op("pool", lambda e, c=c: e.tensor_copy(out=ucb[:, c, :], in_=uc[:, c, :]),
                         reads=[r_uc[c]], writes=[r_ucb[c]])
            for c in range(8):
                s = cnt["t"] % NS
                cnt["t"] += 1
                th, aa, a2, thx, bb = tA[s]
                r_th, r_aa, r_a2, r_thx, r_bb = r_tA[s]
                psr, rpsr = gslots[0]
                psx, rpsx = gslots[1]
                P.op("pe", lambda e, c=c, psr=psr: e.matmul(psr, Wa[:, c, :], ucb[:, c, :], start=True, stop=True),
                     reads=[rW, r_ucb[c]], writes=rpsr)
                P.op("pe", lambda e, c=c, psx=psx: e.matmul(psx, Wx[:, c, :], ucb[:, c, :], start=True, stop=True),
                     reads=[rW, r_ucb[c]], writes=rpsx)
                P.op("act", lambda e, c=c, th=th, psr=psr: e.activation(out=th, in_=psr, func=AF.Tanh, scale=0.5, bias=dcol(0, c)),
                     reads=rpsr + [rW], writes=[r_th])
                P.op("act", lambda e, c=c, thx=thx, psx=psx: e.activation(out=thx, in_=psx, func=AF.Tanh, scale=0.5, bias=dcol(1, c)),
                     reads=rpsx + [rW], writes=[r_thx])
                P.op("act", lambda e, c=c, th=th, aa=aa: e.activation(out=aa, in_=th, func=AF.Exp, scale=dcol(2, c), bias=dcol(2, c)),
                     reads=[r_th, rW], writes=[r_aa])
                P.op("act", lambda e, c=c, th=th, a2=a2: e.activation(out=a2, in_=th, func=AF.Exp, scale=dcol(3, c), bias=dcol(3, c)),
                     reads=[r_th, rW], writes=[r_a2])
                P.op("act", lambda e, a2=a2: e.activation(out=a2, in_=a2, func=AF.Ln, scale=-1.0, bias=self.epsc[:, 1:2]),
                     reads=[r_a2], writes=[r_a2])
                P.op("act", lambda e, a2=a2: e.activation(out=a2, in_=a2, func=AF.Exp, scale=0.5),
                     reads=[r_a2], writes=[r_a2])
                P.op("dve", lambda e, c=c, thx=thx, bb=bb: e.scalar_tensor_tensor(
                    out=bb, in0=thx, scalar=1.0, in1=uc[:, c, :], op0=ALU.add, op1=ALU.mult),
                    reads=[r_thx, r_uc[c]], writes=[r_bb])
                P.op("dve", lambda e, a2=a2, bb=bb: e.scalar_tensor_tensor(
                    out=bb, in0=bb, scalar=0.5, in1=a2, op0=ALU.mult, op1=ALU.mult),
                    reads=[r_bb, r_a2], writes=[r_bb])
                P.op("dve", lambda e, c=c, aa=aa, bb=bb: e.tensor_tensor_scan(
                    out=hb[:, c, :], data0=aa, data1=bb, initial=hst[:, c:c + 1], op0=ALU.mult, op1=ALU.add),
                    reads=[r_aa, r_bb, r_hst], writes=[r_h[c]])
            P.op("dve", lambda e: e.tensor_copy(out=hst, in_=hb[:, :, T - 1]), reads=r_h, writes=[r_hst])
            P.op("dve", lambda e: e.tensor_tensor(out=yb.rearrange("p c t -> p (c t)"), in0=hb.rearrange("p c t -> p (c t)"),
                                                  in1=gate.rearrange("p c t -> p (c t)"), op=ALU.mult),
                 reads=r_h + r_gate, writes=[r_yb])
            for m in range(8):
                ps, rps = pslots[cnt["p"] % 4]
                cnt["p"] += 1
                for kc in range(8):
                    P.op("pe", lambda e, kc=kc, m=m, ps=ps: e.matmul(
                        ps, Wout[:, kc, m * 128:(m + 1) * 128], yb[:, kc, :], start=(kc == 0), stop=(kc == 7)),
                        reads=[rW, r_yb], writes=rps, inc=(kc == 7))
                P.op("act", lambda e, m=m, ps=ps: e.activation(out=hout[:, m, :], in_=ps, func=AF.Copy),
                     reads=rps, writes=[r_hout[m]])
                P.op("act", lambda e, m=m, ps=ps: e.activation(out=sq2[:, m, :], in_=ps, func=AF.Square),
                     reads=rps, writes=[r_sq2])
            self.post_norm_residual(layer, 1, hout, r_hout, sq2, r_sq2, xt, r_xt, T, n2slot, ms, r_ms, rstd, r_rstd)
            P.dma("sp", dst.ap[:, :, i * T:(i + 1) * T], xt, reads=[r_xt], writes=dst.blocks(i * T, T))
        load(0)
        for i in range(NT):
            if i + 1 < NT:
                load(i + 1)
            tile(i, xts[i % 2], r_xts[i % 2])
        P.barrier()

    def post_norm_residual(self, layer, j, hout, r_hout, sq2, r_sq2, xt, r_xt, T, slot, ms, r_ms, rstd, r_rstd):
        P = self.P
        self.rstd_from_sq(sq2, r_sq2, 8, T, slot, ms, r_ms, rstd, r_rstd, float(D))
        for c in range(8):
            P.op("dve", lambda e, c=c: e.scalar_tensor_tensor(
                out=hout[:, c, :], in0=hout[:, c, :], scalar=self.gcol(layer, j, c), in1=rstd,
                op0=ALU.mult, op1=ALU.mult), reads=[r_hout[c], r_rstd], writes=[r_hout[c]])
            P.op("pool", lambda e, c=c: e.tensor_tensor(out=xt[:, c, :], in0=xt[:, c, :], in1=hout[:, c, :], op=ALU.add),
                 reads=[r_hout[c], r_xt], writes=[r_xt])


    def sincos(self, th, out_sin, out_cos, tmp, r, eng="dve"):
        P = self.P
        MAGIC = 12582912.0
        TWO_PI = 6.283185307179586
        for (shift, dst) in ((0.0, out_sin), (1.5707963267948966, out_cos)):
            P.op(eng, lambda e, shift=shift: e.tensor_scalar(out=tmp, in0=th, scalar1=shift, scalar2=1.0 / TWO_PI, op0=ALU.add, op1=ALU.mult),
                 reads=[r], writes=[r])
            P.op(eng, lambda e: e.tensor_scalar(out=tmp, in0=tmp, scalar1=MAGIC, scalar2=None, op0=ALU.add), reads=[r], writes=[r])
            P.op(eng, lambda e: e.tensor_scalar(out=tmp, in0=tmp, scalar1=-MAGIC, scalar2=-TWO_PI, op0=ALU.add, op1=ALU.mult), reads=[r], writes=[r])
            P.op(eng, lambda e, shift=shift, dst=dst: e.scalar_tensor_tensor(out=dst, in0=th, scalar=shift, in1=tmp, op0=ALU.add, op1=ALU.add),
                 reads=[r], writes=[r])
            P.op("act", lambda e, dst=dst: e.activation(out=dst, in_=dst, func=AF.Sin), reads=[r], writes=[r])

    def s5a_phase(self, layer, src, w_in, su, sb):
        P, A = self.P, self.A
        T = 256
        NT = self.L // T
        A.off = self.persist_end
        rW = Res()
        Win = A.alloc((8, 1024), BF16)
        P.dma("pool", Win, w_in, writes=[rW])
        xt = [A.alloc((8, T), F32) for _ in range(2)]; r_xt = [Res() for _ in range(2)]
        sq = [A.alloc((8, T), BF16) for _ in range(2)]; r_sq = [Res() for _ in range(2)]
        xn = [A.alloc((8, T), BF16) for _ in range(2)]; r_xn = [Res() for _ in range(2)]
        ms = [A.alloc(T, F32) for _ in range(2)]; r_ms = [Res() for _ in range(2)]
        rstd = [A.alloc(T, F32) for _ in range(2)]; r_rstd = [Res() for _ in range(2)]
        uo = [A.alloc((8, T), F32) for _ in range(2)]; r_uo = [Res() for _ in range(2)]
        ubo = [A.alloc((8, T), BF16) for _ in range(2)]; r_ubo = [Res() for _ in range(2)]
        pslots = [self.full(b) for b in range(4)]
        nslots = [self.full(6), self.full(7)]
        cnt = {"p": 0}

        def load(i):
            b = i % 2
            P.dma("sp", xt[b], src.ap[:, :, i * T:(i + 1) * T], reads=src.blocks(i * T, T), writes=[r_xt[b]])

        def pre(i):
            b = i % 2
            P.op("act", lambda e: e.activation(out=sq[b].rearrange("p c t -> p (c t)"), in_=xt[b].rearrange("p c t -> p (c t)"), func=AF.Square),
                 reads=[r_xt[b]], writes=[r_sq[b]])
            self.rstd_from_sq(sq[b], r_sq[b], 8, T, nslots[b], ms[b], r_ms[b], rstd[b], r_rstd[b], float(D))
            for c in range(8):
                P.op("dve", lambda e, c=c: e.scalar_tensor_tensor(
                    out=xn[b][:, c, :], in0=xt[b][:, c, :], scalar=self.gcol(layer, 0, c), in1=rstd[b],
                    op0=ALU.mult, op1=ALU.mult), reads=[r_xt[b], r_rstd[b]], writes=[r_xn[b]])

        def mm(i):
            b = i % 2
            for m2 in range(4):
                ps, rps = pslots[cnt["p"] % len(pslots)]
                cnt["p"] += 1
                for hh in range(2):
                    mm_ = 2 * m2 + hh
                    for kc in range(8):
                        P.op("pe", lambda e, kc=kc, mm_=mm_, hh=hh, ps=ps: e.matmul(
                            ps[:, hh * T:(hh + 1) * T], Win[:, kc, mm_ * 128:(mm_ + 1) * 128], xn[b][:, kc, :],
                            start=(kc == 0), stop=(kc == 7)), reads=[rW, r_xn[b]], writes=rps, inc=(kc == 7 and hh == 1))
                o1 = uo[b][:, 2 * m2:2 * m2 + 2, :].rearrange("p c t -> p (c t)")
                o2 = ubo[b][:, 2 * m2:2 * m2 + 2, :].rearrange("p c t -> p (c t)")
                P.op("act", lambda e, o1=o1, ps=ps: e.activation(out=o1, in_=ps, func=AF.Copy), reads=rps, writes=[r_uo[b]])
                P.op("act", lambda e, o2=o2, ps=ps: e.activation(out=o2, in_=ps, func=AF.Copy), reads=rps, writes=[r_ubo[b]])
            P.dma("sp", su.ap[:, :, i * T:(i + 1) * T], uo[b], reads=[r_uo[b]])
            P.dma("sp", sb.ap[:, :, i * T:(i + 1) * T], ubo[b], reads=[r_ubo[b]])

        load(0)
        pre(0)
        if NT > 1:
            load(1)
        for i in range(NT):
            if i + 1 < NT:
                pre(i + 1)
            mm(i)
            if i + 2 < NT:
                load(i + 2)
        P.barrier()

    def s5c_phase(self, layer, src, dst, w_out, sy):
        P, A = self.P, self.A
        T = 256
        NT = self.L // T
        A.off = self.persist_end
        rW = Res()
        Wout = A.alloc((8, 2048), BF16)
        for hh in range(2):
            P.dma("pool", Wout[:, :, hh * 1024:(hh + 1) * 1024], w_out[:, :, hh * 1024:(hh + 1) * 1024], writes=[rW])
        xt = [A.alloc((8, T), F32) for _ in range(2)]; r_xt = [Res() for _ in range(2)]
        yb = [A.alloc((8, T), BF16) for _ in range(2)]; r_yb = [Res() for _ in range(2)]
        hout = [A.alloc((8, T), F32) for _ in range(2)]; r_hout = [[Res() for _ in range(8)] for _ in range(2)]
        sq2 = [A.alloc((8, T), BF16) for _ in range(2)]; r_sq2 = [Res() for _ in range(2)]
        tg = [A.alloc(T, F32) for _ in range(3)]; r_tg = [Res() for _ in range(3)]
        ms = [A.alloc(T, F32) for _ in range(2)]; r_ms = [Res() for _ in range(2)]
        rstd = [A.alloc(T, F32) for _ in range(2)]; r_rstd = [Res() for _ in range(2)]
        pslots = [self.full(b) for b in range(5)]
        nslots = [self.full(6), self.full(7)]
        cnt = {"p": 0, "t": 0}

        def load(i):
            b = i % 2
            P.dma("sp", yb[b], sy.ap[:, :, i * T:(i + 1) * T], writes=[r_yb[b]])
            P.dma("sp", xt[b], src.ap[:, :, i * T:(i + 1) * T], reads=src.blocks(i * T, T), writes=[r_xt[b]])

        def body(i):
            b = i % 2
            for mo in range(8):
                ps, rps = pslots[cnt["p"] % len(pslots)]
                cnt["p"] += 1
                for hh in range(2):
                    col = (hh * 8 + mo) * 128
                    for kc in range(8):
                        P.op("pe", lambda e, kc=kc, col=col, hh=hh, ps=ps: e.matmul(
                            ps[:, hh * T:(hh + 1) * T], Wout[:, kc, col:col + 128], yb[b][:, kc, :],
                            start=(kc == 0), stop=(kc == 7)), reads=[rW, r_yb[b]], writes=rps, inc=(kc == 7 and hh == 1))
                yy = cnt["t"] % 3
                cnt["t"] += 1
                P.op("act", lambda e, yy=yy, ps=ps: e.activation(out=tg[yy], in_=ps[:, T:2 * T], func=AF.Tanh, scale=0.5),
                     reads=rps, writes=[r_tg[yy]])
                P.op("dve", lambda e, yy=yy, ps=ps: e.scalar_tensor_tensor(
                    out=tg[yy], in0=tg[yy], scalar=1.0, in1=ps[:, 0:T], op0=ALU.add, op1=ALU.mult),
                    reads=rps + [r_tg[yy]], writes=[r_tg[yy]])
                P.op("act", lambda e, mo=mo, yy=yy: e.activation(out=hout[b][:, mo, :], in_=tg[yy], func=AF.Copy, scale=0.5),
                     reads=[r_tg[yy]], writes=[r_hout[b][mo]])
                P.op("act", lambda e, mo=mo, yy=yy: e.activation(out=sq2[b][:, mo, :], in_=tg[yy], func=AF.Square, scale=0.5),
                     reads=[r_tg[yy]], writes=[r_sq2[b]])
            self.post_norm_residual(layer, 1, hout[b], r_hout[b], sq2[b], r_sq2[b], xt[b], r_xt[b], T, nslots[b], ms[b], r_ms[b], rstd[b], r_rstd[b])
            P.dma("sp", dst.ap[:, :, i * T:(i + 1) * T], xt[b], reads=[r_xt[b]], writes=dst.blocks(i * T, T))

        load(0)
        for i in range(NT):
            if i + 1 < NT:
                load(i + 1)
            body(i)
        P.barrier()

    def s5b_phase(self, layer, su, sb, sy, par, par_bc, bpad_re, bpad_im, cpad_re, cpad_im, dvec):
        P, A = self.P, self.A
        T = 256
        NT = self.L // T
        A.off = self.persist_end
        rW = Res()
        Bre = A.alloc((32, 128), BF16)
        Bim = A.alloc((32, 128), BF16)
        Cre = A.alloc((32, 128), BF16)
        Cimn = A.alloc((32, 128), BF16)
        cosT = A.alloc((32, T), F32)
        sinT = A.alloc((32, T), F32)
        Dv = A.alloc(8, F32)
        SP = A.alloc((12, 32), F32)
        RT = A.alloc((2, 32), F32)
        GL = A.alloc((2, 32), F32)
        INIT = A.alloc((2, 32), F32)
        base = A.off
        P.dma("pool", Cre, cpad_re, writes=[rW])
        P.dma("pool", Cimn, cpad_im, writes=[rW])
        P.op("pool", lambda e: e.tensor_scalar(out=Cimn.rearrange("p a b -> p (a b)"), in0=Cimn.rearrange("p a b -> p (a b)"),
                                               scalar1=-1.0, scalar2=None, op0=ALU.mult), reads=[rW], writes=[rW])
        P.dma("sp", Dv, dvec, writes=[rW])
        P.dma("sp", SP[:, 0:3, :], par, writes=[rW])
        Q = 1024
        tl = [A.alloc(Q, F32) for _ in range(14)]
        rq = Res()
        for qi in range(4):
            self.s5_bbar_quarter(qi, Q, tl, rq, rW, par_bc, bpad_re, bpad_im, Bre, Bim)
        sp = [SP[:, i, :] for i in range(12)]
        self.s5_disc(sp[0], sp[1], sp[2], sp[3], sp[4], sp[5], sp[6], sp[7], sp[8], sp[9], sp[10], sp[11], rW, unit=True)
        MAG = sp[3]
        rm_re, rm_im = sp[6], sp[5]
        P.barrier()
        A.off = base
        ta = A.alloc((32, 128), F32)
        tb = A.alloc((32, 128), F32)
        P.op("dve", lambda e: e.memset(cosT[:, :, 0:1], 1.0), writes=[rW])
        P.op("dve", lambda e: e.memset(sinT[:, :, 0:1], 0.0), writes=[rW])
        m = 1
        while m <= T // 2:
            bre_ = rm_re.unsqueeze(2).to_broadcast([128, 32, m])
            bim_ = rm_im.unsqueeze(2).to_broadcast([128, 32, m])
            P.op("dve", lambda e, m=m, b=bre_: e.tensor_tensor(out=ta[:, :, 0:m], in0=cosT[:, :, 0:m], in1=b, op=ALU.mult), reads=[rW], writes=[rW])
            P.op("dve", lambda e, m=m, b=bim_: e.tensor_tensor(out=tb[:, :, 0:m], in0=sinT[:, :, 0:m], in1=b, op=ALU.mult), reads=[rW], writes=[rW])
            P.op("dve", lambda e, m=m: e.tensor_tensor(out=cosT[:, :, m:2 * m], in0=ta[:, :, 0:m], in1=tb[:, :, 0:m], op=ALU.subtract), reads=[rW], writes=[rW])
            P.op("dve", lambda e, m=m, b=bim_: e.tensor_tensor(out=ta[:, :, 0:m], in0=cosT[:, :, 0:m], in1=b, op=ALU.mult), reads=[rW], writes=[rW])
            P.op("dve", lambda e, m=m, b=bre_: e.tensor_tensor(out=tb[:, :, 0:m], in0=sinT[:, :, 0:m], in1=b, op=ALU.mult), reads=[rW], writes=[rW])
            P.op("dve", lambda e, m=m: e.tensor_tensor(out=sinT[:, :, m:2 * m], in0=ta[:, :, 0:m], in1=tb[:, :, 0:m], op=ALU.add), reads=[rW], writes=[rW])
            P.op("dve", lambda e: e.tensor_tensor(out=sp[7], in0=rm_re, in1=rm_re, op=ALU.mult), reads=[rW], writes=[rW])
            P.op("dve", lambda e: e.tensor_tensor(out=sp[8], in0=rm_im, in1=rm_im, op=ALU.mult), reads=[rW], writes=[rW])
            P.op("dve", lambda e: e.scalar_tensor_tensor(out=rm_im, in0=rm_re, scalar=2.0, in1=rm_im, op0=ALU.mult, op1=ALU.mult), reads=[rW], writes=[rW])
            P.op("dve", lambda e: e.tensor_tensor(out=rm_re, in0=sp[7], in1=sp[8], op=ALU.subtract), reads=[rW], writes=[rW])
            m *= 2
        P.op("dve", lambda e: e.tensor_copy(out=RT[:, 0, :], in_=rm_re), reads=[rW], writes=[rW])
        P.op("dve", lambda e: e.tensor_copy(out=RT[:, 1, :], in_=rm_im), reads=[rW], writes=[rW])
        P.op("dve", lambda e: e.memset(INIT, 0.0), writes=[rW])
        P.barrier()
        A.off = base
        u2 = [A.alloc((8, T), F32) for _ in range(2)]
        ub2 = [A.alloc((8, T), BF16) for _ in range(2)]
        r_ld = [Res() for _ in range(2)]
        yb2 = [A.alloc((8, T), BF16) for _ in range(2)]; r_yb2 = [Res() for _ in range(2)]
        NB = 3
        bu = [A.alloc(3 * T, F32) for _ in range(NB)]; r_bu = [Res() for _ in range(NB)]
        t1 = [A.alloc(2 * T, F32) for _ in range(NB)]; r_t1 = [Res() for _ in range(NB)]
        t2 = [A.alloc(2 * T, F32) for _ in range(NB)]; r_t2 = [Res() for _ in range(NB)]
        G = [A.alloc(3 * T, F32) for _ in range(NB)]; r_G = [Res() for _ in range(NB)]
        t3 = [A.alloc(2 * T, F32) for _ in range(NB)]; r_t3 = [Res() for _ in range(NB)]
        t4 = [A.alloc(2 * T, F32) for _ in range(NB)]; r_t4 = [Res() for _ in range(NB)]
        hb = [A.alloc((4, 2, T), BF16) for _ in range(2)]; r_hb = [Res() for _ in range(2)]
        yv = [A.alloc(T, F32) for _ in range(2)]; r_yv = [Res() for _ in range(2)]
        r_GL = Res()
        r_INIT = Res()
        bslots = [self.full(0), self.full(1)]
        yslots = [self.full(4)]

        def b2(ap2):
            return ap2.unsqueeze(1).to_broadcast([128, 2, T])

        def v2(ap, off):
            return ap[:, off:off + 2 * T].rearrange("p (a b) -> p a b", a=2)

        def tile_chunks(i, u, ub, yb, r_u, r_ub, r_yb):
            def s1(k):
                kc = k // 4
                s = k % NB
                ps, rps = bslots[k % 2]
                P.op("pe", lambda e: e.matmul(ps[:, 0:T], Bre[:, k, :], ub[:, kc, :], start=True, stop=True),
                     reads=[rW, r_ub[kc]], writes=rps, inc=False)
                P.op("pe", lambda e: e.matmul(ps[:, T:2 * T], Bim[:, k, :], ub[:, kc, :], start=True, stop=True),
                     reads=[rW, r_ub[kc]], writes=rps)
                P.op("act", lambda e: e.activation(out=bu[s][:, 0:2 * T], in_=ps, func=AF.Copy), reads=rps, writes=[r_bu[s]])
                P.op("act", lambda e: e.activation(out=bu[s][:, 2 * T:3 * T], in_=ps[:, 0:T], func=AF.Copy, scale=-1.0), reads=rps, writes=[r_bu[s]])
                P.op("dve", lambda e: e.tensor_tensor(out=v2(t1[s], 0), in0=v2(bu[s], 0), in1=b2(cosT[:, k, :]), op=ALU.mult),
                     reads=[r_bu[s], rW], writes=[r_t1[s]])
                P.op("dve", lambda e: e.tensor_tensor(out=v2(t2[s], 0), in0=v2(bu[s], T), in1=b2(sinT[:, k, :]), op=ALU.mult),
                     reads=[r_bu[s], rW], writes=[r_t2[s]])
                P.op("dve", lambda e: e.tensor_tensor(out=t1[s], in0=t1[s], in1=t2[s], op=ALU.add),
                     reads=[r_t1[s], r_t2[s]], writes=[r_t1[s]])

            def s2a(k):
                kk = k % 4
                s = k % NB
                hs = (k // 4) % 2
                magb = MAG[:, k:k + 1].to_broadcast([128, T])
                P.op("dve", lambda e: e.tensor_tensor_scan(
                    out=G[s][:, T:2 * T], data0=magb, data1=t1[s][:, 0:T], initial=INIT[:, 0, k:k + 1], op0=ALU.mult, op1=ALU.add),
                    reads=[r_t1[s], rW, r_INIT], writes=[r_G[s]])
                P.op("dve", lambda e: e.tensor_tensor_scan(
                    out=G[s][:, 2 * T:3 * T], data0=magb, data1=t1[s][:, T:2 * T], initial=INIT[:, 1, k:k + 1], op0=ALU.mult, op1=ALU.add),
                    reads=[r_t1[s], rW, r_INIT], writes=[r_G[s]])
                P.op("act", lambda e: e.activation(out=G[s][:, 0:T], in_=G[s][:, 2 * T:3 * T], func=AF.Copy, scale=-1.0), reads=[r_G[s]], writes=[r_G[s]])
                P.op("act", lambda e: e.activation(out=GL[:, :, k], in_=v2(G[s], T)[:, :, T - 1], func=AF.Copy),
                     reads=[r_G[s]], writes=[r_GL])

            def s2b(k):
                s = k % NB
                P.op("dve", lambda e: e.tensor_tensor(out=v2(t3[s], 0), in0=v2(G[s], T), in1=b2(cosT[:, k, :]), op=ALU.mult),
                     reads=[r_G[s], rW], writes=[r_t3[s]])
                P.op("dve", lambda e: e.tensor_tensor(out=v2(t4[s], 0), in0=v2(G[s], 0), in1=b2(sinT[:, k, :]), op=ALU.mult),
                     reads=[r_G[s], rW], writes=[r_t4[s]])

            def s3(k):
                kk = k % 4
                s = k % NB
                hs = (k // 4) % 2
                P.op("dve", lambda e: e.tensor_tensor(out=hb[hs][:, kk, :, :], in0=v2(t3[s], 0), in1=v2(t4[s], 0), op=ALU.add),
                     reads=[r_t3[s], r_t4[s]], writes=[r_hb[hs]])
                if kk == 3:
                    mo = k // 4
                    psy, rpsy = yslots[0]
                    for k4 in range(4):
                        kq = mo * 4 + k4
                        P.op("pe", lambda e, kq=kq, k4=k4, hs=hs, psy=psy: e.matmul(
                            psy[:, 0:T], Cre[:, kq, :], hb[hs][:, k4, 0, :], start=(k4 == 0), stop=False),
                            reads=[rW, r_hb[hs]], writes=rpsy, inc=False)
                        P.op("pe", lambda e, kq=kq, k4=k4, hs=hs, psy=psy: e.matmul(
                            psy[:, 0:T], Cimn[:, kq, :], hb[hs][:, k4, 1, :], start=False, stop=(k4 == 3)),
                            reads=[rW, r_hb[hs]], writes=rpsy, inc=(k4 == 3))
                    yy = mo % 2
                    P.op("dve", lambda e, mo=mo, yy=yy, psy=psy: e.scalar_tensor_tensor(
                        out=yv[yy], in0=u[:, mo, :], scalar=Dv[:, mo:mo + 1], in1=psy[:, 0:T], op0=ALU.mult, op1=ALU.add),
                        reads=rpsy + [r_u[mo], rW], writes=[r_yv[yy]])
                    P.op("act", lambda e, mo=mo, yy=yy: e.activation(out=yb[:, mo, :], in_=yv[yy], func=AF.Gelu_apprx_tanh),
                         reads=[r_yv[yy]], writes=[r_yb])

            s1(0)
            s1(1)
            s2a(0)
            s2b(0)
            for k in range(32):
                if k + 2 < 32:
                    s1(k + 2)
                if k + 1 < 32:
                    s2a(k + 1)
                s3(k)
                if k + 1 < 32:
                    s2b(k + 1)
            P.op("dve", lambda e: e.tensor_tensor(out=INIT[:, 0, :], in0=RT[:, 0, :], in1=GL[:, 0, :], op=ALU.mult), reads=[r_GL, rW], writes=[r_INIT])
            P.op("dve", lambda e: e.tensor_tensor(out=INIT[:, 1, :], in0=RT[:, 1, :], in1=GL[:, 1, :], op=ALU.mult), reads=[r_GL, rW], writes=[r_INIT])
            P.op("dve", lambda e: e.tensor_tensor(out=INIT[:, 0, :], in0=INIT[:, 0, :], in1=INIT[:, 1, :], op=ALU.subtract), reads=[r_INIT], writes=[r_INIT])
            P.op("dve", lambda e: e.tensor_tensor(out=INIT[:, 1, :], in0=RT[:, 0, :], in1=GL[:, 1, :], op=ALU.mult), reads=[r_GL, rW], writes=[r_INIT])
            P.op("dve", lambda e: e.tensor_tensor(out=GL[:, 0, :], in0=RT[:, 1, :], in1=GL[:, 0, :], op=ALU.mult), reads=[r_GL, rW], writes=[r_GL])
            P.op("dve", lambda e: e.tensor_tensor(out=INIT[:, 1, :], in0=INIT[:, 1, :], in1=GL[:, 0, :], op=ALU.add), reads=[r_GL, r_INIT], writes=[r_INIT])
            P.dma("sp", sy.ap[:, :, i * T:(i + 1) * T], yb, reads=[r_yb])

        def load(i):
            b_ = i % 2
            P.dma("sp", u2[b_], su.ap[:, :, i * T:(i + 1) * T], writes=[r_ld[b_]])
            P.dma("sp", ub2[b_], sb.ap[:, :, i * T:(i + 1) * T], writes=[r_ld[b_]])

        load(0)
        for i in range(NT):
            if i + 1 < NT:
                load(i + 1)
            rl = r_ld[i % 2]
            tile_chunks(i, u2[i % 2], ub2[i % 2], yb2[i % 2], [rl] * 8, [rl] * 8, r_yb2[i % 2])
        P.barrier()

    def s5_bbar_quarter(self, qi, Q, tl, rq, rW, par_bc, bpad_re, bpad_im, Bre, Bim):
        P = self.P
        cs = slice(qi * Q, (qi + 1) * Q)
        lre, lim, lst, bre, bim, t0, t1, t2, t3, t4, t5, t6, t7, t8 = tl
        P.dma("sp", lre, par_bc[0:1, cs].partition_broadcast(128), writes=[rq])
        P.dma("sp", lim, par_bc[1:2, cs].partition_broadcast(128), writes=[rq])
        P.dma("sp", lst, par_bc[2:3, cs].partition_broadcast(128), writes=[rq])
        P.dma("sp", bre, bpad_re.rearrange("p a b -> p (a b)")[:, cs], writes=[rq])
        P.dma("sp", bim, bpad_im.rearrange("p a b -> p (a b)")[:, cs], writes=[rq])
        self.s5_disc(lre, lim, lst, t0, t1, t2, t3, t4, t5, t6, t7, t8, rq)
        bo_re = Bre.rearrange("p a b -> p (a b)")[:, cs]
        bo_im = Bim.rearrange("p a b -> p (a b)")[:, cs]
        P.op("dve", lambda e: e.tensor_tensor(out=t0, in0=t5, in1=bre, op=ALU.mult), reads=[rq], writes=[rq])
        P.op("dve", lambda e: e.tensor_tensor(out=t1, in0=t6, in1=bim, op=ALU.mult), reads=[rq], writes=[rq])
        P.op("dve", lambda e: e.tensor_tensor(out=bo_re, in0=t0, in1=t1, op=ALU.subtract), reads=[rq], writes=[rq, rW])
        P.op("dve", lambda e: e.tensor_tensor(out=t0, in0=t5, in1=bim, op=ALU.mult), reads=[rq], writes=[rq])
        P.op("dve", lambda e: e.tensor_tensor(out=t1, in0=t6, in1=bre, op=ALU.mult), reads=[rq], writes=[rq])
        P.op("dve", lambda e: e.tensor_tensor(out=bo_im, in0=t0, in1=t1, op=ALU.add), reads=[rq], writes=[rq, rW])

    def s5_disc(self, lre, lim, lst, t0, t1, t2, t3, t4, t5, t6, t7, t8, r, unit=False):
        P = self.P
        P.op("dve", lambda e: e.tensor_scalar(out=lre, in0=lre, scalar1=-1e-4, scalar2=None, op0=ALU.min), reads=[r], writes=[r])
        P.op("act", lambda e: e.activation(out=lst, in_=lst, func=AF.Exp), reads=[r], writes=[r])
        P.op("dve", lambda e: e.tensor_tensor(out=t0, in0=lre, in1=lst, op=ALU.mult), reads=[r], writes=[r])
        P.op("act", lambda e: e.activation(out=t0, in_=t0, func=AF.Exp), reads=[r], writes=[r])
        P.op("dve", lambda e: e.tensor_tensor(out=t1, in0=lim, in1=lst, op=ALU.mult), reads=[r], writes=[r])
        self.sincos(t1, t2, t3, t4, r)
        if unit:
            return
        P.op("dve", lambda e: e.tensor_tensor(out=t2, in0=t2, in1=t0, op=ALU.mult), reads=[r], writes=[r])
        P.op("dve", lambda e: e.tensor_tensor(out=t3, in0=t3, in1=t0, op=ALU.mult), reads=[r], writes=[r])
        P.op("dve", lambda e: e.tensor_scalar(out=t3, in0=t3, scalar1=-1.0, scalar2=None, op0=ALU.add), reads=[r], writes=[r])
        P.op("dve", lambda e: e.tensor_tensor(out=t4, in0=lre, in1=lre, op=ALU.mult), reads=[r], writes=[r])
        P.op("dve", lambda e: e.tensor_tensor(out=t7, in0=lim, in1=lim, op=ALU.mult), reads=[r], writes=[r])
        P.op("dve", lambda e: e.tensor_tensor(out=t4, in0=t4, in1=t7, op=ALU.add), reads=[r], writes=[r])
        P.op("dve", lambda e: e.reciprocal(out=t4, in_=t4), reads=[r], writes=[r])
        P.op("dve", lambda e: e.tensor_tensor(out=t5, in0=t3, in1=lre, op=ALU.mult), reads=[r], writes=[r])
        P.op("dve", lambda e: e.tensor_tensor(out=t7, in0=t2, in1=lim, op=ALU.mult), reads=[r], writes=[r])
        P.op("dve", lambda e: e.tensor_tensor(out=t5, in0=t5, in1=t7, op=ALU.add), reads=[r], writes=[r])
        P.op("dve", lambda e: e.tensor_tensor(out=t5, in0=t5, in1=t4, op=ALU.mult), reads=[r], writes=[r])
        P.op("dve", lambda e: e.tensor_tensor(out=t6, in0=t2, in1=lre, op=ALU.mult), reads=[r], writes=[r])
        P.op("dve", lambda e: e.tensor_tensor(out=t7, in0=t3, in1=lim, op=ALU.mult), reads=[r], writes=[r])
        P.op("dve", lambda e: e.tensor_tensor(out=t6, in0=t6, in1=t7, op=ALU.subtract), reads=[r], writes=[r])
        P.op("dve", lambda e: e.tensor_tensor(out=t6, in0=t6, in1=t4, op=ALU.mult), reads=[r], writes=[r])


    def ssd_a_phase(self, layer, src, w_in, cw, zs, xs, bt, ct, dtt):
        P, A = self.P, self.A
        T = 512
        NT = self.L // T
        A.off = self.persist_end
        rW = Res()
        Win = A.alloc((8, 6176), BF16)
        rWin = [Res() for _ in range(8)]
        for kc in range(8):
            P.dma("pool", Win[:, :, kc * 772:(kc + 1) * 772], w_in[:, :, kc * 772:(kc + 1) * 772], writes=[rWin[kc]])
        CW = A.alloc((5, 32), F32)
        P.dma("sp", CW, cw, writes=[rW])
        H = A.alloc((32, 4), F32)
        r_H = [Res() for _ in range(32)]
        P.op("pool", lambda e: e.memset(H, 0.0), writes=r_H)
        xts = [A.alloc((8, T), F32) for _ in range(2)]; r_xts = [Res() for _ in range(2)]
        sq = A.alloc((8, T), BF16); r_sq = Res()
        xn = A.alloc((8, T), BF16); r_xn = Res()
        ms = A.alloc(T, F32); r_ms = Res()
        rstd = A.alloc(T, F32); r_rstd = Res()
        dtr = A.alloc((4, 32), F32); r_dtr = Res()
        NU = 3
        ubuf = [A.alloc(T + 4, F32) for _ in range(NU)]; r_ub = [Res() for _ in range(NU)]
        uc = [A.alloc(T, F32) for _ in range(NU)]; r_uc = [Res() for _ in range(NU)]
        NSG = 3
        stg = [A.alloc((4, T), BF16) for _ in range(NSG)]; r_stg = [Res() for _ in range(NSG)]
        pslots = [self.full(b) for b in range(5)]
        dslot = self.full(5)
        nslot = self.full(6)
        cnt = {"p": 0, "u": 0, "g": 0}

        def load(i):
            P.dma("sp", xts[i % 2], src.ap[:, :, i * T:(i + 1) * T], reads=src.blocks(i * T, T), writes=[r_xts[i % 2]])

        def tile(i, xt, r_xt):
            tsl = slice(i * T, (i + 1) * T)
            P.op("act", lambda e: e.activation(out=sq.rearrange("p c t -> p (c t)"), in_=xt.rearrange("p c t -> p (c t)"), func=AF.Square),
                 reads=[r_xt], writes=[r_sq])
            self.rstd_from_sq(sq, r_sq, 8, T, nslot, ms, r_ms, rstd, r_rstd, float(D))
            for c in range(8):
                P.op("dve", lambda e, c=c: e.scalar_tensor_tensor(
                    out=xn[:, c, :], in0=xt[:, c, :], scalar=self.gcol(layer, 0, c), in1=rstd,
                    op0=ALU.mult, op1=ALU.mult), reads=[r_xt, r_rstd], writes=[r_xn])
            def flush(m, sg):
                m0 = m - 3
                if m < 16:
                    dd, c0 = zs, m0
                elif m < 32:
                    dd, c0 = xs, m0 - 16
                elif m < 40:
                    dd, c0 = bt, m0 - 32
                else:
                    dd, c0 = ct, m0 - 40
                P.dma("sp", dd.ap[:, c0:c0 + 4, tsl], stg[sg], reads=[r_stg[sg]])

            def fin(m, sg, q4, ub_i):
                P.op("act", lambda e: e.activation(out=stg[sg][:, q4, :], in_=uc[ub_i], func=AF.Silu),
                     reads=[r_uc[ub_i]], writes=[r_stg[sg]])
                if q4 == 3:
                    flush(m, sg)

            def chunk_m(m, sg, q4):
                ps, rps = pslots[cnt["p"] % len(pslots)]
                cnt["p"] += 1
                for kc in range(8):
                    P.op("pe", lambda e, kc=kc: e.matmul(
                        ps, Win[:, kc, m * 128:(m + 1) * 128], xn[:, kc, :], start=(kc == 0), stop=(kc == 7)),
                        reads=[rWin[(m * 128) // 772], rWin[(m * 128 + 127) // 772], r_xn], writes=rps, inc=(kc == 7))
                if m < 16:
                    P.op("act", lambda e: e.activation(out=stg[sg][:, q4, :], in_=ps, func=AF.Silu),
                         reads=rps, writes=[r_stg[sg]])
                    if q4 == 3:
                        flush(m, sg)
                    return None
                cc = m - 16
                ub_i = cnt["u"] % NU
                cnt["u"] += 1
                ub_, uc_ = ubuf[ub_i], uc[ub_i]
                P.op("pool", lambda e: e.tensor_copy(out=ub_[:, 1:4], in_=H[:, cc, 0:3]),
                     reads=[r_H[cc]], writes=[r_ub[ub_i]])
                P.op("act", lambda e: e.activation(out=ub_[:, 4:4 + T], in_=ps, func=AF.Copy),
                     reads=rps, writes=[r_ub[ub_i]])
                P.op("act", lambda e: e.activation(out=uc_, in_=ps, func=AF.Identity,
                                                   scale=CW[:, 3, cc:cc + 1], bias=CW[:, 4, cc:cc + 1]),
                     reads=rps + [rW], writes=[r_uc[ub_i]])
                for k in range(3):
                    P.op("dve", lambda e, k=k: e.scalar_tensor_tensor(
                        out=uc_, in0=ub_[:, 1 + k:1 + k + T], scalar=CW[:, k, cc:cc + 1], in1=uc_,
                        op0=ALU.mult, op1=ALU.add), reads=[r_ub[ub_i], r_uc[ub_i], rW], writes=[r_uc[ub_i]])
                P.op("pool", lambda e: e.tensor_copy(out=H[:, cc, 0:3], in_=ub_[:, T + 1:T + 4]),
                     reads=[r_ub[ub_i]], writes=[r_H[cc]])
                return (m, sg, q4, ub_i)

            pend = None
            for m in range(48):
                sg = cnt["g"] % NSG
                q4 = m % 4
                if q4 == 3:
                    cnt["g"] += 1
                nxt = chunk_m(m, sg, q4)
                if pend is not None:
                    fin(*pend)
                pend = nxt
            if pend is not None:
                fin(*pend)
            psd, rpsd = dslot
            for j in range(4):
                for kc in range(8):
                    P.op("pe", lambda e, j=j, kc=kc: e.matmul(
                        psd[:, j * 32:(j + 1) * 32], xn[:, kc, j * 128:(j + 1) * 128], Win[:, kc, 6144:6176],
                        start=(kc == 0), stop=(kc == 7)), reads=[rWin[7], r_xn], writes=rpsd, inc=(kc == 7 and j == 3))
            P.op("act", lambda e: e.activation(out=dtr.rearrange("p a b -> p (a b)"), in_=psd[:, 0:128], func=AF.Copy),
                 reads=rpsd, writes=[r_dtr])
            P.dma("sp", dtt.ap[i * 4:(i + 1) * 4].rearrange("j p h -> p j h"), dtr, reads=[r_dtr])
        load(0)
        for i in range(NT):
            if i + 1 < NT:
                load(i + 1)
            tile(i, xts[i % 2], r_xts[i % 2])
        P.barrier()


    def ssd_b_phase(self, layer, src, dst, zs, xs, bt, ct, dtt, w_out, consts, hp_bc, dch, ngv):
        P, A = self.P, self.A
        T = 256
        NT = self.L // T
        A.off = self.persist_end
        rW = Res()
        Wout = A.alloc((16, 1024), BF16)
        for hh in range(2):
            P.dma("pool", Wout[:, hh * 8:(hh + 1) * 8, :], w_out[:, hh * 8:(hh + 1) * 8, :], writes=[rW])
        CF = A.alloc((4, 128), F32)
        P.dma("sp", CF, consts, writes=[rW])
        IDF, U, MS, ONES = CF[:, 0, :], CF[:, 1, :], CF[:, 2, :], CF[:, 3, :]
        identb = A.alloc(128, BF16)
        P.op("dve", lambda e: e.tensor_copy(out=identb, in_=IDF), reads=[rW], writes=[rW])
        Ub = A.alloc(128, BF16)
        MSb = A.alloc(128, BF16)
        ONESb = A.alloc(128, BF16)
        P.op("dve", lambda e: e.tensor_copy(out=Ub, in_=U), reads=[rW], writes=[rW])
        P.op("dve", lambda e: e.tensor_copy(out=MSb, in_=MS), reads=[rW], writes=[rW])
        P.op("dve", lambda e: e.tensor_copy(out=ONESb, in_=ONES), reads=[rW], writes=[rW])

        HB = A.alloc((2, 32), F32)
        P.dma("sp", HB[:, 0, :], hp_bc[0:1, :].partition_broadcast(128), writes=[rW])
        P.dma("sp", HB[:, 1, :], hp_bc[1:2, :].partition_broadcast(128), writes=[rW])
        P.op("act", lambda e: e.activation(out=HB[:, 1, :], in_=HB[:, 1, :], func=AF.Exp), reads=[rW], writes=[rW])
        P.op("dve", lambda e: e.tensor_scalar(out=HB[:, 1, :], in0=HB[:, 1, :], scalar1=-1.0, scalar2=None, op0=ALU.mult), reads=[rW], writes=[rW])
        DBIAS, ANEG = HB[:, 0, :], HB[:, 1, :]
        DCH = A.alloc(16, F32)
        NG = A.alloc(16, F32)
        P.dma("sp", DCH, dch, writes=[rW])
        P.dma("sp", NG, ngv, writes=[rW])
        diagD = A.alloc((16, 128), BF16)
        for m in range(16):
            P.op("dve", lambda e, m=m: e.tensor_scalar(out=diagD[:, m, :], in0=IDF, scalar1=DCH[:, m:m + 1], scalar2=None, op0=ALU.mult),
                 reads=[rW], writes=[rW])
        ent = A.alloc(2048, F32); r_ent = [Res() for _ in range(8)]
        entb = A.alloc(2048, BF16); r_entb = [Res() for _ in range(8)]
        P.op("pool", lambda e: e.memset(ent, 0.0), writes=r_ent)
        P.op("pool", lambda e: e.memset(entb, 0.0), writes=r_entb)
        zs_t = [A.alloc((16, T), BF16) for _ in range(2)]
        xs_t = [A.alloc((16, T), BF16) for _ in range(2)]
        bt_t = [A.alloc((8, T), BF16) for _ in range(2)]
        ct_t = [A.alloc((8, T), BF16) for _ in range(2)]
        dtr_t = [A.alloc((2, 32), F32) for _ in range(2)]
        xt = [A.alloc((8, T), F32) for _ in range(2)]
        r_ld = [Res() for _ in range(2)]
        r_xt = [Res() for _ in range(2)]
        smset = []
        for _ in range(2):
            t = [A.alloc(32, F32) for _ in range(8)]
            t += [A.alloc(32, BF16), A.alloc(32, BF16), A.alloc(32, F32)]
            smset.append(t)
        r_smset = [Res(), Res()]
        xdt2 = [A.alloc(2048, BF16) for _ in range(2)]; r_xdt2 = [Res(), Res()]
        xdtd2 = [A.alloc(2048, BF16) for _ in range(2)]; r_xdtd2 = [Res(), Res()]
        btok2 = [A.alloc(1024, BF16) for _ in range(2)]; r_btok2 = [Res(), Res()]
        NR = 2
        Rg = [A.alloc((4, 128), BF16) for _ in range(NR)]; r_Rg = [Res() for _ in range(NR)]
        Rl = [A.alloc((4, 128), BF16) for _ in range(NR)]
        Lx = [A.alloc((4, 128), F32) for _ in range(NR)]; r_Lx = [Res() for _ in range(NR)]
        Ex = [A.alloc((4, 128), F32) for _ in range(NR)]; r_Ex = [Res() for _ in range(NR)]
        cbm = [A.alloc(128, F32) for _ in range(NR)]; r_cbm = [Res() for _ in range(NR)]
        Mg = [A.alloc((4, 128), BF16) for _ in range(NR)]; r_Mg = [Res() for _ in range(NR)]
        Cd = [A.alloc((4, 128), BF16) for _ in range(NR)]; r_Cd = [Res() for _ in range(NR)]
        v = A.alloc((16, T), F32); r_v = [Res() for _ in range(8)]
        sqv = A.alloc((16, T), BF16); r_sqv = Res()
        vn = A.alloc((16, T), BF16); r_vn = [Res() for _ in range(8)]
        msg = A.alloc(T, F32); r_msg = Res()
        rsg = [A.alloc(T, F32) for _ in range(2)]; r_rsg = [Res() for _ in range(2)]
        hout = A.alloc((8, T), F32); r_hout = [Res() for _ in range(8)]
        sq2 = A.alloc((8, T), BF16); r_sq2 = Res()
        ms = A.alloc(T, F32); r_ms = Res()
        rstd = A.alloc(T, F32); r_rstd = Res()
        tslots = [self.full(0), self.full(1)]
        dtslot = self.full(2)
        lslot = self.full(3)
        aslot = self.full(4)
        yslot = self.full(5)
        sslot = self.full(6)
        cslot = self.full(7)
        cnt = {"t": 0, "r": 0}

        def load(i):
            b = i % 2
            tsl = slice(i * T, (i + 1) * T)
            P.dma("sp", zs_t[b], zs.ap[:, :, tsl], writes=[r_ld[b]])
            P.dma("sp", xs_t[b], xs.ap[:, :, tsl], writes=[r_ld[b]])
            P.dma("sp", bt_t[b], bt.ap[:, :, tsl], writes=[r_ld[b]])
            P.dma("sp", ct_t[b], ct.ap[:, :, tsl], writes=[r_ld[b]])
            P.dma("sp", dtr_t[b], dtt.ap[i * 2:(i + 1) * 2].rearrange("j p h -> p j h"), writes=[r_ld[b]])
            P.dma("sp", xt[b], src.ap[:, :, tsl], reads=src.blocks(i * T, T), writes=[r_xt[b]])

        def bc4(ap2):
            return ap2.unsqueeze(1).to_broadcast([128, 4, 128])

        def pre_a(i, c, pbi):
            b = i % 2
            cols = slice(c * 128, (c + 1) * 128)
            rl = [r_ld[b]]
            dv, av, dtv, dA, dsv, eav, dtds, lv, dAh, dAl, dAt = smset[pbi]
            r_sm = r_smset[pbi]
            xdt, xdtd, btok = xdt2[pbi], xdtd2[pbi], btok2[pbi]
            r_xdt, r_xdtd, r_btok = r_xdt2[pbi], r_xdtd2[pbi], r_btok2[pbi]
            P.op("dve", lambda e: e.tensor_tensor(out=dv, in0=dtr_t[b][:, c, :], in1=DBIAS, op=ALU.add), reads=rl + [rW], writes=[r_sm])
            P.op("dve", lambda e: e.scalar_tensor_tensor(out=av, in0=dv, scalar=-1.0, in1=dv, op0=ALU.mult, op1=ALU.max), reads=[r_sm], writes=[r_sm])
            P.op("act", lambda e: e.activation(out=av, in_=av, func=AF.Exp, scale=-1.0), reads=[r_sm], writes=[r_sm])
            P.op("act", lambda e: e.activation(out=lv, in_=av, func=AF.Ln, bias=1.0), reads=[r_sm], writes=[r_sm])
            P.op("dve", lambda e: e.scalar_tensor_tensor(out=dtv, in0=dv, scalar=0.0, in1=lv, op0=ALU.max, op1=ALU.add), reads=[r_sm], writes=[r_sm])
            P.op("dve", lambda e: e.tensor_tensor(out=dA, in0=dtv, in1=ANEG, op=ALU.mult), reads=[r_sm, rW], writes=[r_sm])
            P.op("dve", lambda e: e.tensor_copy(out=dAh, in_=dA), reads=[r_sm], writes=[r_sm])
            P.op("dve", lambda e: e.tensor_tensor(out=dAt, in0=dA, in1=dAh, op=ALU.subtract), reads=[r_sm], writes=[r_sm])
            P.op("dve", lambda e: e.tensor_copy(out=dAl, in_=dAt), reads=[r_sm], writes=[r_sm])
            psd, rpsd = dtslot
            P.op("pe", lambda e: e.matmul(psd[:, 0:32], MS, dA, start=True, stop=True), reads=[rW, r_sm], writes=rpsd, inc=False)
            P.op("pe", lambda e: e.matmul(psd[:, 32:64], ONES, dA, start=True, stop=True), reads=[rW, r_sm], writes=rpsd)
            P.op("act", lambda e: e.activation(out=dsv, in_=psd[:, 0:32], func=AF.Exp), reads=rpsd, writes=[r_sm])
            P.op("act", lambda e: e.activation(out=eav, in_=psd[:, 32:64], func=AF.Exp), reads=rpsd, writes=[r_sm])
            P.op("dve", lambda e: e.tensor_tensor(out=dtds, in0=dtv, in1=dsv, op=ALU.mult), reads=[r_sm], writes=[r_sm])

        def pre_b(i, c, pbi):
            b = i % 2
            cols = slice(c * 128, (c + 1) * 128)
            rl = [r_ld[b]]
            dv, av, dtv, dA, dsv, eav, dtds, lv, dAh, dAl, dAt = smset[pbi]
            r_sm = r_smset[pbi]
            xdt, xdtd, btok = xdt2[pbi], xdtd2[pbi], btok2[pbi]
            r_xdt, r_xdtd, r_btok = r_xdt2[pbi], r_xdtd2[pbi], r_btok2[pbi]
            for half in range(2):
                pst, rpst = tslots[cnt["t"] % 2]
                cnt["t"] += 1
                pb = pst.bitcast(BF16)
                for q in range(8):
                    m = half * 8 + q
                    P.op("pe", lambda e, m=m, q=q, pb=pb: e.transpose(pb[:, q * 128:(q + 1) * 128], xs_t[b][:, m, cols], identb),
                         reads=rl + [rW], writes=rpst, inc=(q == 7))
                pv = pb.rearrange("p (h d) -> p h d", h=16)
                hs = slice(half * 16, (half + 1) * 16)
                o1 = xdt[:, half * 1024:(half + 1) * 1024].rearrange("p (h d) -> p h d", h=16)
                o2 = xdtd[:, half * 1024:(half + 1) * 1024].rearrange("p (h d) -> p h d", h=16)
                P.op("dve", lambda e, pv=pv, o1=o1, hs=hs: e.tensor_tensor(out=o1, in0=pv, in1=dtv[:, hs].unsqueeze(2).to_broadcast([128, 16, 64]), op=ALU.mult),
                     reads=rpst + [r_sm], writes=[r_xdt])
                P.op("dve", lambda e, pv=pv, o2=o2, hs=hs: e.tensor_tensor(out=o2, in0=pv, in1=dtds[:, hs].unsqueeze(2).to_broadcast([128, 16, 64]), op=ALU.mult),
                     reads=rpst + [r_sm], writes=[r_xdtd])

        def pre_c(i, c, pbi):
            b = i % 2
            cols = slice(c * 128, (c + 1) * 128)
            rl = [r_ld[b]]
            dv, av, dtv, dA, dsv, eav, dtds, lv, dAh, dAl, dAt = smset[pbi]
            r_sm = r_smset[pbi]
            xdt, xdtd, btok = xdt2[pbi], xdtd2[pbi], btok2[pbi]
            r_xdt, r_xdtd, r_btok = r_xdt2[pbi], r_xdtd2[pbi], r_btok2[pbi]
            pst, rpst = tslots[cnt["t"] % 2]
            cnt["t"] += 1
            pb = pst.bitcast(BF16)
            for g in range(8):
                P.op("pe", lambda e, g=g, pb=pb: e.transpose(pb[:, g * 128:(g + 1) * 128], bt_t[b][:, g, cols], identb),
                     reads=rl + [rW], writes=rpst, inc=(g == 7))
            P.op("act", lambda e, pb=pb: e.activation(out=btok, in_=pb, func=AF.Copy), reads=rpst, writes=[r_btok])

        def groups(i, c, pbi, hooks):
            b = i % 2
            cols = slice(c * 128, (c + 1) * 128)
            rl = [r_ld[b]]
            dv, av, dtv, dA, dsv, eav, dtds, lv, dAh, dAl, dAt = smset[pbi]
            r_sm = r_smset[pbi]
            xdt, xdtd, btok = xdt2[pbi], xdtd2[pbi], btok2[pbi]
            r_xdt, r_xdtd, r_btok = r_xdt2[pbi], r_xdtd2[pbi], r_btok2[pbi]
            def g1(g):
                rr = g % NR
                Rv = Rg[rr].rearrange("p a b -> p (a b)")
                Rlv = Rl[rr].rearrange("p a b -> p (a b)")
                P.op("dve", lambda e, g=g, rr=rr: e.tensor_tensor(out=Rg[rr], in0=bc4(Ub), in1=dAh[:, 4 * g:4 * g + 4].unsqueeze(2).to_broadcast([128, 4, 128]), op=ALU.mult),
                     reads=[rW, r_sm], writes=[r_Rg[rr]])
                P.op("dve", lambda e, g=g, rr=rr: e.tensor_tensor(out=Rl[rr], in0=bc4(Ub), in1=dAl[:, 4 * g:4 * g + 4].unsqueeze(2).to_broadcast([128, 4, 128]), op=ALU.mult),
                     reads=[rW, r_sm], writes=[r_Rg[rr]])
                psl, rpsl = lslot
                psa, rpsa = aslot
                P.op("pe", lambda e, Rv=Rv: e.matmul(psl, MSb, Rv, start=True, stop=False), reads=[rW, r_Rg[rr]], writes=rpsl, inc=False)
                P.op("pe", lambda e, Rlv=Rlv: e.matmul(psl, MSb, Rlv, start=False, stop=True), reads=[rW, r_Rg[rr]], writes=rpsl)
                P.op("pe", lambda e, Rv=Rv: e.matmul(psa, ONESb, Rv, start=True, stop=False), reads=[rW, r_Rg[rr]], writes=rpsa, inc=False)
                P.op("pe", lambda e, Rlv=Rlv: e.matmul(psa, ONESb, Rlv, start=False, stop=True), reads=[rW, r_Rg[rr]], writes=rpsa)
                P.op("act", lambda e, rr=rr: e.activation(out=Lx[rr].rearrange("p a b -> p (a b)"), in_=psl, func=AF.Exp), reads=rpsl, writes=[r_Lx[rr]])
                P.op("act", lambda e, rr=rr: e.activation(out=Ex[rr].rearrange("p a b -> p (a b)"), in_=psa, func=AF.Exp), reads=rpsa, writes=[r_Ex[rr]])
                psc, rpsc = cslot
                P.op("pe", lambda e, g=g: e.matmul(psc[:, 0:128], bt_t[b][:, g, cols], ct_t[b][:, g, cols], start=True, stop=True),
                     reads=rl, writes=rpsc)
                P.op("dve", lambda e, rr=rr: e.tensor_tensor(out=cbm[rr], in0=psc[:, 0:128], in1=U, op=ALU.mult), reads=rpsc + [rW], writes=[r_cbm[rr]])
                P.op("dve", lambda e, rr=rr: e.tensor_tensor(out=Mg[rr], in0=Lx[rr], in1=bc4(cbm[rr]), op=ALU.mult),
                     reads=[r_Lx[rr], r_cbm[rr]], writes=[r_Mg[rr]])
                P.op("dve", lambda e, rr=rr, g=g: e.tensor_tensor(out=Cd[rr], in0=Ex[rr], in1=bc4(ct_t[b][:, g, cols]), op=ALU.mult),
                     reads=[r_Ex[rr]] + rl, writes=[r_Cd[rr]])

            def g2(g):
                rr = g % NR
                psy, rpsy = yslot
                for mm in range(2):
                    m = 2 * g + mm
                    mc = slice(mm * 128, (mm + 1) * 128)
                    P.op("pe", lambda e, m=m, mc=mc: e.matmul(psy[:, mc], diagD[:, m, :], xs_t[b][:, m, cols], start=True, stop=False),
                         reads=rl + [rW], writes=rpsy, inc=False)
                    for hh in range(2):
                        h = 2 * m + hh
                        hc = slice(h * 64, (h + 1) * 64)
                        pr = slice(hh * 64, (hh + 1) * 64)
                        P.op("pe", lambda e, mc=mc, hc=hc, pr=pr, rr=rr, mm=mm, hh=hh: e.matmul(
                            psy[pr, mc], xdt[:, hc], Mg[rr][:, 2 * mm + hh, :], start=False, stop=False),
                            reads=[r_xdt, r_Mg[rr]], writes=rpsy, inc=False)
                        P.op("pe", lambda e, mc=mc, hc=hc, pr=pr, rr=rr, mm=mm, hh=hh: e.matmul(
                            psy[pr, mc], entb[:, hc], Cd[rr][:, 2 * mm + hh, :], start=False, stop=(hh == 1)),
                            reads=[r_entb[g], r_Cd[rr]], writes=rpsy, inc=(hh == 1 and mm == 1))
                P.op("dve", lambda e, g=g: e.tensor_tensor(out=v[:, 2 * g:2 * g + 2, cols], in0=psy[:, 0:256].rearrange("p (a b) -> p a b", a=2),
                                                           in1=zs_t[b][:, 2 * g:2 * g + 2, cols], op=ALU.mult),
                     reads=rpsy + rl, writes=[r_v[g]])
                pss, rpss = sslot
                gs = slice(g * 256, (g + 1) * 256)
                P.op("pe", lambda e, g=g, gs=gs: e.matmul(pss[:, 0:256], btok[:, g * 128:(g + 1) * 128], xdtd[:, gs], start=True, stop=True),
                     reads=[r_btok, r_xdtd], writes=rpss)
                ev = ent[:, gs].rearrange("p (h d) -> p h d", h=4)
                P.op("dve", lambda e, g=g, ev=ev: e.tensor_tensor(out=ev, in0=ev, in1=eav[:, 4 * g:4 * g + 4].unsqueeze(2).to_broadcast([128, 4, 64]), op=ALU.mult),
                     reads=[r_ent[g], r_sm], writes=[r_ent[g]])
                P.op("dve", lambda e, gs=gs: e.tensor_tensor(out=ent[:, gs], in0=ent[:, gs], in1=pss[:, 0:256], op=ALU.add),
                     reads=[r_ent[g]] + rpss, writes=[r_ent[g]])
                P.op("act", lambda e, gs=gs: e.activation(out=entb[:, gs], in_=ent[:, gs], func=AF.Copy), reads=[r_ent[g]], writes=[r_entb[g]])

            g1(0)
            for g in range(8):
                if g + 1 < 8:
                    g1(g + 1)
                g2(g)
                if g in hooks:
                    hooks[g]()

        def tail(i):
            b = i % 2
            P.op("act", lambda e: e.activation(out=sqv.rearrange("p c t -> p (c t)"), in_=v.rearrange("p c t -> p (c t)"), func=AF.Square),
                 reads=r_v, writes=[r_sqv])
            for g in range(8):
                k2 = g % 2
                self.rstd_from_sq(sqv[:, 2 * g:2 * g + 2, :], r_sqv, 2, T, dtslot, msg, r_msg, rsg[k2], r_rsg[k2], 256.0)
                for mm in range(2):
                    m = 2 * g + mm
                    P.op("dve", lambda e, m=m, k2=k2: e.scalar_tensor_tensor(
                        out=vn[:, m, :], in0=v[:, m, :], scalar=NG[:, m:m + 1], in1=rsg[k2], op0=ALU.mult, op1=ALU.mult),
                        reads=[r_v[g], r_rsg[k2], rW], writes=[r_vn[g]])
            for m2 in range(4):
                ps, rps = tslots[cnt["t"] % 2]
                cnt["t"] += 1
                for hh in range(2):
                    mo = 2 * m2 + hh
                    for kc in range(16):
                        P.op("pe", lambda e, kc=kc, mo=mo, hh=hh, ps=ps: e.matmul(
                            ps[:, hh * T:(hh + 1) * T], Wout[:, kc, mo * 128:(mo + 1) * 128], vn[:, kc, :],
                            start=(kc == 0), stop=(kc == 15)), reads=[rW, r_vn[kc // 2]], writes=rps, inc=(kc == 15 and hh == 1))
                ho = hout[:, 2 * m2:2 * m2 + 2, :].rearrange("p c t -> p (c t)")
                so = sq2[:, 2 * m2:2 * m2 + 2, :].rearrange("p c t -> p (c t)")
                P.op("act", lambda e, ho=ho, ps=ps: e.activation(out=ho, in_=ps, func=AF.Copy), reads=rps, writes=r_hout[2 * m2:2 * m2 + 2])
                P.op("act", lambda e, so=so, ps=ps: e.activation(out=so, in_=ps, func=AF.Square), reads=rps, writes=[r_sq2])
            self.post_norm_residual(layer, 1, hout, r_hout, sq2, r_sq2, xt[b], r_xt[b], T, cslot, ms, r_ms, rstd, r_rstd)
            P.dma("sp", dst.ap[:, :, i * T:(i + 1) * T], xt[b], reads=[r_xt[b]], writes=dst.blocks(i * T, T))

        chunks = [(i, c) for i in range(NT) for c in range(2)]
        load(0)
        pre_a(0, 0, 0)
        pre_b(0, 0, 0)
        pre_c(0, 0, 0)
        for n, (i, c) in enumerate(chunks):
            if c == 0 and i + 1 < NT:
                load(i + 1)
            hooks = {}
            if n + 1 < len(chunks):
                ni, ncc = chunks[n + 1]
                nb = (n + 1) % 2
                hooks[1] = (lambda ni=ni, ncc=ncc, nb=nb: pre_a(ni, ncc, nb))
                hooks[3] = (lambda ni=ni, ncc=ncc, nb=nb: pre_b(ni, ncc, nb))
                hooks[5] = (lambda ni=ni, ncc=ncc, nb=nb: pre_c(ni, ncc, nb))
            groups(i, c, n % 2, hooks)
            if c == 1:
                tail(i)
        P.barrier()


def x_to_dev(xb):
    L = xb.shape[0]
    return np.ascontiguousarray(xb.T.reshape(8, 128, L).transpose(1, 0, 2))


def x_from_dev(xd):
    L = xd.shape[2]
    return np.ascontiguousarray(xd.transpose(1, 0, 2).reshape(1024, L).T)


def lay_norm_g(norm_g):
    return np.ascontiguousarray(norm_g.reshape(4, 4, 8, 128).transpose(3, 0, 1, 2).reshape(128, 128))


def lay_rows(w):
    K, N = w.shape
    return np.ascontiguousarray(w.reshape(K // 128, 128, N).transpose(1, 0, 2))


def lay_vec(v):
    return np.ascontiguousarray(v.reshape(8, 128).T)


def lay_s5(lam_re, lam_im, log_step, b_re, b_im, c_re, c_im, d):
    f = np.float32
    par = np.zeros((128, 3, 32), f)
    l4 = lam_re.reshape(32, 2, 64)
    par[:, 0, :] = l4.transpose(1, 2, 0).reshape(128, 32)
    par[:, 1, :] = lam_im.reshape(32, 2, 64).transpose(1, 2, 0).reshape(128, 32)
    par[:, 2, :] = np.repeat(log_step.reshape(32, 2, 1), 64, axis=2).transpose(1, 2, 0).reshape(128, 32)
    par_bc = np.stack([lam_re.reshape(4096), lam_im.reshape(4096), np.repeat(log_step, 64)]).astype(f)

    def bpad(b):
        out = np.zeros((8, 16, 32, 2, 64), f)
        bb = b.reshape(8, 4, 2, 64, 16)
        for kc in range(8):
            for j in range(4):
                g8 = j * 2
                for gg in range(2):
                    out[g8 + gg, :, kc * 4 + j, gg, :] = bb[kc, j, gg].T
        return np.ascontiguousarray(out.reshape(128, 32, 128))

    def cpad(c):
        out = np.zeros((2, 64, 32, 8, 16), f)
        for g in range(64):
            out[g % 2, :, g // 2, g % 8, :] = c[g].T
        return np.ascontiguousarray(out.reshape(128, 32, 128))

    return {"par": par, "par_bc": np.ascontiguousarray(par_bc), "bpad_re": bpad(b_re), "bpad_im": bpad(b_im),
            "cpad_re": cpad(c_re), "cpad_im": cpad(c_im), "dvec": lay_vec(d).astype(f)}


def ssd_consts():
    j = np.arange(128)
    ident = np.eye(128, dtype=np.float32)
    U = (j[:, None] <= j[None, :]).astype(np.float32)
    MS = (j[:, None] > j[None, :]).astype(np.float32)
    ones = np.ones((128, 128), np.float32)
    return np.ascontiguousarray(np.stack([ident, U, MS, ones], axis=1))


def lay_ssd(conv_w, conv_b, dt_bias, a_log, d_skip, norm_g):
    f = np.float32
    cw = np.zeros((128, 5, 32), f)
    for k in range(4):
        cw[:, k, :] = conv_w[k].reshape(32, 128).T
    cw[:, 4, :] = conv_b.reshape(32, 128).T
    hp = np.stack([dt_bias, a_log]).astype(f)
    dch = np.repeat(d_skip, 64).reshape(16, 128).T
    ng = norm_g.reshape(16, 128).T
    return {"cw": np.ascontiguousarray(cw), "hp_bc": np.ascontiguousarray(hp),
            "dch": np.ascontiguousarray(dch).astype(f), "ngv": np.ascontiguousarray(ng).astype(f)}


def lay_rg_vec(conv_w, conv_b, b_a, b_x, lam):
    vs = [conv_w[0], conv_w[1], conv_w[2], conv_w[3], conv_b, b_a, b_x, lam]
    return np.ascontiguousarray(np.stack([lay_vec(v) for v in vs], axis=1)).astype(np.float32)


def add_layer(B, layer, src, mid, dst):
    L = B.L
    kind = layer % 3
    pre = "l%d_" % layer
    if kind == 0:
        w_in = B.inp(pre + "w_in", [128, 8, 2048])
        w_a = B.inp(pre + "w_a", [16, 64, 64])
        w_x = B.inp(pre + "w_x", [16, 64, 64])
        vec = B.inp(pre + "vec", [128, 8, 8])
        w_out = B.inp(pre + "w_out", [128, 8, 1024])
        B.rglru_phase(layer, src, mid, w_in, w_a, w_x, vec, w_out)
    elif kind == 1:
        w_in = B.inp(pre + "w_in", [128, 8, 6176])
        w_out = B.inp(pre + "w_out", [128, 16, 1024])
        cw = B.inp(pre + "cw", [128, 5, 32])
        hp_bc = B.inp(pre + "hp_bc", [2, 32])
        dch = B.inp(pre + "dch", [128, 16])
        ngv = B.inp(pre + "ngv", [128, 16])
        consts = B.inp(pre + "consts", [128, 4, 128])
        zs = DramX(B.scratch(pre + "zs", [128, 16, L], BF16), L)
        xs = DramX(B.scratch(pre + "xs", [128, 16, L], BF16), L)
        bt = DramX(B.scratch(pre + "bt", [128, 8, L], BF16), L)
        ct = DramX(B.scratch(pre + "ct", [128, 8, L], BF16), L)
        dtt = DramX(B.scratch(pre + "dtt", [L // 128, 128, 32], F32), L)
        B.ssd_a_phase(layer, src, w_in, cw, zs, xs, bt, ct, dtt)
        B.ssd_b_phase(layer, src, mid, zs, xs, bt, ct, dtt, w_out, consts, hp_bc, dch, ngv)
    else:
        w_in = B.inp(pre + "w_in", [128, 8, 1024])
        w_out = B.inp(pre + "w_out", [128, 8, 2048])
        par = B.inp(pre + "par", [128, 3, 32])
        par_bc = B.inp(pre + "par_bc", [3, 4096])
        bre = B.inp(pre + "bpad_re", [128, 32, 128])
        bim = B.inp(pre + "bpad_im", [128, 32, 128])
        cre = B.inp(pre + "cpad_re", [128, 32, 128])
        cim = B.inp(pre + "cpad_im", [128, 32, 128])
        dvec = B.inp(pre + "dvec", [128, 8])
        su = DramX(B.scratch(pre + "su", [128, 8, L]), L)
        sb = DramX(B.scratch(pre + "sb", [128, 8, L], BF16), L)
        sy = DramX(B.scratch(pre + "sy", [128, 8, L], BF16), L)
        B.s5a_phase(layer, src, w_in, su, sb)
        B.s5b_phase(layer, su, sb, sy, par, par_bc, bre, bim, cre, cim, dvec)
        B.s5c_phase(layer, src, mid, w_out, sy)
    if ONLY_MIXER:
        return
    w1 = B.inp(pre + "w1", [128, 8, 4096])
    w2 = B.inp(pre + "w2", [128, 32, 1024])
    B.mlp_phase(layer, mid, dst, w1, w2)


def layer_params(layer, inp):
    kind = layer % 3
    j = layer // 3
    pre = "l%d_" % layer
    d = {}
    f = np.float32
    if kind == 0:
        d["w_in"] = lay_rows(inp["rg_w_in"][j])
        d["w_a"] = np.ascontiguousarray(inp["rg_w_a"][j])
        d["w_x"] = np.ascontiguousarray(inp["rg_w_x"][j])
        d["vec"] = lay_rg_vec(inp["rg_conv_w"][j], inp["rg_conv_b"][j], inp["rg_b_a"][j], inp["rg_b_x"][j], inp["rg_lam"][j])
        d["w_out"] = lay_rows(inp["rg_w_out"][j])
    elif kind == 1:
        d["w_in"] = lay_rows(inp["ssd_w_in"][j])
        d["w_out"] = lay_rows(inp["ssd_w_out"][j])
        d.update(lay_ssd(inp["ssd_conv_w"][j], inp["ssd_conv_b"][j], inp["ssd_dt_bias"][j], inp["ssd_a_log"][j],
                         inp["ssd_d"][j], inp["ssd_norm_g"][j]))
        d["consts"] = ssd_consts()
    else:
        d["w_in"] = lay_rows(inp["s5_w_in"][j])
        d["w_out"] = lay_rows(inp["s5_w_out"][j])
        d.update(lay_s5(inp["s5_lam_re"][j], inp["s5_lam_im"][j], inp["s5_log_step"][j], inp["s5_b_re"][j], inp["s5_b_im"][j],
                        inp["s5_c_re"][j], inp["s5_c_im"][j], inp["s5_d"][j]))
    d["w1"] = lay_rows(inp["mlp_w1"][layer])
    d["w2"] = lay_rows(inp["mlp_w2"][layer])
    return {pre + k: np.ascontiguousarray(v, dtype=f) for k, v in d.items()}


def build_program(L, layers):
    B = Builder(L)
    xin = DramX(B.inp("x", [128, 8, L]), L)
    xout = DramX(B.outp("y", [128, 8, L]), L)
    xres = DramX(B.scratch("xres", [128, 8, L]), L)
    n = len(layers)
    for idx, layer in enumerate(layers):
        src = xin if idx == 0 else xres
        dst = xout if idx == n - 1 else xres
        add_layer(B, layer, src, xres, dst)
    B.P.barrier(["sp"])
    B.P.emit()
    return B


ONLY_MIXER = False
LAUNCH_GROUPS = [[0, 1, 2, 3]]


def kernel(**inp):
    inp = {k: np.asarray(v) for k, v in inp.items()}
    x = inp["x"]
    nb, L, _ = x.shape
    xs = [x_to_dev(np.asarray(x[b], dtype=np.float32)) for b in range(nb)]
    g = lay_norm_g(inp["norm_g"].astype(np.float32))
    for layers in LAUNCH_GROUPS:
        B = build_program(L, layers)
        common = {"norm_g": g}
        for layer in layers:
            common.update(layer_params(layer, inp))
        common = {k: v for k, v in common.items() if k in B.ext}
        in_maps = [dict(common, x=xs[b]) for b in range(nb)]
        res = run_bass_kernel_spmd(B.nc, in_maps, core_ids=list(range(nb)))
        xs = [np.asarray(res.results[b]["y"]) for b in range(nb)]
    out = np.stack([x_from_dev(xd) for xd in xs]).astype(np.float32)
    return out
```

```python
import numpy as np
import concourse.bass as bass
import concourse.mybir as mybir
from concourse.bass_utils import run_bass_kernel_spmd

F32 = mybir.dt.float32
BF16 = mybir.dt.bfloat16
AF = mybir.ActivationFunctionType
ALU = mybir.AluOpType

D = 1024
SEQ = 4096
NCORES = 8
EPS = 1e-6


class Res:
    __slots__ = ("w", "r")

    def __init__(self):
        self.w = None
        self.r = {}


class Prog:
    ENGS = ("pe", "act", "dve", "pool", "sp")

    def __init__(self, nc, n_dma_sems=40, same_engine_sync=("act", "dve", "pool")):
        self.nc = nc
        self.q = {e: [] for e in self.ENGS}
        self.cnt = {e: 0 for e in self.ENGS}
        self.sems = {e: nc.alloc_semaphore("sem_" + e) for e in self.ENGS}
        self.n_dma = n_dma_sems
        for i in range(n_dma_sems):
            self.sems[("d", i)] = nc.alloc_semaphore("sem_d%d" % i)
        self.dma_tot = [0] * n_dma_sems
        self.dma_i = 0
        self.waited = {e: {} for e in self.ENGS}
        self.ses = set(same_engine_sync)
        self.ninst = 0

    def _deps(self, eng, reads, writes, extra=()):
        deps = {}

        def need(tok):
            if tok is None:
                return
            k, v = tok
            if k == eng and eng not in self.ses:
                return
            if deps.get(k, 0) < v:
                deps[k] = v

        for r in reads:
            need(r.w)
        for w in writes:
            need(w.w)
            for k, v in w.r.items():
                need((k, v))
        for tok in extra:
            need(tok)
        wd = self.waited[eng]
        waits = []
        for k, v in deps.items():
            if wd.get(k, 0) < v:
                wd[k] = v
                waits.append((self.sems[k], v))
        return waits

    def _mark(self, tok, reads, writes):
        k, v = tok
        for r in reads:
            if r.r.get(k, 0) < v:
                r.r[k] = v
        for w in writes:
            w.w = tok
            w.r = {}

    def op(self, eng, fn, reads=(), writes=(), inc=True):
        waits = self._deps(eng, reads, writes)
        tok = (eng, self.cnt[eng] + 1)
        sem = self.sems[eng]
        if inc:
            self.cnt[eng] += 1

            def closure(e):
                for s, v in waits:
                    e.wait_ge(s, v)
                fn(e).then_inc(sem, 1)
        else:
            def closure(e):
                for s, v in waits:
                    e.wait_ge(s, v)
                fn(e)
        self.q[eng].append(closure)
        self._mark(tok, reads, writes)
        self.ninst += 1
        return tok

    def dma(self, eng, out, in_, reads=(), writes=(), **kw):
        idx = self.dma_i % self.n_dma
        self.dma_i += 1
        key = ("d", idx)
        extra = ()
        if self.dma_tot[idx] > 0:
            extra = ((key, 16 * self.dma_tot[idx]),)
        waits = self._deps(eng, reads, writes, extra)
        self.dma_tot[idx] += 1
        tok = (key, 16 * self.dma_tot[idx])
        sem = self.sems[key]

        def closure(e):
            for s, v in waits:
                e.wait_ge(s, v)
            e.dma_start(out=out, in_=in_, **kw).then_inc(sem, 16)
        self.q[eng].append(closure)
        self._mark(tok, reads, writes)
        self.ninst += 1
        return tok

    def all_tokens(self):
        toks = [(e, self.cnt[e]) for e in self.ENGS if self.cnt[e] > 0]
        toks += [(("d", i), 16 * n) for i, n in enumerate(self.dma_tot) if n > 0]
        return toks

    def barrier(self, engs=None):
        toks = self.all_tokens()
        for eng in (engs or self.ENGS):
            wd = self.waited[eng]
            ws = []
            for k, v in toks:
                if k == eng:
                    continue
                if wd.get(k, 0) < v:
                    wd[k] = v
                    ws.append((self.sems[k], v))
            if ws:
                def closure(e, ws=ws):
                    for s, v in ws:
                        e.wait_ge(s, v)
                self.q[eng].append(closure)

    def emit(self):
        nc = self.nc
        q = self.q
        with nc.Block() as block:
            @block.tensor
            def _(e):
                for f in q["pe"]:
                    f(e)

            @block.scalar
            def _(e):
                for f in q["act"]:
                    f(e)

            @block.vector
            def _(e):
                for f in q["dve"]:
                    f(e)

            @block.gpsimd
            def _(e):
                for f in q["pool"]:
                    f(e)

            @block.sync
            def _(e):
                for f in q["sp"]:
                    f(e)


class Arena:
    def __init__(self, nc, nbytes=212800):
        self.t = nc.alloc_sbuf_tensor("arena", [128, nbytes // 2], BF16)
        self.nbytes = nbytes
        self.off = 0

    def alloc(self, free, dtype):
        if isinstance(free, int):
            free = (free,)
        n = 1
        for s in free:
            n *= s
        sz = n * (4 if dtype == F32 else 2)
        off = (self.off + 63) // 64 * 64
        assert off + sz <= self.nbytes, ("SBUF arena overflow", off, sz, self.nbytes)
        self.off = off + sz
        ap = self.t[:, off // 2:(off + sz) // 2]
        if dtype == F32:
            ap = ap.bitcast(F32)
        if len(free) == 2:
            ap = ap.rearrange("p (a b) -> p a b", a=free[0])
        elif len(free) == 3:
            ap = ap.rearrange("p (a b c) -> p a b c", a=free[0], b=free[1])
        return ap


class DramX:
    def __init__(self, ap, L):
        self.ap = ap
        self.res = [Res() for _ in range(L // 256)]

    def blocks(self, t0, T):
        return self.res[t0 // 256:(t0 + T) // 256]


class Builder:
    def __init__(self, L):
        self.L = L
        nc = self.nc = bass.Bass("TRN2", target_bir_lowering=False)
        self.P = Prog(nc)
        self.A = Arena(nc)
        self.bank = [nc.alloc_psum_tensor("bank%d" % i, [128, 512], F32) for i in range(8)]
        self.bres = [Res() for _ in range(8)]
        self.ext = {}
        A = self.A
        self.ones = A.alloc(128, BF16)
        self.g_sb = A.alloc(128, F32)
        self.epsc = A.alloc(8, F32)
        self.persist_end = A.off
        P = self.P
        r = Res()
        P.op("pool", lambda e: e.memset(self.ones, 1.0), writes=[r])
        P.op("pool", lambda e: e.memset(self.epsc[:, 0:1], EPS), writes=[r])
        P.op("pool", lambda e: e.memset(self.epsc[:, 1:2], 1.0), writes=[r])
        g = self.inp("norm_g", [128, 128])
        P.dma("sp", self.g_sb, g, writes=[r])
        P.barrier()

    def inp(self, name, shape, dtype=F32):
        ap = self.nc.dram_tensor(name, list(shape), dtype, kind="ExternalInput").ap()
        self.ext[name] = ap
        return ap

    def outp(self, name, shape, dtype=F32):
        return self.nc.dram_tensor(name, list(shape), dtype, kind="ExternalOutput").ap()

    def scratch(self, name, shape, dtype=F32):
        return self.nc.dram_tensor(name, list(shape), dtype, kind="Internal").ap()

    def full(self, b):
        return self.bank[b][:, :], [self.bres[b]]

    def gcol(self, layer, j, c):
        i = (layer * 4 + j) * 8 + c
        return self.g_sb[:, i:i + 1]

    def rstd_from_sq(self, sq, r_sq, nchunks, T, slot, ms, r_ms, rstd, r_rstd, denom):
        P = self.P
        ps, rps = slot
        for c in range(nchunks):
            P.op("pe", lambda e, c=c: e.matmul(ps[:, 0:T], self.ones, sq[:, c, :], start=(c == 0), stop=(c == nchunks - 1)),
                 reads=[r_sq], writes=rps, inc=(c == nchunks - 1))
        P.op("act", lambda e: e.activation(out=ms, in_=ps[:, 0:T], func=AF.Ln, scale=1.0 / denom, bias=self.epsc[:, 0:1]),
             reads=rps, writes=[r_ms])
        P.op("act", lambda e: e.activation(out=rstd, in_=ms, func=AF.Exp, scale=-0.5),
             reads=[r_ms], writes=[r_rstd])

    def mlp_phase(self, layer, src, dst, w1, w2):
        P, A = self.P, self.A
        T = 256
        NT = self.L // T
        A.off = self.persist_end
        W1 = A.alloc((8, 4096), BF16)
        W2 = A.alloc((32, 1024), BF16)
        rW1 = [Res() for _ in range(8)]
        rW2 = [Res() for _ in range(8)]
        for kc in range(8):
            P.dma("pool", W1[:, :, kc * 512:(kc + 1) * 512], w1[:, :, kc * 512:(kc + 1) * 512], writes=[rW1[kc]])
        for g in range(8):
            P.dma("pool", W2[:, 4 * g:4 * g + 4, :], w2[:, 4 * g:4 * g + 4, :], writes=[rW2[g]])
        xt = [A.alloc((8, T), F32) for _ in range(2)]
        r_xt = [Res() for _ in range(2)]
        xn = [A.alloc((8, T), BF16) for _ in range(2)]
        r_xn = [Res() for _ in range(2)]
        sq = [A.alloc((8, T), BF16) for _ in range(2)]
        r_sq = [Res() for _ in range(2)]
        ms = [A.alloc(T, F32) for _ in range(2)]
        r_ms = [Res() for _ in range(2)]
        rstd = [A.alloc(T, F32) for _ in range(2)]
        r_rstd = [Res() for _ in range(2)]
        h = A.alloc((32, T), BF16)
        r_h = [Res() for _ in range(32)]
        hout = A.alloc((8, T), F32)
        r_hout = [Res() for _ in range(8)]
        sq2 = A.alloc((8, T), BF16)
        r_sq2 = Res()
        NTMP = 2
        tmp = [A.alloc(2 * T, F32) for _ in range(NTMP)]
        r_tmp = [Res() for _ in range(NTMP)]
        ms2 = A.alloc(T, F32)
        r_ms2 = Res()
        rstd2 = A.alloc(T, F32)
        r_rstd2 = Res()
        hslots = [self.full(b) for b in range(4)]
        oslots = [self.full(4), self.full(5)]
        nslots = [self.full(6), self.full(6)]
        n2slot = self.full(7)
        cnt = {"h": 0, "o": 0}

        def load(i):
            b = i % 2
            P.dma("sp", xt[b], src.ap[:, :, i * T:(i + 1) * T], reads=src.blocks(i * T, T), writes=[r_xt[b]])

        def pre(i):
            b = i % 2
            xf = xt[b].rearrange("p c t -> p (c t)")
            sf = sq[b].rearrange("p c t -> p (c t)")
            P.op("act", lambda e: e.activation(out=sf, in_=xf, func=AF.Square), reads=[r_xt[b]], writes=[r_sq[b]])
            self.rstd_from_sq(sq[b], r_sq[b], 8, T, nslots[b], ms[b], r_ms[b], rstd[b], r_rstd[b], float(D))
            for c in range(8):
                P.op("dve", lambda e, c=c: e.scalar_tensor_tensor(
                    out=xn[b][:, c, :], in0=xt[b][:, c, :], scalar=self.gcol(layer, 2, c), in1=rstd[b],
                    op0=ALU.mult, op1=ALU.mult), reads=[r_xt[b], r_rstd[b]], writes=[r_xn[b]])

        def mm1(i):
            b = i % 2
            for m2 in range(16):
                ps, rps = hslots[cnt["h"] % len(hslots)]
                tp = cnt["h"] % NTMP
                cnt["h"] += 1
                for hh in range(2):
                    m = 2 * m2 + hh
                    for kc in range(8):
                        P.op("pe", lambda e, kc=kc, m=m, hh=hh, ps=ps: e.matmul(
                            ps[:, hh * T:(hh + 1) * T], W1[:, kc, m * 128:(m + 1) * 128], xn[b][:, kc, :],
                            start=(kc == 0), stop=(kc == 7)),
                            reads=[rW1[m // 4], r_xn[b]], writes=rps, inc=(kc == 7 and hh == 1))
                P.op("act", lambda e, ps=ps, tp=tp: e.activation(out=tmp[tp], in_=ps, func=AF.Relu),
                     reads=rps, writes=[r_tmp[tp]])
                hv = h[:, 2 * m2:2 * m2 + 2, :].rearrange("p c t -> p (c t)")
                P.op("pool", lambda e, hv=hv, tp=tp: e.tensor_tensor(out=hv, in0=tmp[tp], in1=tmp[tp], op=ALU.mult),
                     reads=[r_tmp[tp]], writes=[r_h[2 * m2], r_h[2 * m2 + 1]])

        def mm2(i):
            for m2 in range(4):
                ps, rps = oslots[cnt["o"] % len(oslots)]
                cnt["o"] += 1
                for hh in range(2):
                    m = 2 * m2 + hh
                    for kc in range(32):
                        P.op("pe", lambda e, kc=kc, m=m, hh=hh, ps=ps: e.matmul(
                            ps[:, hh * T:(hh + 1) * T], W2[:, kc, m * 128:(m + 1) * 128], h[:, kc, :],
                            start=(kc == 0), stop=(kc == 31)),
                            reads=[rW2[kc // 4], r_h[kc]], writes=rps, inc=(kc == 31 and hh == 1))
                ho = hout[:, 2 * m2:2 * m2 + 2, :].rearrange("p c t -> p (c t)")
                so = sq2[:, 2 * m2:2 * m2 + 2, :].rearrange("p c t -> p (c t)")
                P.op("act", lambda e, ho=ho, ps=ps: e.activation(out=ho, in_=ps, func=AF.Copy),
                     reads=rps, writes=[r_hout[2 * m2], r_hout[2 * m2 + 1]])
                P.op("act", lambda e, so=so, ps=ps: e.activation(out=so, in_=ps, func=AF.Square),
                     reads=rps, writes=[r_sq2])

        def post(i):
            b = i % 2
            self.rstd_from_sq(sq2, r_sq2, 8, T, n2slot, ms2, r_ms2, rstd2, r_rstd2, float(D))
            for c in range(8):
                P.op("dve", lambda e, c=c: e.scalar_tensor_tensor(
                    out=hout[:, c, :], in0=hout[:, c, :], scalar=self.gcol(layer, 3, c), in1=rstd2,
                    op0=ALU.mult, op1=ALU.mult), reads=[r_hout[c], r_rstd2], writes=[r_hout[c]])
                P.op("pool", lambda e, c=c: e.tensor_tensor(out=xt[b][:, c, :], in0=xt[b][:, c, :], in1=hout[:, c, :], op=ALU.add),
                     reads=[r_hout[c], r_xt[b]], writes=[r_xt[b]])
            P.dma("sp", dst.ap[:, :, i * T:(i + 1) * T], xt[b], reads=[r_xt[b]], writes=dst.blocks(i * T, T))

        load(0)
        pre(0)
        if NT > 1:
            load(1)
        for i in range(NT):
            mm1(i)
            if i + 1 < NT:
                pre(i + 1)
            mm2(i)
            post(i)
            if i + 2 < NT:
                load(i + 2)
        P.barrier()


    def rglru_phase(self, layer, src, dst, w_in, w_a, w_x, vec, w_out):
        P, A = self.P, self.A
        T = 512
        NT = self.L // T
        A.off = self.persist_end
        Win = A.alloc((8, 2048), BF16)
        Wout = A.alloc((8, 1024), BF16)
        Wa = A.alloc((8, 128), BF16)
        Wx = A.alloc((8, 128), BF16)
        V = A.alloc((8, 8), F32)
        rW = Res()
        rWin = [Res() for _ in range(8)]
        for kc in range(8):
            cg = (kc + 4) % 8
            P.dma("pool", Win[:, :, cg * 256:(cg + 1) * 256], w_in[:, :, cg * 256:(cg + 1) * 256], writes=[rWin[cg]])
        P.dma("pool", Wout, w_out, writes=[rW])
        P.op("dve", lambda e: e.memset(Wa, 0.0), writes=[rW])
        P.op("dve", lambda e: e.memset(Wx, 0.0), writes=[rW])
        for hh in range(2):
            for (Wd, wsrc) in ((Wa, w_a), (Wx, w_x)):
                sv = wsrc.rearrange("(c hh) i j -> hh i c j", hh=2)[hh]
                P.dma("pool", Wd[hh * 64:(hh + 1) * 64, :, hh * 64:(hh + 1) * 64], sv, writes=[rW])
        P.dma("sp", V, vec, writes=[rW])
        DV = A.alloc((4, 8), F32)
        halfc = A.alloc(512, F32)
        P.op("pool", lambda e: e.memset(halfc, 0.5), writes=[rW])
        P.op("dve", lambda e: e.tensor_scalar(out=DV[:, 0, :], in0=V[:, 5, :], scalar1=0.5, scalar2=None, op0=ALU.mult), reads=[rW], writes=[rW])
        P.op("dve", lambda e: e.tensor_scalar(out=DV[:, 1, :], in0=V[:, 6, :], scalar1=0.5, scalar2=None, op0=ALU.mult), reads=[rW], writes=[rW])
        P.op("act", lambda e: e.activation(out=DV[:, 3, :], in_=V[:, 7, :], func=AF.Exp, scale=-1.0), reads=[rW], writes=[rW])
        P.op("dve", lambda e: e.tensor_scalar(out=DV[:, 3, :], in0=DV[:, 3, :], scalar1=1.0, scalar2=None, op0=ALU.add), reads=[rW], writes=[rW])
        P.op("act", lambda e: e.activation(out=DV[:, 3, :], in_=DV[:, 3, :], func=AF.Ln), reads=[rW], writes=[rW])
        P.op("dve", lambda e: e.tensor_scalar(out=DV[:, 2, :], in0=DV[:, 3, :], scalar1=-4.0, scalar2=None, op0=ALU.mult), reads=[rW], writes=[rW])
        P.op("dve", lambda e: e.tensor_scalar(out=DV[:, 3, :], in0=DV[:, 3, :], scalar1=-8.0, scalar2=None, op0=ALU.mult), reads=[rW], writes=[rW])

        xts = [A.alloc((8, T), F32) for _ in range(2)]; r_xts = [Res() for _ in range(2)]
        sq = A.alloc((8, T), BF16); r_sq = Res()
        xn = A.alloc((8, T), BF16); r_xn = Res()
        gate = A.alloc((8, T), F32); r_gate = [Res() for _ in range(8)]
        hout, r_hout = gate, r_gate
        UB = T + 4
        ubuf = A.alloc((8, UB), F32); r_ub = [Res() for _ in range(8)]
        uc = A.alloc((8, T), F32); r_uc = [Res() for _ in range(8)]
        ucb = A.alloc((8, T), BF16); r_ucb = [Res() for _ in range(8)]
        hb = A.alloc((8, T), F32); r_h = [Res() for _ in range(8)]
        hst = A.alloc(8, F32); r_hst = Res()
        ms = A.alloc(T, F32); r_ms = Res()
        rstd = A.alloc(T, F32); r_rstd = Res()
        NS = 2
        tA = [[A.alloc(T, F32) for _ in range(5)] for _ in range(NS)]
        r_tA = [[Res() for _ in range(5)] for _ in range(NS)]
        yb = xn
        r_yb = r_xn
        sq2 = sq
        r_sq2 = r_sq
        pslots = [self.full(b) for b in range(4)]
        gslots = [self.full(4), self.full(5)]
        nslot = self.full(6)
        n2slot = self.full(7)
        cnt = {"p": 0, "g": 0, "t": 0}
        P.op("pool", lambda e: e.memset(ubuf, 0.0), writes=r_ub)
        P.op("pool", lambda e: e.memset(hst, 0.0), writes=[r_hst])

        def vcol(v, c):
            return V[:, v, c:c + 1]

        def dcol(v, c):
            return DV[:, v, c:c + 1]

        def load(i):
            P.dma("sp", xts[i % 2], src.ap[:, :, i * T:(i + 1) * T], reads=src.blocks(i * T, T), writes=[r_xts[i % 2]])

        def tile(i, xt, r_xt):
            xf = xt.rearrange("p c t -> p (c t)")
            sf = sq.rearrange("p c t -> p (c t)")
            P.op("act", lambda e: e.activation(out=sf, in_=xf, func=AF.Square), reads=[r_xt], writes=[r_sq])
            self.rstd_from_sq(sq, r_sq, 8, T, nslot, ms, r_ms, rstd, r_rstd, float(D))
            for c in range(8):
                P.op("dve", lambda e, c=c: e.scalar_tensor_tensor(
                    out=xn[:, c, :], in0=xt[:, c, :], scalar=self.gcol(layer, 0, c), in1=rstd,
                    op0=ALU.mult, op1=ALU.mult), reads=[r_xt, r_rstd], writes=[r_xn])
            for m in list(range(8, 16)) + list(range(8)):
                ps, rps = pslots[cnt["p"] % 4]
                cnt["p"] += 1
                for kc in range(8):
                    P.op("pe", lambda e, kc=kc, m=m, ps=ps: e.matmul(
                        ps, Win[:, kc, m * 128:(m + 1) * 128], xn[:, kc, :], start=(kc == 0), stop=(kc == 7)),
                        reads=[rWin[m // 2], r_xn], writes=rps, inc=(kc == 7))
                if m < 8:
                    P.op("act", lambda e, m=m, ps=ps: e.activation(out=gate[:, m, :], in_=ps, func=AF.Gelu_apprx_tanh),
                         reads=rps, writes=[r_gate[m]])
                else:
                    c = m - 8
                    P.op("act", lambda e, c=c, ps=ps: e.activation(out=ubuf[:, c, 4:4 + T], in_=ps, func=AF.Copy),
                         reads=rps, writes=[r_ub[c]])
                    P.op("act", lambda e, c=c, ps=ps: e.activation(out=uc[:, c, :], in_=ps, func=AF.Identity,
                                                                   scale=vcol(3, c), bias=vcol(4, c)),
                         reads=rps + [rW], writes=[r_uc[c]])
                    for k in range(3):
                        P.op("dve", lambda e, c=c, k=k: e.scalar_tensor_tensor(
                            out=uc[:, c, :], in0=ubuf[:, c, 1 + k:1 + k + T], scalar=vcol(k, c), in1=uc[:, c, :],
                            op0=ALU.mult, op1=ALU.add), reads=[r_ub[c], r_uc[c], rW], writes=[r_uc[c]])
                    P.op("pool", lambda e, c=c: e.tensor_copy(out=ubuf[:, c, 1:4], in_=ubuf[:, c, T + 1:T + 4]),
                         reads=[r_ub[c]], writes=[r_ub[c]])
                    P.op("pool", lambda e, c=c: e.tensor_copy(out=ucb[:, c, :], in_=uc[:, c, :]),
                         reads=[r_uc[c]], writes=[r_ucb[c]])
            for c in range(8):
                s = cnt["t"] % NS
                cnt["t"] += 1
                th, aa, a2, thx, bb = tA[s]
                r_th, r_aa, r_a2, r_thx, r_bb = r_tA[s]
                psr, rpsr = gslots[0]
                psx, rpsx = gslots[1]
                P.op("pe", lambda e, c=c, psr=psr: e.matmul(psr, Wa[:, c, :], ucb[:, c, :], start=True, stop=True),
                     reads=[rW, r_ucb[c]], writes=rpsr)
                P.op("pe", lambda e, c=c, psx=psx: e.matmul(psx, Wx[:, c, :], ucb[:, c, :], start=True, stop=True),
                     reads=[rW, r_ucb[c]], writes=rpsx)
                P.op("act", lambda e, c=c, th=th, psr=psr: e.activation(out=th, in_=psr, func=AF.Tanh, scale=0.5, bias=dcol(0, c)),
                     reads=rpsr + [rW], writes=[r_th])
                P.op("act", lambda e, c=c, thx=thx, psx=psx: e.activation(out=thx, in_=psx, func=AF.Tanh, scale=0.5, bias=dcol(1, c)),
                     reads=rpsx + [rW], writes=[r_thx])
                P.op("act", lambda e, c=c, th=th, aa=aa: e.activation(out=aa, in_=th, func=AF.Exp, scale=dcol(2, c), bias=dcol(2, c)),
                     reads=[r_th, rW], writes=[r_aa])
                P.op("act", lambda e, c=c, th=th, a2=a2: e.activation(out=a2, in_=th, func=AF.Exp, scale=dcol(3, c), bias=dcol(3, c)),
                     reads=[r_th, rW], writes=[r_a2])
                P.op("act", lambda e, a2=a2: e.activation(out=a2, in_=a2, func=AF.Ln, scale=-1.0, bias=self.epsc[:, 1:2]),
                     reads=[r_a2], writes=[r_a2])
                P.op("act", lambda e, a2=a2: e.activation(out=a2, in_=a2, func=AF.Exp, scale=0.5),
                     reads=[r_a2], writes=[r_a2])
                P.op("dve", lambda e, c=c, thx=thx, bb=bb: e.scalar_tensor_tensor(
                    out=bb, in0=thx, scalar=1.0, in1=uc[:, c, :], op0=ALU.add, op1=ALU.mult),
                    reads=[r_thx, r_uc[c]], writes=[r_bb])
                P.op("dve", lambda e, a2=a2, bb=bb: e.scalar_tensor_tensor(
                    out=bb, in0=bb, scalar=0.5, in1=a2, op0=ALU.mult, op1=ALU.mult),
                    reads=[r_bb, r_a2], writes=[r_bb])
                P.op("dve", lambda e, c=c, aa=aa, bb=bb: e.tensor_tensor_scan(
                    out=hb[:, c, :], data0=aa, data1=bb, initial=hst[:, c:c + 1], op0=ALU.mult, op1=ALU.add),
                    reads=[r_aa, r_bb, r_hst], writes=[r_h[c]])
            P.op("dve", lambda e: e.tensor_copy(out=hst, in_=hb[:, :, T - 1]), reads=r_h, writes=[r_hst])
            P.op("dve", lambda e: e.tensor_tensor(out=yb.rearrange("p c t -> p (c t)"), in0=hb.rearrange("p c t -> p (c t)"),
                                                  in1=gate.rearrange("p c t -> p (c t)"), op=ALU.mult),
                 reads=r_h + r_gate, writes=[r_yb])
            for m in range(8):
                ps, rps = pslots[cnt["p"] % 4]
                cnt["p"] += 1
                for kc in range(8):
                    P.op("pe", lambda e, kc=kc, m=m, ps=ps: e.matmul(
                        ps, Wout[:, kc, m * 128:(m + 1) * 128], yb[:, kc, :], start=(kc == 0), stop=(kc == 7)),
                        reads=[rW, r_yb], writes=rps, inc=(kc == 7))
                P.op("act", lambda e, m=m, ps=ps: e.activation(out=hout[:, m, :], in_=ps, func=AF.Copy),
                     reads=rps, writes=[r_hout[m]])
                P.op("act", lambda e, m=m, ps=ps: e.activation(out=sq2[:, m, :], in_=ps, func=AF.Square),
                     reads=rps, writes=[r_sq2])
            self.post_norm_residual(layer, 1, hout, r_hout, sq2, r_sq2, xt, r_xt, T, n2slot, ms, r_ms, rstd, r_rstd)
            P.dma("sp", dst.ap[:, :, i * T:(i + 1) * T], xt, reads=[r_xt], writes=dst.blocks(i * T, T))
        load(0)
        for i in range(NT):
            if i + 1 < NT:
                load(i + 1)
            tile(i, xts[i % 2], r_xts[i % 2])
        P.barrier()

    def post_norm_residual(self, layer, j, hout, r_hout, sq2, r_sq2, xt, r_xt, T, slot, ms, r_ms, rstd, r_rstd):
        P = self.P
        self.rstd_from_sq(sq2, r_sq2, 8, T, slot, ms, r_ms, rstd, r_rstd, float(D))
        for c in range(8):
            P.op("dve", lambda e, c=c: e.scalar_tensor_tensor(
                out=hout[:, c, :], in0=hout[:, c, :], scalar=self.gcol(layer, j, c), in1=rstd,
                op0=ALU.mult, op1=ALU.mult), reads=[r_hout[c], r_rstd], writes=[r_hout[c]])
            P.op("dve", lambda e, c=c: e.tensor_tensor(out=xt[:, c, :], in0=xt[:, c, :], in1=hout[:, c, :], op=ALU.add),
                 reads=[r_hout[c], r_xt], writes=[r_xt])


    def sincos(self, th, out_sin, out_cos, tmp, r, eng="dve"):
        P = self.P
        MAGIC = 12582912.0
        TWO_PI = 6.283185307179586
        for (shift, dst) in ((0.0, out_sin), (1.5707963267948966, out_cos)):
            P.op(eng, lambda e, shift=shift: e.tensor_scalar(out=tmp, in0=th, scalar1=shift, scalar2=1.0 / TWO_PI, op0=ALU.add, op1=ALU.mult),
                 reads=[r], writes=[r])
            P.op(eng, lambda e: e.tensor_scalar(out=tmp, in0=tmp, scalar1=MAGIC, scalar2=None, op0=ALU.add), reads=[r], writes=[r])
            P.op(eng, lambda e: e.tensor_scalar(out=tmp, in0=tmp, scalar1=-MAGIC, scalar2=-TWO_PI, op0=ALU.add, op1=ALU.mult), reads=[r], writes=[r])
            P.op(eng, lambda e, shift=shift, dst=dst: e.scalar_tensor_tensor(out=dst, in0=th, scalar=shift, in1=tmp, op0=ALU.add, op1=ALU.add),
                 reads=[r], writes=[r])
            P.op("act", lambda e, dst=dst: e.activation(out=dst, in_=dst, func=AF.Sin), reads=[r], writes=[r])

    def s5a_phase(self, layer, src, w_in, su, sb):
        P, A = self.P, self.A
        T = 256
        NT = self.L // T
        A.off = self.persist_end
        rW = Res()
        Win = A.alloc((8, 1024), BF16)
        P.dma("pool", Win, w_in, writes=[rW])
        xt = [A.alloc((8, T), F32) for _ in range(2)]; r_xt = [Res() for _ in range(2)]
        sq = [A.alloc((8, T), BF16) for _ in range(2)]; r_sq = [Res() for _ in range(2)]
        xn = [A.alloc((8, T), BF16) for _ in range(2)]; r_xn = [Res() for _ in range(2)]
        ms = [A.alloc(T, F32) for _ in range(2)]; r_ms = [Res() for _ in range(2)]
        rstd = [A.alloc(T, F32) for _ in range(2)]; r_rstd = [Res() for _ in range(2)]
        uo = [A.alloc((8, T), F32) for _ in range(2)]; r_uo = [Res() for _ in range(2)]
        ubo = [A.alloc((8, T), BF16) for _ in range(2)]; r_ubo = [Res() for _ in range(2)]
        pslots = [self.full(b) for b in range(4)]
        nslots = [self.full(6), self.full(7)]
        cnt = {"p": 0}

        def load(i):
            b = i % 2
            P.dma("sp", xt[b], src.ap[:, :, i * T:(i + 1) * T], reads=src.blocks(i * T, T), writes=[r_xt[b]])

        def pre(i):
            b = i % 2
            P.op("act", lambda e: e.activation(out=sq[b].rearrange("p c t -> p (c t)"), in_=xt[b].rearrange("p c t -> p (c t)"), func=AF.Square),
                 reads=[r_xt[b]], writes=[r_sq[b]])
            self.rstd_from_sq(sq[b], r_sq[b], 8, T, nslots[b], ms[b], r_ms[b], rstd[b], r_rstd[b], float(D))
            for c in range(8):
                P.op("dve", lambda e, c=c: e.scalar_tensor_tensor(
                    out=xn[b][:, c, :], in0=xt[b][:, c, :], scalar=self.gcol(layer, 0, c), in1=rstd[b],
                    op0=ALU.mult, op1=ALU.mult), reads=[r_xt[b], r_rstd[b]], writes=[r_xn[b]])

        def mm(i):
            b = i % 2
            for m2 in range(4):
                ps, rps = pslots[cnt["p"] % len(pslots)]
                cnt["p"] += 1
                for hh in range(2):
                    mm_ = 2 * m2 + hh
                    for kc in range(8):
                        P.op("pe", lambda e, kc=kc, mm_=mm_, hh=hh, ps=ps: e.matmul(
                            ps[:, hh * T:(hh + 1) * T], Win[:, kc, mm_ * 128:(mm_ + 1) * 128], xn[b][:, kc, :],
                            start=(kc == 0), stop=(kc == 7)), reads=[rW, r_xn[b]], writes=rps, inc=(kc == 7 and hh == 1))
                o1 = uo[b][:, 2 * m2:2 * m2 + 2, :].rearrange("p c t -> p (c t)")
                o2 = ubo[b][:, 2 * m2:2 * m2 + 2, :].rearrange("p c t -> p (c t)")
                P.op("act", lambda e, o1=o1, ps=ps: e.activation(out=o1, in_=ps, func=AF.Copy), reads=rps, writes=[r_uo[b]])
                P.op("act", lambda e, o2=o2, ps=ps: e.activation(out=o2, in_=ps, func=AF.Copy), reads=rps, writes=[r_ubo[b]])
            P.dma("sp", su.ap[:, :, i * T:(i + 1) * T], uo[b], reads=[r_uo[b]])
            P.dma("sp", sb.ap[:, :, i * T:(i + 1) * T], ubo[b], reads=[r_ubo[b]])

        load(0)
        pre(0)
        if NT > 1:
            load(1)
        for i in range(NT):
            if i + 1 < NT:
                pre(i + 1)
            mm(i)
            if i + 2 < NT:
                load(i + 2)
        P.barrier()

    def s5c_phase(self, layer, src, dst, w_out, sy):
        P, A = self.P, self.A
        T = 256
        NT = self.L // T
        A.off = self.persist_end
        rW = Res()
        Wout = A.alloc((8, 2048), BF16)
        for hh in range(2):
            P.dma("pool", Wout[:, :, hh * 1024:(hh + 1) * 1024], w_out[:, :, hh * 1024:(hh + 1) * 1024], writes=[rW])
        xt = [A.alloc((8, T), F32) for _ in range(2)]; r_xt = [Res() for _ in range(2)]
        yb = [A.alloc((8, T), BF16) for _ in range(2)]; r_yb = [Res() for _ in range(2)]
        hout = [A.alloc((8, T), F32) for _ in range(2)]; r_hout = [[Res() for _ in range(8)] for _ in range(2)]
        sq2 = [A.alloc((8, T), BF16) for _ in range(2)]; r_sq2 = [Res() for _ in range(2)]
        tg = [A.alloc(T, F32) for _ in range(3)]; r_tg = [Res() for _ in range(3)]
        ms = [A.alloc(T, F32) for _ in range(2)]; r_ms = [Res() for _ in range(2)]
        rstd = [A.alloc(T, F32) for _ in range(2)]; r_rstd = [Res() for _ in range(2)]
        pslots = [self.full(b) for b in range(5)]
        nslots = [self.full(6), self.full(7)]
        cnt = {"p": 0, "t": 0}

        def load(i):
            b = i % 2
            P.dma("sp", yb[b], sy.ap[:, :, i * T:(i + 1) * T], writes=[r_yb[b]])
            P.dma("sp", xt[b], src.ap[:, :, i * T:(i + 1) * T], reads=src.blocks(i * T, T), writes=[r_xt[b]])

        def body(i):
            b = i % 2
            for mo in range(8):
                ps, rps = pslots[cnt["p"] % len(pslots)]
                cnt["p"] += 1
                for hh in range(2):
                    col = (hh * 8 + mo) * 128
                    for kc in range(8):
                        P.op("pe", lambda e, kc=kc, col=col, hh=hh, ps=ps: e.matmul(
                            ps[:, hh * T:(hh + 1) * T], Wout[:, kc, col:col + 128], yb[b][:, kc, :],
                            start=(kc == 0), stop=(kc == 7)), reads=[rW, r_yb[b]], writes=rps, inc=(kc == 7 and hh == 1))
                yy = cnt["t"] % 3
                cnt["t"] += 1
                P.op("act", lambda e, yy=yy, ps=ps: e.activation(out=tg[yy], in_=ps[:, T:2 * T], func=AF.Tanh, scale=0.5),
                     reads=rps, writes=[r_tg[yy]])
                P.op("dve", lambda e, yy=yy, ps=ps: e.scalar_tensor_tensor(
                    out=tg[yy], in0=tg[yy], scalar=1.0, in1=ps[:, 0:T], op0=ALU.add, op1=ALU.mult),
                    reads=rps + [r_tg[yy]], writes=[r_tg[yy]])
                P.op("act", lambda e, mo=mo, yy=yy: e.activation(out=hout[b][:, mo, :], in_=tg[yy], func=AF.Copy, scale=0.5),
                     reads=[r_tg[yy]], writes=[r_hout[b][mo]])
                P.op("act", lambda e, mo=mo, yy=yy: e.activation(out=sq2[b][:, mo, :], in_=tg[yy], func=AF.Square, scale=0.5),
                     reads=[r_tg[yy]], writes=[r_sq2[b]])
            self.post_norm_residual(layer, 1, hout[b], r_hout[b], sq2[b], r_sq2[b], xt[b], r_xt[b], T, nslots[b], ms[b], r_ms[b], rstd[b], r_rstd[b])
            P.dma("sp", dst.ap[:, :, i * T:(i + 1) * T], xt[b], reads=[r_xt[b]], writes=dst.blocks(i * T, T))

        load(0)
        for i in range(NT):
            if i + 1 < NT:
                load(i + 1)
            body(i)
        P.barrier()

    def s5b_phase(self, layer, su, sb, sy, par, par_bc, bpad_re, bpad_im, cpad_re, cpad_im, dvec):
        P, A = self.P, self.A
        T = 256
        NT = self.L // T
        A.off = self.persist_end
        rW = Res()
        Bre = A.alloc((32, 128), BF16)
        Bim = A.alloc((32, 128), BF16)
        Cre = A.alloc((32, 128), BF16)
        Cimn = A.alloc((32, 128), BF16)
        cosT = A.alloc((32, T), F32)
        sinT = A.alloc((32, T), F32)
        Dv = A.alloc(8, F32)
        SP = A.alloc((12, 32), F32)
        RT = A.alloc((2, 32), F32)
        GL = A.alloc((2, 32), F32)
        INIT = A.alloc((2, 32), F32)
        base = A.off
        P.dma("pool", Cre, cpad_re, writes=[rW])
        P.dma("pool", Cimn, cpad_im, writes=[rW])
        P.op("pool", lambda e: e.tensor_scalar(out=Cimn.rearrange("p a b -> p (a b)"), in0=Cimn.rearrange("p a b -> p (a b)"),
                                               scalar1=-1.0, scalar2=None, op0=ALU.mult), reads=[rW], writes=[rW])
        P.dma("sp", Dv, dvec, writes=[rW])
        P.dma("sp", SP[:, 0:3, :], par, writes=[rW])
        Q = 1024
        tl = [A.alloc(Q, F32) for _ in range(14)]
        rq = Res()
        for qi in range(4):
            self.s5_bbar_quarter(qi, Q, tl, rq, rW, par_bc, bpad_re, bpad_im, Bre, Bim)
        sp = [SP[:, i, :] for i in range(12)]
        self.s5_disc(sp[0], sp[1], sp[2], sp[3], sp[4], sp[5], sp[6], sp[7], sp[8], sp[9], sp[10], sp[11], rW, unit=True)
        MAG = sp[3]
        rm_re, rm_im = sp[6], sp[5]
        P.barrier()
        A.off = base
        ta = A.alloc((32, 128), F32)
        tb = A.alloc((32, 128), F32)
        P.op("dve", lambda e: e.memset(cosT[:, :, 0:1], 1.0), writes=[rW])
        P.op("dve", lambda e: e.memset(sinT[:, :, 0:1], 0.0), writes=[rW])
        m = 1
        while m <= T // 2:
            bre_ = rm_re.unsqueeze(2).to_broadcast([128, 32, m])
            bim_ = rm_im.unsqueeze(2).to_broadcast([128, 32, m])
            P.op("dve", lambda e, m=m, b=bre_: e.tensor_tensor(out=ta[:, :, 0:m], in0=cosT[:, :, 0:m], in1=b, op=ALU.mult), reads=[rW], writes=[rW])
            P.op("dve", lambda e, m=m, b=bim_: e.tensor_tensor(out=tb[:, :, 0:m], in0=sinT[:, :, 0:m], in1=b, op=ALU.mult), reads=[rW], writes=[rW])
            P.op("dve", lambda e, m=m: e.tensor_tensor(out=cosT[:, :, m:2 * m], in0=ta[:, :, 0:m], in1=tb[:, :, 0:m], op=ALU.subtract), reads=[rW], writes=[rW])
            P.op("dve", lambda e, m=m, b=bim_: e.tensor_tensor(out=ta[:, :, 0:m], in0=cosT[:, :, 0:m], in1=b, op=ALU.mult), reads=[rW], writes=[rW])
            P.op("dve", lambda e, m=m, b=bre_: e.tensor_tensor(out=tb[:, :, 0:m], in0=sinT[:, :, 0:m], in1=b, op=ALU.mult), reads=[rW], writes=[rW])
            P.op("dve", lambda e, m=m: e.tensor_tensor(out=sinT[:, :, m:2 * m], in0=ta[:, :, 0:m], in1=tb[:, :, 0:m], op=ALU.add), reads=[rW], writes=[rW])
            P.op("dve", lambda e: e.tensor_tensor(out=sp[7], in0=rm_re, in1=rm_re, op=ALU.mult), reads=[rW], writes=[rW])
            P.op("dve", lambda e: e.tensor_tensor(out=sp[8], in0=rm_im, in1=rm_im, op=ALU.mult), reads=[rW], writes=[rW])
            P.op("dve", lambda e: e.scalar_tensor_tensor(out=rm_im, in0=rm_re, scalar=2.0, in1=rm_im, op0=ALU.mult, op1=ALU.mult), reads=[rW], writes=[rW])
            P.op("dve", lambda e: e.tensor_tensor(out=rm_re, in0=sp[7], in1=sp[8], op=ALU.subtract), reads=[rW], writes=[rW])
            m *= 2
        P.op("dve", lambda e: e.tensor_copy(out=RT[:, 0, :], in_=rm_re), reads=[rW], writes=[rW])
        P.op("dve", lambda e: e.tensor_copy(out=RT[:, 1, :], in_=rm_im), reads=[rW], writes=[rW])
        P.op("dve", lambda e: e.memset(INIT, 0.0), writes=[rW])
        P.barrier()
        A.off = base
        u2 = [A.alloc((8, T), F32) for _ in range(2)]
        ub2 = [A.alloc((8, T), BF16) for _ in range(2)]
        r_ld = [Res() for _ in range(2)]
        yb2 = [A.alloc((8, T), BF16) for _ in range(2)]; r_yb2 = [Res() for _ in range(2)]
        NB = 3
        bu = [A.alloc(3 * T, F32) for _ in range(NB)]; r_bu = [Res() for _ in range(NB)]
        t1 = [A.alloc(2 * T, F32) for _ in range(NB)]; r_t1 = [Res() for _ in range(NB)]
        t2 = [A.alloc(2 * T, F32) for _ in range(NB)]; r_t2 = [Res() for _ in range(NB)]
        G = [A.alloc(3 * T, F32) for _ in range(NB)]; r_G = [Res() for _ in range(NB)]
        t3 = [A.alloc(2 * T, F32) for _ in range(NB)]; r_t3 = [Res() for _ in range(NB)]
        t4 = [A.alloc(2 * T, F32) for _ in range(NB)]; r_t4 = [Res() for _ in range(NB)]
        hb = [A.alloc((4, 2, T), BF16) for _ in range(2)]; r_hb = [Res() for _ in range(2)]
        yv = [A.alloc(T, F32) for _ in range(2)]; r_yv = [Res() for _ in range(2)]
        r_GL = Res()
        r_INIT = Res()
        bslots = [self.full(0), self.full(1)]
        yslots = [self.full(4)]

        def b2(ap2):
            return ap2.unsqueeze(1).to_broadcast([128, 2, T])

        def v2(ap, off):
            return ap[:, off:off + 2 * T].rearrange("p (a b) -> p a b", a=2)

        def tile_chunks(i, u, ub, yb, r_u, r_ub, r_yb):
            def s1(k):
                kc = k // 4
                s = k % NB
                ps, rps = bslots[k % 2]
                P.op("pe", lambda e: e.matmul(ps[:, 0:T], Bre[:, k, :], ub[:, kc, :], start=True, stop=True),
                     reads=[rW, r_ub[kc]], writes=rps, inc=False)
                P.op("pe", lambda e: e.matmul(ps[:, T:2 * T], Bim[:, k, :], ub[:, kc, :], start=True, stop=True),
                     reads=[rW, r_ub[kc]], writes=rps)
                P.op("act", lambda e: e.activation(out=bu[s][:, 0:2 * T], in_=ps, func=AF.Copy), reads=rps, writes=[r_bu[s]])
                P.op("act", lambda e: e.activation(out=bu[s][:, 2 * T:3 * T], in_=ps[:, 0:T], func=AF.Copy, scale=-1.0), reads=rps, writes=[r_bu[s]])
                P.op("dve", lambda e: e.tensor_tensor(out=v2(t1[s], 0), in0=v2(bu[s], 0), in1=b2(cosT[:, k, :]), op=ALU.mult),
                     reads=[r_bu[s], rW], writes=[r_t1[s]])
                P.op("dve", lambda e: e.tensor_tensor(out=v2(t2[s], 0), in0=v2(bu[s], T), in1=b2(sinT[:, k, :]), op=ALU.mult),
                     reads=[r_bu[s], rW], writes=[r_t2[s]])
                P.op("dve", lambda e: e.tensor_tensor(out=t1[s], in0=t1[s], in1=t2[s], op=ALU.add),
                     reads=[r_t1[s], r_t2[s]], writes=[r_t1[s]])

            def s2a(k):
                kk = k % 4
                s = k % NB
                hs = (k // 4) % 2
                magb = MAG[:, k:k + 1].to_broadcast([128, T])
                P.op("dve", lambda e: e.tensor_tensor_scan(
                    out=G[s][:, T:2 * T], data0=magb, data1=t1[s][:, 0:T], initial=INIT[:, 0, k:k + 1], op0=ALU.mult, op1=ALU.add),
                    reads=[r_t1[s], rW, r_INIT], writes=[r_G[s]])
                P.op("dve", lambda e: e.tensor_tensor_scan(
                    out=G[s][:, 2 * T:3 * T], data0=magb, data1=t1[s][:, T:2 * T], initial=INIT[:, 1, k:k + 1], op0=ALU.mult, op1=ALU.add),
                    reads=[r_t1[s], rW, r_INIT], writes=[r_G[s]])
                P.op("act", lambda e: e.activation(out=G[s][:, 0:T], in_=G[s][:, 2 * T:3 * T], func=AF.Copy, scale=-1.0), reads=[r_G[s]], writes=[r_G[s]])
                P.op("act", lambda e: e.activation(out=GL[:, :, k], in_=v2(G[s], T)[:, :, T - 1], func=AF.Copy),
                     reads=[r_G[s]], writes=[r_GL])

            def s2b(k):
                s = k % NB
                P.op("dve", lambda e: e.tensor_tensor(out=v2(t3[s], 0), in0=v2(G[s], T), in1=b2(cosT[:, k, :]), op=ALU.mult),
                     reads=[r_G[s], rW], writes=[r_t3[s]])
                P.op("dve", lambda e: e.tensor_tensor(out=v2(t4[s], 0), in0=v2(G[s], 0), in1=b2(sinT[:, k, :]), op=ALU.mult),
                     reads=[r_G[s], rW], writes=[r_t4[s]])

            def s3(k):
                kk = k % 4
                s = k % NB
                hs = (k // 4) % 2
                P.op("dve", lambda e: e.tensor_tensor(out=hb[hs][:, kk, :, :], in0=v2(t3[s], 0), in1=v2(t4[s], 0), op=ALU.add),
                     reads=[r_t3[s], r_t4[s]], writes=[r_hb[hs]])
                if kk == 3:
                    mo = k // 4
                    psy, rpsy = yslots[0]
                    for k4 in range(4):
                        kq = mo * 4 + k4
                        P.op("pe", lambda e, kq=kq, k4=k4, hs=hs, psy=psy: e.matmul(
                            psy[:, 0:T], Cre[:, kq, :], hb[hs][:, k4, 0, :], start=(k4 == 0), stop=False),
                            reads=[rW, r_hb[hs]], writes=rpsy, inc=False)
                        P.op("pe", lambda e, kq=kq, k4=k4, hs=hs, psy=psy: e.matmul(
                            psy[:, 0:T], Cimn[:, kq, :], hb[hs][:, k4, 1, :], start=False, stop=(k4 == 3)),
                            reads=[rW, r_hb[hs]], writes=rpsy, inc=(k4 == 3))
                    yy = mo % 2
                    P.op("dve", lambda e, mo=mo, yy=yy, psy=psy: e.scalar_tensor_tensor(
                        out=yv[yy], in0=u[:, mo, :], scalar=Dv[:, mo:mo + 1], in1=psy[:, 0:T], op0=ALU.mult, op1=ALU.add),
                        reads=rpsy + [r_u[mo], rW], writes=[r_yv[yy]])
                    P.op("act", lambda e, mo=mo, yy=yy: e.activation(out=yb[:, mo, :], in_=yv[yy], func=AF.Gelu_apprx_tanh),
                         reads=[r_yv[yy]], writes=[r_yb])

            s1(0)
            s1(1)
            s2a(0)
            s2b(0)
            for k in range(32):
                if k + 2 < 32:
                    s1(k + 2)
                if k + 1 < 32:
                    s2a(k + 1)
                s3(k)
                if k + 1 < 32:
                    s2b(k + 1)
            P.op("dve", lambda e: e.tensor_tensor(out=INIT[:, 0, :], in0=RT[:, 0, :], in1=GL[:, 0, :], op=ALU.mult), reads=[r_GL, rW], writes=[r_INIT])
            P.op("dve", lambda e: e.tensor_tensor(out=INIT[:, 1, :], in0=RT[:, 1, :], in1=GL[:, 1, :], op=ALU.mult), reads=[r_GL, rW], writes=[r_INIT])
            P.op("dve", lambda e: e.tensor_tensor(out=INIT[:, 0, :], in0=INIT[:, 0, :], in1=INIT[:, 1, :], op=ALU.subtract), reads=[r_INIT], writes=[r_INIT])
            P.op("dve", lambda e: e.tensor_tensor(out=INIT[:, 1, :], in0=RT[:, 0, :], in1=GL[:, 1, :], op=ALU.mult), reads=[r_GL, rW], writes=[r_INIT])
            P.op("dve", lambda e: e.tensor_tensor(out=GL[:, 0, :], in0=RT[:, 1, :], in1=GL[:, 0, :], op=ALU.mult), reads=[r_GL, rW], writes=[r_GL])
            P.op("dve", lambda e: e.tensor_tensor(out=INIT[:, 1, :], in0=INIT[:, 1, :], in1=GL[:, 0, :], op=ALU.add), reads=[r_GL, r_INIT], writes=[r_INIT])
            P.dma("sp", sy.ap[:, :, i * T:(i + 1) * T], yb, reads=[r_yb])

        def load(i):
            b_ = i % 2
            P.dma("sp", u2[b_], su.ap[:, :, i * T:(i + 1) * T], writes=[r_ld[b_]])
            P.dma("sp", ub2[b_], sb.ap[:, :, i * T:(i + 1) * T], writes=[r_ld[b_]])

        load(0)
        for i in range(NT):
            if i + 1 < NT:
                load(i + 1)
            rl = r_ld[i % 2]
            tile_chunks(i, u2[i % 2], ub2[i % 2], yb2[i % 2], [rl] * 8, [rl] * 8, r_yb2[i % 2])
        P.barrier()

    def s5_bbar_quarter(self, qi, Q, tl, rq, rW, par_bc, bpad_re, bpad_im, Bre, Bim):
        P = self.P
        cs = slice(qi * Q, (qi + 1) * Q)
        lre, lim, lst, bre, bim, t0, t1, t2, t3, t4, t5, t6, t7, t8 = tl
        P.dma("sp", lre, par_bc[0:1, cs].partition_broadcast(128), writes=[rq])
        P.dma("sp", lim, par_bc[1:2, cs].partition_broadcast(128), writes=[rq])
        P.dma("sp", lst, par_bc[2:3, cs].partition_broadcast(128), writes=[rq])
        P.dma("sp", bre, bpad_re.rearrange("p a b -> p (a b)")[:, cs], writes=[rq])
        P.dma("sp", bim, bpad_im.rearrange("p a b -> p (a b)")[:, cs], writes=[rq])
        self.s5_disc(lre, lim, lst, t0, t1, t2, t3, t4, t5, t6, t7, t8, rq)
        bo_re = Bre.rearrange("p a b -> p (a b)")[:, cs]
        bo_im = Bim.rearrange("p a b -> p (a b)")[:, cs]
        P.op("dve", lambda e: e.tensor_tensor(out=t0, in0=t5, in1=bre, op=ALU.mult), reads=[rq], writes=[rq])
        P.op("dve", lambda e: e.tensor_tensor(out=t1, in0=t6, in1=bim, op=ALU.mult), reads=[rq], writes=[rq])
        P.op("dve", lambda e: e.tensor_tensor(out=bo_re, in0=t0, in1=t1, op=ALU.subtract), reads=[rq], writes=[rq, rW])
        P.op("dve", lambda e: e.tensor_tensor(out=t0, in0=t5, in1=bim, op=ALU.mult), reads=[rq], writes=[rq])
        P.op("dve", lambda e: e.tensor_tensor(out=t1, in0=t6, in1=bre, op=ALU.mult), reads=[rq], writes=[rq])
        P.op("dve", lambda e: e.tensor_tensor(out=bo_im, in0=t0, in1=t1, op=ALU.add), reads=[rq], writes=[rq, rW])

    def s5_disc(self, lre, lim, lst, t0, t1, t2, t3, t4, t5, t6, t7, t8, r, unit=False):
        P = self.P
        P.op("dve", lambda e: e.tensor_scalar(out=lre, in0=lre, scalar1=-1e-4, scalar2=None, op0=ALU.min), reads=[r], writes=[r])
        P.op("act", lambda e: e.activation(out=lst, in_=lst, func=AF.Exp), reads=[r], writes=[r])
        P.op("dve", lambda e: e.tensor_tensor(out=t0, in0=lre, in1=lst, op=ALU.mult), reads=[r], writes=[r])
        P.op("act", lambda e: e.activation(out=t0, in_=t0, func=AF.Exp), reads=[r], writes=[r])
        P.op("dve", lambda e: e.tensor_tensor(out=t1, in0=lim, in1=lst, op=ALU.mult), reads=[r], writes=[r])
        self.sincos(t1, t2, t3, t4, r)
        if unit:
            return
        P.op("dve", lambda e: e.tensor_tensor(out=t2, in0=t2, in1=t0, op=ALU.mult), reads=[r], writes=[r])
        P.op("dve", lambda e: e.tensor_tensor(out=t3, in0=t3, in1=t0, op=ALU.mult), reads=[r], writes=[r])
        P.op("dve", lambda e: e.tensor_scalar(out=t3, in0=t3, scalar1=-1.0, scalar2=None, op0=ALU.add), reads=[r], writes=[r])
        P.op("dve", lambda e: e.tensor_tensor(out=t4, in0=lre, in1=lre, op=ALU.mult), reads=[r], writes=[r])
        P.op("dve", lambda e: e.tensor_tensor(out=t7, in0=lim, in1=lim, op=ALU.mult), reads=[r], writes=[r])
        P.op("dve", lambda e: e.tensor_tensor(out=t4, in0=t4, in1=t7, op=ALU.add), reads=[r], writes=[r])
        P.op("dve", lambda e: e.reciprocal(out=t4, in_=t4), reads=[r], writes=[r])
        P.op("dve", lambda e: e.tensor_tensor(out=t5, in0=t3, in1=lre, op=ALU.mult), reads=[r], writes=[r])
        P.op("dve", lambda e: e.tensor_tensor(out=t7, in0=t2, in1=lim, op=ALU.mult), reads=[r], writes=[r])
        P.op("dve", lambda e: e.tensor_tensor(out=t5, in0=t5, in1=t7, op=ALU.add), reads=[r], writes=[r])
        P.op("dve", lambda e: e.tensor_tensor(out=t5, in0=t5, in1=t4, op=ALU.mult), reads=[r], writes=[r])
        P.op("dve", lambda e: e.tensor_tensor(out=t6, in0=t2, in1=lre, op=ALU.mult), reads=[r], writes=[r])
        P.op("dve", lambda e: e.tensor_tensor(out=t7, in0=t3, in1=lim, op=ALU.mult), reads=[r], writes=[r])
        P.op("dve", lambda e: e.tensor_tensor(out=t6, in0=t6, in1=t7, op=ALU.subtract), reads=[r], writes=[r])
        P.op("dve", lambda e: e.tensor_tensor(out=t6, in0=t6, in1=t4, op=ALU.mult), reads=[r], writes=[r])


    def ssd_a_phase(self, layer, src, w_in, cw, zs, xs, bt, ct, dtt):
        P, A = self.P, self.A
        T = 512
        NT = self.L // T
        A.off = self.persist_end
        rW = Res()
        Win = A.alloc((8, 6176), BF16)
        rWin = [Res() for _ in range(8)]
        for kc in range(8):
            P.dma("pool", Win[:, :, kc * 772:(kc + 1) * 772], w_in[:, :, kc * 772:(kc + 1) * 772], writes=[rWin[kc]])
        CW = A.alloc((5, 32), F32)
        P.dma("sp", CW, cw, writes=[rW])
        H = A.alloc((32, 4), F32)
        r_H = [Res() for _ in range(32)]
        P.op("pool", lambda e: e.memset(H, 0.0), writes=r_H)
        xts = [A.alloc((8, T), F32) for _ in range(2)]; r_xts = [Res() for _ in range(2)]
        sq = A.alloc((8, T), BF16); r_sq = Res()
        xn = A.alloc((8, T), BF16); r_xn = Res()
        ms = A.alloc(T, F32); r_ms = Res()
        rstd = A.alloc(T, F32); r_rstd = Res()
        dtr = A.alloc((4, 32), F32); r_dtr = Res()
        NU = 3
        ubuf = [A.alloc(T + 4, F32) for _ in range(NU)]; r_ub = [Res() for _ in range(NU)]
        uc = [A.alloc(T, F32) for _ in range(NU)]; r_uc = [Res() for _ in range(NU)]
        NSG = 3
        stg = [A.alloc((4, T), BF16) for _ in range(NSG)]; r_stg = [Res() for _ in range(NSG)]
        pslots = [self.full(b) for b in range(5)]
        dslot = self.full(5)
        nslot = self.full(6)
        cnt = {"p": 0, "u": 0, "g": 0}

        def load(i):
            P.dma("sp", xts[i % 2], src.ap[:, :, i * T:(i + 1) * T], reads=src.blocks(i * T, T), writes=[r_xts[i % 2]])

        def tile(i, xt, r_xt):
            tsl = slice(i * T, (i + 1) * T)
            P.op("act", lambda e: e.activation(out=sq.rearrange("p c t -> p (c t)"), in_=xt.rearrange("p c t -> p (c t)"), func=AF.Square),
                 reads=[r_xt], writes=[r_sq])
            self.rstd_from_sq(sq, r_sq, 8, T, nslot, ms, r_ms, rstd, r_rstd, float(D))
            for c in range(8):
                P.op("dve", lambda e, c=c: e.scalar_tensor_tensor(
                    out=xn[:, c, :], in0=xt[:, c, :], scalar=self.gcol(layer, 0, c), in1=rstd,
                    op0=ALU.mult, op1=ALU.mult), reads=[r_xt, r_rstd], writes=[r_xn])
            def flush(m, sg):
                m0 = m - 3
                if m < 16:
                    dd, c0 = zs, m0
                elif m < 32:
                    dd, c0 = xs, m0 - 16
                elif m < 40:
                    dd, c0 = bt, m0 - 32
                else:
                    dd, c0 = ct, m0 - 40
                P.dma("sp", dd.ap[:, c0:c0 + 4, tsl], stg[sg], reads=[r_stg[sg]])

            def fin(m, sg, q4, ub_i):
                P.op("act", lambda e: e.activation(out=stg[sg][:, q4, :], in_=uc[ub_i], func=AF.Silu),
                     reads=[r_uc[ub_i]], writes=[r_stg[sg]])
                if q4 == 3:
                    flush(m, sg)

            def chunk_m(m, sg, q4):
                ps, rps = pslots[cnt["p"] % len(pslots)]
                cnt["p"] += 1
                for kc in range(8):
                    P.op("pe", lambda e, kc=kc: e.matmul(
                        ps, Win[:, kc, m * 128:(m + 1) * 128], xn[:, kc, :], start=(kc == 0), stop=(kc == 7)),
                        reads=[rWin[(m * 128) // 772], rWin[(m * 128 + 127) // 772], r_xn], writes=rps, inc=(kc == 7))
                if m < 16:
                    P.op("act", lambda e: e.activation(out=stg[sg][:, q4, :], in_=ps, func=AF.Silu),
                         reads=rps, writes=[r_stg[sg]])
                    if q4 == 3:
                        flush(m, sg)
                    return None
                cc = m - 16
                ub_i = cnt["u"] % NU
                cnt["u"] += 1
                ub_, uc_ = ubuf[ub_i], uc[ub_i]
                P.op("pool", lambda e: e.tensor_copy(out=ub_[:, 1:4], in_=H[:, cc, 0:3]),
                     reads=[r_H[cc]], writes=[r_ub[ub_i]])
                P.op("act", lambda e: e.activation(out=ub_[:, 4:4 + T], in_=ps, func=AF.Copy),
                     reads=rps, writes=[r_ub[ub_i]])
                P.op("act", lambda e: e.activation(out=uc_, in_=ps, func=AF.Identity,
                                                   scale=CW[:, 3, cc:cc + 1], bias=CW[:, 4, cc:cc + 1]),
                     reads=rps + [rW], writes=[r_uc[ub_i]])
                for k in range(3):
                    P.op("dve", lambda e, k=k: e.scalar_tensor_tensor(
                        out=uc_, in0=ub_[:, 1 + k:1 + k + T], scalar=CW[:, k, cc:cc + 1], in1=uc_,
                        op0=ALU.mult, op1=ALU.add), reads=[r_ub[ub_i], r_uc[ub_i], rW], writes=[r_uc[ub_i]])
                P.op("pool", lambda e: e.tensor_copy(out=H[:, cc, 0:3], in_=ub_[:, T + 1:T + 4]),
                     reads=[r_ub[ub_i]], writes=[r_H[cc]])
                return (m, sg, q4, ub_i)

            pend = None
            for m in range(48):
                sg = cnt["g"] % NSG
                q4 = m % 4
                if q4 == 3:
                    cnt["g"] += 1
                nxt = chunk_m(m, sg, q4)
                if pend is not None:
                    fin(*pend)
                pend = nxt
            if pend is not None:
                fin(*pend)
            psd, rpsd = dslot
            for j in range(4):
                for kc in range(8):
                    P.op("pe", lambda e, j=j, kc=kc: e.matmul(
                        psd[:, j * 32:(j + 1) * 32], xn[:, kc, j * 128:(j + 1) * 128], Win[:, kc, 6144:6176],
                        start=(kc == 0), stop=(kc == 7)), reads=[rWin[7], r_xn], writes=rpsd, inc=(kc == 7 and j == 3))
            P.op("act", lambda e: e.activation(out=dtr.rearrange("p a b -> p (a b)"), in_=psd[:, 0:128], func=AF.Copy),
                 reads=rpsd, writes=[r_dtr])
            P.dma("sp", dtt.ap[i * 4:(i + 1) * 4].rearrange("j p h -> p j h"), dtr, reads=[r_dtr])
        load(0)
        for i in range(NT):
            if i + 1 < NT:
                load(i + 1)
            tile(i, xts[i % 2], r_xts[i % 2])
        P.barrier()


    def ssd_b_phase(self, layer, src, dst, zs, xs, bt, ct, dtt, w_out, consts, hp_bc, dch, ngv):
        P, A = self.P, self.A
        T = 256
        NT = self.L // T
        A.off = self.persist_end
        rW = Res()
        Wout = A.alloc((16, 1024), BF16)
        for hh in range(2):
            P.dma("pool", Wout[:, hh * 8:(hh + 1) * 8, :], w_out[:, hh * 8:(hh + 1) * 8, :], writes=[rW])
        CF = A.alloc((4, 128), F32)
        P.dma("sp", CF, consts, writes=[rW])
        IDF, U, MS, ONES = CF[:, 0, :], CF[:, 1, :], CF[:, 2, :], CF[:, 3, :]
        identb = A.alloc(128, BF16)
        P.op("dve", lambda e: e.tensor_copy(out=identb, in_=IDF), reads=[rW], writes=[rW])
        Ub = A.alloc(128, BF16)
        MSb = A.alloc(128, BF16)
        ONESb = A.alloc(128, BF16)
        P.op("dve", lambda e: e.tensor_copy(out=Ub, in_=U), reads=[rW], writes=[rW])
        P.op("dve", lambda e: e.tensor_copy(out=MSb, in_=MS), reads=[rW], writes=[rW])
        P.op("dve", lambda e: e.tensor_copy(out=ONESb, in_=ONES), reads=[rW], writes=[rW])

        HB = A.alloc((2, 32), F32)
        P.dma("sp", HB[:, 0, :], hp_bc[0:1, :].partition_broadcast(128), writes=[rW])
        P.dma("sp", HB[:, 1, :], hp_bc[1:2, :].partition_broadcast(128), writes=[rW])
        P.op("act", lambda e: e.activation(out=HB[:, 1, :], in_=HB[:, 1, :], func=AF.Exp), reads=[rW], writes=[rW])
        P.op("dve", lambda e: e.tensor_scalar(out=HB[:, 1, :], in0=HB[:, 1, :], scalar1=-1.0, scalar2=None, op0=ALU.mult), reads=[rW], writes=[rW])
        DBIAS, ANEG = HB[:, 0, :], HB[:, 1, :]
        DCH = A.alloc(16, F32)
        NG = A.alloc(16, F32)
        P.dma("sp", DCH, dch, writes=[rW])
        P.dma("sp", NG, ngv, writes=[rW])
        diagD = A.alloc((16, 128), BF16)
        for m in range(16):
            P.op("dve", lambda e, m=m: e.tensor_scalar(out=diagD[:, m, :], in0=IDF, scalar1=DCH[:, m:m + 1], scalar2=None, op0=ALU.mult),
                 reads=[rW], writes=[rW])
        ent = A.alloc(2048, F32); r_ent = [Res() for _ in range(8)]
        entb = A.alloc(2048, BF16); r_entb = [Res() for _ in range(8)]
        P.op("pool", lambda e: e.memset(ent, 0.0), writes=r_ent)
        P.op("pool", lambda e: e.memset(entb, 0.0), writes=r_entb)
        zs_t = [A.alloc((16, T), BF16) for _ in range(2)]
        xs_t = [A.alloc((16, T), BF16) for _ in range(2)]
        bt_t = [A.alloc((8, T), BF16) for _ in range(2)]
        ct_t = [A.alloc((8, T), BF16) for _ in range(2)]
        dtr_t = [A.alloc((2, 32), F32) for _ in range(2)]
        xt = [A.alloc((8, T), F32) for _ in range(2)]
        r_ld = [Res() for _ in range(2)]
        r_xt = [Res() for _ in range(2)]
        smset = []
        for _ in range(2):
            t = [A.alloc(32, F32) for _ in range(8)]
            t += [A.alloc(32, BF16), A.alloc(32, BF16), A.alloc(32, F32)]
            smset.append(t)
        r_smset = [Res(), Res()]
        xdt2 = [A.alloc(2048, BF16) for _ in range(2)]; r_xdt2 = [Res(), Res()]
        xdtd2 = [A.alloc(2048, BF16) for _ in range(2)]; r_xdtd2 = [Res(), Res()]
        btok2 = [A.alloc(1024, BF16) for _ in range(2)]; r_btok2 = [Res(), Res()]
        NR = 2
        Rg = [A.alloc((4, 128), BF16) for _ in range(NR)]; r_Rg = [Res() for _ in range(NR)]
        Rl = [A.alloc((4, 128), BF16) for _ in range(NR)]
        Lx = [A.alloc((4, 128), F32) for _ in range(NR)]; r_Lx = [Res() for _ in range(NR)]
        Ex = [A.alloc((4, 128), F32) for _ in range(NR)]; r_Ex = [Res() for _ in range(NR)]
        cbm = [A.alloc(128, F32) for _ in range(NR)]; r_cbm = [Res() for _ in range(NR)]
        Mg = [A.alloc((4, 128), BF16) for _ in range(NR)]; r_Mg = [Res() for _ in range(NR)]
        Cd = [A.alloc((4, 128), BF16) for _ in range(NR)]; r_Cd = [Res() for _ in range(NR)]
        v = A.alloc((16, T), F32); r_v = [Res() for _ in range(8)]
        sqv = A.alloc((16, T), BF16); r_sqv = Res()
        vn = A.alloc((16, T), BF16); r_vn = [Res() for _ in range(8)]
        msg = A.alloc(T, F32); r_msg = Res()
        rsg = [A.alloc(T, F32) for _ in range(2)]; r_rsg = [Res() for _ in range(2)]
        hout = A.alloc((8, T), F32); r_hout = [Res() for _ in range(8)]
        sq2 = A.alloc((8, T), BF16); r_sq2 = Res()
        ms = A.alloc(T, F32); r_ms = Res()
        rstd = A.alloc(T, F32); r_rstd = Res()
        tslots = [self.full(0), self.full(1)]
        dtslot = self.full(2)
        lslot = self.full(3)
        aslot = self.full(4)
        yslot = self.full(5)
        sslot = self.full(6)
        cslot = self.full(7)
        cnt = {"t": 0, "r": 0}

        def load(i):
            b = i % 2
            tsl = slice(i * T, (i + 1) * T)
            P.dma("sp", zs_t[b], zs.ap[:, :, tsl], writes=[r_ld[b]])
            P.dma("sp", xs_t[b], xs.ap[:, :, tsl], writes=[r_ld[b]])
            P.dma("sp", bt_t[b], bt.ap[:, :, tsl], writes=[r_ld[b]])
            P.dma("sp", ct_t[b], ct.ap[:, :, tsl], writes=[r_ld[b]])
            P.dma("sp", dtr_t[b], dtt.ap[i * 2:(i + 1) * 2].rearrange("j p h -> p j h"), writes=[r_ld[b]])
            P.dma("sp", xt[b], src.ap[:, :, tsl], reads=src.blocks(i * T, T), writes=[r_xt[b]])

        def bc4(ap2):
            return ap2.unsqueeze(1).to_broadcast([128, 4, 128])

        def pre_a(i, c, pbi):
            b = i % 2
            cols = slice(c * 128, (c + 1) * 128)
            rl = [r_ld[b]]
            dv, av, dtv, dA, dsv, eav, dtds, lv, dAh, dAl, dAt = smset[pbi]
            r_sm = r_smset[pbi]
            xdt, xdtd, btok = xdt2[pbi], xdtd2[pbi], btok2[pbi]
            r_xdt, r_xdtd, r_btok = r_xdt2[pbi], r_xdtd2[pbi], r_btok2[pbi]
            P.op("dve", lambda e: e.tensor_tensor(out=dv, in0=dtr_t[b][:, c, :], in1=DBIAS, op=ALU.add), reads=rl + [rW], writes=[r_sm])
            P.op("dve", lambda e: e.scalar_tensor_tensor(out=av, in0=dv, scalar=-1.0, in1=dv, op0=ALU.mult, op1=ALU.max), reads=[r_sm], writes=[r_sm])
            P.op("act", lambda e: e.activation(out=av, in_=av, func=AF.Exp, scale=-1.0), reads=[r_sm], writes=[r_sm])
            P.op("act", lambda e: e.activation(out=lv, in_=av, func=AF.Ln, bias=1.0), reads=[r_sm], writes=[r_sm])
            P.op("dve", lambda e: e.scalar_tensor_tensor(out=dtv, in0=dv, scalar=0.0, in1=lv, op0=ALU.max, op1=ALU.add), reads=[r_sm], writes=[r_sm])
            P.op("dve", lambda e: e.tensor_tensor(out=dA, in0=dtv, in1=ANEG, op=ALU.mult), reads=[r_sm, rW], writes=[r_sm])
            P.op("dve", lambda e: e.tensor_copy(out=dAh, in_=dA), reads=[r_sm], writes=[r_sm])
            P.op("dve", lambda e: e.tensor_tensor(out=dAt, in0=dA, in1=dAh, op=ALU.subtract), reads=[r_sm], writes=[r_sm])
            P.op("dve", lambda e: e.tensor_copy(out=dAl, in_=dAt), reads=[r_sm], writes=[r_sm])
            psd, rpsd = dtslot
            P.op("pe", lambda e: e.matmul(psd[:, 0:32], MS, dA, start=True, stop=True), reads=[rW, r_sm], writes=rpsd, inc=False)
            P.op("pe", lambda e: e.matmul(psd[:, 32:64], ONES, dA, start=True, stop=True), reads=[rW, r_sm], writes=rpsd)
            P.op("act", lambda e: e.activation(out=dsv, in_=psd[:, 0:32], func=AF.Exp), reads=rpsd, writes=[r_sm])
            P.op("act", lambda e: e.activation(out=eav, in_=psd[:, 32:64], func=AF.Exp), reads=rpsd, writes=[r_sm])
            P.op("dve", lambda e: e.tensor_tensor(out=dtds, in0=dtv, in1=dsv, op=ALU.mult), reads=[r_sm], writes=[r_sm])

        def pre_b(i, c, pbi):
            b = i % 2
            cols = slice(c * 128, (c + 1) * 128)
            rl = [r_ld[b]]
            dv, av, dtv, dA, dsv, eav, dtds, lv, dAh, dAl, dAt = smset[pbi]
            r_sm = r_smset[pbi]
            xdt, xdtd, btok = xdt2[pbi], xdtd2[pbi], btok2[pbi]
            r_xdt, r_xdtd, r_btok = r_xdt2[pbi], r_xdtd2[pbi], r_btok2[pbi]
            for half in range(2):
                pst, rpst = tslots[cnt["t"] % 2]
                cnt["t"] += 1
                pb = pst.bitcast(BF16)
                for q in range(8):
                    m = half * 8 + q
                    P.op("pe", lambda e, m=m, q=q, pb=pb: e.transpose(pb[:, q * 128:(q + 1) * 128], xs_t[b][:, m, cols], identb),
                         reads=rl + [rW], writes=rpst, inc=(q == 7))
                pv = pb.rearrange("p (h d) -> p h d", h=16)
                hs = slice(half * 16, (half + 1) * 16)
                o1 = xdt[:, half * 1024:(half + 1) * 1024].rearrange("p (h d) -> p h d", h=16)
                o2 = xdtd[:, half * 1024:(half + 1) * 1024].rearrange("p (h d) -> p h d", h=16)
                P.op("dve", lambda e, pv=pv, o1=o1, hs=hs: e.tensor_tensor(out=o1, in0=pv, in1=dtv[:, hs].unsqueeze(2).to_broadcast([128, 16, 64]), op=ALU.mult),
                     reads=rpst + [r_sm], writes=[r_xdt])
                P.op("dve", lambda e, pv=pv, o2=o2, hs=hs: e.tensor_tensor(out=o2, in0=pv, in1=dtds[:, hs].unsqueeze(2).to_broadcast([128, 16, 64]), op=ALU.mult),
                     reads=rpst + [r_sm], writes=[r_xdtd])

        def pre_c(i, c, pbi):
            b = i % 2
            cols = slice(c * 128, (c + 1) * 128)
            rl = [r_ld[b]]
            dv, av, dtv, dA, dsv, eav, dtds, lv, dAh, dAl, dAt = smset[pbi]
            r_sm = r_smset[pbi]
            xdt, xdtd, btok = xdt2[pbi], xdtd2[pbi], btok2[pbi]
            r_xdt, r_xdtd, r_btok = r_xdt2[pbi], r_xdtd2[pbi], r_btok2[pbi]
            pst, rpst = tslots[cnt["t"] % 2]
            cnt["t"] += 1
            pb = pst.bitcast(BF16)
            for g in range(8):
                P.op("pe", lambda e, g=g, pb=pb: e.transpose(pb[:, g * 128:(g + 1) * 128], bt_t[b][:, g, cols], identb),
                     reads=rl + [rW], writes=rpst, inc=(g == 7))
            P.op("act", lambda e, pb=pb: e.activation(out=btok, in_=pb, func=AF.Copy), reads=rpst, writes=[r_btok])

        def groups(i, c, pbi, hooks):
            b = i % 2
            cols = slice(c * 128, (c + 1) * 128)
            rl = [r_ld[b]]
            dv, av, dtv, dA, dsv, eav, dtds, lv, dAh, dAl, dAt = smset[pbi]
            r_sm = r_smset[pbi]
            xdt, xdtd, btok = xdt2[pbi], xdtd2[pbi], btok2[pbi]
            r_xdt, r_xdtd, r_btok = r_xdt2[pbi], r_xdtd2[pbi], r_btok2[pbi]
            def g1(g):
                rr = g % NR
                Rv = Rg[rr].rearrange("p a b -> p (a b)")
                Rlv = Rl[rr].rearrange("p a b -> p (a b)")
                P.op("dve", lambda e, g=g, rr=rr: e.tensor_tensor(out=Rg[rr], in0=bc4(Ub), in1=dAh[:, 4 * g:4 * g + 4].unsqueeze(2).to_broadcast([128, 4, 128]), op=ALU.mult),
                     reads=[rW, r_sm], writes=[r_Rg[rr]])
                P.op("dve", lambda e, g=g, rr=rr: e.tensor_tensor(out=Rl[rr], in0=bc4(Ub), in1=dAl[:, 4 * g:4 * g + 4].unsqueeze(2).to_broadcast([128, 4, 128]), op=ALU.mult),
                     reads=[rW, r_sm], writes=[r_Rg[rr]])
                psl, rpsl = lslot
                psa, rpsa = aslot
                P.op("pe", lambda e, Rv=Rv: e.matmul(psl, MSb, Rv, start=True, stop=False), reads=[rW, r_Rg[rr]], writes=rpsl, inc=False)
                P.op("pe", lambda e, Rlv=Rlv: e.matmul(psl, MSb, Rlv, start=False, stop=True), reads=[rW, r_Rg[rr]], writes=rpsl)
                P.op("pe", lambda e, Rv=Rv: e.matmul(psa, ONESb, Rv, start=True, stop=False), reads=[rW, r_Rg[rr]], writes=rpsa, inc=False)
                P.op("pe", lambda e, Rlv=Rlv: e.matmul(psa, ONESb, Rlv, start=False, stop=True), reads=[rW, r_Rg[rr]], writes=rpsa)
                P.op("act", lambda e, rr=rr: e.activation(out=Lx[rr].rearrange("p a b -> p (a b)"), in_=psl, func=AF.Exp), reads=rpsl, writes=[r_Lx[rr]])
                P.op("act", lambda e, rr=rr: e.activation(out=Ex[rr].rearrange("p a b -> p (a b)"), in_=psa, func=AF.Exp), reads=rpsa, writes=[r_Ex[rr]])
                psc, rpsc = cslot
                P.op("pe", lambda e, g=g: e.matmul(psc[:, 0:128], bt_t[b][:, g, cols], ct_t[b][:, g, cols], start=True, stop=True),
                     reads=rl, writes=rpsc)
                P.op("dve", lambda e, rr=rr: e.tensor_tensor(out=cbm[rr], in0=psc[:, 0:128], in1=U, op=ALU.mult), reads=rpsc + [rW], writes=[r_cbm[rr]])
                P.op("dve", lambda e, rr=rr: e.tensor_tensor(out=Mg[rr], in0=Lx[rr], in1=bc4(cbm[rr]), op=ALU.mult),
                     reads=[r_Lx[rr], r_cbm[rr]], writes=[r_Mg[rr]])
                P.op("dve", lambda e, rr=rr, g=g: e.tensor_tensor(out=Cd[rr], in0=Ex[rr], in1=bc4(ct_t[b][:, g, cols]), op=ALU.mult),
                     reads=[r_Ex[rr]] + rl, writes=[r_Cd[rr]])

            def g2(g):
                rr = g % NR
                psy, rpsy = yslot
                for mm in range(2):
                    m = 2 * g + mm
                    mc = slice(mm * 128, (mm + 1) * 128)
                    P.op("pe", lambda e, m=m, mc=mc: e.matmul(psy[:, mc], diagD[:, m, :], xs_t[b][:, m, cols], start=True, stop=False),
                         reads=rl + [rW], writes=rpsy, inc=False)
                    for hh in range(2):
                        h = 2 * m + hh
                        hc = slice(h * 64, (h + 1) * 64)
                        pr = slice(hh * 64, (hh + 1) * 64)
                        P.op("pe", lambda e, mc=mc, hc=hc, pr=pr, rr=rr, mm=mm, hh=hh: e.matmul(
                            psy[pr, mc], xdt[:, hc], Mg[rr][:, 2 * mm + hh, :], start=False, stop=False),
                            reads=[r_xdt, r_Mg[rr]], writes=rpsy, inc=False)
                        P.op("pe", lambda e, mc=mc, hc=hc, pr=pr, rr=rr, mm=mm, hh=hh: e.matmul(
                            psy[pr, mc], entb[:, hc], Cd[rr][:, 2 * mm + hh, :], start=False, stop=(hh == 1)),
                            reads=[r_entb[g], r_Cd[rr]], writes=rpsy, inc=(hh == 1 and mm == 1))
                P.op("dve", lambda e, g=g: e.tensor_tensor(out=v[:, 2 * g:2 * g + 2, cols], in0=psy[:, 0:256].rearrange("p (a b) -> p a b", a=2),
                                                           in1=zs_t[b][:, 2 * g:2 * g + 2, cols], op=ALU.mult),
                     reads=rpsy + rl, writes=[r_v[g]])
                pss, rpss = sslot
                gs = slice(g * 256, (g + 1) * 256)
                P.op("pe", lambda e, g=g, gs=gs: e.matmul(pss[:, 0:256], btok[:, g * 128:(g + 1) * 128], xdtd[:, gs], start=True, stop=True),
                     reads=[r_btok, r_xdtd], writes=rpss)
                ev = ent[:, gs].rearrange("p (h d) -> p h d", h=4)
                P.op("dve", lambda e, g=g, ev=ev: e.tensor_tensor(out=ev, in0=ev, in1=eav[:, 4 * g:4 * g + 4].unsqueeze(2).to_broadcast([128, 4, 64]), op=ALU.mult),
                     reads=[r_ent[g], r_sm], writes=[r_ent[g]])
                P.op("dve", lambda e, gs=gs: e.tensor_tensor(out=ent[:, gs], in0=ent[:, gs], in1=pss[:, 0:256], op=ALU.add),
                     reads=[r_ent[g]] + rpss, writes=[r_ent[g]])
                P.op("act", lambda e, gs=gs: e.activation(out=entb[:, gs], in_=ent[:, gs], func=AF.Copy), reads=[r_ent[g]], writes=[r_entb[g]])

            g1(0)
            for g in range(8):
                if g + 1 < 8:
                    g1(g + 1)
                g2(g)
                if g in hooks:
                    hooks[g]()

        def tail(i):
            b = i % 2
            P.op("act", lambda e: e.activation(out=sqv.rearrange("p c t -> p (c t)"), in_=v.rearrange("p c t -> p (c t)"), func=AF.Square),
                 reads=r_v, writes=[r_sqv])
            for g in range(8):
                k2 = g % 2
                self.rstd_from_sq(sqv[:, 2 * g:2 * g + 2, :], r_sqv, 2, T, dtslot, msg, r_msg, rsg[k2], r_rsg[k2], 256.0)
                for mm in range(2):
                    m = 2 * g + mm
                    P.op("dve", lambda e, m=m, k2=k2: e.scalar_tensor_tensor(
                        out=vn[:, m, :], in0=v[:, m, :], scalar=NG[:, m:m + 1], in1=rsg[k2], op0=ALU.mult, op1=ALU.mult),
                        reads=[r_v[g], r_rsg[k2], rW], writes=[r_vn[g]])
            for m2 in range(4):
                ps, rps = tslots[cnt["t"] % 2]
                cnt["t"] += 1
                for hh in range(2):
                    mo = 2 * m2 + hh
                    for kc in range(16):
                        P.op("pe", lambda e, kc=kc, mo=mo, hh=hh, ps=ps: e.matmul(
                            ps[:, hh * T:(hh + 1) * T], Wout[:, kc, mo * 128:(mo + 1) * 128], vn[:, kc, :],
                            start=(kc == 0), stop=(kc == 15)), reads=[rW, r_vn[kc // 2]], writes=rps, inc=(kc == 15 and hh == 1))
                ho = hout[:, 2 * m2:2 * m2 + 2, :].rearrange("p c t -> p (c t)")
                so = sq2[:, 2 * m2:2 * m2 + 2, :].rearrange("p c t -> p (c t)")
                P.op("act", lambda e, ho=ho, ps=ps: e.activation(out=ho, in_=ps, func=AF.Copy), reads=rps, writes=r_hout[2 * m2:2 * m2 + 2])
                P.op("act", lambda e, so=so, ps=ps: e.activation(out=so, in_=ps, func=AF.Square), reads=rps, writes=[r_sq2])
            self.post_norm_residual(layer, 1, hout, r_hout, sq2, r_sq2, xt[b], r_xt[b], T, cslot, ms, r_ms, rstd, r_rstd)
            P.dma("sp", dst.ap[:, :, i * T:(i + 1) * T], xt[b], reads=[r_xt[b]], writes=dst.blocks(i * T, T))

        chunks = [(i, c) for i in range(NT) for c in range(2)]
        load(0)
        pre_a(0, 0, 0)
        pre_b(0, 0, 0)
        pre_c(0, 0, 0)
        for n, (i, c) in enumerate(chunks):
            if c == 0 and i + 1 < NT:
                load(i + 1)
            hooks = {}
            if n + 1 < len(chunks):
                ni, ncc = chunks[n + 1]
                nb = (n + 1) % 2
                hooks[1] = (lambda ni=ni, ncc=ncc, nb=nb: pre_a(ni, ncc, nb))
                hooks[3] = (lambda ni=ni, ncc=ncc, nb=nb: pre_b(ni, ncc, nb))
                hooks[5] = (lambda ni=ni, ncc=ncc, nb=nb: pre_c(ni, ncc, nb))
            groups(i, c, n % 2, hooks)
            if c == 1:
                tail(i)
        P.barrier()


def x_to_dev(xb):
    L = xb.shape[0]
    return np.ascontiguousarray(xb.T.reshape(8, 128, L).transpose(1, 0, 2))


def x_from_dev(xd):
    L = xd.shape[2]
    return np.ascontiguousarray(xd.transpose(1, 0, 2).reshape(1024, L).T)


def lay_norm_g(norm_g):
    return np.ascontiguousarray(norm_g.reshape(4, 4, 8, 128).transpose(3, 0, 1, 2).reshape(128, 128))


def lay_rows(w):
    K, N = w.shape
    return np.ascontiguousarray(w.reshape(K // 128, 128, N).transpose(1, 0, 2))


def lay_vec(v):
    return np.ascontiguousarray(v.reshape(8, 128).T)


def lay_s5(lam_re, lam_im, log_step, b_re, b_im, c_re, c_im, d):
    f = np.float32
    par = np.zeros((128, 3, 32), f)
    l4 = lam_re.reshape(32, 2, 64)
    par[:, 0, :] = l4.transpose(1, 2, 0).reshape(128, 32)
    par[:, 1, :] = lam_im.reshape(32, 2, 64).transpose(1, 2, 0).reshape(128, 32)
    par[:, 2, :] = np.repeat(log_step.reshape(32, 2, 1), 64, axis=2).transpose(1, 2, 0).reshape(128, 32)
    par_bc = np.stack([lam_re.reshape(4096), lam_im.reshape(4096), np.repeat(log_step, 64)]).astype(f)

    def bpad(b):
        out = np.zeros((8, 16, 32, 2, 64), f)
        bb = b.reshape(8, 4, 2, 64, 16)
        for kc in range(8):
            for j in range(4):
                g8 = j * 2
                for gg in range(2):
                    out[g8 + gg, :, kc * 4 + j, gg, :] = bb[kc, j, gg].T
        return np.ascontiguousarray(out.reshape(128, 32, 128))

    def cpad(c):
        out = np.zeros((2, 64, 32, 8, 16), f)
        for g in range(64):
            out[g % 2, :, g // 2, g % 8, :] = c[g].T
        return np.ascontiguousarray(out.reshape(128, 32, 128))

    return {"par": par, "par_bc": np.ascontiguousarray(par_bc), "bpad_re": bpad(b_re), "bpad_im": bpad(b_im),
            "cpad_re": cpad(c_re), "cpad_im": cpad(c_im), "dvec": lay_vec(d).astype(f)}


def ssd_consts():
    j = np.arange(128)
    ident = np.eye(128, dtype=np.float32)
    U = (j[:, None] <= j[None, :]).astype(np.float32)
    MS = (j[:, None] > j[None, :]).astype(np.float32)
    ones = np.ones((128, 128), np.float32)
    return np.ascontiguousarray(np.stack([ident, U, MS, ones], axis=1))


def lay_ssd(conv_w, conv_b, dt_bias, a_log, d_skip, norm_g):
    f = np.float32
    cw = np.zeros((128, 5, 32), f)
    for k in range(4):
        cw[:, k, :] = conv_w[k].reshape(32, 128).T
    cw[:, 4, :] = conv_b.reshape(32, 128).T
    hp = np.stack([dt_bias, a_log]).astype(f)
    dch = np.repeat(d_skip, 64).reshape(16, 128).T
    ng = norm_g.reshape(16, 128).T
    return {"cw": np.ascontiguousarray(cw), "hp_bc": np.ascontiguousarray(hp),
            "dch": np.ascontiguousarray(dch).astype(f), "ngv": np.ascontiguousarray(ng).astype(f)}


def lay_rg_vec(conv_w, conv_b, b_a, b_x, lam):
    vs = [conv_w[0], conv_w[1], conv_w[2], conv_w[3], conv_b, b_a, b_x, lam]
    return np.ascontiguousarray(np.stack([lay_vec(v) for v in vs], axis=1)).astype(np.float32)


def add_layer(B, layer, src, mid, dst):
    L = B.L
    kind = layer % 3
    pre = "l%d_" % layer
    if kind == 0:
        w_in = B.inp(pre + "w_in", [128, 8, 2048])
        w_a = B.inp(pre + "w_a", [16, 64, 64])
        w_x = B.inp(pre + "w_x", [16, 64, 64])
        vec = B.inp(pre + "vec", [128, 8, 8])
        w_out = B.inp(pre + "w_out", [128, 8, 1024])
        B.rglru_phase(layer, src, mid, w_in, w_a, w_x, vec, w_out)
    elif kind == 1:
        w_in = B.inp(pre + "w_in", [128, 8, 6176])
        w_out = B.inp(pre + "w_out", [128, 16, 1024])
        cw = B.inp(pre + "cw", [128, 5, 32])
        hp_bc = B.inp(pre + "hp_bc", [2, 32])
        dch = B.inp(pre + "dch", [128, 16])
        ngv = B.inp(pre + "ngv", [128, 16])
        consts = B.inp(pre + "consts", [128, 4, 128])
        zs = DramX(B.scratch(pre + "zs", [128, 16, L], BF16), L)
        xs = DramX(B.scratch(pre + "xs", [128, 16, L], BF16), L)
        bt = DramX(B.scratch(pre + "bt", [128, 8, L], BF16), L)
        ct = DramX(B.scratch(pre + "ct", [128, 8, L], BF16), L)
        dtt = DramX(B.scratch(pre + "dtt", [L // 128, 128, 32], F32), L)
        B.ssd_a_phase(layer, src, w_in, cw, zs, xs, bt, ct, dtt)
        B.ssd_b_phase(layer, src, mid, zs, xs, bt, ct, dtt, w_out, consts, hp_bc, dch, ngv)
    else:
        w_in = B.inp(pre + "w_in", [128, 8, 1024])
        w_out = B.inp(pre + "w_out", [128, 8, 2048])
        par = B.inp(pre + "par", [128, 3, 32])
        par_bc = B.inp(pre + "par_bc", [3, 4096])
        bre = B.inp(pre + "bpad_re", [128, 32, 128])
        bim = B.inp(pre + "bpad_im", [128, 32, 128])
        cre = B.inp(pre + "cpad_re", [128, 32, 128])
        cim = B.inp(pre + "cpad_im", [128, 32, 128])
        dvec = B.inp(pre + "dvec", [128, 8])
        su = DramX(B.scratch(pre + "su", [128, 8, L]), L)
        sb = DramX(B.scratch(pre + "sb", [128, 8, L], BF16), L)
        sy = DramX(B.scratch(pre + "sy", [128, 8, L], BF16), L)
        B.s5a_phase(layer, src, w_in, su, sb)
        B.s5b_phase(layer, su, sb, sy, par, par_bc, bre, bim, cre, cim, dvec)
        B.s5c_phase(layer, src, mid, w_out, sy)
    if ONLY_MIXER:
        return
    w1 = B.inp(pre + "w1", [128, 8, 4096])
    w2 = B.inp(pre + "w2", [128, 32, 1024])
    B.mlp_phase(layer, mid, dst, w1, w2)


def layer_params(layer, inp):
    kind = layer % 3
    j = layer // 3
    pre = "l%d_" % layer
    d = {}
    f = np.float32
    if kind == 0:
        d["w_in"] = lay_rows(inp["rg_w_in"][j])
        d["w_a"] = np.ascontiguousarray(inp["rg_w_a"][j])
        d["w_x"] = np.ascontiguousarray(inp["rg_w_x"][j])
        d["vec"] = lay_rg_vec(inp["rg_conv_w"][j], inp["rg_conv_b"][j], inp["rg_b_a"][j], inp["rg_b_x"][j], inp["rg_lam"][j])
        d["w_out"] = lay_rows(inp["rg_w_out"][j])
    elif kind == 1:
        d["w_in"] = lay_rows(inp["ssd_w_in"][j])
        d["w_out"] = lay_rows(inp["ssd_w_out"][j])
        d.update(lay_ssd(inp["ssd_conv_w"][j], inp["ssd_conv_b"][j], inp["ssd_dt_bias"][j], inp["ssd_a_log"][j],
                         inp["ssd_d"][j], inp["ssd_norm_g"][j]))
        d["consts"] = ssd_consts()
    else:
        d["w_in"] = lay_rows(inp["s5_w_in"][j])
        d["w_out"] = lay_rows(inp["s5_w_out"][j])
        d.update(lay_s5(inp["s5_lam_re"][j], inp["s5_lam_im"][j], inp["s5_log_step"][j], inp["s5_b_re"][j], inp["s5_b_im"][j],
                        inp["s5_c_re"][j], inp["s5_c_im"][j], inp["s5_d"][j]))
    d["w1"] = lay_rows(inp["mlp_w1"][layer])
    d["w2"] = lay_rows(inp["mlp_w2"][layer])
    return {pre + k: np.ascontiguousarray(v, dtype=f) for k, v in d.items()}


def build_program(L, layers):
    B = Builder(L)
    xin = DramX(B.inp("x", [128, 8, L]), L)
    xout = DramX(B.outp("y", [128, 8, L]), L)
    xres = DramX(B.scratch("xres", [128, 8, L]), L)
    n = len(layers)
    for idx, layer in enumerate(layers):
        src = xin if idx == 0 else xres
        dst = xout if idx == n - 1 else xres
        add_layer(B, layer, src, xres, dst)
    B.P.barrier(["sp"])
    B.P.emit()
    return B


ONLY_MIXER = False
LAUNCH_GROUPS = [[0, 1, 2, 3]]


def kernel(**inp):
    inp = {k: np.asarray(v) for k, v in inp.items()}
    x = inp["x"]
    nb, L, _ = x.shape
    xs = [x_to_dev(np.asarray(x[b], dtype=np.float32)) for b in range(nb)]
    g = lay_norm_g(inp["norm_g"].astype(np.float32))
    for layers in LAUNCH_GROUPS:
        B = build_program(L, layers)
        common = {"norm_g": g}
        for layer in layers:
            common.update(layer_params(layer, inp))
        common = {k: v for k, v in common.items() if k in B.ext}
        in_maps = [dict(common, x=xs[b]) for b in range(nb)]
        res = run_bass_kernel_spmd(B.nc, in_maps, core_ids=list(range(nb)))
        xs = [np.asarray(res.results[b]["y"]) for b in range(nb)]
    out = np.stack([x_from_dev(xd) for xd in xs]).astype(np.float32)
    return out
```

```python
import numpy as np
import concourse.bass as bass
import concourse.mybir as mybir
from concourse.bass_utils import run_bass_kernel_spmd

F32 = mybir.dt.float32
BF16 = mybir.dt.bfloat16
AF = mybir.ActivationFunctionType
ALU = mybir.AluOpType

D = 1024
SEQ = 4096
NCORES = 8
EPS = 1e-6


class Res:
    __slots__ = ("w", "r")

    def __init__(self):
        self.w = None
        self.r = {}


class Prog:
    ENGS = ("pe", "act", "dve", "pool", "sp")

    def __init__(self, nc, n_dma_sems=40, same_engine_sync=("act", "dve", "pool")):
        self.nc = nc
        self.q = {e: [] for e in self.ENGS}
        self.cnt = {e: 0 for e in self.ENGS}
        self.sems = {e: nc.alloc_semaphore("sem_" + e) for e in self.ENGS}
        self.n_dma = n_dma_sems
        for i in range(n_dma_sems):
            self.sems[("d", i)] = nc.alloc_semaphore("sem_d%d" % i)
        self.dma_tot = [0] * n_dma_sems
        self.dma_i = 0
        self.waited = {e: {} for e in self.ENGS}
        self.ses = set(same_engine_sync)
        self.ninst = 0

    def _deps(self, eng, reads, writes, extra=()):
        deps = {}

        def need(tok):
            if tok is None:
                return
            k, v = tok
            if k == eng and eng not in self.ses:
                return
            if deps.get(k, 0) < v:
                deps[k] = v

        for r in reads:
            need(r.w)
        for w in writes:
            need(w.w)
            for k, v in w.r.items():
                need((k, v))
        for tok in extra:
            need(tok)
        wd = self.waited[eng]
        waits = []
        for k, v in deps.items():
            if wd.get(k, 0) < v:
                wd[k] = v
                waits.append((self.sems[k], v))
        return waits

    def _mark(self, tok, reads, writes):
        k, v = tok
        for r in reads:
            if r.r.get(k, 0) < v:
                r.r[k] = v
        for w in writes:
            w.w = tok
            w.r = {}

    def op(self, eng, fn, reads=(), writes=(), inc=True):
        waits = self._deps(eng, reads, writes)
        tok = (eng, self.cnt[eng] + 1)
        sem = self.sems[eng]
        if inc:
            self.cnt[eng] += 1

            def closure(e):
                for s, v in waits:
                    e.wait_ge(s, v)
                fn(e).then_inc(sem, 1)
        else:
            def closure(e):
                for s, v in waits:
                    e.wait_ge(s, v)
                fn(e)
        self.q[eng].append(closure)
        self._mark(tok, reads, writes)
        self.ninst += 1
        return tok

    def dma(self, eng, out, in_, reads=(), writes=(), **kw):
        idx = self.dma_i % self.n_dma
        self.dma_i += 1
        key = ("d", idx)
        extra = ()
        if self.dma_tot[idx] > 0:
            extra = ((key, 16 * self.dma_tot[idx]),)
        waits = self._deps(eng, reads, writes, extra)
        self.dma_tot[idx] += 1
        tok = (key, 16 * self.dma_tot[idx])
        sem = self.sems[key]

        def closure(e):
            for s, v in waits:
                e.wait_ge(s, v)
            e.dma_start(out=out, in_=in_, **kw).then_inc(sem, 16)
        self.q[eng].append(closure)
        self._mark(tok, reads, writes)
        self.ninst += 1
        return tok

    def all_tokens(self):
        toks = [(e, self.cnt[e]) for e in self.ENGS if self.cnt[e] > 0]
        toks += [(("d", i), 16 * n) for i, n in enumerate(self.dma_tot) if n > 0]
        return toks

    def barrier(self, engs=None):
        toks = self.all_tokens()
        for eng in (engs or self.ENGS):
            wd = self.waited[eng]
            ws = []
            for k, v in toks:
                if k == eng:
                    continue
                if wd.get(k, 0) < v:
                    wd[k] = v
                    ws.append((self.sems[k], v))
            if ws:
                def closure(e, ws=ws):
                    for s, v in ws:
                        e.wait_ge(s, v)
                self.q[eng].append(closure)

    def emit(self):
        nc = self.nc
        q = self.q
        with nc.Block() as block:
            @block.tensor
            def _(e):
                for f in q["pe"]:
                    f(e)

            @block.scalar
            def _(e):
                for f in q["act"]:
                    f(e)

            @block.vector
            def _(e):
                for f in q["dve"]:
                    f(e)

            @block.gpsimd
            def _(e):
                for f in q["pool"]:
                    f(e)

            @block.sync
            def _(e):
                for f in q["sp"]:
                    f(e)


class Arena:
    def __init__(self, nc, nbytes=212800):
        self.t = nc.alloc_sbuf_tensor("arena", [128, nbytes // 2], BF16)
        self.nbytes = nbytes
        self.off = 0

    def alloc(self, free, dtype):
        if isinstance(free, int):
            free = (free,)
        n = 1
        for s in free:
            n *= s
        sz = n * (4 if dtype == F32 else 2)
        off = (self.off + 63) // 64 * 64
        assert off + sz <= self.nbytes, ("SBUF arena overflow", off, sz, self.nbytes)
        self.off = off + sz
        ap = self.t[:, off // 2:(off + sz) // 2]
        if dtype == F32:
            ap = ap.bitcast(F32)
        if len(free) == 2:
            ap = ap.rearrange("p (a b) -> p a b", a=free[0])
        elif len(free) == 3:
            ap = ap.rearrange("p (a b c) -> p a b c", a=free[0], b=free[1])
        return ap


class DramX:
    def __init__(self, ap, L):
        self.ap = ap
        self.res = [Res() for _ in range(L // 256)]

    def blocks(self, t0, T):
        return self.res[t0 // 256:(t0 + T) // 256]


class Builder:
    def __init__(self, L):
        self.L = L
        nc = self.nc = bass.Bass("TRN2", target_bir_lowering=False)
        self.P = Prog(nc)
        self.A = Arena(nc)
        self.bank = [nc.alloc_psum_tensor("bank%d" % i, [128, 512], F32) for i in range(8)]
        self.bres = [Res() for _ in range(8)]
        self.ext = {}
        A = self.A
        self.ones = A.alloc(128, BF16)
        self.g_sb = A.alloc(128, F32)
        self.epsc = A.alloc(8, F32)
        self.persist_end = A.off
        P = self.P
        r = Res()
        P.op("pool", lambda e: e.memset(self.ones, 1.0), writes=[r])
        P.op("pool", lambda e: e.memset(self.epsc[:, 0:1], EPS), writes=[r])
        P.op("pool", lambda e: e.memset(self.epsc[:, 1:2], 1.0), writes=[r])
        g = self.inp("norm_g", [128, 128])
        P.dma("sp", self.g_sb, g, writes=[r])
        P.barrier()

    def inp(self, name, shape, dtype=F32):
        ap = self.nc.dram_tensor(name, list(shape), dtype, kind="ExternalInput").ap()
        self.ext[name] = ap
        return ap

    def outp(self, name, shape, dtype=F32):
        return self.nc.dram_tensor(name, list(shape), dtype, kind="ExternalOutput").ap()

    def scratch(self, name, shape, dtype=F32):
        return self.nc.dram_tensor(name, list(shape), dtype, kind="Internal").ap()

    def full(self, b):
        return self.bank[b][:, :], [self.bres[b]]

    def gcol(self, layer, j, c):
        i = (layer * 4 + j) * 8 + c
        return self.g_sb[:, i:i + 1]

    def rstd_from_sq(self, sq, r_sq, nchunks, T, slot, ms, r_ms, rstd, r_rstd, denom):
        P = self.P
        ps, rps = slot
        for c in range(nchunks):
            P.op("pe", lambda e, c=c: e.matmul(ps[:, 0:T], self.ones, sq[:, c, :], start=(c == 0), stop=(c == nchunks - 1)),
                 reads=[r_sq], writes=rps, inc=(c == nchunks - 1))
        P.op("act", lambda e: e.activation(out=ms, in_=ps[:, 0:T], func=AF.Ln, scale=1.0 / denom, bias=self.epsc[:, 0:1]),
             reads=rps, writes=[r_ms])
        P.op("act", lambda e: e.activation(out=rstd, in_=ms, func=AF.Exp, scale=-0.5),
             reads=[r_ms], writes=[r_rstd])

    def mlp_phase(self, layer, src, dst, w1, w2):
        P, A = self.P, self.A
        T = 256
        NT = self.L // T
        A.off = self.persist_end
        W1 = A.alloc((8, 4096), BF16)
        W2 = A.alloc((32, 1024), BF16)
        rW1 = [Res() for _ in range(8)]
        rW2 = [Res() for _ in range(8)]
        for kc in range(8):
            P.dma("pool", W1[:, :, kc * 512:(kc + 1) * 512], w1[:, :, kc * 512:(kc + 1) * 512], writes=[rW1[kc]])
        for g in range(8):
            P.dma("pool", W2[:, 4 * g:4 * g + 4, :], w2[:, 4 * g:4 * g + 4, :], writes=[rW2[g]])
        xt = [A.alloc((8, T), F32) for _ in range(2)]
        r_xt = [Res() for _ in range(2)]
        xn = [A.alloc((8, T), BF16) for _ in range(2)]
        r_xn = [Res() for _ in range(2)]
        sq = [A.alloc((8, T), BF16) for _ in range(2)]
        r_sq = [Res() for _ in range(2)]
        ms = [A.alloc(T, F32) for _ in range(2)]
        r_ms = [Res() for _ in range(2)]
        rstd = [A.alloc(T, F32) for _ in range(2)]
        r_rstd = [Res() for _ in range(2)]
        h = A.alloc((32, T), BF16)
        r_h = [Res() for _ in range(32)]
        hout = A.alloc((8, T), F32)
        r_hout = [Res() for _ in range(8)]
        sq2 = A.alloc((8, T), BF16)
        r_sq2 = Res()
        NTMP = 2
        tmp = [A.alloc(2 * T, F32) for _ in range(NTMP)]
        r_tmp = [Res() for _ in range(NTMP)]
        ms2 = A.alloc(T, F32)
        r_ms2 = Res()
        rstd2 = A.alloc(T, F32)
        r_rstd2 = Res()
        hslots = [self.full(b) for b in range(4)]
        oslots = [self.full(4), self.full(5)]
        nslots = [self.full(6), self.full(6)]
        n2slot = self.full(7)
        cnt = {"h": 0, "o": 0}

        def load(i):
            b = i % 2
            P.dma("sp", xt[b], src.ap[:, :, i * T:(i + 1) * T], reads=src.blocks(i * T, T), writes=[r_xt[b]])

        def pre(i):
            b = i % 2
            xf = xt[b].rearrange("p c t -> p (c t)")
            sf = sq[b].rearrange("p c t -> p (c t)")
            P.op("act", lambda e: e.activation(out=sf, in_=xf, func=AF.Square), reads=[r_xt[b]], writes=[r_sq[b]])
            self.rstd_from_sq(sq[b], r_sq[b], 8, T, nslots[b], ms[b], r_ms[b], rstd[b], r_rstd[b], float(D))
            for c in range(8):
                P.op("dve", lambda e, c=c: e.scalar_tensor_tensor(
                    out=xn[b][:, c, :], in0=xt[b][:, c, :], scalar=self.gcol(layer, 2, c), in1=rstd[b],
                    op0=ALU.mult, op1=ALU.mult), reads=[r_xt[b], r_rstd[b]], writes=[r_xn[b]])

        def mm1(i):
            b = i % 2
            for m2 in range(16):
                ps, rps = hslots[cnt["h"] % len(hslots)]
                tp = cnt["h"] % NTMP
                cnt["h"] += 1
                for hh in range(2):
                    m = 2 * m2 + hh
                    for kc in range(8):
                        P.op("pe", lambda e, kc=kc, m=m, hh=hh, ps=ps: e.matmul(
                            ps[:, hh * T:(hh + 1) * T], W1[:, kc, m * 128:(m + 1) * 128], xn[b][:, kc, :],
                            start=(kc == 0), stop=(kc == 7)),
                            reads=[rW1[m // 4], r_xn[b]], writes=rps, inc=(kc == 7 and hh == 1))
                P.op("act", lambda e, ps=ps, tp=tp: e.activation(out=tmp[tp], in_=ps, func=AF.Relu),
                     reads=rps, writes=[r_tmp[tp]])
                hv = h[:, 2 * m2:2 * m2 + 2, :].rearrange("p c t -> p (c t)")
                P.op("pool", lambda e, hv=hv, tp=tp: e.tensor_tensor(out=hv, in0=tmp[tp], in1=tmp[tp], op=ALU.mult),
                     reads=[r_tmp[tp]], writes=[r_h[2 * m2], r_h[2 * m2 + 1]])

        def mm2(i):
            for m2 in range(4):
                ps, rps = oslots[cnt["o"] % len(oslots)]
                cnt["o"] += 1
                for hh in range(2):
                    m = 2 * m2 + hh
                    for kc in range(32):
                        P.op("pe", lambda e, kc=kc, m=m, hh=hh, ps=ps: e.matmul(
                            ps[:, hh * T:(hh + 1) * T], W2[:, kc, m * 128:(m + 1) * 128], h[:, kc, :],
                            start=(kc == 0), stop=(kc == 31)),
                            reads=[rW2[kc // 4], r_h[kc]], writes=rps, inc=(kc == 31 and hh == 1))
                ho = hout[:, 2 * m2:2 * m2 + 2, :].rearrange("p c t -> p (c t)")
                so = sq2[:, 2 * m2:2 * m2 + 2, :].rearrange("p c t -> p (c t)")
                P.op("act", lambda e, ho=ho, ps=ps: e.activation(out=ho, in_=ps, func=AF.Copy),
                     reads=rps, writes=[r_hout[2 * m2], r_hout[2 * m2 + 1]])
                P.op("act", lambda e, so=so, ps=ps: e.activation(out=so, in_=ps, func=AF.Square),
                     reads=rps, writes=[r_sq2])

        def post(i):
            b = i % 2
            self.rstd_from_sq(sq2, r_sq2, 8, T, n2slot, ms2, r_ms2, rstd2, r_rstd2, float(D))
            for c in range(8):
                P.op("dve", lambda e, c=c: e.scalar_tensor_tensor(
                    out=hout[:, c, :], in0=hout[:, c, :], scalar=self.gcol(layer, 3, c), in1=rstd2,
                    op0=ALU.mult, op1=ALU.mult), reads=[r_hout[c], r_rstd2], writes=[r_hout[c]])
                P.op("pool", lambda e, c=c: e.tensor_tensor(out=xt[b][:, c, :], in0=xt[b][:, c, :], in1=hout[:, c, :], op=ALU.add),
                     reads=[r_hout[c], r_xt[b]], writes=[r_xt[b]])
            P.dma("sp", dst.ap[:, :, i * T:(i + 1) * T], xt[b], reads=[r_xt[b]], writes=dst.blocks(i * T, T))

        load(0)
        pre(0)
        if NT > 1:
            load(1)
        for i in range(NT):
            mm1(i)
            if i + 1 < NT:
                pre(i + 1)
            mm2(i)
            post(i)
            if i + 2 < NT:
                load(i + 2)
        P.barrier()


    def rglru_phase(self, layer, src, dst, w_in, w_a, w_x, vec, w_out):
        P, A = self.P, self.A
        T = 512
        NT = self.L // T
        A.off = self.persist_end
        Win = A.alloc((8, 2048), BF16)
        Wout = A.alloc((8, 1024), BF16)
        Wa = A.alloc((8, 128), BF16)
        Wx = A.alloc((8, 128), BF16)
        V = A.alloc((8, 8), F32)
        rW = Res()
        rWin = [Res() for _ in range(8)]
        for kc in range(8):
            cg = (kc + 4) % 8
            P.dma("pool", Win[:, :, cg * 256:(cg + 1) * 256], w_in[:, :, cg * 256:(cg + 1) * 256], writes=[rWin[cg]])
        P.dma("pool", Wout, w_out, writes=[rW])
        P.op("dve", lambda e: e.memset(Wa, 0.0), writes=[rW])
        P.op("dve", lambda e: e.memset(Wx, 0.0), writes=[rW])
        for hh in range(2):
            for (Wd, wsrc) in ((Wa, w_a), (Wx, w_x)):
                sv = wsrc.rearrange("(c hh) i j -> hh i c j", hh=2)[hh]
                P.dma("pool", Wd[hh * 64:(hh + 1) * 64, :, hh * 64:(hh + 1) * 64], sv, writes=[rW])
        P.dma("sp", V, vec, writes=[rW])
        DV = A.alloc((4, 8), F32)
        halfc = A.alloc(512, F32)
        P.op("pool", lambda e: e.memset(halfc, 0.5), writes=[rW])
        P.op("dve", lambda e: e.tensor_scalar(out=DV[:, 0, :], in0=V[:, 5, :], scalar1=0.5, scalar2=None, op0=ALU.mult), reads=[rW], writes=[rW])
        P.op("dve", lambda e: e.tensor_scalar(out=DV[:, 1, :], in0=V[:, 6, :], scalar1=0.5, scalar2=None, op0=ALU.mult), reads=[rW], writes=[rW])
        P.op("act", lambda e: e.activation(out=DV[:, 3, :], in_=V[:, 7, :], func=AF.Exp, scale=-1.0), reads=[rW], writes=[rW])
        P.op("dve", lambda e: e.tensor_scalar(out=DV[:, 3, :], in0=DV[:, 3, :], scalar1=1.0, scalar2=None, op0=ALU.add), reads=[rW], writes=[rW])
        P.op("act", lambda e: e.activation(out=DV[:, 3, :], in_=DV[:, 3, :], func=AF.Ln), reads=[rW], writes=[rW])
        P.op("dve", lambda e: e.tensor_scalar(out=DV[:, 2, :], in0=DV[:, 3, :], scalar1=-4.0, scalar2=None, op0=ALU.mult), reads=[rW], writes=[rW])
        P.op("dve", lambda e: e.tensor_scalar(out=DV[:, 3, :], in0=DV[:, 3, :], scalar1=-8.0, scalar2=None, op0=ALU.mult), reads=[rW], writes=[rW])

        xts = [A.alloc((8, T), F32) for _ in range(2)]; r_xts = [Res() for _ in range(2)]
        sq = A.alloc((8, T), BF16); r_sq = Res()
        xn = A.alloc((8, T), BF16); r_xn = Res()
        gate = A.alloc((8, T), F32); r_gate = [Res() for _ in range(8)]
        hout, r_hout = gate, r_gate
        UB = T + 4
        ubuf = A.alloc((8, UB), F32); r_ub = [Res() for _ in range(8)]
        uc = A.alloc((8, T), F32); r_uc = [Res() for _ in range(8)]
        ucb = A.alloc((8, T), BF16); r_ucb = [Res() for _ in range(8)]
        hb = A.alloc((8, T), F32); r_h = [Res() for _ in range(8)]
        hst = A.alloc(8, F32); r_hst = Res()
        ms = A.alloc(T, F32); r_ms = Res()
        rstd = A.alloc(T, F32); r_rstd = Res()
        NS = 2
        tA = [[A.alloc(T, F32) for _ in range(5)] for _ in range(NS)]
        r_tA = [[Res() for _ in range(5)] for _ in range(NS)]
        yb = xn
        r_yb = r_xn
        sq2 = sq
        r_sq2 = r_sq
        pslots = [self.full(b) for b in range(4)]
        gslots = [self.full(4), self.full(5)]
        nslot = self.full(6)
        n2slot = self.full(7)
        cnt = {"p": 0, "g": 0, "t": 0}
        P.op("pool", lambda e: e.memset(ubuf, 0.0), writes=r_ub)
        P.op("pool", lambda e: e.memset(hst, 0.0), writes=[r_hst])

        def vcol(v, c):
            return V[:, v, c:c + 1]

        def dcol(v, c):
            return DV[:, v, c:c + 1]

        def load(i):
            P.dma("sp", xts[i % 2], src.ap[:, :, i * T:(i + 1) * T], reads=src.blocks(i * T, T), writes=[r_xts[i % 2]])

        def tile(i, xt, r_xt):
            xf = xt.rearrange("p c t -> p (c t)")
            sf = sq.rearrange("p c t -> p (c t)")
            P.op("act", lambda e: e.activation(out=sf, in_=xf, func=AF.Square), reads=[r_xt], writes=[r_sq])
            self.rstd_from_sq(sq, r_sq, 8, T, nslot, ms, r_ms, rstd, r_rstd, float(D))
            for c in range(8):
                P.op("dve", lambda e, c=c: e.scalar_tensor_tensor(
                    out=xn[:, c, :], in0=xt[:, c, :], scalar=self.gcol(layer, 0, c), in1=rstd,
                    op0=ALU.mult, op1=ALU.mult), reads=[r_xt, r_rstd], writes=[r_xn])
            for m in list(range(8, 16)) + list(range(8)):
                ps, rps = pslots[cnt["p"] % 4]
                cnt["p"] += 1
                for kc in range(8):
                    P.op("pe", lambda e, kc=kc, m=m, ps=ps: e.matmul(
                        ps, Win[:, kc, m * 128:(m + 1) * 128], xn[:, kc, :], start=(kc == 0), stop=(kc == 7)),
                        reads=[rWin[m // 2], r_xn], writes=rps, inc=(kc == 7))
                if m < 8:
                    P.op("act", lambda e, m=m, ps=ps: e.activation(out=gate[:, m, :], in_=ps, func=AF.Gelu_apprx_tanh),
                         reads=rps, writes=[r_gate[m]])
                else:
                    c = m - 8
                    P.op("act", lambda e, c=c, ps=ps: e.activation(out=ubuf[:, c, 4:4 + T], in_=ps, func=AF.Copy),
                         reads=rps, writes=[r_ub[c]])
                    P.op("act", lambda e, c=c, ps=ps: e.activation(out=uc[:, c, :], in_=ps, func=AF.Identity,
                                                                   scale=vcol(3, c), bias=vcol(4, c)),
                         reads=rps + [rW], writes=[r_uc[c]])
                    for k in range(3):
                        P.op("dve", lambda e, c=c, k=k: e.scalar_tensor_tensor(
                            out=uc[:, c, :], in0=ubuf[:, c, 1 + k:1 + k + T], scalar=vcol(k, c), in1=uc[:, c, :],
                            op0=ALU.mult, op1=ALU.add), reads=[r_ub[c], r_uc[c], rW], writes=[r_uc[c]])
                    P.op("pool", lambda e, c=c: e.tensor_copy(out=ubuf[:, c, 1:4], in_=ubuf[:, c, T + 1:T + 4]),
                         reads=[r_ub[c]], writes=[r_ub[c]])
                    P.op("pool", lambda e, c=c: e.tensor_copy(out=ucb[:, c, :], in_=uc[:, c, :]),
                         reads=[r_uc[c]], writes=[r_ucb[c]])
            for c in range(8):
                s = cnt["t"] % NS
                cnt["t"] += 1
                th, aa, a2, thx, bb = tA[s]
                r_th, r_aa, r_a2, r_thx, r_bb = r_tA[s]
                psr, rpsr = gslots[0]
                psx, rpsx = gslots[1]
                P.op("pe", lambda e, c=c, psr=psr: e.matmul(psr, Wa[:, c, :], ucb[:, c, :], start=True, stop=True),
                     reads=[rW, r_ucb[c]], writes=rpsr)
                P.op("pe", lambda e, c=c, psx=psx: e.matmul(psx, Wx[:, c, :], ucb[:, c, :], start=True, stop=True),
                     reads=[rW, r_ucb[c]], writes=rpsx)
                P.op("act", lambda e, c=c, th=th, psr=psr: e.activation(out=th, in_=psr, func=AF.Tanh, scale=0.5, bias=dcol(0, c)),
                     reads=rpsr + [rW], writes=[r_th])
                P.op("act", lambda e, c=c, thx=thx, psx=psx: e.activation(out=thx, in_=psx, func=AF.Tanh, scale=0.5, bias=dcol(1, c)),
                     reads=rpsx + [rW], writes=[r_thx])
                P.op("act", lambda e, c=c, th=th, aa=aa: e.activation(out=aa, in_=th, func=AF.Exp, scale=dcol(2, c), bias=dcol(2, c)),
                     reads=[r_th, rW], writes=[r_aa])
                P.op("act", lambda e, c=c, th=th, a2=a2: e.activation(out=a2, in_=th, func=AF.Exp, scale=dcol(3, c), bias=dcol(3, c)),
                     reads=[r_th, rW], writes=[r_a2])
                P.op("act", lambda e, a2=a2: e.activation(out=a2, in_=a2, func=AF.Ln, scale=-1.0, bias=self.epsc[:, 1:2]),
                     reads=[r_a2], writes=[r_a2])
                P.op("act", lambda e, a2=a2: e.activation(out=a2, in_=a2, func=AF.Exp, scale=0.5),
                     reads=[r_a2], writes=[r_a2])
                P.op("dve", lambda e, c=c, thx=thx, bb=bb: e.scalar_tensor_tensor(
                    out=bb, in0=thx, scalar=1.0, in1=uc[:, c, :], op0=ALU.add, op1=ALU.mult),
                    reads=[r_thx, r_uc[c]], writes=[r_bb])
                P.op("dve", lambda e, a2=a2, bb=bb: e.scalar_tensor_tensor(
                    out=bb, in0=bb, scalar=0.5, in1=a2, op0=ALU.mult, op1=ALU.mult),
                    reads=[r_bb, r_a2], writes=[r_bb])
                P.op("dve", lambda e, c=c, aa=aa, bb=bb: e.tensor_tensor_scan(
                    out=hb[:, c, :], data0=aa, data1=bb, initial=hst[:, c:c + 1], op0=ALU.mult, op1=ALU.add),
                    reads=[r_aa, r_bb, r_hst], writes=[r_h[c]])
            P.op("dve", lambda e: e.tensor_copy(out=hst, in_=hb[:, :, T - 1]), reads=r_h, writes=[r_hst])
            P.op("dve", lambda e: e.tensor_tensor(out=yb.rearrange("p c t -> p (c t)"), in0=hb.rearrange("p c t -> p (c t)"),
                                                  in1=gate.rearrange("p c t -> p (c t)"), op=ALU.mult),
                 reads=r_h + r_gate, writes=[r_yb])
            for m in range(8):
                ps, rps = pslots[cnt["p"] % 4]
                cnt["p"] += 1
                for kc in range(8):
                    P.op("pe", lambda e, kc=kc, m=m, ps=ps: e.matmul(
                        ps, Wout[:, kc, m * 128:(m + 1) * 128], yb[:, kc, :], start=(kc == 0), stop=(kc == 7)),
                        reads=[rW, r_yb], writes=rps, inc=(kc == 7))
                P.op("act", lambda e, m=m, ps=ps: e.activation(out=hout[:, m, :], in_=ps, func=AF.Copy),
                     reads=rps, writes=[r_hout[m]])
                P.op("act", lambda e, m=m, ps=ps: e.activation(out=sq2[:, m, :], in_=ps, func=AF.Square),
                     reads=rps, writes=[r_sq2])
            self.post_norm_residual(layer, 1, hout, r_hout, sq2, r_sq2, xt, r_xt, T, n2slot, ms, r_ms, rstd, r_rstd)
            P.dma("sp", dst.ap[:, :, i * T:(i + 1) * T], xt, reads=[r_xt], writes=dst.blocks(i * T, T))
        load(0)
        for i in range(NT):
            if i + 1 < NT:
                load(i + 1)
            tile(i, xts[i % 2], r_xts[i % 2])
        P.barrier()

    def post_norm_residual(self, layer, j, hout, r_hout, sq2, r_sq2, xt, r_xt, T, slot, ms, r_ms, rstd, r_rstd):
        P = self.P
        self.rstd_from_sq(sq2, r_sq2, 8, T, slot, ms, r_ms, rstd, r_rstd, float(D))
        for c in range(8):
            P.op("dve", lambda e, c=c: e.scalar_tensor_tensor(
                out=hout[:, c, :], in0=hout[:, c, :], scalar=self.gcol(layer, j, c), in1=rstd,
                op0=ALU.mult, op1=ALU.mult), reads=[r_hout[c], r_rstd], writes=[r_hout[c]])
            P.op("pool", lambda e, c=c: e.tensor_tensor(out=xt[:, c, :], in0=xt[:, c, :], in1=hout[:, c, :], op=ALU.add),
                 reads=[r_hout[c], r_xt], writes=[r_xt])


    def sincos(self, th, out_sin, out_cos, tmp, r, eng="dve"):
        P = self.P
        MAGIC = 12582912.0
        TWO_PI = 6.283185307179586
        for (shift, dst) in ((0.0, out_sin), (1.5707963267948966, out_cos)):
            P.op(eng, lambda e, shift=shift: e.tensor_scalar(out=tmp, in0=th, scalar1=shift, scalar2=1.0 / TWO_PI, op0=ALU.add, op1=ALU.mult),
                 reads=[r], writes=[r])
            P.op(eng, lambda e: e.tensor_scalar(out=tmp, in0=tmp, scalar1=MAGIC, scalar2=None, op0=ALU.add), reads=[r], writes=[r])
            P.op(eng, lambda e: e.tensor_scalar(out=tmp, in0=tmp, scalar1=-MAGIC, scalar2=-TWO_PI, op0=ALU.add, op1=ALU.mult), reads=[r], writes=[r])
            P.op(eng, lambda e, shift=shift, dst=dst: e.scalar_tensor_tensor(out=dst, in0=th, scalar=shift, in1=tmp, op0=ALU.add, op1=ALU.add),
                 reads=[r], writes=[r])
            P.op("act", lambda e, dst=dst: e.activation(out=dst, in_=dst, func=AF.Sin), reads=[r], writes=[r])

    def s5a_phase(self, layer, src, w_in, su, sb):
        P, A = self.P, self.A
        T = 256
        NT = self.L // T
        A.off = self.persist_end
        rW = Res()
        Win = A.alloc((8, 1024), BF16)
        P.dma("pool", Win, w_in, writes=[rW])
        xt = [A.alloc((8, T), F32) for _ in range(2)]; r_xt = [Res() for _ in range(2)]
        sq = [A.alloc((8, T), BF16) for _ in range(2)]; r_sq = [Res() for _ in range(2)]
        xn = [A.alloc((8, T), BF16) for _ in range(2)]; r_xn = [Res() for _ in range(2)]
        ms = [A.alloc(T, F32) for _ in range(2)]; r_ms = [Res() for _ in range(2)]
        rstd = [A.alloc(T, F32) for _ in range(2)]; r_rstd = [Res() for _ in range(2)]
        uo = [A.alloc((8, T), F32) for _ in range(2)]; r_uo = [Res() for _ in range(2)]
        ubo = [A.alloc((8, T), BF16) for _ in range(2)]; r_ubo = [Res() for _ in range(2)]
        pslots = [self.full(b) for b in range(4)]
        nslots = [self.full(6), self.full(7)]
        cnt = {"p": 0}

        def load(i):
            b = i % 2
            P.dma("sp", xt[b], src.ap[:, :, i * T:(i + 1) * T], reads=src.blocks(i * T, T), writes=[r_xt[b]])

        def pre(i):
            b = i % 2
            P.op("act", lambda e: e.activation(out=sq[b].rearrange("p c t -> p (c t)"), in_=xt[b].rearrange("p c t -> p (c t)"), func=AF.Square),
                 reads=[r_xt[b]], writes=[r_sq[b]])
            self.rstd_from_sq(sq[b], r_sq[b], 8, T, nslots[b], ms[b], r_ms[b], rstd[b], r_rstd[b], float(D))
            for c in range(8):
                P.op("dve", lambda e, c=c: e.scalar_tensor_tensor(
                    out=xn[b][:, c, :], in0=xt[b][:, c, :], scalar=self.gcol(layer, 0, c), in1=rstd[b],
                    op0=ALU.mult, op1=ALU.mult), reads=[r_xt[b], r_rstd[b]], writes=[r_xn[b]])

        def mm(i):
            b = i % 2
            for m2 in range(4):
                ps, rps = pslots[cnt["p"] % len(pslots)]
                cnt["p"] += 1
                for hh in range(2):
                    mm_ = 2 * m2 + hh
                    for kc in range(8):
                        P.op("pe", lambda e, kc=kc, mm_=mm_, hh=hh, ps=ps: e.matmul(
                            ps[:, hh * T:(hh + 1) * T], Win[:, kc, mm_ * 128:(mm_ + 1) * 128], xn[b][:, kc, :],
                            start=(kc == 0), stop=(kc == 7)), reads=[rW, r_xn[b]], writes=rps, inc=(kc == 7 and hh == 1))
                o1 = uo[b][:, 2 * m2:2 * m2 + 2, :].rearrange("p c t -> p (c t)")
                o2 = ubo[b][:, 2 * m2:2 * m2 + 2, :].rearrange("p c t -> p (c t)")
                P.op("act", lambda e, o1=o1, ps=ps: e.activation(out=o1, in_=ps, func=AF.Copy), reads=rps, writes=[r_uo[b]])
                P.op("act", lambda e, o2=o2, ps=ps: e.activation(out=o2, in_=ps, func=AF.Copy), reads=rps, writes=[r_ubo[b]])
            P.dma("sp", su.ap[:, :, i * T:(i + 1) * T], uo[b], reads=[r_uo[b]])
            P.dma("sp", sb.ap[:, :, i * T:(i + 1) * T], ubo[b], reads=[r_ubo[b]])

        load(0)
        pre(0)
        if NT > 1:
            load(1)
        for i in range(NT):
            if i + 1 < NT:
                pre(i + 1)
            mm(i)
            if i + 2 < NT:
                load(i + 2)
        P.barrier()

    def s5c_phase(self, layer, src, dst, w_out, sy):
        P, A = self.P, self.A
        T = 256
        NT = self.L // T
        A.off = self.persist_end
        rW = Res()
        Wout = A.alloc((8, 2048), BF16)
        for hh in range(2):
            P.dma("pool", Wout[:, :, hh * 1024:(hh + 1) * 1024], w_out[:, :, hh * 1024:(hh + 1) * 1024], writes=[rW])
        xt = [A.alloc((8, T), F32) for _ in range(2)]; r_xt = [Res() for _ in range(2)]
        yb = [A.alloc((8, T), BF16) for _ in range(2)]; r_yb = [Res() for _ in range(2)]
        hout = [A.alloc((8, T), F32) for _ in range(2)]; r_hout = [[Res() for _ in range(8)] for _ in range(2)]
        sq2 = [A.alloc((8, T), BF16) for _ in range(2)]; r_sq2 = [Res() for _ in range(2)]
        tg = [A.alloc(T, F32) for _ in range(3)]; r_tg = [Res() for _ in range(3)]
        ms = [A.alloc(T, F32) for _ in range(2)]; r_ms = [Res() for _ in range(2)]
        rstd = [A.alloc(T, F32) for _ in range(2)]; r_rstd = [Res() for _ in range(2)]
        pslots = [self.full(b) for b in range(5)]
        nslots = [self.full(6), self.full(7)]
        cnt = {"p": 0, "t": 0}

        def load(i):
            b = i % 2
            P.dma("sp", yb[b], sy.ap[:, :, i * T:(i + 1) * T], writes=[r_yb[b]])
            P.dma("sp", xt[b], src.ap[:, :, i * T:(i + 1) * T], reads=src.blocks(i * T, T), writes=[r_xt[b]])

        def body(i):
            b = i % 2
            for mo in range(8):
                ps, rps = pslots[cnt["p"] % len(pslots)]
                cnt["p"] += 1
                for hh in range(2):
                    col = (hh * 8 + mo) * 128
                    for kc in range(8):
                        P.op("pe", lambda e, kc=kc, col=col, hh=hh, ps=ps: e.matmul(
                            ps[:, hh * T:(hh + 1) * T], Wout[:, kc, col:col + 128], yb[b][:, kc, :],
                            start=(kc == 0), stop=(kc == 7)), reads=[rW, r_yb[b]], writes=rps, inc=(kc == 7 and hh == 1))
                yy = cnt["t"] % 3
                cnt["t"] += 1
                P.op("act", lambda e, yy=yy, ps=ps: e.activation(out=tg[yy], in_=ps[:, T:2 * T], func=AF.Tanh, scale=0.5),
                     reads=rps, writes=[r_tg[yy]])
                P.op("dve", lambda e, yy=yy, ps=ps: e.scalar_tensor_tensor(
                    out=tg[yy], in0=tg[yy], scalar=1.0, in1=ps[:, 0:T], op0=ALU.add, op1=ALU.mult),
                    reads=rps + [r_tg[yy]], writes=[r_tg[yy]])
                P.op("act", lambda e, mo=mo, yy=yy: e.activation(out=hout[b][:, mo, :], in_=tg[yy], func=AF.Copy, scale=0.5),
                     reads=[r_tg[yy]], writes=[r_hout[b][mo]])
                P.op("act", lambda e, mo=mo, yy=yy: e.activation(out=sq2[b][:, mo, :], in_=tg[yy], func=AF.Square, scale=0.5),
                     reads=[r_tg[yy]], writes=[r_sq2[b]])
            self.post_norm_residual(layer, 1, hout[b], r_hout[b], sq2[b], r_sq2[b], xt[b], r_xt[b], T, nslots[b], ms[b], r_ms[b], rstd[b], r_rstd[b])
            P.dma("sp", dst.ap[:, :, i * T:(i + 1) * T], xt[b], reads=[r_xt[b]], writes=dst.blocks(i * T, T))

        load(0)
        for i in range(NT):
            if i + 1 < NT:
                load(i + 1)
            body(i)
        P.barrier()

    def s5b_phase(self, layer, su, sb, sy, par, par_bc, bpad_re, bpad_im, cpad_re, cpad_im, dvec):
        P, A = self.P, self.A
        T = 256
        NT = self.L // T
        A.off = self.persist_end
        rW = Res()
        Bre = A.alloc((32, 128), BF16)
        Bim = A.alloc((32, 128), BF16)
        Cre = A.alloc((32, 128), BF16)
        Cimn = A.alloc((32, 128), BF16)
        cosT = A.alloc((32, T), F32)
        sinT = A.alloc((32, T), F32)
        Dv = A.alloc(8, F32)
        SP = A.alloc((12, 32), F32)
        RT = A.alloc((2, 32), F32)
        GL = A.alloc((2, 32), F32)
        INIT = A.alloc((2, 32), F32)
        base = A.off
        P.dma("pool", Cre, cpad_re, writes=[rW])
        P.dma("pool", Cimn, cpad_im, writes=[rW])
        P.op("pool", lambda e: e.tensor_scalar(out=Cimn.rearrange("p a b -> p (a b)"), in0=Cimn.rearrange("p a b -> p (a b)"),
                                               scalar1=-1.0, scalar2=None, op0=ALU.mult), reads=[rW], writes=[rW])
        P.dma("sp", Dv, dvec, writes=[rW])
        P.dma("sp", SP[:, 0:3, :], par, writes=[rW])
        Q = 1024
        tl = [A.alloc(Q, F32) for _ in range(14)]
        rq = Res()
        for qi in range(4):
            self.s5_bbar_quarter(qi, Q, tl, rq, rW, par_bc, bpad_re, bpad_im, Bre, Bim)
        sp = [SP[:, i, :] for i in range(12)]
        self.s5_disc(sp[0], sp[1], sp[2], sp[3], sp[4], sp[5], sp[6], sp[7], sp[8], sp[9], sp[10], sp[11], rW, unit=True)
        MAG = sp[3]
        rm_re, rm_im = sp[6], sp[5]
        P.barrier()
        A.off = base
        ta = A.alloc((32, 128), F32)
        tb = A.alloc((32, 128), F32)
        P.op("dve", lambda e: e.memset(cosT[:, :, 0:1], 1.0), writes=[rW])
        P.op("dve", lambda e: e.memset(sinT[:, :, 0:1], 0.0), writes=[rW])
        m = 1
        while m <= T // 2:
            bre_ = rm_re.unsqueeze(2).to_broadcast([128, 32, m])
            bim_ = rm_im.unsqueeze(2).to_broadcast([128, 32, m])
            P.op("dve", lambda e, m=m, b=bre_: e.tensor_tensor(out=ta[:, :, 0:m], in0=cosT[:, :, 0:m], in1=b, op=ALU.mult), reads=[rW], writes=[rW])
            P.op("dve", lambda e, m=m, b=bim_: e.tensor_tensor(out=tb[:, :, 0:m], in0=sinT[:, :, 0:m], in1=b, op=ALU.mult), reads=[rW], writes=[rW])
            P.op("dve", lambda e, m=m: e.tensor_tensor(out=cosT[:, :, m:2 * m], in0=ta[:, :, 0:m], in1=tb[:, :, 0:m], op=ALU.subtract), reads=[rW], writes=[rW])
            P.op("dve", lambda e, m=m, b=bim_: e.tensor_tensor(out=ta[:, :, 0:m], in0=cosT[:, :, 0:m], in1=b, op=ALU.mult), reads=[rW], writes=[rW])
            P.op("dve", lambda e, m=m, b=bre_: e.tensor_tensor(out=tb[:, :, 0:m], in0=sinT[:, :, 0:m], in1=b, op=ALU.mult), reads=[rW], writes=[rW])
            P.op("dve", lambda e, m=m: e.tensor_tensor(out=sinT[:, :, m:2 * m], in0=ta[:, :, 0:m], in1=tb[:, :, 0:m], op=ALU.add), reads=[rW], writes=[rW])
            P.op("dve", lambda e: e.tensor_tensor(out=sp[7], in0=rm_re, in1=rm_re, op=ALU.mult), reads=[rW], writes=[rW])
            P.op("dve", lambda e: e.tensor_tensor(out=sp[8], in0=rm_im, in1=rm_im, op=ALU.mult), reads=[rW], writes=[rW])
            P.op("dve", lambda e: e.scalar_tensor_tensor(out=rm_im, in0=rm_re, scalar=2.0, in1=rm_im, op0=ALU.mult, op1=ALU.mult), reads=[rW], writes=[rW])
            P.op("dve", lambda e: e.tensor_tensor(out=rm_re, in0=sp[7], in1=sp[8], op=ALU.subtract), reads=[rW], writes=[rW])
            m *= 2
        P.op("dve", lambda e: e.tensor_copy(out=RT[:, 0, :], in_=rm_re), reads=[rW], writes=[rW])
        P.op("dve", lambda e: e.tensor_copy(out=RT[:, 1, :], in_=rm_im), reads=[rW], writes=[rW])
        P.op("dve", lambda e: e.memset(INIT, 0.0), writes=[rW])
        P.barrier()
        A.off = base
        u2 = [A.alloc((8, T), F32) for _ in range(2)]
        ub2 = [A.alloc((8, T), BF16) for _ in range(2)]
        r_ld = [Res() for _ in range(2)]
        yb2 = [A.alloc((8, T), BF16) for _ in range(2)]; r_yb2 = [Res() for _ in range(2)]
        NB = 3
        bu = [A.alloc(3 * T, F32) for _ in range(NB)]; r_bu = [Res() for _ in range(NB)]
        t1 = [A.alloc(2 * T, F32) for _ in range(NB)]; r_t1 = [Res() for _ in range(NB)]
        t2 = [A.alloc(2 * T, F32) for _ in range(NB)]; r_t2 = [Res() for _ in range(NB)]
        G = [A.alloc(3 * T, F32) for _ in range(NB)]; r_G = [Res() for _ in range(NB)]
        t3 = [A.alloc(2 * T, F32) for _ in range(NB)]; r_t3 = [Res() for _ in range(NB)]
        t4 = [A.alloc(2 * T, F32) for _ in range(NB)]; r_t4 = [Res() for _ in range(NB)]
        hb = [A.alloc((4, 2, T), BF16) for _ in range(2)]; r_hb = [Res() for _ in range(2)]
        yv = [A.alloc(T, F32) for _ in range(2)]; r_yv = [Res() for _ in range(2)]
        r_GL = Res()
        r_INIT = Res()
        bslots = [self.full(0), self.full(1), self.full(2)]
        yslots = [self.full(4), self.full(5)]

        def b2(ap2):
            return ap2.unsqueeze(1).to_broadcast([128, 2, T])

        def v2(ap, off):
            return ap[:, off:off + 2 * T].rearrange("p (a b) -> p a b", a=2)

        def tile_chunks(i, u, ub, yb, r_u, r_ub, r_yb):
            def s1(k):
                kc = k // 4
                s = k % NB
                ps, rps = bslots[k % 3]
                P.op("pe", lambda e: e.matmul(ps[:, 0:T], Bre[:, k, :], ub[:, kc, :], start=True, stop=True),
                     reads=[rW, r_ub[kc]], writes=rps, inc=False)
                P.op("pe", lambda e: e.matmul(ps[:, T:2 * T], Bim[:, k, :], ub[:, kc, :], start=True, stop=True),
                     reads=[rW, r_ub[kc]], writes=rps)
                P.op("act", lambda e: e.activation(out=bu[s][:, 0:2 * T], in_=ps, func=AF.Copy), reads=rps, writes=[r_bu[s]])
                P.op("act", lambda e: e.activation(out=bu[s][:, 2 * T:3 * T], in_=ps[:, 0:T], func=AF.Copy, scale=-1.0), reads=rps, writes=[r_bu[s]])
                P.op("dve", lambda e: e.tensor_tensor(out=v2(t1[s], 0), in0=v2(bu[s], 0), in1=b2(cosT[:, k, :]), op=ALU.mult),
                     reads=[r_bu[s], rW], writes=[r_t1[s]])
                P.op("dve", lambda e: e.tensor_tensor(out=v2(t2[s], 0), in0=v2(bu[s], T), in1=b2(sinT[:, k, :]), op=ALU.mult),
                     reads=[r_bu[s], rW], writes=[r_t2[s]])
                P.op("dve", lambda e: e.tensor_tensor(out=t1[s], in0=t1[s], in1=t2[s], op=ALU.add),
                     reads=[r_t1[s], r_t2[s]], writes=[r_t1[s]])

            def s2a(k):
                kk = k % 4
                s = k % NB
                hs = (k // 4) % 2
                magb = MAG[:, k:k + 1].to_broadcast([128, T])
                P.op("dve", lambda e: e.tensor_tensor_scan(
                    out=G[s][:, T:2 * T], data0=magb, data1=t1[s][:, 0:T], initial=INIT[:, 0, k:k + 1], op0=ALU.mult, op1=ALU.add),
                    reads=[r_t1[s], rW, r_INIT], writes=[r_G[s]])
                P.op("dve", lambda e: e.tensor_tensor_scan(
                    out=G[s][:, 2 * T:3 * T], data0=magb, data1=t1[s][:, T:2 * T], initial=INIT[:, 1, k:k + 1], op0=ALU.mult, op1=ALU.add),
                    reads=[r_t1[s], rW, r_INIT], writes=[r_G[s]])
                P.op("act", lambda e: e.activation(out=G[s][:, 0:T], in_=G[s][:, 2 * T:3 * T], func=AF.Copy, scale=-1.0), reads=[r_G[s]], writes=[r_G[s]])
                P.op("act", lambda e: e.activation(out=GL[:, :, k], in_=v2(G[s], T)[:, :, T - 1], func=AF.Copy),
                     reads=[r_G[s]], writes=[r_GL])

            def s2b(k):
                s = k % NB
                P.op("dve", lambda e: e.tensor_tensor(out=v2(t3[s], 0), in0=v2(G[s], T), in1=b2(cosT[:, k, :]), op=ALU.mult),
                     reads=[r_G[s], rW], writes=[r_t3[s]])
                P.op("dve", lambda e: e.tensor_tensor(out=v2(t4[s], 0), in0=v2(G[s], 0), in1=b2(sinT[:, k, :]), op=ALU.mult),
                     reads=[r_G[s], rW], writes=[r_t4[s]])

            def s3(k):
                kk = k % 4
                s = k % NB
                hs = (k // 4) % 2
                P.op("dve", lambda e: e.tensor_tensor(out=hb[hs][:, kk, :, :], in0=v2(t3[s], 0), in1=v2(t4[s], 0), op=ALU.add),
                     reads=[r_t3[s], r_t4[s]], writes=[r_hb[hs]])
                if kk == 3:
                    mo = k // 4
                    psy, rpsy = yslots[mo % 2]
                    for k4 in range(4):
                        kq = mo * 4 + k4
                        P.op("pe", lambda e, kq=kq, k4=k4, hs=hs, psy=psy: e.matmul(
                            psy[:, 0:T], Cre[:, kq, :], hb[hs][:, k4, 0, :], start=(k4 == 0), stop=False),
                            reads=[rW, r_hb[hs]], writes=rpsy, inc=False)
                        P.op("pe", lambda e, kq=kq, k4=k4, hs=hs, psy=psy: e.matmul(
                            psy[:, 0:T], Cimn[:, kq, :], hb[hs][:, k4, 1, :], start=False, stop=(k4 == 3)),
                            reads=[rW, r_hb[hs]], writes=rpsy, inc=(k4 == 3))
                    yy = mo % 2
                    P.op("dve", lambda e, mo=mo, yy=yy, psy=psy: e.scalar_tensor_tensor(
                        out=yv[yy], in0=u[:, mo, :], scalar=Dv[:, mo:mo + 1], in1=psy[:, 0:T], op0=ALU.mult, op1=ALU.add),
                        reads=rpsy + [r_u[mo], rW], writes=[r_yv[yy]])
                    P.op("act", lambda e, mo=mo, yy=yy: e.activation(out=yb[:, mo, :], in_=yv[yy], func=AF.Gelu_apprx_tanh),
                         reads=[r_yv[yy]], writes=[r_yb])

            s1(0)
            s1(1)
            s2a(0)
            s2b(0)
            for k in range(32):
                if k + 2 < 32:
                    s1(k + 2)
                if k + 1 < 32:
                    s2a(k + 1)
                s3(k)
                if k + 1 < 32:
                    s2b(k + 1)
            P.op("dve", lambda e: e.tensor_tensor(out=INIT[:, 0, :], in0=RT[:, 0, :], in1=GL[:, 0, :], op=ALU.mult), reads=[r_GL, rW], writes=[r_INIT])
            P.op("dve", lambda e: e.tensor_tensor(out=INIT[:, 1, :], in0=RT[:, 1, :], in1=GL[:, 1, :], op=ALU.mult), reads=[r_GL, rW], writes=[r_INIT])
            P.op("dve", lambda e: e.tensor_tensor(out=INIT[:, 0, :], in0=INIT[:, 0, :], in1=INIT[:, 1, :], op=ALU.subtract), reads=[r_INIT], writes=[r_INIT])
            P.op("dve", lambda e: e.tensor_tensor(out=INIT[:, 1, :], in0=RT[:, 0, :], in1=GL[:, 1, :], op=ALU.mult), reads=[r_GL, rW], writes=[r_INIT])
            P.op("dve", lambda e: e.tensor_tensor(out=GL[:, 0, :], in0=RT[:, 1, :], in1=GL[:, 0, :], op=ALU.mult), reads=[r_GL, rW], writes=[r_GL])
            P.op("dve", lambda e: e.tensor_tensor(out=INIT[:, 1, :], in0=INIT[:, 1, :], in1=GL[:, 0, :], op=ALU.add), reads=[r_GL, r_INIT], writes=[r_INIT])
            P.dma("sp", sy.ap[:, :, i * T:(i + 1) * T], yb, reads=[r_yb])

        def load(i):
            b_ = i % 2
            P.dma("sp", u2[b_], su.ap[:, :, i * T:(i + 1) * T], writes=[r_ld[b_]])
            P.dma("sp", ub2[b_], sb.ap[:, :, i * T:(i + 1) * T], writes=[r_ld[b_]])

        load(0)
        for i in range(NT):
            if i + 1 < NT:
                load(i + 1)
            rl = r_ld[i % 2]
            tile_chunks(i, u2[i % 2], ub2[i % 2], yb2[i % 2], [rl] * 8, [rl] * 8, r_yb2[i % 2])
        P.barrier()

    def s5_bbar_quarter(self, qi, Q, tl, rq, rW, par_bc, bpad_re, bpad_im, Bre, Bim):
        P = self.P
        cs = slice(qi * Q, (qi + 1) * Q)
        lre, lim, lst, bre, bim, t0, t1, t2, t3, t4, t5, t6, t7, t8 = tl
        P.dma("sp", lre, par_bc[0:1, cs].partition_broadcast(128), writes=[rq])
        P.dma("sp", lim, par_bc[1:2, cs].partition_broadcast(128), writes=[rq])
        P.dma("sp", lst, par_bc[2:3, cs].partition_broadcast(128), writes=[rq])
        P.dma("sp", bre, bpad_re.rearrange("p a b -> p (a b)")[:, cs], writes=[rq])
        P.dma("sp", bim, bpad_im.rearrange("p a b -> p (a b)")[:, cs], writes=[rq])
        self.s5_disc(lre, lim, lst, t0, t1, t2, t3, t4, t5, t6, t7, t8, rq)
        bo_re = Bre.rearrange("p a b -> p (a b)")[:, cs]
        bo_im = Bim.rearrange("p a b -> p (a b)")[:, cs]
        P.op("dve", lambda e: e.tensor_tensor(out=t0, in0=t5, in1=bre, op=ALU.mult), reads=[rq], writes=[rq])
        P.op("dve", lambda e: e.tensor_tensor(out=t1, in0=t6, in1=bim, op=ALU.mult), reads=[rq], writes=[rq])
        P.op("dve", lambda e: e.tensor_tensor(out=bo_re, in0=t0, in1=t1, op=ALU.subtract), reads=[rq], writes=[rq, rW])
        P.op("dve", lambda e: e.tensor_tensor(out=t0, in0=t5, in1=bim, op=ALU.mult), reads=[rq], writes=[rq])
        P.op("dve", lambda e: e.tensor_tensor(out=t1, in0=t6, in1=bre, op=ALU.mult), reads=[rq], writes=[rq])
        P.op("dve", lambda e: e.tensor_tensor(out=bo_im, in0=t0, in1=t1, op=ALU.add), reads=[rq], writes=[rq, rW])

    def s5_disc(self, lre, lim, lst, t0, t1, t2, t3, t4, t5, t6, t7, t8, r, unit=False):
        P = self.P
        P.op("dve", lambda e: e.tensor_scalar(out=lre, in0=lre, scalar1=-1e-4, scalar2=None, op0=ALU.min), reads=[r], writes=[r])
        P.op("act", lambda e: e.activation(out=lst, in_=lst, func=AF.Exp), reads=[r], writes=[r])
        P.op("dve", lambda e: e.tensor_tensor(out=t0, in0=lre, in1=lst, op=ALU.mult), reads=[r], writes=[r])
        P.op("act", lambda e: e.activation(out=t0, in_=t0, func=AF.Exp), reads=[r], writes=[r])
        P.op("dve", lambda e: e.tensor_tensor(out=t1, in0=lim, in1=lst, op=ALU.mult), reads=[r], writes=[r])
        self.sincos(t1, t2, t3, t4, r)
        if unit:
            return
        P.op("dve", lambda e: e.tensor_tensor(out=t2, in0=t2, in1=t0, op=ALU.mult), reads=[r], writes=[r])
        P.op("dve", lambda e: e.tensor_tensor(out=t3, in0=t3, in1=t0, op=ALU.mult), reads=[r], writes=[r])
        P.op("dve", lambda e: e.tensor_scalar(out=t3, in0=t3, scalar1=-1.0, scalar2=None, op0=ALU.add), reads=[r], writes=[r])
        P.op("dve", lambda e: e.tensor_tensor(out=t4, in0=lre, in1=lre, op=ALU.mult), reads=[r], writes=[r])
        P.op("dve", lambda e: e.tensor_tensor(out=t7, in0=lim, in1=lim, op=ALU.mult), reads=[r], writes=[r])
        P.op("dve", lambda e: e.tensor_tensor(out=t4, in0=t4, in1=t7, op=ALU.add), reads=[r], writes=[r])
        P.op("dve", lambda e: e.reciprocal(out=t4, in_=t4), reads=[r], writes=[r])
        P.op("dve", lambda e: e.tensor_tensor(out=t5, in0=t3, in1=lre, op=ALU.mult), reads=[r], writes=[r])
        P.op("dve", lambda e: e.tensor_tensor(out=t7, in0=t2, in1=lim, op=ALU.mult), reads=[r], writes=[r])
        P.op("dve", lambda e: e.tensor_tensor(out=t5, in0=t5, in1=t7, op=ALU.add), reads=[r], writes=[r])
        P.op("dve", lambda e: e.tensor_tensor(out=t5, in0=t5, in1=t4, op=ALU.mult), reads=[r], writes=[r])
        P.op("dve", lambda e: e.tensor_tensor(out=t6, in0=t2, in1=lre, op=ALU.mult), reads=[r], writes=[r])
        P.op("dve", lambda e: e.tensor_tensor(out=t7, in0=t3, in1=lim, op=ALU.mult), reads=[r], writes=[r])
        P.op("dve", lambda e: e.tensor_tensor(out=t6, in0=t6, in1=t7, op=ALU.subtract), reads=[r], writes=[r])
        P.op("dve", lambda e: e.tensor_tensor(out=t6, in0=t6, in1=t4, op=ALU.mult), reads=[r], writes=[r])


    def ssd_a_phase(self, layer, src, w_in, cw, zs, xs, bt, ct, dtt):
        P, A = self.P, self.A
        T = 512
        NT = self.L // T
        A.off = self.persist_end
        rW = Res()
        Win = A.alloc((8, 6176), BF16)
        rWin = [Res() for _ in range(8)]
        for kc in range(8):
            P.dma("pool", Win[:, :, kc * 772:(kc + 1) * 772], w_in[:, :, kc * 772:(kc + 1) * 772], writes=[rWin[kc]])
        CW = A.alloc((5, 32), F32)
        P.dma("sp", CW, cw, writes=[rW])
        H = A.alloc((32, 4), F32)
        r_H = [Res() for _ in range(32)]
        P.op("pool", lambda e: e.memset(H, 0.0), writes=r_H)
        xts = [A.alloc((8, T), F32) for _ in range(2)]; r_xts = [Res() for _ in range(2)]
        sq = A.alloc((8, T), BF16); r_sq = Res()
        xn = A.alloc((8, T), BF16); r_xn = Res()
        ms = A.alloc(T, F32); r_ms = Res()
        rstd = A.alloc(T, F32); r_rstd = Res()
        dtr = A.alloc((4, 32), F32); r_dtr = Res()
        NU = 3
        ubuf = [A.alloc(T + 4, F32) for _ in range(NU)]; r_ub = [Res() for _ in range(NU)]
        uc = [A.alloc(T, F32) for _ in range(NU)]; r_uc = [Res() for _ in range(NU)]
        NSG = 3
        stg = [A.alloc((4, T), BF16) for _ in range(NSG)]; r_stg = [Res() for _ in range(NSG)]
        pslots = [self.full(b) for b in range(5)]
        dslot = self.full(5)
        nslot = self.full(6)
        cnt = {"p": 0, "u": 0, "g": 0}

        def load(i):
            P.dma("sp", xts[i % 2], src.ap[:, :, i * T:(i + 1) * T], reads=src.blocks(i * T, T), writes=[r_xts[i % 2]])

        def tile(i, xt, r_xt):
            tsl = slice(i * T, (i + 1) * T)
            P.op("act", lambda e: e.activation(out=sq.rearrange("p c t -> p (c t)"), in_=xt.rearrange("p c t -> p (c t)"), func=AF.Square),
                 reads=[r_xt], writes=[r_sq])
            self.rstd_from_sq(sq, r_sq, 8, T, nslot, ms, r_ms, rstd, r_rstd, float(D))
            for c in range(8):
                P.op("dve", lambda e, c=c: e.scalar_tensor_tensor(
                    out=xn[:, c, :], in0=xt[:, c, :], scalar=self.gcol(layer, 0, c), in1=rstd,
                    op0=ALU.mult, op1=ALU.mult), reads=[r_xt, r_rstd], writes=[r_xn])
            def flush(m, sg):
                m0 = m - 3
                if m < 16:
                    dd, c0 = zs, m0
                elif m < 32:
                    dd, c0 = xs, m0 - 16
                elif m < 40:
                    dd, c0 = bt, m0 - 32
                else:
                    dd, c0 = ct, m0 - 40
                P.dma("sp", dd.ap[:, c0:c0 + 4, tsl], stg[sg], reads=[r_stg[sg]])

            def fin(m, sg, q4, ub_i):
                P.op("act", lambda e: e.activation(out=stg[sg][:, q4, :], in_=uc[ub_i], func=AF.Silu),
                     reads=[r_uc[ub_i]], writes=[r_stg[sg]])
                if q4 == 3:
                    flush(m, sg)

            def chunk_m(m, sg, q4):
                ps, rps = pslots[cnt["p"] % len(pslots)]
                cnt["p"] += 1
                for kc in range(8):
                    P.op("pe", lambda e, kc=kc: e.matmul(
                        ps, Win[:, kc, m * 128:(m + 1) * 128], xn[:, kc, :], start=(kc == 0), stop=(kc == 7)),
                        reads=[rWin[(m * 128) // 772], rWin[(m * 128 + 127) // 772], r_xn], writes=rps, inc=(kc == 7))
                if m < 16:
                    P.op("act", lambda e: e.activation(out=stg[sg][:, q4, :], in_=ps, func=AF.Silu),
                         reads=rps, writes=[r_stg[sg]])
                    if q4 == 3:
                        flush(m, sg)
                    return None
                cc = m - 16
                ub_i = cnt["u"] % NU
                cnt["u"] += 1
                ub_, uc_ = ubuf[ub_i], uc[ub_i]
                P.op("pool", lambda e: e.tensor_copy(out=ub_[:, 1:4], in_=H[:, cc, 0:3]),
                     reads=[r_H[cc]], writes=[r_ub[ub_i]])
                P.op("act", lambda e: e.activation(out=ub_[:, 4:4 + T], in_=ps, func=AF.Copy),
                     reads=rps, writes=[r_ub[ub_i]])
                P.op("act", lambda e: e.activation(out=uc_, in_=ps, func=AF.Identity,
                                                   scale=CW[:, 3, cc:cc + 1], bias=CW[:, 4, cc:cc + 1]),
                     reads=rps + [rW], writes=[r_uc[ub_i]])
                for k in range(3):
                    P.op("dve", lambda e, k=k: e.scalar_tensor_tensor(
                        out=uc_, in0=ub_[:, 1 + k:1 + k + T], scalar=CW[:, k, cc:cc + 1], in1=uc_,
                        op0=ALU.mult, op1=ALU.add), reads=[r_ub[ub_i], r_uc[ub_i], rW], writes=[r_uc[ub_i]])
                P.op("pool", lambda e: e.tensor_copy(out=H[:, cc, 0:3], in_=ub_[:, T + 1:T + 4]),
                     reads=[r_ub[ub_i]], writes=[r_H[cc]])
                return (m, sg, q4, ub_i)

            pend = None
            for m in range(48):
                sg = cnt["g"] % NSG
                q4 = m % 4
                if q4 == 3:
                    cnt["g"] += 1
                nxt = chunk_m(m, sg, q4)
                if pend is not None:
                    fin(*pend)
                pend = nxt
            if pend is not None:
                fin(*pend)
            psd, rpsd = dslot
            for j in range(4):
                for kc in range(8):
                    P.op("pe", lambda e, j=j, kc=kc: e.matmul(
                        psd[:, j * 32:(j + 1) * 32], xn[:, kc, j * 128:(j + 1) * 128], Win[:, kc, 6144:6176],
                        start=(kc == 0), stop=(kc == 7)), reads=[rWin[7], r_xn], writes=rpsd, inc=(kc == 7 and j == 3))
            P.op("act", lambda e: e.activation(out=dtr.rearrange("p a b -> p (a b)"), in_=psd[:, 0:128], func=AF.Copy),
                 reads=rpsd, writes=[r_dtr])
            P.dma("sp", dtt.ap[i * 4:(i + 1) * 4].rearrange("j p h -> p j h"), dtr, reads=[r_dtr])
        load(0)
        for i in range(NT):
            if i + 1 < NT:
                load(i + 1)
            tile(i, xts[i % 2], r_xts[i % 2])
        P.barrier()


    def ssd_b_phase(self, layer, src, dst, zs, xs, bt, ct, dtt, w_out, consts, hp_bc, dch, ngv):
        P, A = self.P, self.A
        T = 256
        NT = self.L // T
        A.off = self.persist_end
        rW = Res()
        Wout = A.alloc((16, 1024), BF16)
        for hh in range(2):
            P.dma("pool", Wout[:, hh * 8:(hh + 1) * 8, :], w_out[:, hh * 8:(hh + 1) * 8, :], writes=[rW])
        CF = A.alloc((4, 128), F32)
        P.dma("sp", CF, consts, writes=[rW])
        IDF, U, MS, ONES = CF[:, 0, :], CF[:, 1, :], CF[:, 2, :], CF[:, 3, :]
        identb = A.alloc(128, BF16)
        P.op("dve", lambda e: e.tensor_copy(out=identb, in_=IDF), reads=[rW], writes=[rW])
        Ub = A.alloc(128, BF16)
        MSb = A.alloc(128, BF16)
        ONESb = A.alloc(128, BF16)
        P.op("dve", lambda e: e.tensor_copy(out=Ub, in_=U), reads=[rW], writes=[rW])
        P.op("dve", lambda e: e.tensor_copy(out=MSb, in_=MS), reads=[rW], writes=[rW])
        P.op("dve", lambda e: e.tensor_copy(out=ONESb, in_=ONES), reads=[rW], writes=[rW])

        HB = A.alloc((2, 32), F32)
        P.dma("sp", HB[:, 0, :], hp_bc[0:1, :].partition_broadcast(128), writes=[rW])
        P.dma("sp", HB[:, 1, :], hp_bc[1:2, :].partition_broadcast(128), writes=[rW])
        P.op("act", lambda e: e.activation(out=HB[:, 1, :], in_=HB[:, 1, :], func=AF.Exp), reads=[rW], writes=[rW])
        P.op("dve", lambda e: e.tensor_scalar(out=HB[:, 1, :], in0=HB[:, 1, :], scalar1=-1.0, scalar2=None, op0=ALU.mult), reads=[rW], writes=[rW])
        DBIAS, ANEG = HB[:, 0, :], HB[:, 1, :]
        DCH = A.alloc(16, F32)
        NG = A.alloc(16, F32)
        P.dma("sp", DCH, dch, writes=[rW])
        P.dma("sp", NG, ngv, writes=[rW])
        diagD = A.alloc((16, 128), BF16)
        for m in range(16):
            P.op("dve", lambda e, m=m: e.tensor_scalar(out=diagD[:, m, :], in0=IDF, scalar1=DCH[:, m:m + 1], scalar2=None, op0=ALU.mult),
                 reads=[rW], writes=[rW])
        ent = A.alloc(2048, F32); r_ent = [Res() for _ in range(8)]
        entb = A.alloc(2048, BF16); r_entb = [Res() for _ in range(8)]
        P.op("pool", lambda e: e.memset(ent, 0.0), writes=r_ent)
        P.op("pool", lambda e: e.memset(entb, 0.0), writes=r_entb)
        zs_t = [A.alloc((16, T), BF16) for _ in range(2)]
        xs_t = [A.alloc((16, T), BF16) for _ in range(2)]
        bt_t = [A.alloc((8, T), BF16) for _ in range(2)]
        ct_t = [A.alloc((8, T), BF16) for _ in range(2)]
        dtr_t = [A.alloc((2, 32), F32) for _ in range(2)]
        xt = [A.alloc((8, T), F32) for _ in range(2)]
        r_ld = [Res() for _ in range(2)]
        r_xt = [Res() for _ in range(2)]
        smset = []
        for _ in range(2):
            t = [A.alloc(32, F32) for _ in range(8)]
            t += [A.alloc(32, BF16), A.alloc(32, BF16), A.alloc(32, F32)]
            smset.append(t)
        r_smset = [Res(), Res()]
        xdt2 = [A.alloc(2048, BF16) for _ in range(2)]; r_xdt2 = [Res(), Res()]
        xdtd2 = [A.alloc(2048, BF16) for _ in range(2)]; r_xdtd2 = [Res(), Res()]
        btok2 = [A.alloc(1024, BF16) for _ in range(2)]; r_btok2 = [Res(), Res()]
        NR = 2
        Rg = [A.alloc((4, 128), BF16) for _ in range(NR)]; r_Rg = [Res() for _ in range(NR)]
        Rl = [A.alloc((4, 128), BF16) for _ in range(NR)]
        Lx = [A.alloc((4, 128), F32) for _ in range(NR)]; r_Lx = [Res() for _ in range(NR)]
        Ex = [A.alloc((4, 128), F32) for _ in range(NR)]; r_Ex = [Res() for _ in range(NR)]
        cbm = [A.alloc(128, F32) for _ in range(NR)]; r_cbm = [Res() for _ in range(NR)]
        Mg = [A.alloc((4, 128), BF16) for _ in range(NR)]; r_Mg = [Res() for _ in range(NR)]
        Cd = [A.alloc((4, 128), BF16) for _ in range(NR)]; r_Cd = [Res() for _ in range(NR)]
        v = A.alloc((16, T), F32); r_v = [Res() for _ in range(8)]
        sqv = A.alloc((16, T), BF16); r_sqv = Res()
        vn = A.alloc((16, T), BF16); r_vn = [Res() for _ in range(8)]
        msg = A.alloc(T, F32); r_msg = Res()
        rsg = [A.alloc(T, F32) for _ in range(2)]; r_rsg = [Res() for _ in range(2)]
        hout = A.alloc((8, T), F32); r_hout = [Res() for _ in range(8)]
        sq2 = A.alloc((8, T), BF16); r_sq2 = Res()
        ms = A.alloc(T, F32); r_ms = Res()
        rstd = A.alloc(T, F32); r_rstd = Res()
        tslots = [self.full(0), self.full(1)]
        dtslot = self.full(2)
        lslot = self.full(3)
        aslot = self.full(4)
        yslot = self.full(5)
        sslot = self.full(6)
        cslot = self.full(7)
        cnt = {"t": 0, "r": 0}

        def load(i):
            b = i % 2
            tsl = slice(i * T, (i + 1) * T)
            P.dma("sp", zs_t[b], zs.ap[:, :, tsl], writes=[r_ld[b]])
            P.dma("sp", xs_t[b], xs.ap[:, :, tsl], writes=[r_ld[b]])
            P.dma("sp", bt_t[b], bt.ap[:, :, tsl], writes=[r_ld[b]])
            P.dma("sp", ct_t[b], ct.ap[:, :, tsl], writes=[r_ld[b]])
            P.dma("sp", dtr_t[b], dtt.ap[i * 2:(i + 1) * 2].rearrange("j p h -> p j h"), writes=[r_ld[b]])
            P.dma("sp", xt[b], src.ap[:, :, tsl], reads=src.blocks(i * T, T), writes=[r_xt[b]])

        def bc4(ap2):
            return ap2.unsqueeze(1).to_broadcast([128, 4, 128])

        def pre_a(i, c, pbi):
            b = i % 2
            cols = slice(c * 128, (c + 1) * 128)
            rl = [r_ld[b]]
            dv, av, dtv, dA, dsv, eav, dtds, lv, dAh, dAl, dAt = smset[pbi]
            r_sm = r_smset[pbi]
            xdt, xdtd, btok = xdt2[pbi], xdtd2[pbi], btok2[pbi]
            r_xdt, r_xdtd, r_btok = r_xdt2[pbi], r_xdtd2[pbi], r_btok2[pbi]
            P.op("dve", lambda e: e.tensor_tensor(out=dv, in0=dtr_t[b][:, c, :], in1=DBIAS, op=ALU.add), reads=rl + [rW], writes=[r_sm])
            P.op("dve", lambda e: e.scalar_tensor_tensor(out=av, in0=dv, scalar=-1.0, in1=dv, op0=ALU.mult, op1=ALU.max), reads=[r_sm], writes=[r_sm])
            P.op("act", lambda e: e.activation(out=av, in_=av, func=AF.Exp, scale=-1.0), reads=[r_sm], writes=[r_sm])
            P.op("act", lambda e: e.activation(out=lv, in_=av, func=AF.Ln, bias=1.0), reads=[r_sm], writes=[r_sm])
            P.op("dve", lambda e: e.scalar_tensor_tensor(out=dtv, in0=dv, scalar=0.0, in1=lv, op0=ALU.max, op1=ALU.add), reads=[r_sm], writes=[r_sm])
            P.op("dve", lambda e: e.tensor_tensor(out=dA, in0=dtv, in1=ANEG, op=ALU.mult), reads=[r_sm, rW], writes=[r_sm])
            P.op("dve", lambda e: e.tensor_copy(out=dAh, in_=dA), reads=[r_sm], writes=[r_sm])
            P.op("dve", lambda e: e.tensor_tensor(out=dAt, in0=dA, in1=dAh, op=ALU.subtract), reads=[r_sm], writes=[r_sm])
            P.op("dve", lambda e: e.tensor_copy(out=dAl, in_=dAt), reads=[r_sm], writes=[r_sm])
            psd, rpsd = dtslot
            P.op("pe", lambda e: e.matmul(psd[:, 0:32], MS, dA, start=True, stop=True), reads=[rW, r_sm], writes=rpsd, inc=False)
            P.op("pe", lambda e: e.matmul(psd[:, 32:64], ONES, dA, start=True, stop=True), reads=[rW, r_sm], writes=rpsd)
            P.op("act", lambda e: e.activation(out=dsv, in_=psd[:, 0:32], func=AF.Exp), reads=rpsd, writes=[r_sm])
            P.op("act", lambda e: e.activation(out=eav, in_=psd[:, 32:64], func=AF.Exp), reads=rpsd, writes=[r_sm])
            P.op("dve", lambda e: e.tensor_tensor(out=dtds, in0=dtv, in1=dsv, op=ALU.mult), reads=[r_sm], writes=[r_sm])

        def pre_b(i, c, pbi):
            b = i % 2
            cols = slice(c * 128, (c + 1) * 128)
            rl = [r_ld[b]]
            dv, av, dtv, dA, dsv, eav, dtds, lv, dAh, dAl, dAt = smset[pbi]
            r_sm = r_smset[pbi]
            xdt, xdtd, btok = xdt2[pbi], xdtd2[pbi], btok2[pbi]
            r_xdt, r_xdtd, r_btok = r_xdt2[pbi], r_xdtd2[pbi], r_btok2[pbi]
            for half in range(2):
                pst, rpst = tslots[cnt["t"] % 2]
                cnt["t"] += 1
                pb = pst.bitcast(BF16)
                for q in range(8):
                    m = half * 8 + q
                    P.op("pe", lambda e, m=m, q=q, pb=pb: e.transpose(pb[:, q * 128:(q + 1) * 128], xs_t[b][:, m, cols], identb),
                         reads=rl + [rW], writes=rpst, inc=(q == 7))
                pv = pb.rearrange("p (h d) -> p h d", h=16)
                hs = slice(half * 16, (half + 1) * 16)
                o1 = xdt[:, half * 1024:(half + 1) * 1024].rearrange("p (h d) -> p h d", h=16)
                o2 = xdtd[:, half * 1024:(half + 1) * 1024].rearrange("p (h d) -> p h d", h=16)
                P.op("dve", lambda e, pv=pv, o1=o1, hs=hs: e.tensor_tensor(out=o1, in0=pv, in1=dtv[:, hs].unsqueeze(2).to_broadcast([128, 16, 64]), op=ALU.mult),
                     reads=rpst + [r_sm], writes=[r_xdt])
                P.op("dve", lambda e, pv=pv, o2=o2, hs=hs: e.tensor_tensor(out=o2, in0=pv, in1=dtds[:, hs].unsqueeze(2).to_broadcast([128, 16, 64]), op=ALU.mult),
                     reads=rpst + [r_sm], writes=[r_xdtd])

        def pre_c(i, c, pbi):
            b = i % 2
            cols = slice(c * 128, (c + 1) * 128)
            rl = [r_ld[b]]
            dv, av, dtv, dA, dsv, eav, dtds, lv, dAh, dAl, dAt = smset[pbi]
            r_sm = r_smset[pbi]
            xdt, xdtd, btok = xdt2[pbi], xdtd2[pbi], btok2[pbi]
            r_xdt, r_xdtd, r_btok = r_xdt2[pbi], r_xdtd2[pbi], r_btok2[pbi]
            pst, rpst = tslots[cnt["t"] % 2]
            cnt["t"] += 1
            pb = pst.bitcast(BF16)
            for g in range(8):
                P.op("pe", lambda e, g=g, pb=pb: e.transpose(pb[:, g * 128:(g + 1) * 128], bt_t[b][:, g, cols], identb),
                     reads=rl + [rW], writes=rpst, inc=(g == 7))
            P.op("act", lambda e, pb=pb: e.activation(out=btok, in_=pb, func=AF.Copy), reads=rpst, writes=[r_btok])

        def groups(i, c, pbi, hooks):
            b = i % 2
            cols = slice(c * 128, (c + 1) * 128)
            rl = [r_ld[b]]
            dv, av, dtv, dA, dsv, eav, dtds, lv, dAh, dAl, dAt = smset[pbi]
            r_sm = r_smset[pbi]
            xdt, xdtd, btok = xdt2[pbi], xdtd2[pbi], btok2[pbi]
            r_xdt, r_xdtd, r_btok = r_xdt2[pbi], r_xdtd2[pbi], r_btok2[pbi]
            def g1(g):
                rr = g % NR
                Rv = Rg[rr].rearrange("p a b -> p (a b)")
                Rlv = Rl[rr].rearrange("p a b -> p (a b)")
                P.op("dve", lambda e, g=g, rr=rr: e.tensor_tensor(out=Rg[rr], in0=bc4(Ub), in1=dAh[:, 4 * g:4 * g + 4].unsqueeze(2).to_broadcast([128, 4, 128]), op=ALU.mult),
                     reads=[rW, r_sm], writes=[r_Rg[rr]])
                P.op("dve", lambda e, g=g, rr=rr: e.tensor_tensor(out=Rl[rr], in0=bc4(Ub), in1=dAl[:, 4 * g:4 * g + 4].unsqueeze(2).to_broadcast([128, 4, 128]), op=ALU.mult),
                     reads=[rW, r_sm], writes=[r_Rg[rr]])
                psl, rpsl = lslot
                psa, rpsa = aslot
                P.op("pe", lambda e, Rv=Rv: e.matmul(psl, MSb, Rv, start=True, stop=False), reads=[rW, r_Rg[rr]], writes=rpsl, inc=False)
                P.op("pe", lambda e, Rlv=Rlv: e.matmul(psl, MSb, Rlv, start=False, stop=True), reads=[rW, r_Rg[rr]], writes=rpsl)
                P.op("pe", lambda e, Rv=Rv: e.matmul(psa, ONESb, Rv, start=True, stop=False), reads=[rW, r_Rg[rr]], writes=rpsa, inc=False)
                P.op("pe", lambda e, Rlv=Rlv: e.matmul(psa, ONESb, Rlv, start=False, stop=True), reads=[rW, r_Rg[rr]], writes=rpsa)
                P.op("act", lambda e, rr=rr: e.activation(out=Lx[rr].rearrange("p a b -> p (a b)"), in_=psl, func=AF.Exp), reads=rpsl, writes=[r_Lx[rr]])
                P.op("act", lambda e, rr=rr: e.activation(out=Ex[rr].rearrange("p a b -> p (a b)"), in_=psa, func=AF.Exp), reads=rpsa, writes=[r_Ex[rr]])
                psc, rpsc = cslot
                P.op("pe", lambda e, g=g: e.matmul(psc[:, 0:128], bt_t[b][:, g, cols], ct_t[b][:, g, cols], start=True, stop=True),
                     reads=rl, writes=rpsc)
                P.op("dve", lambda e, rr=rr: e.tensor_tensor(out=cbm[rr], in0=psc[:, 0:128], in1=U, op=ALU.mult), reads=rpsc + [rW], writes=[r_cbm[rr]])
                P.op("dve", lambda e, rr=rr: e.tensor_tensor(out=Mg[rr], in0=Lx[rr], in1=bc4(cbm[rr]), op=ALU.mult),
                     reads=[r_Lx[rr], r_cbm[rr]], writes=[r_Mg[rr]])
                P.op("dve", lambda e, rr=rr, g=g: e.tensor_tensor(out=Cd[rr], in0=Ex[rr], in1=bc4(ct_t[b][:, g, cols]), op=ALU.mult),
                     reads=[r_Ex[rr]] + rl, writes=[r_Cd[rr]])

            def g2(g):
                rr = g % NR
                psy, rpsy = yslot
                for mm in range(2):
                    m = 2 * g + mm
                    mc = slice(mm * 128, (mm + 1) * 128)
                    P.op("pe", lambda e, m=m, mc=mc: e.matmul(psy[:, mc], diagD[:, m, :], xs_t[b][:, m, cols], start=True, stop=False),
                         reads=rl + [rW], writes=rpsy, inc=False)
                    for hh in range(2):
                        h = 2 * m + hh
                        hc = slice(h * 64, (h + 1) * 64)
                        pr = slice(hh * 64, (hh + 1) * 64)
                        P.op("pe", lambda e, mc=mc, hc=hc, pr=pr, rr=rr, mm=mm, hh=hh: e.matmul(
                            psy[pr, mc], xdt[:, hc], Mg[rr][:, 2 * mm + hh, :], start=False, stop=False),
                            reads=[r_xdt, r_Mg[rr]], writes=rpsy, inc=False)
                        P.op("pe", lambda e, mc=mc, hc=hc, pr=pr, rr=rr, mm=mm, hh=hh: e.matmul(
                            psy[pr, mc], entb[:, hc], Cd[rr][:, 2 * mm + hh, :], start=False, stop=(hh == 1)),
                            reads=[r_entb[g], r_Cd[rr]], writes=rpsy, inc=(hh == 1 and mm == 1))
                P.op("dve", lambda e, g=g: e.tensor_tensor(out=v[:, 2 * g:2 * g + 2, cols], in0=psy[:, 0:256].rearrange("p (a b) -> p a b", a=2),
                                                           in1=zs_t[b][:, 2 * g:2 * g + 2, cols], op=ALU.mult),
                     reads=rpsy + rl, writes=[r_v[g]])
                pss, rpss = sslot
                gs = slice(g * 256, (g + 1) * 256)
                P.op("pe", lambda e, g=g, gs=gs: e.matmul(pss[:, 0:256], btok[:, g * 128:(g + 1) * 128], xdtd[:, gs], start=True, stop=True),
                     reads=[r_btok, r_xdtd], writes=rpss)
                ev = ent[:, gs].rearrange("p (h d) -> p h d", h=4)
                P.op("dve", lambda e, g=g, ev=ev: e.tensor_tensor(out=ev, in0=ev, in1=eav[:, 4 * g:4 * g + 4].unsqueeze(2).to_broadcast([128, 4, 64]), op=ALU.mult),
                     reads=[r_ent[g], r_sm], writes=[r_ent[g]])
                P.op("dve", lambda e, gs=gs: e.tensor_tensor(out=ent[:, gs], in0=ent[:, gs], in1=pss[:, 0:256], op=ALU.add),
                     reads=[r_ent[g]] + rpss, writes=[r_ent[g]])
                P.op("act", lambda e, gs=gs: e.activation(out=entb[:, gs], in_=ent[:, gs], func=AF.Copy), reads=[r_ent[g]], writes=[r_entb[g]])

            g1(0)
            for g in range(8):
                if g + 1 < 8:
                    g1(g + 1)
                g2(g)
                if g in hooks:
                    hooks[g]()

        def tail(i):
            b = i % 2
            P.op("act", lambda e: e.activation(out=sqv.rearrange("p c t -> p (c t)"), in_=v.rearrange("p c t -> p (c t)"), func=AF.Square),
                 reads=r_v, writes=[r_sqv])
            for g in range(8):
                k2 = g % 2
                self.rstd_from_sq(sqv[:, 2 * g:2 * g + 2, :], r_sqv, 2, T, dtslot, msg, r_msg, rsg[k2], r_rsg[k2], 256.0)
                for mm in range(2):
                    m = 2 * g + mm
                    P.op("dve", lambda e, m=m, k2=k2: e.scalar_tensor_tensor(
                        out=vn[:, m, :], in0=v[:, m, :], scalar=NG[:, m:m + 1], in1=rsg[k2], op0=ALU.mult, op1=ALU.mult),
                        reads=[r_v[g], r_rsg[k2], rW], writes=[r_vn[g]])
            for m2 in range(4):
                ps, rps = tslots[cnt["t"] % 2]
                cnt["t"] += 1
                for hh in range(2):
                    mo = 2 * m2 + hh
                    for kc in range(16):
                        P.op("pe", lambda e, kc=kc, mo=mo, hh=hh, ps=ps: e.matmul(
                            ps[:, hh * T:(hh + 1) * T], Wout[:, kc, mo * 128:(mo + 1) * 128], vn[:, kc, :],
                            start=(kc == 0), stop=(kc == 15)), reads=[rW, r_vn[kc // 2]], writes=rps, inc=(kc == 15 and hh == 1))
                ho = hout[:, 2 * m2:2 * m2 + 2, :].rearrange("p c t -> p (c t)")
                so = sq2[:, 2 * m2:2 * m2 + 2, :].rearrange("p c t -> p (c t)")
                P.op("act", lambda e, ho=ho, ps=ps: e.activation(out=ho, in_=ps, func=AF.Copy), reads=rps, writes=r_hout[2 * m2:2 * m2 + 2])
                P.op("act", lambda e, so=so, ps=ps: e.activation(out=so, in_=ps, func=AF.Square), reads=rps, writes=[r_sq2])
            self.post_norm_residual(layer, 1, hout, r_hout, sq2, r_sq2, xt[b], r_xt[b], T, cslot, ms, r_ms, rstd, r_rstd)
            P.dma("sp", dst.ap[:, :, i * T:(i + 1) * T], xt[b], reads=[r_xt[b]], writes=dst.blocks(i * T, T))

        chunks = [(i, c) for i in range(NT) for c in range(2)]
        load(0)
        pre_a(0, 0, 0)
        pre_b(0, 0, 0)
        pre_c(0, 0, 0)
        for n, (i, c) in enumerate(chunks):
            if c == 0 and i + 1 < NT:
                load(i + 1)
            hooks = {}
            if n + 1 < len(chunks):
                ni, ncc = chunks[n + 1]
                nb = (n + 1) % 2
                hooks[1] = (lambda ni=ni, ncc=ncc, nb=nb: pre_a(ni, ncc, nb))
                hooks[3] = (lambda ni=ni, ncc=ncc, nb=nb: pre_b(ni, ncc, nb))
                hooks[5] = (lambda ni=ni, ncc=ncc, nb=nb: pre_c(ni, ncc, nb))
            groups(i, c, n % 2, hooks)
            if c == 1:
                tail(i)
        P.barrier()


def x_to_dev(xb):
    L = xb.shape[0]
    return np.ascontiguousarray(xb.T.reshape(8, 128, L).transpose(1, 0, 2))


def x_from_dev(xd):
    L = xd.shape[2]
    return np.ascontiguousarray(xd.transpose(1, 0, 2).reshape(1024, L).T)


def lay_norm_g(norm_g):
    return np.ascontiguousarray(norm_g.reshape(4, 4, 8, 128).transpose(3, 0, 1, 2).reshape(128, 128))


def lay_rows(w):
    K, N = w.shape
    return np.ascontiguousarray(w.reshape(K // 128, 128, N).transpose(1, 0, 2))


def lay_vec(v):
    return np.ascontiguousarray(v.reshape(8, 128).T)


def lay_s5(lam_re, lam_im, log_step, b_re, b_im, c_re, c_im, d):
    f = np.float32
    par = np.zeros((128, 3, 32), f)
    l4 = lam_re.reshape(32, 2, 64)
    par[:, 0, :] = l4.transpose(1, 2, 0).reshape(128, 32)
    par[:, 1, :] = lam_im.reshape(32, 2, 64).transpose(1, 2, 0).reshape(128, 32)
    par[:, 2, :] = np.repeat(log_step.reshape(32, 2, 1), 64, axis=2).transpose(1, 2, 0).reshape(128, 32)
    par_bc = np.stack([lam_re.reshape(4096), lam_im.reshape(4096), np.repeat(log_step, 64)]).astype(f)

    def bpad(b):
        out = np.zeros((8, 16, 32, 2, 64), f)
        bb = b.reshape(8, 4, 2, 64, 16)
        for kc in range(8):
            for j in range(4):
                g8 = j * 2
                for gg in range(2):
                    out[g8 + gg, :, kc * 4 + j, gg, :] = bb[kc, j, gg].T
        return np.ascontiguousarray(out.reshape(128, 32, 128))

    def cpad(c):
        out = np.zeros((2, 64, 32, 8, 16), f)
        for g in range(64):
            out[g % 2, :, g // 2, g % 8, :] = c[g].T
        return np.ascontiguousarray(out.reshape(128, 32, 128))

    return {"par": par, "par_bc": np.ascontiguousarray(par_bc), "bpad_re": bpad(b_re), "bpad_im": bpad(b_im),
            "cpad_re": cpad(c_re), "cpad_im": cpad(c_im), "dvec": lay_vec(d).astype(f)}


def ssd_consts():
    j = np.arange(128)
    ident = np.eye(128, dtype=np.float32)
    U = (j[:, None] <= j[None, :]).astype(np.float32)
    MS = (j[:, None] > j[None, :]).astype(np.float32)
    ones = np.ones((128, 128), np.float32)
    return np.ascontiguousarray(np.stack([ident, U, MS, ones], axis=1))


def lay_ssd(conv_w, conv_b, dt_bias, a_log, d_skip, norm_g):
    f = np.float32
    cw = np.zeros((128, 5, 32), f)
    for k in range(4):
        cw[:, k, :] = conv_w[k].reshape(32, 128).T
    cw[:, 4, :] = conv_b.reshape(32, 128).T
    hp = np.stack([dt_bias, a_log]).astype(f)
    dch = np.repeat(d_skip, 64).reshape(16, 128).T
    ng = norm_g.reshape(16, 128).T
    return {"cw": np.ascontiguousarray(cw), "hp_bc": np.ascontiguousarray(hp),
            "dch": np.ascontiguousarray(dch).astype(f), "ngv": np.ascontiguousarray(ng).astype(f)}


def lay_rg_vec(conv_w, conv_b, b_a, b_x, lam):
    vs = [conv_w[0], conv_w[1], conv_w[2], conv_w[3], conv_b, b_a, b_x, lam]
    return np.ascontiguousarray(np.stack([lay_vec(v) for v in vs], axis=1)).astype(np.float32)


def add_layer(B, layer, src, mid, dst):
    L = B.L
    kind = layer % 3
    pre = "l%d_" % layer
    if kind == 0:
        w_in = B.inp(pre + "w_in", [128, 8, 2048])
        w_a = B.inp(pre + "w_a", [16, 64, 64])
        w_x = B.inp(pre + "w_x", [16, 64, 64])
        vec = B.inp(pre + "vec", [128, 8, 8])
        w_out = B.inp(pre + "w_out", [128, 8, 1024])
        B.rglru_phase(layer, src, mid, w_in, w_a, w_x, vec, w_out)
    elif kind == 1:
        w_in = B.inp(pre + "w_in", [128, 8, 6176])
        w_out = B.inp(pre + "w_out", [128, 16, 1024])
        cw = B.inp(pre + "cw", [128, 5, 32])
        hp_bc = B.inp(pre + "hp_bc", [2, 32])
        dch = B.inp(pre + "dch", [128, 16])
        ngv = B.inp(pre + "ngv", [128, 16])
        consts = B.inp(pre + "consts", [128, 4, 128])
        zs = DramX(B.scratch(pre + "zs", [128, 16, L], BF16), L)
        xs = DramX(B.scratch(pre + "xs", [128, 16, L], BF16), L)
        bt = DramX(B.scratch(pre + "bt", [128, 8, L], BF16), L)
        ct = DramX(B.scratch(pre + "ct", [128, 8, L], BF16), L)
        dtt = DramX(B.scratch(pre + "dtt", [L // 128, 128, 32], F32), L)
        B.ssd_a_phase(layer, src, w_in, cw, zs, xs, bt, ct, dtt)
        B.ssd_b_phase(layer, src, mid, zs, xs, bt, ct, dtt, w_out, consts, hp_bc, dch, ngv)
    else:
        w_in = B.inp(pre + "w_in", [128, 8, 1024])
        w_out = B.inp(pre + "w_out", [128, 8, 2048])
        par = B.inp(pre + "par", [128, 3, 32])
        par_bc = B.inp(pre + "par_bc", [3, 4096])
        bre = B.inp(pre + "bpad_re", [128, 32, 128])
        bim = B.inp(pre + "bpad_im", [128, 32, 128])
        cre = B.inp(pre + "cpad_re", [128, 32, 128])
        cim = B.inp(pre + "cpad_im", [128, 32, 128])
        dvec = B.inp(pre + "dvec", [128, 8])
        su = DramX(B.scratch(pre + "su", [128, 8, L]), L)
        sb = DramX(B.scratch(pre + "sb", [128, 8, L], BF16), L)
        sy = DramX(B.scratch(pre + "sy", [128, 8, L], BF16), L)
        B.s5a_phase(layer, src, w_in, su, sb)
        B.s5b_phase(layer, su, sb, sy, par, par_bc, bre, bim, cre, cim, dvec)
        B.s5c_phase(layer, src, mid, w_out, sy)
    if ONLY_MIXER:
        return
    w1 = B.inp(pre + "w1", [128, 8, 4096])
    w2 = B.inp(pre + "w2", [128, 32, 1024])
    B.mlp_phase(layer, mid, dst, w1, w2)


def layer_params(layer, inp):
    kind = layer % 3
    j = layer // 3
    pre = "l%d_" % layer
    d = {}
    f = np.float32
    if kind == 0:
        d["w_in"] = lay_rows(inp["rg_w_in"][j])
        d["w_a"] = np.ascontiguousarray(inp["rg_w_a"][j])
        d["w_x"] = np.ascontiguousarray(inp["rg_w_x"][j])
        d["vec"] = lay_rg_vec(inp["rg_conv_w"][j], inp["rg_conv_b"][j], inp["rg_b_a"][j], inp["rg_b_x"][j], inp["rg_lam"][j])
        d["w_out"] = lay_rows(inp["rg_w_out"][j])
    elif kind == 1:
        d["w_in"] = lay_rows(inp["ssd_w_in"][j])
        d["w_out"] = lay_rows(inp["ssd_w_out"][j])
        d.update(lay_ssd(inp["ssd_conv_w"][j], inp["ssd_conv_b"][j], inp["ssd_dt_bias"][j], inp["ssd_a_log"][j],
                         inp["ssd_d"][j], inp["ssd_norm_g"][j]))
        d["consts"] = ssd_consts()
    else:
        d["w_in"] = lay_rows(inp["s5_w_in"][j])
        d["w_out"] = lay_rows(inp["s5_w_out"][j])
        d.update(lay_s5(inp["s5_lam_re"][j], inp["s5_lam_im"][j], inp["s5_log_step"][j], inp["s5_b_re"][j], inp["s5_b_im"][j],
                        inp["s5_c_re"][j], inp["s5_c_im"][j], inp["s5_d"][j]))
    d["w1"] = lay_rows(inp["mlp_w1"][layer])
    d["w2"] = lay_rows(inp["mlp_w2"][layer])
    return {pre + k: np.ascontiguousarray(v, dtype=f) for k, v in d.items()}


def build_program(L, layers):
    B = Builder(L)
    xin = DramX(B.inp("x", [128, 8, L]), L)
    xout = DramX(B.outp("y", [128, 8, L]), L)
    xres = DramX(B.scratch("xres", [128, 8, L]), L)
    n = len(layers)
    for idx, layer in enumerate(layers):
        src = xin if idx == 0 else xres
        dst = xout if idx == n - 1 else xres
        add_layer(B, layer, src, xres, dst)
    B.P.barrier(["sp"])
    B.P.emit()
    return B


ONLY_MIXER = False
LAUNCH_GROUPS = [[0, 1, 2, 3]]


def kernel(**inp):
    inp = {k: np.asarray(v) for k, v in inp.items()}
    x = inp["x"]
    nb, L, _ = x.shape
    xs = [x_to_dev(np.asarray(x[b], dtype=np.float32)) for b in range(nb)]
    g = lay_norm_g(inp["norm_g"].astype(np.float32))
    for layers in LAUNCH_GROUPS:
        B = build_program(L, layers)
        common = {"norm_g": g}
        for layer in layers:
            common.update(layer_params(layer, inp))
        common = {k: v for k, v in common.items() if k in B.ext}
        in_maps = [dict(common, x=xs[b]) for b in range(nb)]
        res = run_bass_kernel_spmd(B.nc, in_maps, core_ids=list(range(nb)))
        xs = [np.asarray(res.results[b]["y"]) for b in range(nb)]
    out = np.stack([x_from_dev(xd) for xd in xs]).astype(np.float32)
    return out
```
